# Optimizing a Trainium2 kernel written in Bass

```python
import jax, jax.numpy as jnp
from jax import lax
import numpy as np

D_MODEL = 1024
BATCH = 1
SEQ = 16384
DEPTH = 2

HEAD_DIM = 64
D_FF = 2816
D_PLE = 256
ATTN_Q_HEADS = 8
ATTN_KV_HEADS = 2
ATTN_GROUP = ATTN_Q_HEADS // ATTN_KV_HEADS
WINDOW = 128
ROPE_THETA = 500000.0
ROPE_DIM = HEAD_DIM // 4
MLSTM_HEADS = 4
MLSTM_CHUNK = 64
MLSTM_CONV = 4
GATE_CAP = 15.0
RWKV_HEADS = 4
RWKV_W_RANK = 64
RWKV_A_RANK = 64
RWKV_G_RANK = 128
RWKV_GN_EPS = 64e-5
NORM_EPS = 1e-6
NEG_INF = -1e30

ATTN_W = ATTN_Q_HEADS * HEAD_DIM
KV_W = ATTN_KV_HEADS * HEAD_DIM
MLSTM_W = MLSTM_HEADS * HEAD_DIM
RWKV_W = RWKV_HEADS * HEAD_DIM
D_MIX = ATTN_W + MLSTM_W + RWKV_W
ATTN_COLS = ATTN_W + 2 * KV_W
MLSTM_COLS = 4 * MLSTM_W + 2 * MLSTM_HEADS
RWKV_COLS = 3 * RWKV_W + RWKV_W_RANK + RWKV_A_RANK + RWKV_G_RANK
D_IN = ATTN_COLS + MLSTM_COLS + RWKV_COLS

kernel_name = "hybrid_mlstm_swa_rwkv7_block"


def rms_norm(x, gain):
    xf = x.astype(jnp.float32)
    y = xf * lax.rsqrt(jnp.mean(xf * xf, axis=-1, keepdims=True) + NORM_EPS)
    return (y * gain.astype(jnp.float32)).astype(x.dtype)


def swiglu(x, w_in, w_out):
    gate, up = jnp.split(x @ w_in, 2, axis=-1)
    return (jax.nn.silu(gate) * up) @ w_out


def rope_tables(positions):
    inv_freq = ROPE_THETA ** (-jnp.arange(0, ROPE_DIM, 2, dtype=jnp.float32) / ROPE_DIM)
    ang = positions.astype(jnp.float32)[..., None] * inv_freq
    return jnp.cos(ang)[:, :, None, :], jnp.sin(ang)[:, :, None, :]


def partial_rope(x, cos, sin):
    half = ROPE_DIM // 2
    x1 = x[..., :half].astype(jnp.float32)
    x2 = x[..., half:ROPE_DIM].astype(jnp.float32)
    rot = jnp.concatenate([x1 * cos - x2 * sin, x2 * cos + x1 * sin], axis=-1).astype(x.dtype)
    return jnp.concatenate([rot, x[..., ROPE_DIM:]], axis=-1)


def sliding_window_gqa(q, k, v, sinks):
    B, S = q.shape[:2]
    nb = S // WINDOW
    qb = q.reshape(B, nb, WINDOW, ATTN_KV_HEADS, ATTN_GROUP, HEAD_DIM)

    def band_keys(t):
        tb = t.reshape(B, nb, WINDOW, ATTN_KV_HEADS, HEAD_DIM)
        prev = jnp.pad(tb[:, :-1], ((0, 0), (1, 0), (0, 0), (0, 0), (0, 0)))
        return jnp.concatenate([prev, tb], axis=2)

    kb, vb = band_keys(k), band_keys(v)
    scores = jnp.einsum('bnqhgd,bnkhd->bnhgqk', qb, kb).astype(jnp.float32) * (HEAD_DIM ** -0.5)
    qi = jnp.arange(WINDOW)[:, None]
    kj = jnp.arange(2 * WINDOW)[None, :]
    lag = qi + WINDOW - kj
    in_band = (lag >= 0) & (lag < WINDOW)
    blk = jnp.arange(nb)[:, None, None]
    valid = in_band[None] & ((blk > 0) | (kj >= WINDOW)[None])
    scores = jnp.where(valid[None, :, None, None], scores, NEG_INF)
    sink = jnp.broadcast_to(
        sinks.astype(jnp.float32).reshape(1, 1, ATTN_KV_HEADS, ATTN_GROUP, 1, 1),
        scores.shape[:-1] + (1,))
    probs = jax.nn.softmax(jnp.concatenate([scores, sink], axis=-1), axis=-1)[..., :-1]
    out = jnp.einsum('bnhgqk,bnkhd->bnqhgd', probs.astype(v.dtype), vb)
    return out.reshape(B, S, ATTN_W)


def causal_depthwise_conv(x, w):
    return lax.conv_general_dilated(
        x, w[:, None, :].astype(x.dtype), window_strides=(1,),
        padding=((w.shape[0] - 1, 0),), dimension_numbers=('NWC', 'WIO', 'NWC'),
        feature_group_count=x.shape[-1])


def mlstm_chunkwise(q, k, v, i_pre, f_pre):
    B, S, H, D = q.shape
    L = MLSTM_CHUNK
    nc = S // L

    def chunks(t):
        return jnp.moveaxis(t.reshape((B, nc, L, H) + t.shape[3:]), 3, 1)

    q = chunks(q * (D ** -0.5))
    k = chunks(k)
    v = chunks(v)
    ig = chunks(i_pre)
    g = jnp.cumsum(chunks(jax.nn.log_sigmoid(f_pre)), axis=-1)
    g_tot = g[..., -1]

    a = g_tot[..., None] - g + ig
    m_loc = jnp.max(a, axis=-1)
    w_loc = jnp.exp(a - m_loc[..., None])
    c_loc = jnp.einsum('bhcl,bhclv,bhclk->bhcvk', w_loc, v, k)
    n_loc = jnp.einsum('bhcl,bhclk->bhck', w_loc, k)

    def carry_state(state, xs):
        c, n, m = state
        gt, ml, cl, nl = xs
        m_new = jnp.maximum(gt + m, ml)
        s_old = jnp.exp(gt + m - m_new)
        s_loc = jnp.exp(ml - m_new)
        c_new = s_old[..., None, None] * c + s_loc[..., None, None] * cl
        n_new = s_old[..., None] * n + s_loc[..., None] * nl
        return (c_new, n_new, m_new), (c, n, m)

    init = (jnp.zeros((B, H, D, D), jnp.float32), jnp.zeros((B, H, D), jnp.float32),
            jnp.full((B, H), NEG_INF, jnp.float32))
    xs = tuple(jnp.moveaxis(t, 2, 0) for t in (g_tot, m_loc, c_loc, n_loc))
    _, (c_prev, n_prev, m_prev) = lax.scan(carry_state, init, xs)
    c_prev = jnp.moveaxis(c_prev, 0, 2)
    n_prev = jnp.moveaxis(n_prev, 0, 2)
    m_prev = jnp.moveaxis(m_prev, 0, 2)

    causal = jnp.tril(jnp.ones((L, L), dtype=bool))
    d_intra = jnp.where(causal, g[..., :, None] - g[..., None, :] + ig[..., None, :], NEG_INF)
    inter = g + m_prev[..., None]
    m_row = jnp.maximum(inter, jnp.max(d_intra, axis=-1))
    w_intra = jnp.exp(d_intra - m_row[..., None])
    w_inter = jnp.exp(inter - m_row)
    s = jnp.einsum('bhcld,bhcsd->bhcls', q, k) * w_intra
    num = (jnp.einsum('bhcls,bhcsv->bhclv', s, v)
           + w_inter[..., None] * jnp.einsum('bhcvk,bhclk->bhclv', c_prev, q))
    den = jnp.sum(s, axis=-1) + w_inter * jnp.einsum('bhck,bhclk->bhcl', n_prev, q)
    h = num / jnp.maximum(jnp.abs(den), jnp.exp(-m_row))[..., None]
    return jnp.moveaxis(h, 1, 3).reshape(B, S, H, D)


def rwkv7_scan(r, decay, k, v, a_vec, b_vec):
    B, S, H, D = r.shape

    def step(state, xs):
        r_t, w_t, k_t, v_t, a_t, b_t = xs
        sa = jnp.einsum('bhvk,bhk->bhv', state, a_t)
        state = (state * w_t[:, :, None, :] + sa[..., None] * b_t[:, :, None, :]
                 + v_t[..., None] * k_t[:, :, None, :])
        return state, jnp.einsum('bhvk,bhk->bhv', state, r_t)

    xs = tuple(jnp.moveaxis(t, 1, 0) for t in (r, decay, k, v, a_vec, b_vec))
    _, y = lax.scan(step, jnp.zeros((B, H, D, D), jnp.float32), xs)
    return jnp.moveaxis(y, 0, 1)


def rwkv7_time_mix(u, mu, w0, w_up, a0, a_up, g_up, k_k, k_a, r_k, ln_w, ln_b):
    f32 = jnp.float32
    B, S, _ = u.shape
    u = u.astype(f32)
    prev = jnp.pad(u[:, :-1], ((0, 0), (1, 0), (0, 0)))
    u = u + (prev - u) * mu.astype(f32)
    r, k, v, xw, xa, xg = jnp.split(
        u, [RWKV_W, 2 * RWKV_W, 3 * RWKV_W, 3 * RWKV_W + RWKV_W_RANK,
            3 * RWKV_W + RWKV_W_RANK + RWKV_A_RANK], axis=-1)
    log_w = -jax.nn.softplus(-(w0.astype(f32) + jnp.tanh(xw) @ w_up.astype(f32))) - 0.5
    decay = jnp.exp(-jnp.exp(log_w))
    a = jax.nn.sigmoid(a0.astype(f32) + xa @ a_up.astype(f32))
    g = jax.nn.sigmoid(xg) @ g_up.astype(f32)

    def heads(t):
        return t.reshape(B, S, RWKV_HEADS, HEAD_DIM)

    kk = heads(k * k_k.astype(f32))
    kk = kk * lax.rsqrt(jnp.maximum(jnp.sum(kk * kk, axis=-1, keepdims=True), 1e-24))
    k = k * (1.0 + (a - 1.0) * k_a.astype(f32))
    r_h, k_h, v_h, a_h = heads(r), heads(k), heads(v), heads(a)
    y = rwkv7_scan(r_h, heads(decay), k_h, v_h, -kk, kk * a_h)
    mean = jnp.mean(y, axis=-1, keepdims=True)
    var = jnp.mean(jnp.square(y - mean), axis=-1, keepdims=True)
    y = ((y - mean) * lax.rsqrt(var + RWKV_GN_EPS) * ln_w.astype(f32).reshape(RWKV_HEADS, HEAD_DIM)
         + ln_b.astype(f32).reshape(RWKV_HEADS, HEAD_DIM))
    y = y + jnp.sum(r_h * k_h * r_k.astype(f32), axis=-1, keepdims=True) * v_h
    return y.reshape(B, S, RWKV_W) * g


def token_mixing(h, cos, sin, w_in, attn_sinks, mlstm_conv, mlstm_i_bias, mlstm_f_bias, mlstm_norm,
                 rwkv_mu, rwkv_w0, rwkv_w_up, rwkv_a0, rwkv_a_up, rwkv_g_up, rwkv_k_k, rwkv_k_a,
                 rwkv_r_k, rwkv_ln_w, rwkv_ln_b, w_out):
    f32 = jnp.float32
    B, S, _ = h.shape
    proj = h @ w_in
    attn_p, mlstm_p, rwkv_p = jnp.split(proj, [ATTN_COLS, ATTN_COLS + MLSTM_COLS], axis=-1)

    q_at, k_at, v_at = jnp.split(attn_p, [ATTN_W, ATTN_W + KV_W], axis=-1)
    q_at = partial_rope(q_at.reshape(B, S, ATTN_Q_HEADS, HEAD_DIM), cos, sin)
    k_at = partial_rope(k_at.reshape(B, S, ATTN_KV_HEADS, HEAD_DIM), cos, sin)
    v_at = v_at.reshape(B, S, ATTN_KV_HEADS, HEAD_DIM)
    y_attn = sliding_window_gqa(q_at, k_at, v_at, attn_sinks)

    qk_m, v_m, o_m, i_m, f_m = jnp.split(
        mlstm_p, [2 * MLSTM_W, 3 * MLSTM_W, 4 * MLSTM_W, 4 * MLSTM_W + MLSTM_HEADS], axis=-1)
    qk_m = jax.nn.silu(causal_depthwise_conv(qk_m, mlstm_conv))
    q_m, k_m = jnp.split(qk_m, 2, axis=-1)

    def m_heads(t):
        return t.astype(f32).reshape(B, S, MLSTM_HEADS, HEAD_DIM)

    i_pre = GATE_CAP * jnp.tanh((i_m.astype(f32) + mlstm_i_bias.astype(f32)) / GATE_CAP)
    f_pre = GATE_CAP * jnp.tanh((f_m.astype(f32) + mlstm_f_bias.astype(f32)) / GATE_CAP)
    h_m = mlstm_chunkwise(m_heads(q_m), m_heads(k_m), m_heads(v_m), i_pre, f_pre)
    h_m = (h_m * lax.rsqrt(jnp.mean(h_m * h_m, axis=-1, keepdims=True) + NORM_EPS)
           * mlstm_norm.astype(f32).reshape(MLSTM_HEADS, HEAD_DIM))
    y_mlstm = jax.nn.sigmoid(o_m.astype(f32)) * h_m.reshape(B, S, MLSTM_W)

    y_rwkv = rwkv7_time_mix(rwkv_p, rwkv_mu, rwkv_w0, rwkv_w_up, rwkv_a0, rwkv_a_up, rwkv_g_up,
                            rwkv_k_k, rwkv_k_a, rwkv_r_k, rwkv_ln_w, rwkv_ln_b)

    y = jnp.concatenate([y_attn, y_mlstm.astype(h.dtype), y_rwkv.astype(h.dtype)], axis=-1)
    return y @ w_out


def setup_inputs(seed: int = 0) -> dict:
    key = jax.random.key(seed)
    ks = iter(jax.random.split(key, 48))
    f32 = jnp.float32

    def normal(shape, scale):
        return scale * jax.random.normal(next(ks), shape, f32)

    def uniform(shape, lo, hi):
        return jax.random.uniform(next(ks), shape, f32, lo, hi)

    def gain(width):
        return 1.0 + normal((DEPTH, width), 0.05)

    x = normal((BATCH, SEQ, D_MODEL), 1.0)
    p = normal((DEPTH, BATCH, SEQ, D_PLE), 1.0)
    start = jax.random.randint(next(ks), (BATCH, 1), 0, 4096, jnp.int32)
    positions = start + jnp.arange(SEQ, dtype=jnp.int32)[None, :]
    return {
        'x': x,
        'p': p,
        'positions': positions,
        'ln_ffn1_pre': gain(D_MODEL),
        'ln_ffn1_post': gain(D_MODEL),
        'w_ffn1_in': normal((DEPTH, D_MODEL, 2 * D_FF), D_MODEL ** -0.5),
        'w_ffn1_out': normal((DEPTH, D_FF, D_MODEL), D_FF ** -0.5),
        'ln_mix_pre': gain(D_MODEL),
        'w_in': normal((DEPTH, D_MODEL, D_IN), D_MODEL ** -0.5),
        'attn_sinks': normal((DEPTH, ATTN_Q_HEADS), 0.5),
        'mlstm_conv': normal((DEPTH, MLSTM_CONV, 2 * MLSTM_W), MLSTM_CONV ** -0.5),
        'mlstm_i_bias': normal((DEPTH, MLSTM_HEADS), 0.1),
        'mlstm_f_bias': uniform((DEPTH, MLSTM_HEADS), 3.0, 6.0),
        'mlstm_norm': gain(MLSTM_W),
        'rwkv_mu': uniform((DEPTH, RWKV_COLS), 0.0, 1.0),
        'rwkv_w0': uniform((DEPTH, RWKV_W), -1.5, 0.5),
        'rwkv_w_up': normal((DEPTH, RWKV_W_RANK, RWKV_W), 0.1),
        'rwkv_a0': normal((DEPTH, RWKV_W), 0.1),
        'rwkv_a_up': normal((DEPTH, RWKV_A_RANK, RWKV_W), 0.5 * RWKV_A_RANK ** -0.5),
        'rwkv_g_up': normal((DEPTH, RWKV_G_RANK, RWKV_W), RWKV_G_RANK ** -0.5),
        'rwkv_k_k': 0.85 + normal((DEPTH, RWKV_W), 0.05),
        'rwkv_k_a': 1.0 + normal((DEPTH, RWKV_W), 0.05),
        'rwkv_r_k': normal((DEPTH, RWKV_HEADS, HEAD_DIM), 0.1),
        'rwkv_ln_w': gain(RWKV_W),
        'rwkv_ln_b': normal((DEPTH, RWKV_W), 0.01),
        'w_out': normal((DEPTH, D_MIX, D_MODEL), D_MIX ** -0.5),
        'ln_mix_post': gain(D_MODEL),
        'ln_ffn2_pre': gain(D_MODEL),
        'ln_ffn2_post': gain(D_MODEL),
        'w_ffn2_in': normal((DEPTH, D_MODEL, 2 * D_FF), D_MODEL ** -0.5),
        'w_ffn2_out': normal((DEPTH, D_FF, D_MODEL), D_FF ** -0.5),
        'ln_ple_pre': gain(D_MODEL),
        'w_ple_gate': normal((DEPTH, D_MODEL, D_MODEL), D_MODEL ** -0.5),
        'w_ple_proj': normal((DEPTH, D_PLE, D_MODEL), D_PLE ** -0.5),
        'ln_ple_post': gain(D_MODEL),
    }


def reference(x, p, positions, ln_ffn1_pre, ln_ffn1_post, w_ffn1_in, w_ffn1_out, ln_mix_pre, w_in,
              attn_sinks, mlstm_conv, mlstm_i_bias, mlstm_f_bias, mlstm_norm, rwkv_mu, rwkv_w0,
              rwkv_w_up, rwkv_a0, rwkv_a_up, rwkv_g_up, rwkv_k_k, rwkv_k_a, rwkv_r_k, rwkv_ln_w,
              rwkv_ln_b, w_out, ln_mix_post, ln_ffn2_pre, ln_ffn2_post, w_ffn2_in, w_ffn2_out,
              ln_ple_pre, w_ple_gate, w_ple_proj, ln_ple_post):
    cos, sin = rope_tables(positions)
    for l in range(DEPTH):
        x = x + 0.5 * rms_norm(swiglu(rms_norm(x, ln_ffn1_pre[l]), w_ffn1_in[l], w_ffn1_out[l]),
                               ln_ffn1_post[l])
        mix = token_mixing(rms_norm(x, ln_mix_pre[l]), cos, sin, w_in[l], attn_sinks[l],
                           mlstm_conv[l], mlstm_i_bias[l], mlstm_f_bias[l], mlstm_norm[l],
                           rwkv_mu[l], rwkv_w0[l], rwkv_w_up[l], rwkv_a0[l], rwkv_a_up[l],
                           rwkv_g_up[l], rwkv_k_k[l], rwkv_k_a[l], rwkv_r_k[l], rwkv_ln_w[l],
                           rwkv_ln_b[l], w_out[l])
        x = x + rms_norm(mix, ln_mix_post[l])
        x = x + 0.5 * rms_norm(swiglu(rms_norm(x, ln_ffn2_pre[l]), w_ffn2_in[l], w_ffn2_out[l]),
                               ln_ffn2_post[l])
        gate = jax.nn.sigmoid(rms_norm(x, ln_ple_pre[l]) @ w_ple_gate[l])
        x = x + rms_norm(gate * (p[l] @ w_ple_proj[l]), ln_ple_post[l])
    return x
```

```python
import contextlib
import numpy as np
import concourse.bass as bass
import concourse.mybir as mybir
from concourse.bass_utils import run_bass_kernel_spmd

F32 = mybir.dt.float32
BF16 = mybir.dt.bfloat16
I32 = mybir.dt.int32
AF = mybir.ActivationFunctionType
ALU = mybir.AluOpType
AX = mybir.AxisListType

NCORES = 8
T = 2048
DM = 1024
DFF = 2816
DEPTH = 2


class Reg:
    __slots__ = ("name", "w", "r", "excl")

    def __init__(self, name):
        self.name = name
        self.w = None
        self.r = {}
        self.excl = False


class DSem:
    def __init__(self, key, h):
        self.key = key
        self.h = h
        self.count = 0


class Sched:
    CE = ("pe", "act", "dve", "pool")
    ENG = ("pe", "act", "dve", "pool", "sp")

    def __init__(self, nc, es):
        self.nc = nc
        self.es = es
        self.cnt = {e: 0 for e in self.CE}
        self.sem = {e: es.enter_context(nc.semaphore(f"c_{e}")) for e in self.CE}
        self.seen = {e: {} for e in self.ENG}
        self.nds = 0
        self.nreg = 0
        self.dsems = []
        self.engs = {"pe": nc.tensor, "act": nc.scalar, "dve": nc.vector, "pool": nc.gpsimd, "sp": nc.sync}
        self.ninst = 0
        self.free_dsems = []
        self.scope_stack = []

    def region(self, name=None):
        self.nreg += 1
        return Reg(name or f"r{self.nreg}")

    def regions(self, *shape):
        if len(shape) == 1:
            return [self.region() for _ in range(shape[0])]
        return [self.regions(*shape[1:]) for _ in range(shape[0])]

    def dsem(self, name=None):
        if name is None and self.free_dsems:
            d = self.free_dsems.pop()
        else:
            self.nds += 1
            nm = name or f"d{self.nds}"
            d = DSem("dma_" + nm, self.es.enter_context(self.nc.semaphore("ds_" + nm)))
            self.dsems.append(d)
        if name is None and self.scope_stack:
            self.scope_stack[-1].append(d)
        return d

    def _deps(self, eng, reads, writes):
        waits = {}

        def add(key, sem, val, raw):
            if key == eng and not (raw and eng != "pe"):
                return
            if self.seen[eng].get(key, 0) >= val:
                return
            if key not in waits or waits[key][1] < val:
                waits[key] = (sem, val)

        for r in reads:
            if r.w is not None:
                add(*r.w, True)
        for w in writes:
            if w.w is not None:
                add(*w.w, True)
            for key, (sem, val) in w.r.items():
                add(key, sem, val, False)
        for key, (sem, val) in waits.items():
            self.seen[eng][key] = val
        return list(waits.values())

    def _mark(self, tok, reads, writes):
        key, sem, val = tok
        for r in reads:
            if key not in r.r or r.r[key][1] < val:
                r.r[key] = (sem, val)
        for w in writes:
            w.w = tok
            w.r = {}

    def _emit(self, name, waits, fn, inc):
        e = self.engs[name]
        for sem, val in waits:
            e.wait_ge(sem, val)
            self.ninst += 1
        if fn is not None:
            ins = fn(e)
            ins.then_inc(inc[0], inc[1])
            self.ninst += 1

    def op(self, eng, fn, reads=(), writes=(), pe_wait=()):
        ex = [r for r in reads if r.excl]
        if ex:
            reads = [r for r in reads if not r.excl]
            writes = list(writes) + ex
        waits = self._deps(eng, reads, writes)
        for (key, sem, val) in pe_wait:
            if self.seen[eng].get(key, 0) < val:
                waits.append((sem, val))
                self.seen[eng][key] = val
        self.cnt[eng] += 1
        tok = (eng, self.sem[eng], self.cnt[eng])
        self._emit(eng, waits, fn, (self.sem[eng], 1))
        self._mark(tok, reads, writes)
        return tok

    def dma(self, queue, dsem, fn, reads=(), writes=(), inc=16):
        waits = self._deps(queue, reads, writes)
        dsem.count += inc
        tok = (dsem.key, dsem.h, dsem.count)
        self._emit(queue, waits, fn, (dsem.h, inc))
        self._mark(tok, reads, writes)
        return tok

    def wait_all(self, eng, regs):
        waits = self._deps(eng, regs, ())
        self._emit(eng, waits, None, None)

    def barrier(self):
        for eng in self.ENG:
            waits = []
            for o in self.CE:
                if o != eng and self.cnt[o] > self.seen[eng].get(o, 0):
                    waits.append((self.sem[o], self.cnt[o]))
                    self.seen[eng][o] = self.cnt[o]
            for d in self.dsems:
                if d.count > self.seen[eng].get(d.key, 0):
                    waits.append((d.h, d.count))
                    self.seen[eng][d.key] = d.count
            self._emit(eng, waits, None, None)


_VEC8 = ["ln_ffn1_pre", "ln_ffn1_post", "ln_mix_pre", "ln_mix_post", "ln_ffn2_pre", "ln_ffn2_post",
         "ln_ple_pre", "ln_ple_post", "rwkv_mu"]
_VEC2 = ["mlstm_norm", "rwkv_w0", "rwkv_a0", "rwkv_k_k", "rwkv_k_a", "rwkv_r_k", "rwkv_ln_w", "rwkv_ln_b"]


def param_layout():
    lay = {}
    off = 0
    for l in range(1):
        for n in _VEC8:
            lay[(n, l)] = off
            off += 8
        for n in _VEC2:
            lay[(n, l)] = off
            off += 2
        lay[("conv", l)] = off
        off += 16
        lay[("sink", l)] = off
        off += 4
        lay[("gbias", l)] = off
        off += 1
    for n in ["eps1", "eps4", "gneps", "invfreq", "one", "zero"]:
        lay[n] = off
        off += 1
    lay["_n"] = off
    return lay


LAY = param_layout()


def pack_params(inp, L):
    lay = LAY
    pv = np.zeros((128, lay["_n"]), np.float32)
    l = 0
    for n in _VEC8:
        pv[:, lay[(n, l)]:lay[(n, l)] + 8] = np.asarray(inp[n][L], np.float32).reshape(8, 128).T
    for n in _VEC2:
        pv[:, lay[(n, l)]:lay[(n, l)] + 2] = np.asarray(inp[n][L], np.float32).reshape(2, 128).T
    conv = np.asarray(inp["mlstm_conv"][L], np.float32)
    for tile in range(4):
        for tap in range(4):
            pv[:, lay[("conv", l)] + tile * 4 + tap] = conv[tap, tile * 128:(tile + 1) * 128]
    sk = np.asarray(inp["attn_sinks"][L], np.float32)
    for i in range(4):
        pv[0:64, lay[("sink", l)] + i] = sk[i]
        pv[64:128, lay[("sink", l)] + i] = sk[4 + i]
    pv[0:4, lay[("gbias", l)]] = np.asarray(inp["mlstm_i_bias"][L], np.float32)
    pv[32:36, lay[("gbias", l)]] = np.asarray(inp["mlstm_f_bias"][L], np.float32)
    pv[:, lay["eps1"]] = 1e-6
    pv[:, lay["eps4"]] = 4e-6
    pv[:, lay["gneps"]] = 64e-5
    inv = (500000.0 ** (-np.arange(0, 16, 2, dtype=np.float32) / 16.0)).astype(np.float32)
    for p_ in range(128):
        d = p_ % 64
        pv[p_, lay["invfreq"]] = inv[d % 8] if d < 16 else 0.0
    pv[:, lay["one"]] = 1.0
    return pv


CM = {"ident": 0, "blk": 1, "rowsel": 2, "rmT": 6, "ncur": 7, "nprev": 8, "id2": 9, "ones": 10, "perm": 11, "rm": 12, "_n": 16}
NF32 = 10
NEG = -30000.0


def const_mats():
    cm = np.zeros((128, CM["_n"], 128), np.float32)
    cm[:, CM["ident"], :] = np.eye(128)
    cm[:, CM["ones"], :] = 1.0
    blk = np.zeros((128, 128))
    blk[:64, :64] = 1
    blk[64:, 64:] = 1
    cm[:, CM["blk"], :] = blk
    P = np.zeros((128, 128))
    for hb in (0, 64):
        for i in range(8):
            P[hb + i + 8, hb + i] = -1.0
            P[hb + i, hb + i + 8] = 1.0
    cm[:, CM["perm"], :] = P
    for h in range(4):
        for k in range(128):
            if k % 32 == h:
                cm[k, CM["rowsel"] + h, :] = 1.0
    s_ = (np.arange(128) % 64)[:, None]
    t_ = np.arange(64)[None, :]
    strict = (s_ < t_).astype(np.float32)
    incl = (s_ <= t_).astype(np.float32)
    rm = np.stack([strict, strict, strict, strict, incl, incl, incl, incl], axis=1)
    cm[:, CM["rm"]:CM["rm"] + 4, :] = rm.reshape(128, 4, 128)
    lower = (s_ > t_).astype(np.float32)
    cm[:, CM["rmT"], :] = np.concatenate([lower, lower], axis=1)
    ss = np.arange(128)[:, None]
    tt = np.arange(128)[None, :]
    cm[:, CM["ncur"], :] = np.where(ss > tt, NEG, 0.0)
    cm[:, CM["nprev"], :] = np.where(ss <= tt, NEG, 0.0)
    for p_ in range(128):
        cm[p_, CM["id2"], p_ % 64] = 1.0
    return cm.reshape(128, -1)


W_SHAPES = {
    "w_ffn1_in": [DM, 2 * DFF], "w_ffn1_out": [DFF, DM], "w_in": [DM, 2824],
    "rwkv_w_up": [64, 256], "rwkv_a_up": [64, 256], "rwkv_g_up": [128, 256],
    "w_out": [DM, DM], "w_ffn2_in": [DM, 2 * DFF], "w_ffn2_out": [DFF, DM],
    "w_ple_gate": [DM, DM], "w_ple_proj": [256, DM],
}
HALO_KEYS = ["att", "mls0", "mls1", "rsh0", "rsh1"]
STATE_KEYS = ["mst0", "mst1", "rst0", "rst1"]
PROG_W = {
    "A": ["w_ffn1_in", "w_ffn1_out", "w_in", "rwkv_w_up", "rwkv_a_up", "rwkv_g_up"],
    "A0": ["w_in", "rwkv_w_up", "rwkv_a_up", "rwkv_g_up"],
    "B": ["w_in", "rwkv_w_up", "rwkv_a_up", "rwkv_g_up"],
    "C": ["w_in", "rwkv_w_up", "rwkv_a_up", "rwkv_g_up", "w_out", "w_ffn2_in", "w_ffn2_out", "w_ple_gate", "w_ple_proj"],
    "Cdbg": ["w_in", "rwkv_w_up", "rwkv_a_up", "rwkv_g_up"],
    "R": ["w_ffn1_in", "w_ffn1_out", "w_ple_gate", "w_ple_proj"],
    "F": ["w_ffn1_in", "w_ffn1_out"],
    "C2": ["w_in", "rwkv_w_up", "rwkv_a_up", "rwkv_g_up", "w_out"],
    "G": ["w_ffn2_in", "w_ffn2_out", "w_ple_gate", "w_ple_proj"],
}


class B:
    pass


def build_program(kind, parts="amr"):
    nc = bass.Bass("TRN2", target_bir_lowering=False)
    b = B()
    b.nc = nc
    b.uid = 0
    b.kind = kind
    b.parts = parts
    D = {}
    b.in_names, b.out_names, b.out_regs = [], [], []

    def din(name, shape, dt=F32):
        D[name] = nc.dram_tensor(name, list(shape), dt, kind="ExternalInput").ap()
        b.in_names.append(name)

    din("xT", [DM, T])
    if kind in ("C", "R", "G"):
        din("pT", [256, T])
    din("pos", [1, T], I32)
    din("pvec", [128, LAY["_n"]])
    din("cmat", [128, CM["_n"] * 128])
    din("pcore", [128, 32])
    din("pcm", [128, 128])
    for k in PROG_W[kind]:
        din(k, W_SHAPES[k])
    b.D = D
    if kind in ("A", "A0"):
        b.emit, b.consume = set(HALO_KEYS), set()
    elif kind == "B":
        b.emit, b.consume = set(STATE_KEYS), set(HALO_KEYS)
    else:
        b.emit, b.consume = set(), set(HALO_KEYS + STATE_KEYS)
    want_x_out = kind in ("A", "A0", "C", "Cdbg", "R", "F", "C2", "G")
    if want_x_out:
        outT = nc.dram_tensor("outT", [DM, T], F32, kind="ExternalOutput").ap()
        b.out_names.append("outT")

    with contextlib.ExitStack() as es:
        S = Sched(nc, es)
        b.S = S

        def sb(name, shape, dt):
            return es.enter_context(nc.sbuf_tensor(name, list(shape), dt))

        xT = sb("xT_sb", [128, 8, T], F32)
        rx = S.regions(8, 4)
        pv = sb("pv", [128, LAY["_n"]], F32)
        r_pv = S.region()
        cm = sb("cm", [128, NF32, 128], F32)
        cmb = sb("cmb", [128, CM["_n"], 128], BF16)
        r_cm = S.region()
        pc = sb("pc", [128, 32], F32)
        r_pc = S.region()
        banks = [es.enter_context(nc.psum_tensor(f"bank{i}", [128, 512], F32)) for i in range(8)]
        rb = S.regions(8)
        for r_ in rb:
            r_.excl = True
        b.xT, b.rx, b.pv, b.r_pv, b.cm, b.cmb, b.r_cm, b.pc, b.r_pc, b.banks, b.rb = xT, rx, pv, r_pv, cm, cmb, r_cm, pc, r_pc, banks, rb

        d_c = S.dsem("consts")
        S.dma("sp", d_c, lambda e: e.dma_start(out=pv[:], in_=D["pvec"]), writes=[r_pv])
        S.dma("sp", d_c, lambda e: e.dma_start(out=cm[:], in_=D["cmat"].rearrange("p (a b) -> p a b", b=128)[:, 0:NF32, :]), writes=[r_cm])
        S.dma("pool", d_c, lambda e: e.dma_start(out=cmb[:], in_=D["cmat"].rearrange("p (a b) -> p a b", b=128)), writes=[r_cm])
        S.dma("sp", d_c, lambda e: e.dma_start(out=pc[:], in_=D["pcore"]), writes=[r_pc])
        xv = D["xT"].rearrange("(c p) t -> p c t", p=128)
        for tg in range(4):
            d_x = S.dsem(f"x{tg}")
            S.dma("sp", d_x, (lambda tg: lambda e: e.dma_start(out=xT[:, :, tg * 512:(tg + 1) * 512], in_=xv[:, :, tg * 512:(tg + 1) * 512]))(tg),
                  writes=[rx[c][tg] for c in range(8)])
        S.barrier()

        l = 0
        if kind in ("A", "R", "F"):
            for half in range(2):
                ffn_half(b, l, 1, half * 1024)
        if kind == "R":
            for half in range(2):
                ple_half(b, l, half * 1024)
        if kind in ("A", "A0", "B"):
            with scope(b) as sb1:
                h = sb1("mh", [128, 8, T], BF16)
                rh = S.regions(8, 4)
                with scope(b) as sb2:
                    prenorm(b, sb2, 0, T, "ln_mix_pre", l, h, rh)
                if kind != "B" and "a" in parts:
                    attn_tail(b, l, h, rh)
                if "m" in parts:
                    for hp in range(2):
                        mlstm_part(b, l, hp, h, rh, None, None)
                if "r" in parts:
                    for hp in range(2):
                        rwkv_part(b, l, hp, h, rh, None, None)
        if kind in ("C", "Cdbg", "C2"):
            mix_layer(b, l, debug=(kind == "Cdbg"), parts=parts)
        if kind in ("C", "G"):
            for half in range(2):
                ffn_half(b, l, 2, half * 1024)
            for half in range(2):
                ple_half(b, l, half * 1024)
        S.barrier()

        if want_x_out:
            ov = outT.rearrange("(c p) t -> p c t", p=128)
            r_out = S.region()
            d_o = S.dsem("out")
            for tg in range(4):
                S.dma("sp", d_o, (lambda tg: lambda e: e.dma_start(out=ov[:, :, tg * 512:(tg + 1) * 512], in_=xT[:, :, tg * 512:(tg + 1) * 512]))(tg),
                      reads=[rx[c][tg] for c in range(8)], writes=[r_out])
            b.out_regs.append(r_out)
        S.wait_all("sp", b.out_regs)
    b.ninst = S.ninst
    return nc, b


def pvc(b, key, l=None, i=0):
    off = LAY[(key, l)] if l is not None else LAY[key]
    return b.pv[:, off + i:off + i + 1]


@contextlib.contextmanager
def scope(b):
    with contextlib.ExitStack() as es:
        def sb(name, shape, dt):
            b.uid += 1
            return es.enter_context(b.nc.sbuf_tensor(f"{name}_{b.uid}", list(shape), dt))
        b.S.scope_stack.append([])
        yield sb
        b.S.barrier()
        b.S.free_dsems.extend(b.S.scope_stack.pop())


def rstd_from_sumsq(b, sb_rs, r_rs, tmp, r_tmp, bank, r_bank, scale, eps_key, n=512):
    S = b.S
    S.op("act", lambda e: e.activation(out=tmp, in_=bank, func=AF.Sqrt, bias=pvc(b, eps_key), scale=scale),
         reads=[r_bank, b.r_pv], writes=[r_tmp])
    S.op("dve", lambda e: e.reciprocal(out=sb_rs, in_=tmp), reads=[r_tmp], writes=[r_rs])


def prenorm(b, sb, t0, nt, gkey, l, h, rh):
    S = b.S
    ng = nt // 512
    sq = [sb(f"pn_sq{i}", [128, 8, 512], BF16) for i in range(2)]
    r_sq = S.regions(2)
    tmp = sb("pn_tmp", [128, 512], F32)
    r_tmp = S.region()
    rs = [sb(f"pn_rs{i}", [128, 512], F32) for i in range(2)]
    r_rs = S.regions(2)
    for tg in range(ng):
        g4 = (t0 // 512) + tg
        sl = slice(t0 + tg * 512, t0 + (tg + 1) * 512)
        i = tg % 2
        S.op("act", (lambda i, sl: lambda e: e.activation(out=sq[i][:], in_=b.xT[:, :, sl], func=AF.Square))(i, sl),
             reads=[b.rx[c][g4] for c in range(8)], writes=[r_sq[i]])
        bk = 7 - i
        for c in range(8):
            S.op("pe", (lambda i, c, bk: lambda e: e.matmul(b.banks[bk][:], lhsT=b.cmb[:, CM["ones"], :], rhs=sq[i][:, c, :], start=(c == 0), stop=(c == 7)))(i, c, bk),
                 reads=[r_sq[i], b.r_cm], writes=[b.rb[bk]])
        rstd_from_sumsq(b, rs[i][:], r_rs[i], tmp[:], r_tmp, b.banks[bk][:], b.rb[bk], 1.0 / DM, "eps1")
        for c in range(8):
            S.op("dve", (lambda i, c, sl, tg: lambda e: e.scalar_tensor_tensor(
                out=h[:, c, tg * 512:(tg + 1) * 512], in0=b.xT[:, c, sl], scalar=pvc(b, gkey, l, c), in1=rs[i][:],
                op0=ALU.mult, op1=ALU.mult))(i, c, sl, tg),
                reads=[b.rx[c][g4], r_rs[i], b.r_pv], writes=[rh[c][tg]])


def postnorm_residual(b, sb, t0, nt, gkey, l, y, ry, factor):
    S = b.S
    ng = nt // 512
    sq = [sb(f"po_sq{i}", [128, 8, 512], BF16) for i in range(2)]
    r_sq = S.regions(2)
    tmp = sb("po_tmp", [128, 512], F32)
    r_tmp = S.region()
    rs = [sb(f"po_rs{i}", [128, 512], F32) for i in range(2)]
    r_rs = S.regions(2)
    t2 = [sb(f"po_t2{i}", [128, 512], F32) for i in range(2)]
    r_t2 = S.regions(2)
    scale = 1.0 / (DM * factor * factor)
    eps_key = "eps1" if factor == 1.0 else "eps4"
    n2 = 0
    for tg in range(ng):
        g4 = (t0 // 512) + tg
        sl = slice(t0 + tg * 512, t0 + (tg + 1) * 512)
        ysl = slice(tg * 512, (tg + 1) * 512)
        i = tg % 2
        S.op("act", (lambda i, ysl: lambda e: e.activation(out=sq[i][:], in_=y[:, :, ysl], func=AF.Square))(i, ysl),
             reads=[ry[c][tg] for c in range(8)], writes=[r_sq[i]])
        bk = 7 - i
        for c in range(8):
            S.op("pe", (lambda i, c, bk: lambda e: e.matmul(b.banks[bk][:], lhsT=b.cmb[:, CM["ones"], :], rhs=sq[i][:, c, :], start=(c == 0), stop=(c == 7)))(i, c, bk),
                 reads=[r_sq[i], b.r_cm], writes=[b.rb[bk]])
        rstd_from_sumsq(b, rs[i][:], r_rs[i], tmp[:], r_tmp, b.banks[bk][:], b.rb[bk], scale, eps_key)
        for c in range(8):
            j = n2 % 2
            n2 += 1
            S.op("pool", (lambda i, c, ysl, j: lambda e: e.tensor_tensor(out=t2[j][:], in0=y[:, c, ysl], in1=rs[i][:], op=ALU.mult))(i, c, ysl, j),
                 reads=[ry[c][tg], r_rs[i]], writes=[r_t2[j]])
            S.op("dve", (lambda c, sl, j: lambda e: e.scalar_tensor_tensor(
                out=b.xT[:, c, sl], in0=t2[j][:], scalar=pvc(b, gkey, l, c), in1=b.xT[:, c, sl], op0=ALU.mult, op1=ALU.add))(c, sl, j),
                reads=[r_t2[j], b.rx[c][g4], b.r_pv], writes=[b.rx[c][g4]])


def wview(ap2d, r0, nr, c0, ncol):
    return ap2d[r0:r0 + nr, c0:c0 + ncol].rearrange("(c p) n -> p c n", p=128)


def ffn_half(b, l, which, t0):
    S = b.S
    NT, NG = 1024, 2
    w_in = b.D[f"w_ffn{which}_in"]
    w_out = b.D[f"w_ffn{which}_out"]
    with scope(b) as sb:
        h = sb("h", [128, 8, NT], BF16)
        rh = S.regions(8, NG)
        act = sb("act", [128, 22, NT], BF16)
        ract = S.regions(22, NG)
        with scope(b) as sb2:
            prenorm(b, sb2, t0, NT, f"ln_ffn{which}_pre", l, h, rh)
        with scope(b) as sb2:
            wg = [sb2(f"wg{i}", [128, 8, 256], BF16) for i in range(2)]
            wu = [sb2(f"wu{i}", [128, 8, 256], BF16) for i in range(2)]
            rwg, rwu = S.regions(2), S.regions(2)
            dwg, dwu = [S.dsem() for _ in range(2)], [S.dsem() for _ in range(2)]
            sg = [sb2(f"sg{i}", [128, NT], BF16) for i in range(2)]
            rsg = S.regions(2, NG)
            for mg in range(11):
                s = mg % 2
                S.dma("pool", dwg[s], (lambda s, mg: lambda e: e.dma_start(out=wg[s][:], in_=wview(w_in, 0, DM, mg * 256, 256)))(s, mg), writes=[rwg[s]])
                S.dma("pool", dwu[s], (lambda s, mg: lambda e: e.dma_start(out=wu[s][:], in_=wview(w_in, 0, DM, DFF + mg * 256, 256)))(s, mg), writes=[rwu[s]])
                for mi in range(2):
                    m = mg * 2 + mi
                    par = m % 2
                    for (wt, rw, boff) in ((wg, rwg, 0), (wu, rwu, 2)):
                        for k in range(8):
                            for tg in range(NG):
                                bk = 4 * par + boff + tg
                                S.op("pe", (lambda wt, s, k, mi, tg, bk: lambda e: e.matmul(
                                    b.banks[bk][:], lhsT=wt[s][:, k, mi * 128:(mi + 1) * 128], rhs=h[:, k, tg * 512:(tg + 1) * 512],
                                    start=(k == 0), stop=(k == 7)))(wt, s, k, mi, tg, bk),
                                    reads=[rw[s], rh[k][tg]], writes=[b.rb[bk]])
                    for tg in range(NG):
                        bg, bu = 4 * par + tg, 4 * par + 2 + tg
                        S.op("act", (lambda par, tg, bg: lambda e: e.activation(out=sg[par][:, tg * 512:(tg + 1) * 512], in_=b.banks[bg][:], func=AF.Silu))(par, tg, bg),
                             reads=[b.rb[bg]], writes=[rsg[par][tg]])
                        S.op("dve", (lambda par, tg, bu, m: lambda e: e.tensor_tensor(
                            out=act[:, m, tg * 512:(tg + 1) * 512], in0=b.banks[bu][:], in1=sg[par][:, tg * 512:(tg + 1) * 512], op=ALU.mult))(par, tg, bu, m),
                            reads=[b.rb[bu], rsg[par][tg]], writes=[ract[m][tg]])
        with scope(b) as sb2:
            y = sb2("y", [128, 8, NT], F32)
            ry = S.regions(8, NG)
            with scope(b) as sb3:
                wo = [sb3(f"wo{i}", [128, 11, 512], BF16) for i in range(2)]
                rwo = S.regions(2)
                dwo = [S.dsem() for _ in range(2)]
                for jg in range(2):
                    for a in range(2):
                        S.dma("pool", dwo[a], (lambda a, jg: lambda e: e.dma_start(out=wo[a][:], in_=wview(w_out, a * 1408, 1408, jg * 512, 512)))(a, jg), writes=[rwo[a]])
                    for k in range(22):
                        a, kk = k // 11, k % 11
                        for j in range(4):
                            for tg in range(NG):
                                bk = j * 2 + tg
                                S.op("pe", (lambda a, kk, j, tg, bk, k: lambda e: e.matmul(
                                    b.banks[bk][:], lhsT=wo[a][:, kk, j * 128:(j + 1) * 128], rhs=act[:, k, tg * 512:(tg + 1) * 512],
                                    start=(k == 0), stop=(k == 21)))(a, kk, j, tg, bk, k),
                                    reads=[rwo[a], ract[k][tg]], writes=[b.rb[bk]])
                    for j in range(4):
                        for tg in range(NG):
                            bk = j * 2 + tg
                            c = jg * 4 + j
                            if (j + tg) % 2 == 0:
                                S.op("act", (lambda c, tg, bk: lambda e: e.copy(out=y[:, c, tg * 512:(tg + 1) * 512], in_=b.banks[bk][:]))(c, tg, bk),
                                     reads=[b.rb[bk]], writes=[ry[c][tg]])
                            else:
                                S.op("dve", (lambda c, tg, bk: lambda e: e.tensor_copy(out=y[:, c, tg * 512:(tg + 1) * 512], in_=b.banks[bk][:]))(c, tg, bk),
                                     reads=[b.rb[bk]], writes=[ry[c][tg]])
            with scope(b) as sb3:
                postnorm_residual(b, sb3, t0, NT, f"ln_ffn{which}_post", l, y, ry, 0.5)


def ple_half(b, l, t0):
    S = b.S
    NT, NG = 1024, 2
    wgd = b.D["w_ple_gate"]
    wpd = b.D["w_ple_proj"]
    with scope(b) as sb:
        h = sb("h", [128, 8, NT], BF16)
        rh = S.regions(8, NG)
        y = sb("y", [128, 8, NT], F32)
        ry = S.regions(8, NG)
        with scope(b) as sb2:
            prenorm(b, sb2, t0, NT, "ln_ple_pre", l, h, rh)
        with scope(b) as sb2:
            pt = sb2("pt", [128, 2, NT], BF16)
            r_pt = S.region()
            wp = sb2("wp", [128, 2, DM], BF16)
            r_wp = S.region()
            d1, d2 = S.dsem(), S.dsem()
            S.dma("pool", d1, lambda e: e.dma_start(out=pt[:], in_=b.D["pT"][:, t0:t0 + NT].rearrange("(c p) t -> p c t", p=128)), writes=[r_pt])
            S.dma("pool", d2, lambda e: e.dma_start(out=wp[:], in_=wview(wpd, 0, 256, 0, DM)), writes=[r_wp])
            wg = [sb2(f"wg{i}", [128, 8, 256], BF16) for i in range(2)]
            rwg = S.regions(2)
            dwg = [S.dsem() for _ in range(2)]
            sg = [sb2(f"sg{i}", [128, NT], F32) for i in range(2)]
            rsg = S.regions(2, NG)
            for jg in range(4):
                s = jg % 2
                S.dma("pool", dwg[s], (lambda s, jg: lambda e: e.dma_start(out=wg[s][:], in_=wview(wgd, 0, DM, jg * 256, 256)))(s, jg), writes=[rwg[s]])
                for ji in range(2):
                    j = jg * 2 + ji
                    par = j % 2
                    for k in range(8):
                        for tg in range(NG):
                            bk = 4 * par + tg
                            S.op("pe", (lambda s, k, ji, tg, bk: lambda e: e.matmul(
                                b.banks[bk][:], lhsT=wg[s][:, k, ji * 128:(ji + 1) * 128], rhs=h[:, k, tg * 512:(tg + 1) * 512],
                                start=(k == 0), stop=(k == 7)))(s, k, ji, tg, bk),
                                reads=[rwg[s], rh[k][tg]], writes=[b.rb[bk]])
                    for k in range(2):
                        for tg in range(NG):
                            bk = 4 * par + 2 + tg
                            S.op("pe", (lambda k, j, tg, bk: lambda e: e.matmul(
                                b.banks[bk][:], lhsT=wp[:, k, j * 128:(j + 1) * 128], rhs=pt[:, k, tg * 512:(tg + 1) * 512],
                                start=(k == 0), stop=(k == 1)))(k, j, tg, bk),
                                reads=[r_wp, r_pt], writes=[b.rb[bk]])
                    for tg in range(NG):
                        bg, bu = 4 * par + tg, 4 * par + 2 + tg
                        S.op("act", (lambda par, tg, bg: lambda e: e.activation(out=sg[par][:, tg * 512:(tg + 1) * 512], in_=b.banks[bg][:], func=AF.Sigmoid))(par, tg, bg),
                             reads=[b.rb[bg]], writes=[rsg[par][tg]])
                        S.op("dve", (lambda par, tg, bu, j: lambda e: e.tensor_tensor(
                            out=y[:, j, tg * 512:(tg + 1) * 512], in0=b.banks[bu][:], in1=sg[par][:, tg * 512:(tg + 1) * 512], op=ALU.mult))(par, tg, bu, j),
                            reads=[b.rb[bu], rsg[par][tg]], writes=[ry[j][tg]])
        with scope(b) as sb2:
            postnorm_residual(b, sb2, t0, NT, "ln_ple_post", l, y, ry, 1.0)


def allgather(b, sb, srcs, nrow, ncol, dt, name):
    S, nc = b.S, b.nc
    if name in b.emit:
        xo = nc.dram_tensor(f"x_{name}", [nrow, ncol], dt, kind="ExternalOutput").ap()
        r_o = S.region()
        d1 = S.dsem(f"xo_{name}")
        for (c0, ncs, ap, rr) in srcs:
            S.dma("sp", d1, (lambda c0=c0, ncs=ncs, ap=ap: lambda e: e.dma_start(out=xo[:, c0:c0 + ncs], in_=ap, allow_slow_non_contiguous=True))(), reads=rr, writes=[r_o])
        b.out_regs.append(r_o)
        b.out_names.append(f"x_{name}")
        return None, None
    assert name in b.consume, name
    gi = nc.dram_tensor(f"g_{name}", [NCORES * nrow, ncol], dt, kind="ExternalInput").ap()
    b.in_names.append(f"g_{name}")
    r_blk = S.region()
    d3 = S.dsem()
    blk = sb(f"agblk_{name}", [nrow, NCORES, ncol], dt)
    S.dma("sp", d3, lambda e: e.dma_start(out=blk[:], in_=gi.rearrange("(j p) n -> p j n", p=nrow)), writes=[r_blk])
    return blk, r_blk


def select_prev(b, dst, r_dst, blk, r_blk, nrow, c0, ncs):
    S = b.S
    S.op("dve", lambda e: e.tensor_scalar(out=dst, in0=blk[:, 0, c0:c0 + ncs], scalar1=b.pc[0:nrow, 0:1], scalar2=None, op0=ALU.mult),
         reads=[r_blk, b.r_pc], writes=[r_dst])
    for j in range(1, 7):
        S.op("dve", (lambda j=j: lambda e: e.scalar_tensor_tensor(out=dst, in0=blk[:, j, c0:c0 + ncs], scalar=b.pc[0:nrow, j:j + 1], in1=dst,
                                                                 op0=ALU.mult, op1=ALU.add))(), reads=[r_blk, b.r_pc, r_dst], writes=[r_dst])


def rope_tables(b, cosF, sinF, r_cos, r_sin):
    S = b.S
    PI = float(np.pi)
    C1 = 6.28125
    C2 = float(2 * np.pi - 6.28125)
    with scope(b) as sb:
        posi = sb("posi", [128, T], I32)
        ang = sb("ang", [128, T], F32)
        kf = sb("kf", [128, T], F32)
        ki = sb("ki", [128, T], I32)
        rr = sb("rr", [128, T], F32)
        r_posi, r_ang, r_kf, r_ki, r_rr = S.regions(5)
        d = S.dsem()
        S.dma("sp", d, lambda e: e.dma_start(out=posi[:], in_=b.D["pos"].partition_broadcast(128)), writes=[r_posi])
        S.op("dve", lambda e: e.tensor_copy(out=ang[:], in_=posi[:]), reads=[r_posi], writes=[r_ang])
        S.op("dve", lambda e: e.tensor_scalar(out=ang[:], in0=ang[:], scalar1=pvc(b, "invfreq"), scalar2=None, op0=ALU.mult), reads=[r_ang, b.r_pv], writes=[r_ang])
        for (dst, r_dst, shift) in ((sinF, r_sin, 0.0), (cosF, r_cos, PI / 2)):
            S.op("dve", (lambda shift=shift: lambda e: e.tensor_scalar(out=kf[:], in0=ang[:], scalar1=1.0 / (2 * PI), scalar2=0.5 + shift / (2 * PI), op0=ALU.mult, op1=ALU.add))(),
                 reads=[r_ang], writes=[r_kf])
            S.op("dve", lambda e: e.tensor_copy(out=ki[:], in_=kf[:]), reads=[r_kf], writes=[r_ki])
            S.op("dve", lambda e: e.tensor_copy(out=kf[:], in_=ki[:]), reads=[r_ki], writes=[r_kf])
            S.op("dve", lambda e: e.scalar_tensor_tensor(out=rr[:], in0=kf[:], scalar=-C1, in1=ang[:], op0=ALU.mult, op1=ALU.add), reads=[r_kf, r_ang], writes=[r_rr])
            S.op("dve", lambda e: e.scalar_tensor_tensor(out=rr[:], in0=kf[:], scalar=-C2, in1=rr[:], op0=ALU.mult, op1=ALU.add), reads=[r_kf, r_rr], writes=[r_rr])
            if shift:
                S.op("dve", (lambda shift=shift: lambda e: e.tensor_scalar(out=rr[:], in0=rr[:], scalar1=shift, scalar2=None, op0=ALU.add))(), reads=[r_rr], writes=[r_rr])
            S.op("dve", lambda e: e.tensor_scalar(out=kf[:], in0=rr[:], scalar1=-PI, scalar2=2 * PI, op0=ALU.is_lt, op1=ALU.mult), reads=[r_rr], writes=[r_kf])
            S.op("dve", lambda e: e.tensor_tensor(out=rr[:], in0=rr[:], in1=kf[:], op=ALU.add), reads=[r_rr, r_kf], writes=[r_rr])
            S.op("dve", lambda e: e.tensor_scalar(out=kf[:], in0=rr[:], scalar1=PI, scalar2=-2 * PI, op0=ALU.is_gt, op1=ALU.mult), reads=[r_rr], writes=[r_kf])
            S.op("dve", lambda e: e.tensor_tensor(out=rr[:], in0=rr[:], in1=kf[:], op=ALU.add), reads=[r_rr, r_kf], writes=[r_rr])
            S.op("dve", lambda e: e.tensor_scalar(out=rr[:], in0=rr[:], scalar1=-PI, scalar2=PI, op0=ALU.max, op1=ALU.min), reads=[r_rr], writes=[r_rr])
            S.op("act", (lambda dst=dst: lambda e: e.activation(out=dst[:], in_=rr[:], func=AF.Sin))(), reads=[r_rr], writes=[r_dst])


def attn_tail(b, l, h, rh):
    S = b.S
    w_in = b.D["w_in"]
    bk_ = b.banks
    rb = b.rb
    with scope(b) as sb:
        cosF = sb("cosF", [128, T], F32)
        sinF = sb("sinF", [128, T], F32)
        r_cos, r_sin = S.regions(2)
        rope_tables(b, cosF, sinF, r_cos, r_sin)
        if "1" in b.parts:
            return
        wkv = sb("wkv", [128, 8, 256], BF16)
        r_wkv = S.region()
        dkv = S.dsem()
        S.dma("pool", dkv, lambda e: e.dma_start(out=wkv[:], in_=wview(w_in, 0, DM, 512, 256)), writes=[r_wkv])
        xb = sb("xb", [128, 128], BF16)
        t1 = sb("t1", [128, 128], F32)
        t2 = sb("t2", [128, 128], F32)
        pay = sb("pay", [128, 256], F32)
        r_xb, r_t1, r_t2, r_pay = S.regions(4)
        ts = slice(T - 128, T)
        for k in range(8):
            S.op("pe", (lambda k=k: lambda e: e.matmul(bk_[0][:, 0:128], lhsT=wkv[:, k, 0:128], rhs=h[:, k, ts], start=(k == 0), stop=(k == 7)))(), reads=[r_wkv, rh[k][3]], writes=[rb[0]])
        if "2" in b.parts:
            return
        S.op("act", lambda e: e.copy(out=xb[:], in_=bk_[0][:, 0:128]), reads=[rb[0]], writes=[r_xb])
        if "5" in b.parts:
            return
        if "8" in b.parts:
            S.op("dve", lambda e: e.tensor_tensor(out=t1[:], in0=bk_[0][:, 0:128], in1=cosF[:, 0:128], op=ALU.mult), reads=[rb[0], r_cos], writes=[r_t1])
            return
        if "9" in b.parts:
            S.op("dve", lambda e: e.tensor_tensor(out=t1[:], in0=xb[:], in1=cosF[:, ts], op=ALU.mult), reads=[r_xb, r_cos], writes=[r_t1])
            return
        if "0" in b.parts:
            S.op("dve", lambda e: e.tensor_tensor(out=t1[:], in0=bk_[0][:, 0:128], in1=t2[:], op=ALU.mult), reads=[rb[0], r_xb], writes=[r_t1])
            return
        S.op("dve", lambda e: e.tensor_tensor(out=t1[:], in0=bk_[0][:, 0:128], in1=cosF[:, ts], op=ALU.mult), reads=[rb[0], r_cos], writes=[r_t1])
        if "6" in b.parts:
            return
        S.op("pe", lambda e: e.matmul(bk_[1][:, 0:128], lhsT=b.cmb[:, CM["perm"], :], rhs=xb[:], start=True, stop=True), reads=[r_xb, b.r_cm], writes=[rb[1]])
        if "7" in b.parts:
            return
        S.op("dve", lambda e: e.tensor_tensor(out=t2[:], in0=bk_[1][:, 0:128], in1=sinF[:, ts], op=ALU.mult), reads=[rb[1], r_sin], writes=[r_t2])
        S.op("dve", lambda e: e.tensor_tensor(out=pay[:, 0:128], in0=t1[:], in1=t2[:], op=ALU.add), reads=[r_t1, r_t2], writes=[r_pay])
        if "3" in b.parts:
            return
        for k in range(8):
            S.op("pe", (lambda k=k: lambda e: e.matmul(bk_[2][:, 0:128], lhsT=h[:, k, ts], rhs=wkv[:, k, 128:256], start=(k == 0), stop=(k == 7)))(), reads=[r_wkv, rh[k][3]], writes=[rb[2]])
        S.op("act", lambda e: e.copy(out=pay[:, 128:256], in_=bk_[2][:, 0:128]), reads=[rb[2], r_pay], writes=[r_pay])
        if "4" in b.parts:
            return
        allgather(b, sb, [(0, 256, pay[:], [r_pay])], 128, 256, F32, "att")


def attn_part(b, l, h, rh, ym, rym):
    S = b.S
    w_in = b.D["w_in"]
    bk_ = b.banks
    rb = b.rb
    with scope(b) as sb:
        cosF = sb("cosF", [128, T], F32)
        sinF = sb("sinF", [128, T], F32)
        r_cos, r_sin = S.regions(2)
        rope_tables(b, cosF, sinF, r_cos, r_sin)
        qT = sb("qT", [128, 4, T], BF16)
        rq = S.regions(4, 4)
        kT = sb("kT", [128, 128 + T], BF16)
        rk = S.regions(5)
        vt = sb("vt", [128, 17, 128], BF16)
        rv = S.regions(5)
        nm = sb("nm", [128, 3, 4, 128], BF16)
        r_nm = S.region()
        es = sb("es", [128, 4], F32)
        r_es = S.region()
        pcm_sb = sb("pcm_sb", [128, 128], F32)
        r_pcm = S.region()
        d0 = S.dsem()
        S.dma("sp", d0, lambda e: e.dma_start(out=pcm_sb[:], in_=b.D["pcm"]), writes=[r_pcm])
        for m, src in enumerate([b.cm[:, CM["ncur"], :], b.cm[:, CM["nprev"], :], pcm_sb[:]]):
            S.op("dve", (lambda m=m, src=src: lambda e: e.tensor_copy(out=nm[:, m, :, :], in_=src.unsqueeze(1).to_broadcast([128, 4, 128])))(),
                 reads=[b.r_cm, r_pcm], writes=[r_nm])
        so = LAY[("sink", l)]
        S.op("act", lambda e: e.activation(out=es[:], in_=b.pv[:, so:so + 4], func=AF.Exp), reads=[b.r_pv], writes=[r_es])
        with scope(b) as sb2:
            wq = sb2("wq", [128, 8, 512], BF16)
            wkv = sb2("wkv", [128, 8, 256], BF16)
            r_wq, r_wkv = S.regions(2)
            dq, dkv = S.dsem(), S.dsem()
            for g in range(2):
                for i in range(4):
                    S.dma("pool", dq, (lambda g=g, i=i: lambda e: e.dma_start(
                        out=wq[:, :, i * 128 + g * 64:i * 128 + g * 64 + 64], in_=wview(w_in, 0, DM, g * 256 + i * 64, 64)))(), writes=[r_wq])
            S.dma("pool", dkv, lambda e: e.dma_start(out=wkv[:], in_=wview(w_in, 0, DM, 512, 256)), writes=[r_wkv])
            xb = [sb2(f"xb{i}", [128, 512], BF16) for i in range(2)]
            t1 = [sb2(f"t1{i}", [128, 512], F32) for i in range(2)]
            t2 = [sb2(f"t2{i}", [128, 512], F32) for i in range(2)]
            r_xb, r_t1, r_t2 = S.regions(2), S.regions(2), S.regions(2)
            n = 0
            for ti in range(5):
                for tg in range(4):
                    bk = n % 4
                    i2 = n % 2
                    n += 1
                    ts = slice(tg * 512, (tg + 1) * 512)
                    for k in range(8):
                        lhs = wq[:, k, ti * 128:(ti + 1) * 128] if ti < 4 else wkv[:, k, 0:128]
                        S.op("pe", (lambda lhs=lhs, k=k, ts=ts, bk=bk: lambda e: e.matmul(bk_[bk][:], lhsT=lhs, rhs=h[:, k, ts], start=(k == 0), stop=(k == 7)))(),
                             reads=[r_wq if ti < 4 else r_wkv, rh[k][tg]], writes=[rb[bk]])
                    S.op("act", (lambda i2=i2, bk=bk: lambda e: e.copy(out=xb[i2][:], in_=bk_[bk][:]))(), reads=[rb[bk]], writes=[r_xb[i2]])
                    S.op("dve", (lambda i2=i2, bk=bk, ts=ts: lambda e: e.tensor_tensor(out=t1[i2][:], in0=bk_[bk][:], in1=cosF[:, ts], op=ALU.mult))(),
                         reads=[rb[bk], r_cos], writes=[r_t1[i2]])
                    S.op("pe", (lambda i2=i2, bk=bk: lambda e: e.matmul(bk_[4 + bk][:], lhsT=b.cmb[:, CM["perm"], :], rhs=xb[i2][:], start=True, stop=True))(),
                         reads=[r_xb[i2], b.r_cm], writes=[rb[4 + bk]])
                    S.op("dve", (lambda i2=i2, bk=bk, ts=ts: lambda e: e.tensor_tensor(out=t2[i2][:], in0=bk_[4 + bk][:], in1=sinF[:, ts], op=ALU.mult))(),
                         reads=[rb[4 + bk], r_sin], writes=[r_t2[i2]])
                    if ti < 4:
                        dst, r_dst = qT[:, ti, ts], rq[ti][tg]
                    else:
                        dst, r_dst = kT[:, 128 + tg * 512:128 + (tg + 1) * 512], rk[1 + tg]
                    S.op("pool", (lambda i2=i2, dst=dst: lambda e: e.tensor_tensor(out=dst, in0=t1[i2][:], in1=t2[i2][:], op=ALU.add))(),
                         reads=[r_t1[i2], r_t2[i2]], writes=[r_dst])
            for g4 in range(4):
                for tt4 in range(4):
                    tt = g4 * 4 + tt4
                    for k in range(8):
                        S.op("pe", (lambda g4=g4, tt4=tt4, tt=tt, k=k: lambda e: e.matmul(bk_[g4][:, tt4 * 128:(tt4 + 1) * 128], lhsT=h[:, k, tt * 128:(tt + 1) * 128],
                                                                                 rhs=wkv[:, k, 128:256], start=(k == 0), stop=(k == 7)))(),
                             reads=[r_wkv, rh[k][g4]], writes=[rb[g4]])
                S.op("act", (lambda g4=g4: lambda e: e.copy(out=vt[:, 1 + g4 * 4:5 + g4 * 4, :], in_=bk_[g4][:].rearrange("p (a c) -> p a c", c=128)))(),
                     reads=[rb[g4]], writes=[rv[1 + g4]])
        with scope(b) as sb2:
            blk, r_blk = allgather(b, sb2, [(0, 128, kT[:, T:T + 128], [rk[4]]), (128, 128, vt[:, 16, :], [rv[4]])], 128, 256, F32, "att")
            select_prev(b, kT[:, 0:128], rk[0], blk, r_blk, 128, 0, 128)
            select_prev(b, vt[:, 0, :], rv[0], blk, r_blk, 128, 128, 128)
        with scope(b) as sb2:
            Pp = [sb2(f"Pp{i}", [128, 512], BF16) for i in range(2)]
            Pc = [sb2(f"Pc{i}", [128, 512], BF16) for i in range(2)]
            r_Pp, r_Pc = S.regions(2), S.regions(2)
            dn = [sb2(f"dn{i}", [128, 512], F32) for i in range(2)]
            r_dn = S.regions(2)
            n = 0
            for qb in range(16):
                qs = slice(qb * 128, (qb + 1) * 128)
                bN, bD = 4 + (qb % 2), 6 + (qb % 2)
                for g in range(2):
                    ps = slice(64 * g, 64 * g + 64)
                    i2 = n % 2
                    bA, bB = 2 * i2, 2 * i2 + 1
                    n += 1
                    mprev = 2 if qb == 0 else 1
                    for (bS, kc0, mi, P_, rP, rkk) in ((bA, qb * 128, mprev, Pp, r_Pp, rk[(qb * 128) // 512 + (0 if qb % 4 else 0)]),
                                                      (bB, (qb + 1) * 128, 0, Pc, r_Pc, None)):
                        kcs = slice(kc0, kc0 + 128)
                        rkr = rk[0] if kc0 < 128 else rk[1 + (kc0 - 128) // 512]
                        S.op("pe", (lambda bS=bS, ps=ps, kcs=kcs, qs=qs: lambda e: e.matmul(bk_[bS][:], lhsT=kT[ps, kcs], rhs=qT[ps, :, qs], start=True, stop=False))(),
                             reads=[rkr] + [rq[i][qb // 4] for i in range(4)], writes=[rb[bS]])
                        S.op("pe", (lambda bS=bS, mi=mi: lambda e: e.matmul(bk_[bS][:], lhsT=b.cmb[:, CM["ident"], :], rhs=nm[:, mi, :, :], start=False, stop=True))(),
                             reads=[r_nm, b.r_cm], writes=[rb[bS]])
                        S.op("act", (lambda bS=bS, P_=P_, i2=i2: lambda e: e.activation(out=P_[i2][:], in_=bk_[bS][:], func=AF.Exp, scale=0.125))(),
                             reads=[rb[bS]], writes=[rP[i2]])
                    for (bO, lo) in ((bN, None), (bD, "ones")):
                        for si, (P_, rP, vtile) in enumerate(((Pp, r_Pp, qb), (Pc, r_Pc, qb + 1))):
                            lhs = vt[:, vtile, 64 * g:64 * g + 64] if lo is None else b.cmb[:, CM["ones"], 0:64]
                            rvr = rv[0] if vtile == 0 else rv[1 + (vtile - 1) // 4]
                            S.op("pe", (lambda bO=bO, ps=ps, lhs=lhs, P_=P_, i2=i2, si=si: lambda e: e.matmul(bk_[bO][ps, :], lhsT=lhs, rhs=P_[i2][:], start=(si == 0), stop=(si == 1)))(),
                                 reads=[rvr, rP[i2], b.r_cm], writes=[rb[bO]])
                j2 = qb % 2
                S.op("dve", (lambda bD=bD, j2=j2: lambda e: e.tensor_tensor(out=dn[j2][:].rearrange("p (a c) -> p a c", c=128), in0=bk_[bD][:].rearrange("p (a c) -> p a c", c=128),
                                                                        in1=es[:, 0:4].unsqueeze(2).to_broadcast([128, 4, 128]), op=ALU.add))(),
                     reads=[rb[bD], r_es], writes=[r_dn[j2]])
                S.op("dve", (lambda j2=j2: lambda e: e.reciprocal(out=dn[j2][:], in_=dn[j2][:]))(), reads=[r_dn[j2]], writes=[r_dn[j2]])
                S.op("dve", (lambda bN=bN, j2=j2, qs=qs: lambda e: e.tensor_tensor(out=ym[:, 0:4, qs], in0=bk_[bN][:].rearrange("p (a c) -> p a c", c=128),
                                                                               in1=dn[j2][:].rearrange("p (a c) -> p a c", c=128), op=ALU.mult))(),
                     reads=[rb[bN], r_dn[j2]], writes=[rym[i][qb // 4] for i in range(4)])


def dbg_stop(b, tag):
    if tag not in b.parts:
        return False
    S = b.S
    b.uid += 1
    xo = b.nc.dram_tensor(f"x_dbg{b.uid}", [128, 1], F32, kind="ExternalOutput").ap()
    r_o = S.region()
    d1 = S.dsem()
    S.dma("sp", d1, lambda e: e.dma_start(out=xo, in_=b.pv[:, 0:1]), reads=[b.r_pv], writes=[r_o])
    b.out_regs.append(r_o)
    b.out_names.append(f"x_dbg{b.uid}")
    return True


def mlstm_part(b, l, hp, h, rh, ym, rym):
    S = b.S
    w_in = b.D["w_in"]
    bk_ = b.banks
    rb = b.rb
    MB = 768
    NCH = 16
    with scope(b) as sb:
        numS = sb("numS", [128, T], F32)
        denS = sb("denS", [128, T], F32)
        r_num, r_den = S.regions(NCH), S.regions(NCH)
        qseg = sb("qseg", [128, T], BF16)
        r_qseg = S.regions(NCH)
        osig = sb("osig", [128, T], BF16)
        r_osig = S.regions(4)
        Cf = sb("Cf", [128, 128], F32)
        Cb = sb("Cb", [128, 128], BF16)
        r_Cf, r_Cb = S.regions(2)
        gtot = sb("gtot", [128, 1], F32)
        r_gtot = S.region()
        with scope(b) as sb1:
            qT = sb1("mqT", [128, T], BF16)
            kT = sb1("mkT", [128, T], BF16)
            r_qT, r_kT = S.regions(4), S.regions(4)
            vaug = sb1("vtokm", [128, NCH, 2, 64], BF16)
            r_vaug = S.regions(4)
            GR = sb1("GR", [128, T], F32)
            r_GR = S.regions(4)
            r_GRall = S.region()
            gb15 = sb1("gb15", [128, 1], F32)
            r_gb = S.region()
            go = LAY[("gbias", l)]
            S.op("dve", lambda e: e.tensor_scalar(out=gb15[:], in0=b.pv[:, go:go + 1], scalar1=1.0 / 15.0, scalar2=None, op0=ALU.mult), reads=[b.r_pv], writes=[r_gb])
            with scope(b) as sb2:
                wm = sb2("wm", [128, 8, 4, 128], BF16)
                wgt = sb2("wgt", [128, 8, 8], BF16)
                r_wm, r_wgt = S.regions(2)
                dm_, dg_ = S.dsem(), S.dsem()
                for j in range(4):
                    S.dma("pool", dm_, (lambda j=j: lambda e: e.dma_start(out=wm[:, :, j, :], in_=wview(w_in, 0, DM, MB + j * 256 + hp * 128, 128)))(), writes=[r_wm])
                S.dma("pool", dg_, lambda e: e.dma_start(out=wgt[:], in_=wview(w_in, 0, DM, MB + 1024, 8)), writes=[r_wgt])
                raw = [sb2(f"raw{j}", [128, 515], F32) for j in range(2)]
                r_raw = S.regions(2)
                ctmp = sb2("ctmp", [128, 512], F32)
                r_ctmp = S.region()
                tail = sb2("mtail", [128, 6], F32)
                r_tail = S.region()
                n = 0
                for j in range(2):
                    for k in range(8):
                        S.op("pe", (lambda j=j, k=k: lambda e: e.matmul(bk_[j][:, 0:128], lhsT=wm[:, k, j, :], rhs=h[:, k, T - 128:T], start=(k == 0), stop=(k == 7)))(),
                             reads=[r_wm, rh[k][3]], writes=[rb[j]])
                    S.op("act", (lambda j=j: lambda e: e.copy(out=tail[:, 3 * j:3 * j + 3], in_=bk_[j][:, 125:128]))(), reads=[rb[j]], writes=[r_tail])
                with scope(b) as sb3:
                    blk, r_blk = allgather(b, sb3, [(0, 6, tail[:], [r_tail])], 128, 6, F32, f"mls{hp}")
                    if blk is None:
                        return
                    for j in range(2):
                        select_prev(b, raw[j][:, 0:3], r_raw[j], blk, r_blk, 128, 3 * j, 3)
                co = LAY[("conv", l)]
                for tg in range(4):
                    ts = slice(tg * 512, (tg + 1) * 512)
                    for j in range(2):
                        bk = n % 4
                        n += 1
                        for k in range(8):
                            S.op("pe", (lambda j=j, k=k, ts=ts, bk=bk: lambda e: e.matmul(bk_[bk][:], lhsT=wm[:, k, j, :], rhs=h[:, k, ts], start=(k == 0), stop=(k == 7)))(),
                                 reads=[r_wm, rh[k][tg]], writes=[rb[bk]])
                        S.op("act", (lambda j=j, bk=bk: lambda e: e.copy(out=raw[j][:, 3:515], in_=bk_[bk][:]))(), reads=[rb[bk], r_raw[j]], writes=[r_raw[j]])
                        tile_ = j * 2 + hp
                        wc = [b.pv[:, co + tile_ * 4 + tap:co + tile_ * 4 + tap + 1] for tap in range(4)]
                        S.op("dve", (lambda j=j, wc=wc: lambda e: e.tensor_scalar(out=ctmp[:], in0=raw[j][:, 3:515], scalar1=wc[3], scalar2=None, op0=ALU.mult))(),
                             reads=[r_raw[j], b.r_pv], writes=[r_ctmp])
                        for tap in range(3):
                            S.op("dve", (lambda j=j, wc=wc, tap=tap: lambda e: e.scalar_tensor_tensor(out=ctmp[:], in0=raw[j][:, tap:tap + 512], scalar=wc[tap], in1=ctmp[:], op0=ALU.mult, op1=ALU.add))(),
                                 reads=[r_raw[j], r_ctmp, b.r_pv], writes=[r_ctmp])
                        S.op("dve", (lambda j=j: lambda e: e.tensor_copy(out=raw[j][:, 0:3], in_=raw[j][:, 512:515]))(), reads=[r_raw[j]], writes=[r_raw[j]])
                        if j == 0:
                            S.op("act", lambda e: e.activation(out=ctmp[:], in_=ctmp[:], func=AF.Silu), reads=[r_ctmp], writes=[r_ctmp])
                            S.op("dve", (lambda ts=ts: lambda e: e.tensor_scalar(out=qT[:, ts], in0=ctmp[:], scalar1=0.125, scalar2=None, op0=ALU.mult))(), reads=[r_ctmp], writes=[r_qT[tg]])
                        else:
                            S.op("act", (lambda ts=ts: lambda e: e.activation(out=kT[:, ts], in_=ctmp[:], func=AF.Silu))(), reads=[r_ctmp], writes=[r_kT[tg]])
                if dbg_stop(b, "1"):
                    return
                for tg in range(4):
                    bk = n % 8
                    n += 1
                    ts = slice(tg * 512, (tg + 1) * 512)
                    for k in range(8):
                        S.op("pe", (lambda k=k, ts=ts, bk=bk: lambda e: e.matmul(bk_[bk][:], lhsT=wm[:, k, 3, :], rhs=h[:, k, ts], start=(k == 0), stop=(k == 7)))(),
                             reads=[r_wm, rh[k][tg]], writes=[rb[bk]])
                    S.op("act", (lambda ts=ts, bk=bk: lambda e: e.activation(out=osig[:, ts], in_=bk_[bk][:], func=AF.Sigmoid))(), reads=[rb[bk]], writes=[r_osig[tg]])
                if dbg_stop(b, "2"):
                    return
                for tg in range(4):
                    bk = n % 8
                    n += 1
                    ts = slice(tg * 512, (tg + 1) * 512)
                    for (p0, c0) in ((0, 0), (32, 4)):
                        for k in range(8):
                            S.op("pe", (lambda k=k, ts=ts, bk=bk, p0=p0, c0=c0: lambda e: e.matmul(bk_[bk][p0:p0 + 4, :], lhsT=wgt[:, k, c0:c0 + 4], rhs=h[:, k, ts], start=(k == 0), stop=(k == 7)))(),
                                 reads=[r_wgt, rh[k][tg]], writes=[rb[bk]])
                    for p0 in (0, 32):
                        S.op("act", (lambda ts=ts, bk=bk, p0=p0: lambda e: e.activation(out=GR[p0:p0 + 4, ts], in_=bk_[bk][p0:p0 + 4, :], func=AF.Tanh, bias=gb15[p0:p0 + 4, 0:1], scale=1.0 / 15.0))(),
                             reads=[rb[bk], r_gb], writes=[r_GR[tg]])
                if dbg_stop(b, "3"):
                    return
                for g4 in range(4):
                    bk = n % 8
                    n += 1
                    for tt4 in range(4):
                        tt = g4 * 4 + tt4
                        for k in range(8):
                            S.op("pe", (lambda tt4=tt4, tt=tt, k=k, bk=bk: lambda e: e.matmul(bk_[bk][:, tt4 * 128:(tt4 + 1) * 128], lhsT=h[:, k, tt * 128:(tt + 1) * 128], rhs=wm[:, k, 2, :],
                                                                                      start=(k == 0), stop=(k == 7)))(), reads=[r_wm, rh[k][g4]], writes=[rb[bk]])
                    S.op("act", (lambda g4=g4, bk=bk: lambda e: e.copy(out=vaug[:, g4 * 4:g4 * 4 + 4, :, :], in_=bk_[bk][:].rearrange("p (a c d) -> p a c d", a=4, c=2)))(),
                         reads=[rb[bk]], writes=[r_vaug[g4]])
                if dbg_stop(b, "4"):
                    return
                allGR = r_GR
                S.op("dve", lambda e: e.tensor_scalar(out=GR[0:4, :], in0=GR[0:4, :], scalar1=15.0, scalar2=None, op0=ALU.mult), reads=allGR, writes=[r_GRall])
                S.op("act", lambda e: e.activation(out=GR[32:36, :], in_=GR[32:36, :], func=AF.Exp, scale=-15.0), reads=allGR, writes=[r_GRall])
                S.op("act", lambda e: e.activation(out=GR[32:36, :], in_=GR[32:36, :], func=AF.Ln, bias=pvc(b, "one")[32:36, :], scale=1.0), reads=[r_GRall, b.r_pv], writes=[r_GRall])
                S.op("dve", lambda e: e.tensor_scalar(out=GR[32:36, :], in0=GR[32:36, :], scalar1=-0.5, scalar2=None, op0=ALU.mult), reads=[r_GRall], writes=[r_GRall])
                S.op("dve", lambda e: e.tensor_tensor_scan(out=GR[32:36, :], data0=GR[32:36, :], data1=GR[32:36, :], initial=0.0, op0=ALU.add, op1=ALU.add), reads=[r_GRall], writes=[r_GRall])
            if dbg_stop(b, "5"):
                return
            with scope(b) as sb2:
                S.op("pool", lambda e: e.memset(Cf[:], 0.0), writes=[r_Cf])
                S.op("pool", lambda e: e.memset(Cb[:], 0.0), writes=[r_Cb])
                negG = [sb2(f"negG{i}", [128, 1], F32) for i in range(2)]
                r_negG = S.regions(2)
                S.op("pool", lambda e: e.memset(negG[1][:], 0.0), writes=[r_negG[1]])
                itok = [sb2(f"itok{i}", [128, 8], F32) for i in range(2)]
                atok = [sb2(f"atok{i}", [128, 4], F32) for i in range(2)]
                r_itok, r_atok = S.regions(2), S.regions(2)
                eT = [sb2(f"eT{i}", [128, 2, 128], F32) for i in range(2)]
                r_eT = S.regions(2)
                PT = [sb2(f"PT{i}", [128, 2, 128], BF16) for i in range(2)]
                r_PT = S.regions(2)
                E1 = [sb2(f"E1{i}", [128, 128], F32) for i in range(2)]
                E2 = [sb2(f"E2{i}", [128, 128], F32) for i in range(2)]
                r_E1, r_E2 = S.regions(2), S.regions(2)
                qh = [sb2(f"qh{i}", [128, 128], BF16) for i in range(2)]
                r_qh = S.regions(2)
                kh = [sb2(f"kh{i}", [128, 128], BF16) for i in range(2)]
                r_kh = S.regions(2)
                for j in range(NCH):
                    i2 = j % 2
                    cs = slice(j * 128, (j + 1) * 128)
                    g4 = j // 4
                    S.op("pe", (lambda cs=cs: lambda e: e.matmul(bk_[0][:, 0:4], lhsT=GR[0:4, cs], rhs=b.cm[0:4, CM["ident"], 0:4], start=True, stop=True))(), reads=[r_GRall, b.r_cm], writes=[rb[0]])
                    S.op("pe", (lambda cs=cs: lambda e: e.matmul(bk_[0][:, 4:8], lhsT=GR[32:36, cs], rhs=b.cm[32:36, CM["ident"], 32:36], start=True, stop=True))(), reads=[r_GRall, b.r_cm], writes=[rb[0]])
                    S.op("dve", (lambda i2=i2: lambda e: e.tensor_copy(out=itok[i2][:], in_=bk_[0][:, 0:8]))(), reads=[rb[0]], writes=[r_itok[i2]])
                    S.op("dve", (lambda i2=i2: lambda e: e.tensor_tensor(out=atok[i2][:], in0=itok[i2][:, 0:4], in1=itok[i2][:, 4:8], op=ALU.subtract))(), reads=[r_itok[i2]], writes=[r_atok[i2]])
                    if j == 1 and dbg_stop(b, "W"):
                        return
                    for hh in range(2):
                        hd = 2 * hp + hh
                        ps = slice(64 * hh, 64 * hh + 64)
                        S.op("pe", (lambda hh=hh, hd=hd, cs=cs: lambda e: e.matmul(bk_[1][:, hh * 128:(hh + 1) * 128], lhsT=b.cm[32:36, CM["rowsel"] + hd, :], rhs=GR[32:36, cs], start=True, stop=False))(),
                             reads=[r_GRall, b.r_cm], writes=[rb[1]])
                        S.op("pe", (lambda hh=hh: lambda e: e.matmul(bk_[1][:, hh * 128:(hh + 1) * 128], lhsT=b.cm[:, CM["ident"], :], rhs=b.cm[:, CM["ncur"], :], start=False, stop=True))(),
                             reads=[b.r_cm], writes=[rb[1]])
                        S.op("act", (lambda hh=hh, hd=hd, i2=i2: lambda e: e.activation(out=eT[i2][:, hh, :], in_=bk_[1][:, hh * 128:(hh + 1) * 128], func=AF.Exp, bias=atok[i2][:, hd:hd + 1], scale=1.0))(),
                             reads=[rb[1], r_atok[i2]], writes=[r_eT[i2]])
                        S.op("pe", (lambda hh=hh, ps=ps, cs=cs: lambda e: e.matmul(bk_[2][:, hh * 128:(hh + 1) * 128], lhsT=kT[ps, cs], rhs=qT[ps, cs], start=True, stop=True))(),
                             reads=[r_kT[g4], r_qT[g4]], writes=[rb[2]])
                        S.op("pe", (lambda hh=hh, hd=hd, ps=ps, cs=cs: lambda e: e.matmul(bk_[3][ps, 0:128], lhsT=b.cm[32:36, CM["rowsel"] + hd, 0:64], rhs=GR[32:36, cs], start=True, stop=True))(),
                             reads=[r_GRall, b.r_cm], writes=[rb[3]])
                    if j == 1 and dbg_stop(b, "Y"):
                        return
                    S.op("dve", (lambda i2=i2: lambda e: e.tensor_tensor(out=PT[i2][:], in0=bk_[2][:, 0:256].rearrange("p (a c) -> p a c", c=128), in1=eT[i2][:], op=ALU.mult))(),
                         reads=[rb[2], r_eT[i2]], writes=[r_PT[i2]])
                    S.op("act", (lambda i2=i2: lambda e: e.activation(out=E1[i2][:], in_=bk_[3][:, 0:128], func=AF.Exp, bias=negG[1 - i2][:, 0:1], scale=1.0))(),
                         reads=[rb[3], r_negG[1 - i2]], writes=[r_E1[i2]])
                    S.op("act", (lambda i2=i2: lambda e: e.activation(out=E2[i2][:], in_=bk_[3][:, 0:128], func=AF.Exp))(), reads=[rb[3]], writes=[r_E2[i2]])
                    S.op("dve", (lambda i2=i2: lambda e: e.tensor_scalar(out=negG[i2][:], in0=bk_[3][:, 127:128], scalar1=-1.0, scalar2=None, op0=ALU.mult))(), reads=[rb[3]], writes=[r_negG[i2]])
                    if j == NCH - 1:
                        S.op("dve", lambda e: e.tensor_copy(out=gtot[:], in_=bk_[3][:, 127:128]), reads=[rb[3]], writes=[r_gtot])
                    S.op("dve", (lambda i2=i2, cs=cs: lambda e: e.tensor_tensor(out=qh[i2][:], in0=qT[:, cs], in1=E1[i2][:], op=ALU.mult))(), reads=[r_qT[g4], r_E1[i2]], writes=[r_qh[i2]])
                    S.op("pool", (lambda i2=i2, cs=cs: lambda e: e.tensor_tensor(out=qseg[:, cs], in0=qT[:, cs], in1=E2[i2][:], op=ALU.mult))(), reads=[r_qT[g4], r_E2[i2]], writes=[r_qseg[j]])
                    if j == 1 and dbg_stop(b, "Z"):
                        return
                    S.op("pe", (lambda cs=cs: lambda e: e.matmul(bk_[6][:, 0:128], lhsT=kT[:, cs], rhs=b.cmb[:, CM["ident"], :], start=True, stop=True))(), reads=[r_kT[g4], b.r_cm], writes=[rb[6]])
                    for hh in range(2):
                        S.op("dve", (lambda hh=hh, i2=i2: lambda e: e.tensor_scalar(out=kh[i2][:, hh * 64:(hh + 1) * 64], in0=bk_[6][:, hh * 64:(hh + 1) * 64],
                                                                                scalar1=eT[i2][:, hh, 127:128], scalar2=None, op0=ALU.mult))(), reads=[rb[6], r_eT[i2]], writes=[r_kh[i2]])
                    for hh in range(2):
                        ps = slice(64 * hh, 64 * hh + 64)
                        S.op("pe", (lambda hh=hh, ps=ps, j=j, i2=i2: lambda e: e.matmul(bk_[4][ps, 0:128], lhsT=vaug[:, j, hh, :], rhs=PT[i2][:, hh, :], start=True, stop=False))(),
                             reads=[r_vaug[g4], r_PT[i2]], writes=[rb[4]])
                        S.op("pe", (lambda hh=hh, ps=ps, i2=i2: lambda e: e.matmul(bk_[4][ps, 0:128], lhsT=Cb[ps, 0:64], rhs=qh[i2][ps, :], start=False, stop=True))(),
                             reads=[r_Cb, r_qh[i2]], writes=[rb[4]])
                        S.op("pe", (lambda hh=hh, ps=ps, j=j, i2=i2: lambda e: e.matmul(bk_[5][ps, 0:128], lhsT=b.cmb[:, CM["ones"], 0:64], rhs=PT[i2][:, hh, :], start=True, stop=False))(),
                             reads=[b.r_cm, r_PT[i2]], writes=[rb[5]])
                        S.op("pe", (lambda hh=hh, ps=ps, i2=i2: lambda e: e.matmul(bk_[5][ps, 0:128], lhsT=Cb[ps, 64:128], rhs=qh[i2][ps, :], start=False, stop=True))(),
                             reads=[r_Cb, r_qh[i2]], writes=[rb[5]])
                        S.op("pe", (lambda hh=hh, ps=ps, j=j, i2=i2: lambda e: e.matmul(bk_[7][ps, 0:64], lhsT=kh[i2][:, hh * 64:(hh + 1) * 64], rhs=vaug[:, j, hh, :], start=True, stop=True))(),
                             reads=[r_kh[i2], r_vaug[g4]], writes=[rb[7]])
                        S.op("pe", (lambda hh=hh, ps=ps, j=j, i2=i2: lambda e: e.matmul(bk_[7][ps, 64:128], lhsT=kh[i2][:, hh * 64:(hh + 1) * 64], rhs=b.cmb[:, CM["ones"], 0:64], start=True, stop=True))(),
                             reads=[r_kh[i2], b.r_cm], writes=[rb[7]])
                    if j == 1 and dbg_stop(b, "Q"):
                        return
                    S.op("act", (lambda cs=cs: lambda e: e.copy(out=numS[:, cs], in_=bk_[4][:, 0:128]))(), reads=[rb[4]], writes=[r_num[j]])
                    S.op("act", (lambda cs=cs: lambda e: e.copy(out=denS[:, cs], in_=bk_[5][:, 0:128]))(), reads=[rb[5]], writes=[r_den[j]])
                    S.op("dve", (lambda i2=i2: lambda e: e.scalar_tensor_tensor(out=Cf[:], in0=Cf[:], scalar=E1[i2][:, 127:128], in1=bk_[7][:, 0:128], op0=ALU.mult, op1=ALU.add))(),
                         reads=[r_Cf, r_E1[i2], rb[7]], writes=[r_Cf])
                    S.op("act", lambda e: e.copy(out=Cb[:], in_=Cf[:]), reads=[r_Cf], writes=[r_Cb])
                    if j == 0 and dbg_stop(b, "6"):
                        return
                    if j == 1 and dbg_stop(b, "7"):
                        return
                    if j == NCH - 1 and dbg_stop(b, "8"):
                        return
        with scope(b) as sb1:
            blk, r_blk = allgather(b, sb1, [(0, 128, Cf[:], [r_Cf]), (128, 1, gtot[:], [r_gtot])], 128, 129, F32, f"mst{hp}")
            if blk is None:
                return
            dec = sb1("dec", [128, 8], F32)
            r_dec = S.region()
            S.op("act", lambda e: e.activation(out=dec[:], in_=blk[:, :, 128], func=AF.Exp), reads=[r_blk], writes=[r_dec])
            Cs = sb1("Cs", [128, 128], F32)
            Csb = sb1("Csb", [128, 128], BF16)
            tt_ = sb1("tt_", [128, 128], F32)
            r_Cs, r_Csb, r_tt = S.regions(3)
            S.op("pool", lambda e: e.memset(Cs[:], 0.0), writes=[r_Cs])
            for jc in range(7):
                S.op("dve", (lambda jc=jc: lambda e: e.scalar_tensor_tensor(out=tt_[:], in0=Cs[:], scalar=dec[:, jc:jc + 1], in1=blk[:, jc, 0:128], op0=ALU.mult, op1=ALU.add))(),
                     reads=[r_Cs, r_dec, r_blk], writes=[r_tt])
                S.op("dve", lambda e: e.tensor_tensor(out=tt_[:], in0=tt_[:], in1=Cs[:], op=ALU.subtract), reads=[r_tt, r_Cs], writes=[r_tt])
                S.op("dve", (lambda jc=jc: lambda e: e.scalar_tensor_tensor(out=Cs[:], in0=tt_[:], scalar=b.pc[:, 8 + jc:9 + jc], in1=Cs[:], op0=ALU.mult, op1=ALU.add))(),
                     reads=[r_tt, r_Cs, b.r_pc], writes=[r_Cs])
            S.op("act", lambda e: e.copy(out=Csb[:], in_=Cs[:]), reads=[r_Cs], writes=[r_Csb])
            hT = [sb1(f"hT{i}", [128, 512], F32) for i in range(2)]
            dd = [sb1(f"dd{i}", [128, 512], F32) for i in range(2)]
            sq = [sb1(f"msq{i}", [128, 512], BF16) for i in range(2)]
            tmp = sb1("mtmp", [128, 512], F32)
            rs = [sb1(f"mrs{i}", [128, 512], F32) for i in range(2)]
            r_hT, r_dd, r_sq, r_rs = S.regions(2), S.regions(2), S.regions(2), S.regions(2)
            r_tmp = S.region()
            go2 = LAY[("mlstm_norm", l)]
            for tg in range(4):
                i2 = tg % 2
                ts = slice(tg * 512, (tg + 1) * 512)
                chs = [r for r in range(tg * 4, tg * 4 + 4)]
                bN, bD, bQ = 0 + i2, 2 + i2, 4 + i2
                for hh in range(2):
                    ps = slice(64 * hh, 64 * hh + 64)
                    S.op("pe", (lambda ps=ps, ts=ts, bN=bN: lambda e: e.matmul(bk_[bN][ps, :], lhsT=Csb[ps, 0:64], rhs=qseg[ps, ts], start=True, stop=True))(),
                         reads=[r_Csb] + [r_qseg[c] for c in chs], writes=[rb[bN]])
                    S.op("pe", (lambda ps=ps, ts=ts, bD=bD: lambda e: e.matmul(bk_[bD][ps, :], lhsT=Csb[ps, 64:128], rhs=qseg[ps, ts], start=True, stop=True))(),
                         reads=[r_Csb] + [r_qseg[c] for c in chs], writes=[rb[bD]])
                S.op("dve", (lambda i2=i2, ts=ts, bN=bN: lambda e: e.tensor_tensor(out=hT[i2][:], in0=bk_[bN][:], in1=numS[:, ts], op=ALU.add))(), reads=[rb[bN]] + [r_num[c] for c in chs], writes=[r_hT[i2]])
                S.op("dve", (lambda i2=i2, ts=ts, bD=bD: lambda e: e.tensor_tensor(out=dd[i2][:], in0=bk_[bD][:], in1=denS[:, ts], op=ALU.add))(), reads=[rb[bD]] + [r_den[c] for c in chs], writes=[r_dd[i2]])
                S.op("dve", (lambda i2=i2: lambda e: e.scalar_tensor_tensor(out=dd[i2][:], in0=dd[i2][:], scalar=-1.0, in1=dd[i2][:], op0=ALU.mult, op1=ALU.max))(), reads=[r_dd[i2]], writes=[r_dd[i2]])
                S.op("dve", (lambda i2=i2: lambda e: e.tensor_scalar(out=dd[i2][:], in0=dd[i2][:], scalar1=1.0, scalar2=None, op0=ALU.max))(), reads=[r_dd[i2]], writes=[r_dd[i2]])
                S.op("dve", (lambda i2=i2: lambda e: e.reciprocal(out=dd[i2][:], in_=dd[i2][:]))(), reads=[r_dd[i2]], writes=[r_dd[i2]])
                S.op("dve", (lambda i2=i2: lambda e: e.tensor_tensor(out=hT[i2][:], in0=hT[i2][:], in1=dd[i2][:], op=ALU.mult))(), reads=[r_hT[i2], r_dd[i2]], writes=[r_hT[i2]])
                S.op("act", (lambda i2=i2: lambda e: e.activation(out=sq[i2][:], in_=hT[i2][:], func=AF.Square))(), reads=[r_hT[i2]], writes=[r_sq[i2]])
                S.op("pe", (lambda i2=i2, bQ=bQ: lambda e: e.matmul(bk_[bQ][:], lhsT=b.cmb[:, CM["blk"], :], rhs=sq[i2][:], start=True, stop=True))(), reads=[r_sq[i2], b.r_cm], writes=[rb[bQ]])
                rstd_from_sumsq(b, rs[i2][:], r_rs[i2], tmp[:], r_tmp, bk_[bQ][:], rb[bQ], 1.0 / 64.0, "eps1")
                S.op("pool", (lambda i2=i2: lambda e: e.tensor_tensor(out=hT[i2][:], in0=hT[i2][:], in1=rs[i2][:], op=ALU.mult))(), reads=[r_hT[i2], r_rs[i2]], writes=[r_hT[i2]])
                S.op("dve", (lambda i2=i2, ts=ts: lambda e: e.scalar_tensor_tensor(out=ym[:, 4 + hp, ts], in0=hT[i2][:], scalar=b.pv[:, go2 + hp:go2 + hp + 1], in1=osig[:, ts], op0=ALU.mult, op1=ALU.mult))(),
                     reads=[r_hT[i2], r_osig[tg], b.r_pv], writes=[rym[4 + hp][tg]])


def rwkv_part(b, l, hp, h, rh, ym, rym):
    S = b.S
    w_in = b.D["w_in"]
    bk_ = b.banks
    rb = b.rb
    RB = 1800
    EM = float(np.exp(-0.5))
    cmf, cmb = b.cm, b.cmb

    def col(key, i=0):
        o = LAY[(key, l)] + i
        return b.pv[:, o:o + 1]

    last_row = {}

    def mm(out, lhsT, rhs, reads, writes, start=True, stop=True):
        tag = (lhsT.base_partition(), lhsT.partition_size())
        extra = []
        for w in writes:
            prev = last_row.get(id(w))
            if prev is not None and prev[0] != tag:
                extra.append(prev[1])
        tok = S.op("pe", lambda e: e.matmul(out, lhsT=lhsT, rhs=rhs, start=start, stop=stop), reads=reads, writes=writes, pe_wait=extra)
        for w in writes:
            last_row[id(w)] = (tag, tok)

    with scope(b) as sb:
        Yloc = sb("Yloc", [128, T], BF16)
        M2p = sb("M2p", [128, T], BF16)
        gT = sb("gT", [128, T], BF16)
        bv = sb("bv", [128, T], BF16)
        r_Yloc, r_M2p, r_gT, r_bv = S.regions(4), S.regions(4), S.regions(4), S.regions(4)
        Sf = sb("Sf", [128, 128], F32)
        Sb_ = sb("Sb", [128, 128], BF16)
        r_Sf, r_Sb = S.regions(2)
        with scope(b) as sb1:
            wr = sb1("wr", [128, 8, 5, 128], BF16)
            r_wr = S.region()
            dwr = S.dsem()
            for X, c0 in enumerate([RB + hp * 128, RB + 256 + hp * 128, RB + 512 + hp * 128, RB + 768, RB + 896]):
                S.dma("pool", dwr, (lambda X=X, c0=c0: lambda e: e.dma_start(out=wr[:, :, X, :], in_=wview(w_in, 0, DM, c0, 128)))(), writes=[r_wr])
            wup = sb1("wup", [128, 128], BF16)
            aup = sb1("aup", [128, 128], BF16)
            gup = sb1("gup", [128, 128], BF16)
            r_lr = S.region()
            dlr = S.dsem()
            S.dma("pool", dlr, lambda e: e.dma_start(out=wup[0:64, :], in_=b.D["rwkv_w_up"][:, hp * 128:(hp + 1) * 128]), writes=[r_lr])
            S.dma("pool", dlr, lambda e: e.dma_start(out=aup[64:128, :], in_=b.D["rwkv_a_up"][:, hp * 128:(hp + 1) * 128]), writes=[r_lr])
            S.dma("pool", dlr, lambda e: e.dma_start(out=gup[:], in_=b.D["rwkv_g_up"][:, hp * 128:(hp + 1) * 128]), writes=[r_lr])
            omk = sb1("omk", [128, 1], F32)
            r_omk = S.region()
            S.op("dve", lambda e: e.tensor_scalar(out=omk[:], in0=col("rwkv_k_a", hp), scalar1=-1.0, scalar2=1.0, op0=ALU.mult, op1=ALU.add), reads=[b.r_pv], writes=[r_omk])
            carry = sb1("carry", [128, 5], F32)
            r_carry = S.regions(5)
            tail = sb1("tail", [128, 5], F32)
            r_tail = S.region()
            for X in range(5):
                for k in range(8):
                    mm(bk_[X % 2][:, 0:128], wr[:, k, X, :], h[:, k, T - 128:T], [r_wr, rh[k][3]], [rb[X % 2]], start=(k == 0), stop=(k == 7))
                S.op("act", (lambda X=X: lambda e: e.copy(out=tail[:, X:X + 1], in_=bk_[X % 2][:, 127:128]))(), reads=[rb[X % 2]], writes=[r_tail])
            with scope(b) as sb2:
                blk, r_blk = allgather(b, sb2, [(0, 5, tail[:], [r_tail])], 128, 5, F32, f"rsh{hp}")
                if blk is None:
                    return
                r_call = S.region()
                select_prev(b, carry[:], r_call, blk, r_blk, 128, 0, 5)
            for X in range(5):
                r_carry[X] = r_call
            r_carry = [S.region() for _ in range(5)]
            for X in range(5):
                r_carry[X].w = r_call.w
            S.op("pool", lambda e: e.memset(Sf[:], 0.0), writes=[r_Sf])
            S.op("dve", lambda e: e.tensor_copy(out=Sf[:, 64:128], in_=cmf[:, CM["id2"], 0:64]), reads=[b.r_cm, r_Sf], writes=[r_Sf])
            S.op("act", lambda e: e.copy(out=Sb_[:], in_=Sf[:]), reads=[r_Sf], writes=[r_Sb])
            raw = sb1("rraw", [128, 513], F32)
            r_raw = S.region()
            us = [sb1(f"us{i}", [128, 512], F32) for i in range(3)]
            r_us = S.regions(3)
            R = [sb1(f"R{i}", [128, 512], F32) for i in range(7)]
            rR = S.regions(7)
            twx = sb1("twx", [128, 512], BF16)
            sgb = sb1("sgb", [128, 512], BF16)
            sqk = sgb
            rkr = sgb
            r_twx, r_sgb = S.regions(2)
            r_sqk = r_sgb
            r_rkr = r_sgb
            base8 = sb1("base8", [128, 8], F32)
            cumC8 = sb1("cumC8", [128, 8], F32)
            ecum8 = sb1("ecum8", [128, 8], F32)
            r_base8, r_cumC8, r_ecum8 = S.regions(3)
            fm = [sb1(f"fm{i}", [128, 512], BF16) for i in range(6)]
            r_fm = S.regions(6)
            tok = [sb1(f"tok{i}", [128, 4, 128], BF16) for i in range(4)]
            r_tok = S.regions(4, 4)
            vb = sb1("vb", [128, 512], BF16)
            r_vb = S.region()
            Am = sb1("Am", [128, 4, 2, 64], BF16)
            AmT = sb1("AmT", [128, 2, 64], BF16)
            r_Am, r_AmT = S.regions(2)
            Pb = [sb1(f"Pb{i}", [128, 2, 2, 64], BF16) for i in range(2)]
            r_Pb = S.regions(2)
            Zf = sb1("Zf", [128, 2, 128], F32)
            Zb = sb1("Zb", [128, 2, 128], BF16)
            r_Zf, r_Zb = S.regions(2)
            M2b = sb1("M2b", [128, 128], BF16)
            M3Tb = sb1("M3Tb", [128, 2, 64], BF16)
            Laug = sb1("Laug", [128, 2, 128], F32)
            r_M2b, r_M3Tb, r_Laug = S.regions(3)
            S.op("pool", lambda e: e.memset(Laug[:], 0.0), writes=[r_Laug])
            S.op("pool", lambda e: e.memset(base8[:], 0.0), writes=[r_base8])
            MU = LAY[("rwkv_mu", l)]
            mucol = [MU + hp, MU + 2 + hp, MU + 4 + hp, MU + 6, MU + 7]
            for tg in range(4):
                ts = slice(tg * 512, (tg + 1) * 512)
                if tg == 0 and dbg_stop(b, "1"):
                    return
                for X in range(5):
                    bkx = X % 2
                    for k in range(8):
                        mm(bk_[bkx][:], wr[:, k, X, :], h[:, k, ts], [r_wr, rh[k][tg]], [rb[bkx]], start=(k == 0), stop=(k == 7))
                    S.op("act", (lambda bkx=bkx: lambda e: e.copy(out=raw[:, 1:513], in_=bk_[bkx][:]))(), reads=[rb[bkx]], writes=[r_raw])
                    S.op("dve", (lambda X=X: lambda e: e.tensor_copy(out=raw[:, 0:1], in_=carry[:, X:X + 1]))(), reads=[r_carry[X], r_raw], writes=[r_raw])
                    S.op("dve", (lambda X=X: lambda e: e.tensor_copy(out=carry[:, X:X + 1], in_=raw[:, 512:513]))(), reads=[r_raw], writes=[r_carry[X]])
                    S.op("dve", lambda e: e.tensor_tensor(out=R[0][:], in0=raw[:, 0:512], in1=raw[:, 1:513], op=ALU.subtract), reads=[r_raw], writes=[rR[0]])
                    dstu = us[X] if X < 3 else R[1]
                    r_dstu = r_us[X] if X < 3 else rR[1]
                    S.op("dve", (lambda X=X, dstu=dstu: lambda e: e.scalar_tensor_tensor(out=dstu[:], in0=R[0][:], scalar=b.pv[:, mucol[X]:mucol[X] + 1], in1=raw[:, 1:513], op0=ALU.mult, op1=ALU.add))(),
                         reads=[rR[0], r_raw, b.r_pv], writes=[r_dstu])
                    if X == 3:
                        S.op("act", lambda e: e.activation(out=twx[0:64, :], in_=R[1][0:64, :], func=AF.Tanh), reads=[rR[1]], writes=[r_twx])
                        S.op("act", lambda e: e.copy(out=twx[64:128, :], in_=R[1][64:128, :]), reads=[rR[1]], writes=[r_twx])
                    if X == 4:
                        S.op("act", lambda e: e.activation(out=sgb[:], in_=R[1][:], func=AF.Sigmoid), reads=[rR[1]], writes=[r_sgb])
                if tg == 0 and dbg_stop(b, "2"):
                    return
                mm(bk_[2][:], wup[0:64, :], twx[0:64, :], [r_lr, r_twx], [rb[2]])
                S.op("act", lambda e: e.activation(out=R[0][:], in_=bk_[2][:], func=AF.Sigmoid, bias=col("rwkv_w0", hp), scale=1.0), reads=[rb[2], b.r_pv], writes=[rR[0]])
                S.op("dve", lambda e: e.tensor_scalar(out=R[0][:], in0=R[0][:], scalar1=-EM, scalar2=None, op0=ALU.mult), reads=[rR[0]], writes=[rR[0]])
                mm(bk_[3][:], aup[64:128, :], twx[64:128, :], [r_lr, r_twx], [rb[3]])
                S.op("act", lambda e: e.activation(out=R[1][:], in_=bk_[3][:], func=AF.Sigmoid, bias=col("rwkv_a0", hp), scale=1.0), reads=[rb[3], b.r_pv], writes=[rR[1]])
                mm(bk_[4][:], gup[:], sgb[:], [r_lr, r_sgb], [rb[4]])
                S.op("act", (lambda ts=ts: lambda e: e.copy(out=gT[:, ts], in_=bk_[4][:]))(), reads=[rb[4]], writes=[r_gT[tg]])
                S.op("dve", lambda e: e.tensor_scalar(out=R[2][:], in0=us[1][:], scalar1=col("rwkv_k_k", hp), scalar2=None, op0=ALU.mult), reads=[r_us[1], b.r_pv], writes=[rR[2]])
                S.op("act", lambda e: e.activation(out=sqk[:], in_=R[2][:], func=AF.Square), reads=[rR[2]], writes=[r_sqk])
                mm(bk_[5][:], cmb[:, CM["blk"], :], sqk[:], [b.r_cm, r_sqk], [rb[5]])
                S.op("act", lambda e: e.activation(out=R[3][:], in_=bk_[5][:], func=AF.Sqrt), reads=[rb[5]], writes=[rR[3]])
                S.op("dve", lambda e: e.tensor_scalar(out=R[3][:], in0=R[3][:], scalar1=1e-12, scalar2=None, op0=ALU.max), reads=[rR[3]], writes=[rR[3]])
                S.op("dve", lambda e: e.reciprocal(out=R[3][:], in_=R[3][:]), reads=[rR[3]], writes=[rR[3]])
                S.op("dve", lambda e: e.tensor_tensor(out=R[2][:], in0=R[2][:], in1=R[3][:], op=ALU.mult), reads=[rR[2], rR[3]], writes=[rR[2]])
                S.op("dve", lambda e: e.tensor_scalar(out=R[3][:], in0=R[1][:], scalar1=col("rwkv_k_a", hp), scalar2=omk[:, 0:1], op0=ALU.mult, op1=ALU.add), reads=[rR[1], r_omk, b.r_pv, rR[3]], writes=[rR[3]])
                S.op("dve", lambda e: e.tensor_tensor(out=R[3][:], in0=R[3][:], in1=us[1][:], op=ALU.mult), reads=[rR[3], r_us[1]], writes=[rR[3]])
                S.op("dve", lambda e: e.scalar_tensor_tensor(out=rkr[:], in0=us[0][:], scalar=col("rwkv_r_k", hp), in1=R[3][:], op0=ALU.mult, op1=ALU.mult), reads=[r_us[0], rR[3], b.r_pv], writes=[r_rkr])
                mm(bk_[6][:], cmb[:, CM["blk"], :], rkr[:], [b.r_cm, r_rkr], [rb[6]])
                S.op("dve", (lambda ts=ts: lambda e: e.tensor_tensor(out=bv[:, ts], in0=bk_[6][:], in1=us[2][:], op=ALU.mult))(), reads=[rb[6], r_us[2]], writes=[r_bv[tg]])
                S.op("act", lambda e: e.copy(out=vb[:], in_=us[2][:]), reads=[r_us[2]], writes=[r_vb])
                S.op("dve", lambda e: e.tensor_scalar(out=R[5][:], in0=R[0][:], scalar1=0.5, scalar2=None, op0=ALU.mult), reads=[rR[0]], writes=[rR[5]])
                S.op("dve", lambda e: e.tensor_tensor_scan(out=R[4][:], data0=R[5][:], data1=R[5][:], initial=0.0, op0=ALU.add, op1=ALU.add), reads=[rR[5]], writes=[rR[4]])
                S.op("dve", lambda e: e.tensor_copy(out=base8[:, 1:8], in_=R[4][:, 63:511:64]), reads=[rR[4], r_base8], writes=[r_base8])
                S.op("dve", lambda e: e.tensor_tensor(out=R[4][:].rearrange("p (a c) -> p a c", c=64), in0=R[4][:].rearrange("p (a c) -> p a c", c=64),
                                                      in1=base8[:, 0:8].unsqueeze(2).to_broadcast([128, 8, 64]), op=ALU.subtract), reads=[rR[4], r_base8], writes=[rR[4]])
                S.op("dve", lambda e: e.tensor_copy(out=cumC8[:], in_=R[4][:, 63:512:64]), reads=[rR[4]], writes=[r_cumC8])
                S.op("act", lambda e: e.activation(out=R[5][:], in_=R[4][:], func=AF.Exp), reads=[rR[4]], writes=[rR[5]])
                S.op("dve", lambda e: e.tensor_copy(out=ecum8[:], in_=R[5][:, 63:512:64]), reads=[rR[5]], writes=[r_ecum8])
                S.op("dve", lambda e: e.tensor_tensor(out=fm[0][:], in0=us[0][:], in1=R[5][:], op=ALU.mult), reads=[r_us[0], rR[5]], writes=[r_fm[0]])
                S.op("dve", lambda e: e.tensor_tensor(out=R[6][:], in0=R[4][:], in1=R[0][:], op=ALU.subtract), reads=[rR[4], rR[0]], writes=[rR[6]])
                S.op("act", lambda e: e.activation(out=R[6][:], in_=R[6][:], func=AF.Exp), reads=[rR[6]], writes=[rR[6]])
                S.op("dve", lambda e: e.scalar_tensor_tensor(out=fm[1][:], in0=R[2][:], scalar=-1.0, in1=R[6][:], op0=ALU.mult, op1=ALU.mult), reads=[rR[2], rR[6]], writes=[r_fm[1]])
                S.op("dve", lambda e: e.tensor_tensor(out=R[1][:], in0=R[1][:], in1=R[2][:], op=ALU.mult), reads=[rR[1], rR[2]], writes=[rR[1]])
                S.op("act", lambda e: e.activation(out=R[5][:], in_=R[4][:], func=AF.Exp, scale=-1.0), reads=[rR[4], r_ecum8, r_fm[0]], writes=[rR[5]])
                S.op("dve", lambda e: e.tensor_tensor(out=fm[2][:], in0=R[1][:], in1=R[5][:], op=ALU.mult), reads=[rR[1], rR[5]], writes=[r_fm[2]])
                S.op("dve", lambda e: e.tensor_tensor(out=fm[3][:], in0=R[3][:], in1=R[5][:], op=ALU.mult), reads=[rR[3], rR[5]], writes=[r_fm[3]])
                S.op("dve", lambda e: e.tensor_tensor(out=R[6][:].rearrange("p (a c) -> p a c", c=64), in0=cumC8[:, 0:8].unsqueeze(2).to_broadcast([128, 8, 64]),
                                                      in1=R[4][:].rearrange("p (a c) -> p a c", c=64), op=ALU.subtract), reads=[rR[4], r_cumC8, rR[6], r_fm[1]], writes=[rR[6]])
                S.op("act", lambda e: e.activation(out=R[6][:], in_=R[6][:], func=AF.Exp), reads=[rR[6]], writes=[rR[6]])
                S.op("dve", lambda e: e.tensor_tensor(out=fm[4][:], in0=R[1][:], in1=R[6][:], op=ALU.mult), reads=[rR[1], rR[6]], writes=[r_fm[4]])
                S.op("dve", lambda e: e.tensor_tensor(out=fm[5][:], in0=R[3][:], in1=R[6][:], op=ALU.mult), reads=[rR[3], rR[6]], writes=[r_fm[5]])
                if tg == 0 and dbg_stop(b, "3"):
                    return
                for qi, src, r_src in ((0, fm[1], r_fm[1]), (1, fm[4], r_fm[4]), (2, fm[5], r_fm[5]), (3, vb, r_vb)):
                    for tl in range(4):
                        S.op("pe", (lambda src=src, tl=tl, qi=qi: lambda e: e.matmul(bk_[7][:, tl * 128:(tl + 1) * 128], lhsT=src[:, tl * 128:(tl + 1) * 128], rhs=cmb[:, CM["ident"], :], start=True, stop=True))(),
                             reads=[r_src, b.r_cm], writes=[rb[7]])
                    S.op("act" if qi % 2 == 0 else "dve", (lambda qi=qi: (lambda e: e.copy(out=tok[qi][:], in_=bk_[7][:].rearrange("p (a c) -> p a c", c=128))) if qi % 2 == 0 else
                                                           (lambda e: e.tensor_copy(out=tok[qi][:], in_=bk_[7][:].rearrange("p (a c) -> p a c", c=128))))(),
                         reads=[rb[7]], writes=r_tok[qi])
                if tg == 0 and dbg_stop(b, "4"):
                    return
                for tl in range(4):
                    gl = tg * 4 + tl
                    rt_, at_, bt_, kt_ = fm[0], fm[1], fm[2], fm[3]
                    for c in range(2):
                        pcs = slice(64 * c, 64 * c + 64)
                        tks = slice(tl * 128 + c * 64, tl * 128 + c * 64 + 64)
                        for hh in range(2):
                            ps = slice(64 * hh, 64 * hh + 64)
                            for wi, (lh, rh_, rl, rr_) in enumerate(((bt_, at_, r_fm[2], r_fm[1]), (kt_, at_, r_fm[3], r_fm[1]), (bt_, rt_, r_fm[2], r_fm[0]), (kt_, rt_, r_fm[3], r_fm[0]))):
                                o0 = (wi * 2 + hh) * 64
                                mm(bk_[0][pcs, o0:o0 + 64], lh[ps, tks], rh_[ps, tks], [rl, rr_], [rb[0]])
                            mm(bk_[1][pcs, hh * 64:(hh + 1) * 64], at_[ps, tks], bt_[ps, tks], [r_fm[1], r_fm[2]], [rb[1]])
                    S.op("dve", lambda e: e.tensor_tensor(out=Am[:].rearrange("p a b c -> p (a b c)"), in0=bk_[0][:], in1=cmb[:, CM["rm"]:CM["rm"] + 4, :].rearrange("p a c -> p (a c)"), op=ALU.mult),
                         reads=[rb[0], b.r_cm], writes=[r_Am])
                    S.op("dve", lambda e: e.tensor_tensor(out=AmT[:].rearrange("p b c -> p (b c)"), in0=bk_[1][:, 0:128], in1=cmf[:, CM["rmT"], :], op=ALU.mult), reads=[rb[1], b.r_cm], writes=[r_AmT])
                    for c in range(2):
                        pcs = slice(64 * c, 64 * c + 64)
                        for hh in range(2):
                            mm(bk_[4][pcs, hh * 64:(hh + 1) * 64], Am[pcs, 1, hh, :], tok[3][pcs, tl, hh * 64:(hh + 1) * 64], [r_Am, r_tok[3][tl]], [rb[4]])
                    S.op("dve", lambda e: e.tensor_copy(out=Zf[:, :, 0:64], in_=bk_[4][:, 0:128].rearrange("p (b c) -> p b c", c=64)), reads=[rb[4], r_Zf], writes=[r_Zf])
                    S.op("pool", (lambda tl=tl: lambda e: e.tensor_copy(out=Zf[:, :, 64:128], in_=tok[0][:, tl, :].rearrange("p (b c) -> p b c", c=64)))(), reads=[r_tok[0][tl], r_Zf], writes=[r_Zf])
                    S.op("act", lambda e: e.copy(out=Zb[:], in_=Zf[:]), reads=[r_Zf], writes=[r_Zb])
                    if gl == 0 and dbg_stop(b, "5"):
                        return
                    for lev in range(6):
                        if lev == 0:
                            Pl = lambda pcs, hh: Am[pcs, 0, hh, :]
                            PlT = lambda pcs, hh: AmT[pcs, hh, :]
                            rP = [r_Am, r_AmT]
                        else:
                            pbuf = Pb[lev % 2]
                            Pl = (lambda pbuf: lambda pcs, hh: pbuf[pcs, 0, hh, :])(pbuf)
                            PlT = (lambda pbuf: lambda pcs, hh: pbuf[pcs, 1, hh, :])(pbuf)
                            rP = [r_Pb[lev % 2]]
                        half = (lev % 2) * 256
                        for c in range(2):
                            pcs = slice(64 * c, 64 * c + 64)
                            for hh in range(2):
                                mm(bk_[3][pcs, half + hh * 128:half + (hh + 1) * 128], Pl(pcs, hh), Zb[pcs, hh, :], rP + [r_Zb], [rb[3]])
                        if lev < 5:
                            for c in range(2):
                                pcs = slice(64 * c, 64 * c + 64)
                                for hh in range(2):
                                    mm(bk_[2][pcs, half + hh * 64:half + (hh + 1) * 64], PlT(pcs, hh), Pl(pcs, hh), rP, [rb[2]])
                                    mm(bk_[2][pcs, half + 128 + hh * 64:half + 128 + (hh + 1) * 64], Pl(pcs, hh), PlT(pcs, hh), rP, [rb[2]])
                            nb = Pb[(lev + 1) % 2]
                            S.op("act", (lambda nb=nb, half=half: lambda e: e.copy(out=nb[:].rearrange("p a b c -> p (a b c)"), in_=bk_[2][:, half:half + 256]))(), reads=[rb[2]], writes=[r_Pb[(lev + 1) % 2]])
                        S.op("dve", (lambda half=half: lambda e: e.tensor_tensor(out=Zf[:].rearrange("p b c -> p (b c)"), in0=bk_[3][:, half:half + 256], in1=Zf[:].rearrange("p b c -> p (b c)"), op=ALU.add))(),
                             reads=[rb[3], r_Zf], writes=[r_Zf])
                        S.op("act", lambda e: e.copy(out=Zb[:], in_=Zf[:]), reads=[r_Zf], writes=[r_Zb])
                    if gl == 0 and dbg_stop(b, "6"):
                        return
                    for c in range(2):
                        pcs = slice(64 * c, 64 * c + 64)
                        for hh in range(2):
                            ps = slice(64 * hh, 64 * hh + 64)
                            mm(bk_[4][ps, 256 + c * 64:256 + (c + 1) * 64], Zb[pcs, hh, 64:128], Am[pcs, 2, hh, :], [r_Zb, r_Am], [rb[4]])
                            mm(bk_[5][ps, c * 64:(c + 1) * 64], Zb[pcs, hh, 64:128], tok[1][pcs, tl, hh * 64:(hh + 1) * 64], [r_Zb, r_tok[1][tl]], [rb[5]])
                    for c in range(2):
                        pcs = slice(64 * c, 64 * c + 64)
                        for hh in range(2):
                            ps = slice(64 * hh, 64 * hh + 64)
                            mm(bk_[5][ps, 128 + c * 64:128 + (c + 1) * 64], tok[2][pcs, tl, hh * 64:(hh + 1) * 64], tok[3][pcs, tl, hh * 64:(hh + 1) * 64], [r_tok[2][tl], r_tok[3][tl]], [rb[5]], start=True, stop=False)
                            mm(bk_[5][ps, 128 + c * 64:128 + (c + 1) * 64], tok[1][pcs, tl, hh * 64:(hh + 1) * 64], Zb[pcs, hh, 0:64], [r_tok[1][tl], r_Zb], [rb[5]], start=False, stop=True)
                    S.op("dve", (lambda tl=tl: lambda e: e.tensor_tensor(out=M2b[:], in0=bk_[4][:, 256:384], in1=fm[0][:, tl * 128:(tl + 1) * 128], op=ALU.add))(), reads=[rb[4], r_fm[0]], writes=[r_M2b])
                    for c in range(2):
                        ch = tl * 2 + c
                        S.op("dve", (lambda c=c, ch=ch: lambda e: e.scalar_tensor_tensor(out=M3Tb[:, c, :], in0=cmf[:, CM["id2"], 0:64], scalar=ecum8[:, ch:ch + 1], in1=bk_[5][:, c * 64:(c + 1) * 64], op0=ALU.mult, op1=ALU.add))(),
                             reads=[rb[5], r_ecum8, b.r_cm], writes=[r_M3Tb])
                    S.op("act", lambda e: e.copy(out=Laug[:, :, 0:64], in_=bk_[5][:, 128:256].rearrange("p (a c) -> p a c", c=64)), reads=[rb[5], r_Laug], writes=[r_Laug])
                    for c in range(2):
                        pcs = slice(64 * c, 64 * c + 64)
                        for hh in range(2):
                            ps = slice(64 * hh, 64 * hh + 64)
                            oc = slice(c * 64, (c + 1) * 64)
                            mm(bk_[6][ps, oc], Zb[pcs, hh, 0:64], Am[pcs, 2, hh, :], [r_Zb, r_Am], [rb[6]], start=True, stop=False)
                            mm(bk_[6][ps, oc], tok[3][pcs, tl, hh * 64:(hh + 1) * 64], Am[pcs, 3, hh, :], [r_tok[3][tl], r_Am], [rb[6]], start=False, stop=False)
                            mm(bk_[6][ps, oc], Sb_[ps, 0:64], M2b[ps, oc], [r_Sb, r_M2b], [rb[6]], start=False, stop=True)
                            mm(bk_[6][ps, 128 + c * 64:128 + (c + 1) * 64], Sb_[ps, 64:128], M2b[ps, oc], [r_Sb, r_M2b], [rb[6]])
                            mm(bk_[7][ps, 0:128], M3Tb[ps, c, :], Sb_[ps, :], [r_M3Tb, r_Sb], [rb[7]])
                        S.op("dve", (lambda c=c: lambda e: e.tensor_tensor(out=Sf[:], in0=bk_[7][:, 0:128], in1=Laug[:, c, :], op=ALU.add))(), reads=[rb[7], r_Laug, r_Sf], writes=[r_Sf])
                        S.op("act", lambda e: e.copy(out=Sb_[:], in_=Sf[:]), reads=[r_Sf], writes=[r_Sb])
                    if gl == 0 and dbg_stop(b, "7"):
                        return
                    tsl = slice(gl * 128, (gl + 1) * 128)
                    S.op("act", (lambda tsl=tsl: lambda e: e.copy(out=Yloc[:, tsl], in_=bk_[6][:, 0:128]))(), reads=[rb[6]], writes=[r_Yloc[tg]])
                    S.op("dve", (lambda tsl=tsl: lambda e: e.tensor_copy(out=M2p[:, tsl], in_=bk_[6][:, 128:256]))(), reads=[rb[6]], writes=[r_M2p[tg]])
        with scope(b) as sb1:
            blk, r_blk = allgather(b, sb1, [(0, 128, Sf[:], [r_Sf])], 128, 128, F32, f"rst{hp}")
            if blk is None:
                return
            blkb = sb1("blkb", [128, 8, 64], BF16)
            r_blkb = S.region()
            S.op("dve", lambda e: e.tensor_copy(out=blkb[:], in_=blk[:, :, 64:128]), reads=[r_blk], writes=[r_blkb])
            MTb = sb1("MTb", [128, 7, 64], BF16)
            r_MTb = S.region()
            for jc in range(7):
                for hh in range(2):
                    ps = slice(64 * hh, 64 * hh + 64)
                    S.op("pe", (lambda jc=jc, ps=ps: lambda e: e.matmul(bk_[0][ps, jc * 64:(jc + 1) * 64], lhsT=blkb[ps, jc, :], rhs=cmb[ps, CM["ident"], ps], start=True, stop=True))(),
                         reads=[r_blkb, b.r_cm], writes=[rb[0]])
            S.op("act", lambda e: e.copy(out=MTb[:].rearrange("p a c -> p (a c)"), in_=bk_[0][:, 0:448]), reads=[rb[0]], writes=[r_MTb])
            Ss = sb1("Ss", [128, 64], F32)
            Ssb = sb1("Ssb", [128, 64], BF16)
            tt_ = sb1("rtt", [128, 64], F32)
            r_Ss, r_Ssb, r_tt = S.regions(3)
            S.op("pool", lambda e: e.memset(Ss[:], 0.0), writes=[r_Ss])
            S.op("pool", lambda e: e.memset(Ssb[:], 0.0), writes=[r_Ssb])
            for jc in range(7):
                for hh in range(2):
                    ps = slice(64 * hh, 64 * hh + 64)
                    mm(bk_[1][ps, 0:64], MTb[ps, jc, :], Ssb[ps, :], [r_MTb, r_Ssb], [rb[1]])
                S.op("dve", (lambda jc=jc: lambda e: e.tensor_tensor(out=tt_[:], in0=bk_[1][:, 0:64], in1=blk[:, jc, 0:64], op=ALU.add))(), reads=[rb[1], r_blk], writes=[r_tt])
                S.op("dve", lambda e: e.tensor_tensor(out=tt_[:], in0=tt_[:], in1=Ss[:], op=ALU.subtract), reads=[r_tt, r_Ss], writes=[r_tt])
                S.op("dve", (lambda jc=jc: lambda e: e.scalar_tensor_tensor(out=Ss[:], in0=tt_[:], scalar=b.pc[:, 8 + jc:9 + jc], in1=Ss[:], op0=ALU.mult, op1=ALU.add))(), reads=[r_tt, r_Ss, b.r_pc], writes=[r_Ss])
                S.op("act", lambda e: e.copy(out=Ssb[:], in_=Ss[:]), reads=[r_Ss], writes=[r_Ssb])
            Y = [sb1(f"Yf{i}", [128, 512], F32) for i in range(2)]
            sq = [sb1(f"rsq{i}", [128, 512], BF16) for i in range(2)]
            rs = [sb1(f"rrs{i}", [128, 512], F32) for i in range(2)]
            tmp = sb1("rtmp", [128, 512], F32)
            r_Y, r_sq, r_rs = S.regions(2), S.regions(2), S.regions(2)
            r_tmp = S.region()
            for tg in range(4):
                i2 = tg % 2
                ts = slice(tg * 512, (tg + 1) * 512)
                bC, bM, bQ = 2 + i2, 4 + i2, 6 + i2
                for hh in range(2):
                    ps = slice(64 * hh, 64 * hh + 64)
                    mm(bk_[bC][ps, :], Ssb[ps, :], M2p[ps, ts], [r_Ssb, r_M2p[tg]], [rb[bC]])
                S.op("dve", (lambda i2=i2, ts=ts, bC=bC: lambda e: e.tensor_tensor(out=Y[i2][:], in0=bk_[bC][:], in1=Yloc[:, ts], op=ALU.add))(), reads=[rb[bC], r_Yloc[tg]], writes=[r_Y[i2]])
                mm(bk_[bM][:], cmf[:, CM["blk"], :], Y[i2][:], [b.r_cm, r_Y[i2]], [rb[bM]])
                S.op("dve", (lambda i2=i2, bM=bM: lambda e: e.scalar_tensor_tensor(out=Y[i2][:], in0=bk_[bM][:], scalar=-1.0 / 64.0, in1=Y[i2][:], op0=ALU.mult, op1=ALU.add))(), reads=[rb[bM], r_Y[i2]], writes=[r_Y[i2]])
                S.op("act", (lambda i2=i2: lambda e: e.activation(out=sq[i2][:], in_=Y[i2][:], func=AF.Square))(), reads=[r_Y[i2]], writes=[r_sq[i2]])
                mm(bk_[bQ][:], cmb[:, CM["blk"], :], sq[i2][:], [b.r_cm, r_sq[i2]], [rb[bQ]])
                rstd_from_sumsq(b, rs[i2][:], r_rs[i2], tmp[:], r_tmp, bk_[bQ][:], rb[bQ], 1.0 / 64.0, "gneps")
                S.op("pool", (lambda i2=i2: lambda e: e.tensor_tensor(out=Y[i2][:], in0=Y[i2][:], in1=rs[i2][:], op=ALU.mult))(), reads=[r_Y[i2], r_rs[i2]], writes=[r_Y[i2]])
                S.op("dve", (lambda i2=i2: lambda e: e.tensor_scalar(out=Y[i2][:], in0=Y[i2][:], scalar1=col("rwkv_ln_w", hp), scalar2=col("rwkv_ln_b", hp), op0=ALU.mult, op1=ALU.add))(), reads=[r_Y[i2], b.r_pv], writes=[r_Y[i2]])
                S.op("dve", (lambda i2=i2, ts=ts: lambda e: e.tensor_tensor(out=Y[i2][:], in0=Y[i2][:], in1=bv[:, ts], op=ALU.add))(), reads=[r_Y[i2], r_bv[tg]], writes=[r_Y[i2]])
                S.op("dve", (lambda i2=i2, ts=ts: lambda e: e.tensor_tensor(out=ym[:, 6 + hp, ts], in0=Y[i2][:], in1=gT[:, ts], op=ALU.mult))(), reads=[r_Y[i2], r_gT[tg]], writes=[rym[6 + hp][tg]])


def mix_layer(b, l, debug=False, parts="amr"):
    S = b.S
    bk_ = b.banks
    rb = b.rb
    w_out = b.D.get("w_out")
    with scope(b) as sb:
        ym = sb("ym", [128, 8, T], BF16)
        rym = S.regions(8, 4)
        with scope(b) as sb1:
            h = sb1("mh", [128, 8, T], BF16)
            rh = S.regions(8, 4)
            with scope(b) as sb2:
                prenorm(b, sb2, 0, T, "ln_mix_pre", l, h, rh)
            if "a" in parts:
                attn_part(b, l, h, rh, ym, rym)
            if "m" in parts:
                for hp in range(2):
                    mlstm_part(b, l, hp, h, rh, ym, rym)
            if "r" in parts:
                for hp in range(2):
                    rwkv_part(b, l, hp, h, rh, ym, rym)
        if debug:
            for c in range(8):
                for tg in range(4):
                    S.op("dve", (lambda c=c, tg=tg: lambda e: e.tensor_copy(out=b.xT[:, c, tg * 512:(tg + 1) * 512], in_=ym[:, c, tg * 512:(tg + 1) * 512]))(),
                         reads=[rym[c][tg], b.rx[c][tg]], writes=[b.rx[c][tg]])
            return
        for half in range(2):
            t0 = half * 1024
            with scope(b) as sb1:
                y = sb1("my", [128, 8, 1024], F32)
                ry = S.regions(8, 2)
                with scope(b) as sb2:
                    wo = [sb2(f"mwo{i}", [128, 8, 256], BF16) for i in range(2)]
                    rwo = S.regions(2)
                    dwo = [S.dsem() for _ in range(2)]
                    for jg in range(4):
                        s = jg % 2
                        cs_ = slice(jg * 256, (jg + 1) * 256)
                        for g in range(2):
                            S.dma("pool", dwo[s], (lambda s=s, g=g, cs_=cs_: lambda e: e.dma_start(out=wo[s][64 * g:64 * g + 64, 0:4, :], in_=w_out[g * 256:(g + 1) * 256, cs_].rearrange("(i d) n -> d i n", d=64)))(), writes=[rwo[s]])
                        S.dma("pool", dwo[s], (lambda s=s, cs_=cs_: lambda e: e.dma_start(out=wo[s][:, 4:8, :], in_=w_out[512:1024, cs_].rearrange("(c p) n -> p c n", p=128)))(), writes=[rwo[s]])
                        for ji in range(2):
                            j = jg * 2 + ji
                            par = j % 2
                            for k in range(8):
                                for tg in range(2):
                                    bk = 4 * par + tg
                                    g4 = half * 2 + tg
                                    S.op("pe", (lambda s=s, k=k, ji=ji, tg=tg, bk=bk: lambda e: e.matmul(bk_[bk][:], lhsT=wo[s][:, k, ji * 128:(ji + 1) * 128], rhs=ym[:, k, t0 + tg * 512:t0 + (tg + 1) * 512], start=(k == 0), stop=(k == 7)))(),
                                         reads=[rwo[s], rym[k][g4]], writes=[rb[bk]])
                            for tg in range(2):
                                bk = 4 * par + tg
                                if tg == 0:
                                    S.op("act", (lambda j=j, tg=tg, bk=bk: lambda e: e.copy(out=y[:, j, tg * 512:(tg + 1) * 512], in_=bk_[bk][:]))(), reads=[rb[bk]], writes=[ry[j][tg]])
                                else:
                                    S.op("dve", (lambda j=j, tg=tg, bk=bk: lambda e: e.tensor_copy(out=y[:, j, tg * 512:(tg + 1) * 512], in_=bk_[bk][:]))(), reads=[rb[bk]], writes=[ry[j][tg]])
                with scope(b) as sb2:
                    postnorm_residual(b, sb2, t0, 1024, "ln_mix_post", l, y, ry, 1.0)


_PROGS = {}


def get_prog(kind, parts="amr"):
    key = (kind, parts)
    if key not in _PROGS:
        _PROGS[key] = build_program(kind, parts)
    return _PROGS[key]


def run_prog(kind, L, inp, xT_list, gathered, parts="amr"):
    nc, b = get_prog(kind, parts)
    pv = pack_params(inp, L)
    cmat = const_mats()
    pos = np.asarray(inp["positions"], np.int32)
    nprev = np.ascontiguousarray(cmat.reshape(128, -1, 128)[:, CM["nprev"], :])
    maps = []
    for c in range(NCORES):
        sl = slice(c * T, (c + 1) * T)
        pcore = np.zeros((128, 32), np.float32)
        if c > 0:
            pcore[:, c - 1] = 1.0
        pcore[:, 8:8 + c] = 1.0
        m = {"xT": np.ascontiguousarray(xT_list[c], np.float32), "pos": np.ascontiguousarray(pos[:, sl]), "pvec": pv, "cmat": cmat,
             "pcore": pcore, "pcm": (np.full((128, 128), NEG, np.float32) if c == 0 else nprev)}
        full = {}
        for name in b.in_names:
            if name in m:
                full[name] = m[name]
            elif name == "pT":
                full[name] = np.ascontiguousarray(np.asarray(inp["p"], np.float32)[L, 0, sl].T)
            elif name.startswith("g_"):
                full[name] = gathered[name[2:]]
            else:
                full[name] = np.ascontiguousarray(np.asarray(inp[name][L], np.float32))
        maps.append(full)
    res = run_bass_kernel_spmd(nc, maps, core_ids=list(range(NCORES)))
    return res.results


def gather_payloads(results, keys):
    return {k: np.ascontiguousarray(np.concatenate([np.asarray(r["x_" + k]) for r in results], axis=0)) for k in keys}


def kernel(**inputs):
    x = np.asarray(inputs["x"], np.float32)[0]
    xT = [np.ascontiguousarray(x[c * T:(c + 1) * T].T) for c in range(NCORES)]
    for L in range(DEPTH):
        rf = run_prog("F", L, inputs, xT, {})
        xT = [np.asarray(r["outT"]) for r in rf]
        ra = run_prog("A0", L, inputs, xT, {})
        g = gather_payloads(ra, HALO_KEYS)
        rb_ = run_prog("B", L, inputs, xT, g)
        g.update(gather_payloads(rb_, STATE_KEYS))
        rc = run_prog("C2", L, inputs, xT, g)
        xT = [np.asarray(r["outT"]) for r in rc]
        rg = run_prog("G", L, inputs, xT, {})
        xT = [np.asarray(r["outT"]) for r in rg]
    out = np.concatenate([t.T for t in xT], axis=0)
    return out[None].astype(np.float32)
```

```python
import contextlib
import numpy as np
import concourse.bass as bass
import concourse.mybir as mybir
from concourse.bass_utils import run_bass_kernel_spmd

F32 = mybir.dt.float32
BF16 = mybir.dt.bfloat16
I32 = mybir.dt.int32
AF = mybir.ActivationFunctionType
ALU = mybir.AluOpType
AX = mybir.AxisListType

NCORES = 8
T = 2048
DM = 1024
DFF = 2816
DEPTH = 2


class Reg:
    __slots__ = ("name", "w", "r", "excl")

    def __init__(self, name):
        self.name = name
        self.w = None
        self.r = {}
        self.excl = False


class DSem:
    def __init__(self, key, h):
        self.key = key
        self.h = h
        self.count = 0


class Sched:
    CE = ("pe", "act", "dve", "pool")
    ENG = ("pe", "act", "dve", "pool", "sp")

    def __init__(self, nc, es):
        self.nc = nc
        self.es = es
        self.cnt = {e: 0 for e in self.CE}
        self.sem = {e: es.enter_context(nc.semaphore(f"c_{e}")) for e in self.CE}
        self.seen = {e: {} for e in self.ENG}
        self.nds = 0
        self.nreg = 0
        self.dsems = []
        self.engs = {"pe": nc.tensor, "act": nc.scalar, "dve": nc.vector, "pool": nc.gpsimd, "sp": nc.sync}
        self.ninst = 0
        self.free_dsems = []
        self.scope_stack = []

    def region(self, name=None):
        self.nreg += 1
        return Reg(name or f"r{self.nreg}")

    def regions(self, *shape):
        if len(shape) == 1:
            return [self.region() for _ in range(shape[0])]
        return [self.regions(*shape[1:]) for _ in range(shape[0])]

    def dsem(self, name=None):
        if name is None and self.free_dsems:
            d = self.free_dsems.pop()
        else:
            self.nds += 1
            nm = name or f"d{self.nds}"
            d = DSem("dma_" + nm, self.es.enter_context(self.nc.semaphore("ds_" + nm)))
            self.dsems.append(d)
        if name is None and self.scope_stack:
            self.scope_stack[-1].append(d)
        return d

    def _deps(self, eng, reads, writes):
        waits = {}

        def add(key, sem, val, raw):
            if key == eng and not (raw and eng != "pe"):
                return
            if self.seen[eng].get(key, 0) >= val:
                return
            if key not in waits or waits[key][1] < val:
                waits[key] = (sem, val)

        for r in reads:
            if r.w is not None:
                add(*r.w, True)
        for w in writes:
            if w.w is not None:
                add(*w.w, True)
            for key, (sem, val) in w.r.items():
                add(key, sem, val, False)
        for key, (sem, val) in waits.items():
            self.seen[eng][key] = val
        return list(waits.values())

    def _mark(self, tok, reads, writes):
        key, sem, val = tok
        for r in reads:
            if key not in r.r or r.r[key][1] < val:
                r.r[key] = (sem, val)
        for w in writes:
            w.w = tok
            w.r = {}

    def _emit(self, name, waits, fn, inc):
        e = self.engs[name]
        for sem, val in waits:
            e.wait_ge(sem, val)
            self.ninst += 1
        if fn is not None:
            ins = fn(e)
            ins.then_inc(inc[0], inc[1])
            self.ninst += 1

    def op(self, eng, fn, reads=(), writes=(), pe_wait=()):
        ex = [r for r in reads if r.excl]
        if ex:
            reads = [r for r in reads if not r.excl]
            writes = list(writes) + ex
        waits = self._deps(eng, reads, writes)
        for (key, sem, val) in pe_wait:
            if self.seen[eng].get(key, 0) < val:
                waits.append((sem, val))
                self.seen[eng][key] = val
        self.cnt[eng] += 1
        tok = (eng, self.sem[eng], self.cnt[eng])
        self._emit(eng, waits, fn, (self.sem[eng], 1))
        self._mark(tok, reads, writes)
        return tok

    def dma(self, queue, dsem, fn, reads=(), writes=(), inc=16):
        waits = self._deps(queue, reads, writes)
        dsem.count += inc
        tok = (dsem.key, dsem.h, dsem.count)
        self._emit(queue, waits, fn, (dsem.h, inc))
        self._mark(tok, reads, writes)
        return tok

    def wait_all(self, eng, regs):
        waits = self._deps(eng, regs, ())
        self._emit(eng, waits, None, None)

    def barrier(self):
        for eng in self.ENG:
            waits = []
            for o in self.CE:
                if o != eng and self.cnt[o] > self.seen[eng].get(o, 0):
                    waits.append((self.sem[o], self.cnt[o]))
                    self.seen[eng][o] = self.cnt[o]
            for d in self.dsems:
                if d.count > self.seen[eng].get(d.key, 0):
                    waits.append((d.h, d.count))
                    self.seen[eng][d.key] = d.count
            self._emit(eng, waits, None, None)


_VEC8 = ["ln_ffn1_pre", "ln_ffn1_post", "ln_mix_pre", "ln_mix_post", "ln_ffn2_pre", "ln_ffn2_post",
         "ln_ple_pre", "ln_ple_post", "rwkv_mu"]
_VEC2 = ["mlstm_norm", "rwkv_w0", "rwkv_a0", "rwkv_k_k", "rwkv_k_a", "rwkv_r_k", "rwkv_ln_w", "rwkv_ln_b"]


def param_layout():
    lay = {}
    off = 0
    for l in range(1):
        for n in _VEC8:
            lay[(n, l)] = off
            off += 8
        for n in _VEC2:
            lay[(n, l)] = off
            off += 2
        lay[("conv", l)] = off
        off += 16
        lay[("sink", l)] = off
        off += 4
        lay[("gbias", l)] = off
        off += 1
    for n in ["eps1", "eps4", "gneps", "invfreq", "one", "zero"]:
        lay[n] = off
        off += 1
    lay["_n"] = off
    return lay


LAY = param_layout()


def pack_params(inp, L):
    lay = LAY
    pv = np.zeros((128, lay["_n"]), np.float32)
    l = 0
    for n in _VEC8:
        pv[:, lay[(n, l)]:lay[(n, l)] + 8] = np.asarray(inp[n][L], np.float32).reshape(8, 128).T
    for n in _VEC2:
        pv[:, lay[(n, l)]:lay[(n, l)] + 2] = np.asarray(inp[n][L], np.float32).reshape(2, 128).T
    conv = np.asarray(inp["mlstm_conv"][L], np.float32)
    for tile in range(4):
        for tap in range(4):
            pv[:, lay[("conv", l)] + tile * 4 + tap] = conv[tap, tile * 128:(tile + 1) * 128]
    sk = np.asarray(inp["attn_sinks"][L], np.float32)
    for i in range(4):
        pv[0:64, lay[("sink", l)] + i] = sk[i]
        pv[64:128, lay[("sink", l)] + i] = sk[4 + i]
    pv[0:4, lay[("gbias", l)]] = np.asarray(inp["mlstm_i_bias"][L], np.float32)
    pv[32:36, lay[("gbias", l)]] = np.asarray(inp["mlstm_f_bias"][L], np.float32)
    pv[:, lay["eps1"]] = 1e-6
    pv[:, lay["eps4"]] = 4e-6
    pv[:, lay["gneps"]] = 64e-5
    inv = (500000.0 ** (-np.arange(0, 16, 2, dtype=np.float32) / 16.0)).astype(np.float32)
    for p_ in range(128):
        d = p_ % 64
        pv[p_, lay["invfreq"]] = inv[d % 8] if d < 16 else 0.0
    pv[:, lay["one"]] = 1.0
    return pv


CM = {"ident": 0, "blk": 1, "rowsel": 2, "rmT": 6, "ncur": 7, "nprev": 8, "id2": 9, "ones": 10, "perm": 11, "rm": 12, "_n": 16}
NF32 = 10
NEG = -30000.0


def const_mats():
    cm = np.zeros((128, CM["_n"], 128), np.float32)
    cm[:, CM["ident"], :] = np.eye(128)
    cm[:, CM["ones"], :] = 1.0
    blk = np.zeros((128, 128))
    blk[:64, :64] = 1
    blk[64:, 64:] = 1
    cm[:, CM["blk"], :] = blk
    P = np.zeros((128, 128))
    for hb in (0, 64):
        for i in range(8):
            P[hb + i + 8, hb + i] = -1.0
            P[hb + i, hb + i + 8] = 1.0
    cm[:, CM["perm"], :] = P
    for h in range(4):
        for k in range(128):
            if k % 32 == h:
                cm[k, CM["rowsel"] + h, :] = 1.0
    s_ = (np.arange(128) % 64)[:, None]
    t_ = np.arange(64)[None, :]
    strict = (s_ < t_).astype(np.float32)
    incl = (s_ <= t_).astype(np.float32)
    rm = np.stack([strict, strict, strict, strict, incl, incl, incl, incl], axis=1)
    cm[:, CM["rm"]:CM["rm"] + 4, :] = rm.reshape(128, 4, 128)
    lower = (s_ > t_).astype(np.float32)
    cm[:, CM["rmT"], :] = np.concatenate([lower, lower], axis=1)
    ss = np.arange(128)[:, None]
    tt = np.arange(128)[None, :]
    cm[:, CM["ncur"], :] = np.where(ss > tt, NEG, 0.0)
    cm[:, CM["nprev"], :] = np.where(ss <= tt, NEG, 0.0)
    for p_ in range(128):
        cm[p_, CM["id2"], p_ % 64] = 1.0
    return cm.reshape(128, -1)


W_SHAPES = {
    "w_ffn1_in": [DM, 2 * DFF], "w_ffn1_out": [DFF, DM], "w_in": [DM, 2824],
    "rwkv_w_up": [64, 256], "rwkv_a_up": [64, 256], "rwkv_g_up": [128, 256],
    "w_out": [DM, DM], "w_ffn2_in": [DM, 2 * DFF], "w_ffn2_out": [DFF, DM],
    "w_ple_gate": [DM, DM], "w_ple_proj": [256, DM],
}
HALO_KEYS = ["att", "mls0", "mls1", "rsh0", "rsh1"]
STATE_KEYS = ["mst0", "mst1", "rst0", "rst1"]
PROG_W = {
    "A": ["w_ffn1_in", "w_ffn1_out", "w_in", "rwkv_w_up", "rwkv_a_up", "rwkv_g_up"],
    "A0": ["w_in", "rwkv_w_up", "rwkv_a_up", "rwkv_g_up"],
    "B": ["w_in", "rwkv_w_up", "rwkv_a_up", "rwkv_g_up"],
    "C": ["w_in", "rwkv_w_up", "rwkv_a_up", "rwkv_g_up", "w_out", "w_ffn2_in", "w_ffn2_out", "w_ple_gate", "w_ple_proj"],
    "Cdbg": ["w_in", "rwkv_w_up", "rwkv_a_up", "rwkv_g_up"],
    "R": ["w_ffn1_in", "w_ffn1_out", "w_ple_gate", "w_ple_proj"],
    "F": ["w_ffn1_in", "w_ffn1_out"],
    "C2": ["w_in", "rwkv_w_up", "rwkv_a_up", "rwkv_g_up", "w_out"],
    "G": ["w_ffn2_in", "w_ffn2_out", "w_ple_gate", "w_ple_proj"],
}


class B:
    pass


def build_program(kind, parts="amr"):
    nc = bass.Bass("TRN2", target_bir_lowering=False)
    b = B()
    b.nc = nc
    b.uid = 0
    b.kind = kind
    b.parts = parts
    b.mode = {'B': 'local', 'C2': 'finish', 'C': 'finish'}.get(kind, 'full')
    D = {}
    b.in_names, b.out_names, b.out_regs = [], [], []

    def din(name, shape, dt=F32):
        D[name] = nc.dram_tensor(name, list(shape), dt, kind="ExternalInput").ap()
        b.in_names.append(name)

    din("xT", [DM, T])
    if kind in ("C", "R", "G"):
        din("pT", [256, T])
    din("pos", [1, T], I32)
    din("pvec", [128, LAY["_n"]])
    din("cmat", [128, CM["_n"] * 128])
    din("pcore", [128, 32])
    din("pcm", [128, 128])
    for k in PROG_W[kind]:
        din(k, W_SHAPES[k])
    b.D = D
    if kind in ("A", "A0"):
        b.emit, b.consume = set(HALO_KEYS), set()
    elif kind == "B":
        b.emit, b.consume = set(STATE_KEYS), set(HALO_KEYS)
    else:
        b.emit, b.consume = set(), set(HALO_KEYS + STATE_KEYS)
    want_x_out = kind in ("A", "A0", "C", "Cdbg", "R", "F", "C2", "G")
    if want_x_out:
        outT = nc.dram_tensor("outT", [DM, T], F32, kind="ExternalOutput").ap()
        b.out_names.append("outT")

    with contextlib.ExitStack() as es:
        S = Sched(nc, es)
        b.S = S

        def sb(name, shape, dt):
            return es.enter_context(nc.sbuf_tensor(name, list(shape), dt))

        xT = sb("xT_sb", [128, 8, T], F32)
        rx = S.regions(8, 4)
        pv = sb("pv", [128, LAY["_n"]], F32)
        r_pv = S.region()
        cm = sb("cm", [128, NF32, 128], F32)
        cmb = sb("cmb", [128, CM["_n"], 128], BF16)
        r_cm = S.region()
        pc = sb("pc", [128, 32], F32)
        r_pc = S.region()
        banks = [es.enter_context(nc.psum_tensor(f"bank{i}", [128, 512], F32)) for i in range(8)]
        rb = S.regions(8)
        for r_ in rb:
            r_.excl = True
        b.xT, b.rx, b.pv, b.r_pv, b.cm, b.cmb, b.r_cm, b.pc, b.r_pc, b.banks, b.rb = xT, rx, pv, r_pv, cm, cmb, r_cm, pc, r_pc, banks, rb

        d_c = S.dsem("consts")
        S.dma("sp", d_c, lambda e: e.dma_start(out=pv[:], in_=D["pvec"]), writes=[r_pv])
        S.dma("sp", d_c, lambda e: e.dma_start(out=cm[:], in_=D["cmat"].rearrange("p (a b) -> p a b", b=128)[:, 0:NF32, :]), writes=[r_cm])
        S.dma("pool", d_c, lambda e: e.dma_start(out=cmb[:], in_=D["cmat"].rearrange("p (a b) -> p a b", b=128)), writes=[r_cm])
        S.dma("sp", d_c, lambda e: e.dma_start(out=pc[:], in_=D["pcore"]), writes=[r_pc])
        xv = D["xT"].rearrange("(c p) t -> p c t", p=128)
        for tg in range(4):
            d_x = S.dsem(f"x{tg}")
            S.dma("sp", d_x, (lambda tg: lambda e: e.dma_start(out=xT[:, :, tg * 512:(tg + 1) * 512], in_=xv[:, :, tg * 512:(tg + 1) * 512]))(tg),
                  writes=[rx[c][tg] for c in range(8)])
        S.barrier()

        l = 0
        if kind in ("A", "R", "F"):
            for half in range(2):
                ffn_half(b, l, 1, half * 1024)
        if kind == "R":
            for half in range(2):
                ple_half(b, l, half * 1024)
        if kind in ("A", "A0", "B"):
            with scope(b) as sb1:
                h = sb1("mh", [128, 8, T], BF16)
                rh = S.regions(8, 4)
                with scope(b) as sb2:
                    prenorm(b, sb2, 0, T, "ln_mix_pre", l, h, rh)
                if kind != "B" and "a" in parts:
                    attn_tail(b, l, h, rh)
                if "m" in parts:
                    for hp in range(2):
                        mlstm_part(b, l, hp, h, rh, None, None)
                if "r" in parts:
                    for hp in range(2):
                        rwkv_part(b, l, hp, h, rh, None, None)
        if kind in ("C", "Cdbg", "C2"):
            mix_layer(b, l, debug=(kind == "Cdbg"), parts=parts)
        if kind in ("C", "G"):
            for half in range(2):
                ffn_half(b, l, 2, half * 1024)
            for half in range(2):
                ple_half(b, l, half * 1024)
        S.barrier()

        if want_x_out:
            ov = outT.rearrange("(c p) t -> p c t", p=128)
            r_out = S.region()
            d_o = S.dsem("out")
            for tg in range(4):
                S.dma("sp", d_o, (lambda tg: lambda e: e.dma_start(out=ov[:, :, tg * 512:(tg + 1) * 512], in_=xT[:, :, tg * 512:(tg + 1) * 512]))(tg),
                      reads=[rx[c][tg] for c in range(8)], writes=[r_out])
            b.out_regs.append(r_out)
        S.wait_all("sp", b.out_regs)
    b.ninst = S.ninst
    return nc, b


def pvc(b, key, l=None, i=0):
    off = LAY[(key, l)] if l is not None else LAY[key]
    return b.pv[:, off + i:off + i + 1]


@contextlib.contextmanager
def scope(b):
    with contextlib.ExitStack() as es:
        def sb(name, shape, dt):
            b.uid += 1
            return es.enter_context(b.nc.sbuf_tensor(f"{name}_{b.uid}", list(shape), dt))
        b.S.scope_stack.append([])
        yield sb
        b.S.barrier()
        b.S.free_dsems.extend(b.S.scope_stack.pop())


def rstd_from_sumsq(b, sb_rs, r_rs, tmp, r_tmp, bank, r_bank, scale, eps_key, n=512):
    S = b.S
    S.op("act", lambda e: e.activation(out=tmp, in_=bank, func=AF.Sqrt, bias=pvc(b, eps_key), scale=scale),
         reads=[r_bank, b.r_pv], writes=[r_tmp])
    S.op("dve", lambda e: e.reciprocal(out=sb_rs, in_=tmp), reads=[r_tmp], writes=[r_rs])


def prenorm(b, sb, t0, nt, gkey, l, h, rh):
    S = b.S
    ng = nt // 512
    sq = [sb(f"pn_sq{i}", [128, 8, 512], BF16) for i in range(2)]
    r_sq = S.regions(2)
    tmp = sb("pn_tmp", [128, 512], F32)
    r_tmp = S.region()
    rs = [sb(f"pn_rs{i}", [128, 512], F32) for i in range(2)]
    r_rs = S.regions(2)
    for tg in range(ng):
        g4 = (t0 // 512) + tg
        sl = slice(t0 + tg * 512, t0 + (tg + 1) * 512)
        i = tg % 2
        S.op("act", (lambda i, sl: lambda e: e.activation(out=sq[i][:], in_=b.xT[:, :, sl], func=AF.Square))(i, sl),
             reads=[b.rx[c][g4] for c in range(8)], writes=[r_sq[i]])
        bk = 7 - i
        for c in range(8):
            S.op("pe", (lambda i, c, bk: lambda e: e.matmul(b.banks[bk][:], lhsT=b.cmb[:, CM["ones"], :], rhs=sq[i][:, c, :], start=(c == 0), stop=(c == 7)))(i, c, bk),
                 reads=[r_sq[i], b.r_cm], writes=[b.rb[bk]])
        rstd_from_sumsq(b, rs[i][:], r_rs[i], tmp[:], r_tmp, b.banks[bk][:], b.rb[bk], 1.0 / DM, "eps1")
        for c in range(8):
            S.op("dve", (lambda i, c, sl, tg: lambda e: e.scalar_tensor_tensor(
                out=h[:, c, tg * 512:(tg + 1) * 512], in0=b.xT[:, c, sl], scalar=pvc(b, gkey, l, c), in1=rs[i][:],
                op0=ALU.mult, op1=ALU.mult))(i, c, sl, tg),
                reads=[b.rx[c][g4], r_rs[i], b.r_pv], writes=[rh[c][tg]])


def postnorm_residual(b, sb, t0, nt, gkey, l, y, ry, factor):
    S = b.S
    ng = nt // 512
    sq = [sb(f"po_sq{i}", [128, 8, 512], BF16) for i in range(2)]
    r_sq = S.regions(2)
    tmp = sb("po_tmp", [128, 512], F32)
    r_tmp = S.region()
    rs = [sb(f"po_rs{i}", [128, 512], F32) for i in range(2)]
    r_rs = S.regions(2)
    t2 = [sb(f"po_t2{i}", [128, 512], F32) for i in range(2)]
    r_t2 = S.regions(2)
    scale = 1.0 / (DM * factor * factor)
    eps_key = "eps1" if factor == 1.0 else "eps4"
    n2 = 0
    for tg in range(ng):
        g4 = (t0 // 512) + tg
        sl = slice(t0 + tg * 512, t0 + (tg + 1) * 512)
        ysl = slice(tg * 512, (tg + 1) * 512)
        i = tg % 2
        S.op("act", (lambda i, ysl: lambda e: e.activation(out=sq[i][:], in_=y[:, :, ysl], func=AF.Square))(i, ysl),
             reads=[ry[c][tg] for c in range(8)], writes=[r_sq[i]])
        bk = 7 - i
        for c in range(8):
            S.op("pe", (lambda i, c, bk: lambda e: e.matmul(b.banks[bk][:], lhsT=b.cmb[:, CM["ones"], :], rhs=sq[i][:, c, :], start=(c == 0), stop=(c == 7)))(i, c, bk),
                 reads=[r_sq[i], b.r_cm], writes=[b.rb[bk]])
        rstd_from_sumsq(b, rs[i][:], r_rs[i], tmp[:], r_tmp, b.banks[bk][:], b.rb[bk], scale, eps_key)
        for c in range(8):
            j = n2 % 2
            n2 += 1
            S.op("pool", (lambda i, c, ysl, j: lambda e: e.tensor_tensor(out=t2[j][:], in0=y[:, c, ysl], in1=rs[i][:], op=ALU.mult))(i, c, ysl, j),
                 reads=[ry[c][tg], r_rs[i]], writes=[r_t2[j]])
            S.op("dve", (lambda c, sl, j: lambda e: e.scalar_tensor_tensor(
                out=b.xT[:, c, sl], in0=t2[j][:], scalar=pvc(b, gkey, l, c), in1=b.xT[:, c, sl], op0=ALU.mult, op1=ALU.add))(c, sl, j),
                reads=[r_t2[j], b.rx[c][g4], b.r_pv], writes=[b.rx[c][g4]])


def wview(ap2d, r0, nr, c0, ncol):
    return ap2d[r0:r0 + nr, c0:c0 + ncol].rearrange("(c p) n -> p c n", p=128)


def ffn_half(b, l, which, t0):
    S = b.S
    NT, NG = 1024, 2
    w_in = b.D[f"w_ffn{which}_in"]
    w_out = b.D[f"w_ffn{which}_out"]
    with scope(b) as sb:
        h = sb("h", [128, 8, NT], BF16)
        rh = S.regions(8, NG)
        act = sb("act", [128, 22, NT], BF16)
        ract = S.regions(22, NG)
        with scope(b) as sb2:
            prenorm(b, sb2, t0, NT, f"ln_ffn{which}_pre", l, h, rh)
        with scope(b) as sb2:
            wg = [sb2(f"wg{i}", [128, 8, 256], BF16) for i in range(2)]
            wu = [sb2(f"wu{i}", [128, 8, 256], BF16) for i in range(2)]
            rwg, rwu = S.regions(2), S.regions(2)
            dwg, dwu = [S.dsem() for _ in range(2)], [S.dsem() for _ in range(2)]
            sg = [sb2(f"sg{i}", [128, NT], BF16) for i in range(2)]
            rsg = S.regions(2, NG)
            for mg in range(11):
                s = mg % 2
                S.dma("pool", dwg[s], (lambda s, mg: lambda e: e.dma_start(out=wg[s][:], in_=wview(w_in, 0, DM, mg * 256, 256)))(s, mg), writes=[rwg[s]])
                S.dma("pool", dwu[s], (lambda s, mg: lambda e: e.dma_start(out=wu[s][:], in_=wview(w_in, 0, DM, DFF + mg * 256, 256)))(s, mg), writes=[rwu[s]])
                for mi in range(2):
                    m = mg * 2 + mi
                    par = m % 2
                    for (wt, rw, boff) in ((wg, rwg, 0), (wu, rwu, 2)):
                        for k in range(8):
                            for tg in range(NG):
                                bk = 4 * par + boff + tg
                                S.op("pe", (lambda wt, s, k, mi, tg, bk: lambda e: e.matmul(
                                    b.banks[bk][:], lhsT=wt[s][:, k, mi * 128:(mi + 1) * 128], rhs=h[:, k, tg * 512:(tg + 1) * 512],
                                    start=(k == 0), stop=(k == 7)))(wt, s, k, mi, tg, bk),
                                    reads=[rw[s], rh[k][tg]], writes=[b.rb[bk]])
                    for tg in range(NG):
                        bg, bu = 4 * par + tg, 4 * par + 2 + tg
                        S.op("act", (lambda par, tg, bg: lambda e: e.activation(out=sg[par][:, tg * 512:(tg + 1) * 512], in_=b.banks[bg][:], func=AF.Silu))(par, tg, bg),
                             reads=[b.rb[bg]], writes=[rsg[par][tg]])
                        S.op("dve", (lambda par, tg, bu, m: lambda e: e.tensor_tensor(
                            out=act[:, m, tg * 512:(tg + 1) * 512], in0=b.banks[bu][:], in1=sg[par][:, tg * 512:(tg + 1) * 512], op=ALU.mult))(par, tg, bu, m),
                            reads=[b.rb[bu], rsg[par][tg]], writes=[ract[m][tg]])
        with scope(b) as sb2:
            y = sb2("y", [128, 8, NT], F32)
            ry = S.regions(8, NG)
            with scope(b) as sb3:
                wo = [sb3(f"wo{i}", [128, 11, 512], BF16) for i in range(2)]
                rwo = S.regions(2)
                dwo = [S.dsem() for _ in range(2)]
                for jg in range(2):
                    for a in range(2):
                        S.dma("pool", dwo[a], (lambda a, jg: lambda e: e.dma_start(out=wo[a][:], in_=wview(w_out, a * 1408, 1408, jg * 512, 512)))(a, jg), writes=[rwo[a]])
                    for k in range(22):
                        a, kk = k // 11, k % 11
                        for j in range(4):
                            for tg in range(NG):
                                bk = j * 2 + tg
                                S.op("pe", (lambda a, kk, j, tg, bk, k: lambda e: e.matmul(
                                    b.banks[bk][:], lhsT=wo[a][:, kk, j * 128:(j + 1) * 128], rhs=act[:, k, tg * 512:(tg + 1) * 512],
                                    start=(k == 0), stop=(k == 21)))(a, kk, j, tg, bk, k),
                                    reads=[rwo[a], ract[k][tg]], writes=[b.rb[bk]])
                    for j in range(4):
                        for tg in range(NG):
                            bk = j * 2 + tg
                            c = jg * 4 + j
                            if (j + tg) % 2 == 0:
                                S.op("act", (lambda c, tg, bk: lambda e: e.copy(out=y[:, c, tg * 512:(tg + 1) * 512], in_=b.banks[bk][:]))(c, tg, bk),
                                     reads=[b.rb[bk]], writes=[ry[c][tg]])
                            else:
                                S.op("dve", (lambda c, tg, bk: lambda e: e.tensor_copy(out=y[:, c, tg * 512:(tg + 1) * 512], in_=b.banks[bk][:]))(c, tg, bk),
                                     reads=[b.rb[bk]], writes=[ry[c][tg]])
            with scope(b) as sb3:
                postnorm_residual(b, sb3, t0, NT, f"ln_ffn{which}_post", l, y, ry, 0.5)


def ple_half(b, l, t0):
    S = b.S
    NT, NG = 1024, 2
    wgd = b.D["w_ple_gate"]
    wpd = b.D["w_ple_proj"]
    with scope(b) as sb:
        h = sb("h", [128, 8, NT], BF16)
        rh = S.regions(8, NG)
        y = sb("y", [128, 8, NT], F32)
        ry = S.regions(8, NG)
        with scope(b) as sb2:
            prenorm(b, sb2, t0, NT, "ln_ple_pre", l, h, rh)
        with scope(b) as sb2:
            pt = sb2("pt", [128, 2, NT], BF16)
            r_pt = S.region()
            wp = sb2("wp", [128, 2, DM], BF16)
            r_wp = S.region()
            d1, d2 = S.dsem(), S.dsem()
            S.dma("pool", d1, lambda e: e.dma_start(out=pt[:], in_=b.D["pT"][:, t0:t0 + NT].rearrange("(c p) t -> p c t", p=128)), writes=[r_pt])
            S.dma("pool", d2, lambda e: e.dma_start(out=wp[:], in_=wview(wpd, 0, 256, 0, DM)), writes=[r_wp])
            wg = [sb2(f"wg{i}", [128, 8, 256], BF16) for i in range(2)]
            rwg = S.regions(2)
            dwg = [S.dsem() for _ in range(2)]
            sg = [sb2(f"sg{i}", [128, NT], F32) for i in range(2)]
            rsg = S.regions(2, NG)
            for jg in range(4):
                s = jg % 2
                S.dma("pool", dwg[s], (lambda s, jg: lambda e: e.dma_start(out=wg[s][:], in_=wview(wgd, 0, DM, jg * 256, 256)))(s, jg), writes=[rwg[s]])
                for ji in range(2):
                    j = jg * 2 + ji
                    par = j % 2
                    for k in range(8):
                        for tg in range(NG):
                            bk = 4 * par + tg
                            S.op("pe", (lambda s, k, ji, tg, bk: lambda e: e.matmul(
                                b.banks[bk][:], lhsT=wg[s][:, k, ji * 128:(ji + 1) * 128], rhs=h[:, k, tg * 512:(tg + 1) * 512],
                                start=(k == 0), stop=(k == 7)))(s, k, ji, tg, bk),
                                reads=[rwg[s], rh[k][tg]], writes=[b.rb[bk]])
                    for k in range(2):
                        for tg in range(NG):
                            bk = 4 * par + 2 + tg
                            S.op("pe", (lambda k, j, tg, bk: lambda e: e.matmul(
                                b.banks[bk][:], lhsT=wp[:, k, j * 128:(j + 1) * 128], rhs=pt[:, k, tg * 512:(tg + 1) * 512],
                                start=(k == 0), stop=(k == 1)))(k, j, tg, bk),
                                reads=[r_wp, r_pt], writes=[b.rb[bk]])
                    for tg in range(NG):
                        bg, bu = 4 * par + tg, 4 * par + 2 + tg
                        S.op("act", (lambda par, tg, bg: lambda e: e.activation(out=sg[par][:, tg * 512:(tg + 1) * 512], in_=b.banks[bg][:], func=AF.Sigmoid))(par, tg, bg),
                             reads=[b.rb[bg]], writes=[rsg[par][tg]])
                        S.op("dve", (lambda par, tg, bu, j: lambda e: e.tensor_tensor(
                            out=y[:, j, tg * 512:(tg + 1) * 512], in0=b.banks[bu][:], in1=sg[par][:, tg * 512:(tg + 1) * 512], op=ALU.mult))(par, tg, bu, j),
                            reads=[b.rb[bu], rsg[par][tg]], writes=[ry[j][tg]])
        with scope(b) as sb2:
            postnorm_residual(b, sb2, t0, NT, "ln_ple_post", l, y, ry, 1.0)


def allgather(b, sb, srcs, nrow, ncol, dt, name):
    S, nc = b.S, b.nc
    if name in b.emit:
        xo = nc.dram_tensor(f"x_{name}", [nrow, ncol], dt, kind="ExternalOutput").ap()
        r_o = S.region()
        d1 = S.dsem(f"xo_{name}")
        for (c0, ncs, ap, rr) in srcs:
            S.dma("sp", d1, (lambda c0=c0, ncs=ncs, ap=ap: lambda e: e.dma_start(out=xo[:, c0:c0 + ncs], in_=ap, allow_slow_non_contiguous=True))(), reads=rr, writes=[r_o])
        b.out_regs.append(r_o)
        b.out_names.append(f"x_{name}")
        return None, None
    assert name in b.consume, name
    gi = nc.dram_tensor(f"g_{name}", [NCORES * nrow, ncol], dt, kind="ExternalInput").ap()
    b.in_names.append(f"g_{name}")
    r_blk = S.region()
    d3 = S.dsem()
    blk = sb(f"agblk_{name}", [nrow, NCORES, ncol], dt)
    S.dma("sp", d3, lambda e: e.dma_start(out=blk[:], in_=gi.rearrange("(j p) n -> p j n", p=nrow)), writes=[r_blk])
    return blk, r_blk


def select_prev(b, dst, r_dst, blk, r_blk, nrow, c0, ncs):
    S = b.S
    S.op("dve", lambda e: e.tensor_scalar(out=dst, in0=blk[:, 0, c0:c0 + ncs], scalar1=b.pc[0:nrow, 0:1], scalar2=None, op0=ALU.mult),
         reads=[r_blk, b.r_pc], writes=[r_dst])
    for j in range(1, 7):
        S.op("dve", (lambda j=j: lambda e: e.scalar_tensor_tensor(out=dst, in0=blk[:, j, c0:c0 + ncs], scalar=b.pc[0:nrow, j:j + 1], in1=dst,
                                                                 op0=ALU.mult, op1=ALU.add))(), reads=[r_blk, b.r_pc, r_dst], writes=[r_dst])


def rope_tables(b, cosF, sinF, r_cos, r_sin):
    S = b.S
    PI = float(np.pi)
    C1 = 6.28125
    C2 = float(2 * np.pi - 6.28125)
    with scope(b) as sb:
        posi = sb("posi", [128, T], I32)
        ang = sb("ang", [128, T], F32)
        kf = sb("kf", [128, T], F32)
        ki = sb("ki", [128, T], I32)
        rr = sb("rr", [128, T], F32)
        r_posi, r_ang, r_kf, r_ki, r_rr = S.regions(5)
        d = S.dsem()
        S.dma("sp", d, lambda e: e.dma_start(out=posi[:], in_=b.D["pos"].partition_broadcast(128)), writes=[r_posi])
        S.op("dve", lambda e: e.tensor_copy(out=ang[:], in_=posi[:]), reads=[r_posi], writes=[r_ang])
        S.op("dve", lambda e: e.tensor_scalar(out=ang[:], in0=ang[:], scalar1=pvc(b, "invfreq"), scalar2=None, op0=ALU.mult), reads=[r_ang, b.r_pv], writes=[r_ang])
        for (dst, r_dst, shift) in ((sinF, r_sin, 0.0), (cosF, r_cos, PI / 2)):
            S.op("dve", (lambda shift=shift: lambda e: e.tensor_scalar(out=kf[:], in0=ang[:], scalar1=1.0 / (2 * PI), scalar2=0.5 + shift / (2 * PI), op0=ALU.mult, op1=ALU.add))(),
                 reads=[r_ang], writes=[r_kf])
            S.op("dve", lambda e: e.tensor_copy(out=ki[:], in_=kf[:]), reads=[r_kf], writes=[r_ki])
            S.op("dve", lambda e: e.tensor_copy(out=kf[:], in_=ki[:]), reads=[r_ki], writes=[r_kf])
            S.op("dve", lambda e: e.scalar_tensor_tensor(out=rr[:], in0=kf[:], scalar=-C1, in1=ang[:], op0=ALU.mult, op1=ALU.add), reads=[r_kf, r_ang], writes=[r_rr])
            S.op("dve", lambda e: e.scalar_tensor_tensor(out=rr[:], in0=kf[:], scalar=-C2, in1=rr[:], op0=ALU.mult, op1=ALU.add), reads=[r_kf, r_rr], writes=[r_rr])
            if shift:
                S.op("dve", (lambda shift=shift: lambda e: e.tensor_scalar(out=rr[:], in0=rr[:], scalar1=shift, scalar2=None, op0=ALU.add))(), reads=[r_rr], writes=[r_rr])
            S.op("dve", lambda e: e.tensor_scalar(out=kf[:], in0=rr[:], scalar1=-PI, scalar2=2 * PI, op0=ALU.is_lt, op1=ALU.mult), reads=[r_rr], writes=[r_kf])
            S.op("dve", lambda e: e.tensor_tensor(out=rr[:], in0=rr[:], in1=kf[:], op=ALU.add), reads=[r_rr, r_kf], writes=[r_rr])
            S.op("dve", lambda e: e.tensor_scalar(out=kf[:], in0=rr[:], scalar1=PI, scalar2=-2 * PI, op0=ALU.is_gt, op1=ALU.mult), reads=[r_rr], writes=[r_kf])
            S.op("dve", lambda e: e.tensor_tensor(out=rr[:], in0=rr[:], in1=kf[:], op=ALU.add), reads=[r_rr, r_kf], writes=[r_rr])
            S.op("dve", lambda e: e.tensor_scalar(out=rr[:], in0=rr[:], scalar1=-PI, scalar2=PI, op0=ALU.max, op1=ALU.min), reads=[r_rr], writes=[r_rr])
            S.op("act", (lambda dst=dst: lambda e: e.activation(out=dst[:], in_=rr[:], func=AF.Sin))(), reads=[r_rr], writes=[r_dst])


def attn_tail(b, l, h, rh):
    S = b.S
    w_in = b.D["w_in"]
    bk_ = b.banks
    rb = b.rb
    with scope(b) as sb:
        cosF = sb("cosF", [128, T], F32)
        sinF = sb("sinF", [128, T], F32)
        r_cos, r_sin = S.regions(2)
        rope_tables(b, cosF, sinF, r_cos, r_sin)
        if "1" in b.parts:
            return
        wkv = sb("wkv", [128, 8, 256], BF16)
        r_wkv = S.region()
        dkv = S.dsem()
        S.dma("pool", dkv, lambda e: e.dma_start(out=wkv[:], in_=wview(w_in, 0, DM, 512, 256)), writes=[r_wkv])
        xb = sb("xb", [128, 128], BF16)
        t1 = sb("t1", [128, 128], F32)
        t2 = sb("t2", [128, 128], F32)
        pay = sb("pay", [128, 256], F32)
        r_xb, r_t1, r_t2, r_pay = S.regions(4)
        ts = slice(T - 128, T)
        for k in range(8):
            S.op("pe", (lambda k=k: lambda e: e.matmul(bk_[0][:, 0:128], lhsT=wkv[:, k, 0:128], rhs=h[:, k, ts], start=(k == 0), stop=(k == 7)))(), reads=[r_wkv, rh[k][3]], writes=[rb[0]])
        if "2" in b.parts:
            return
        S.op("act", lambda e: e.copy(out=xb[:], in_=bk_[0][:, 0:128]), reads=[rb[0]], writes=[r_xb])
        if "5" in b.parts:
            return
        if "8" in b.parts:
            S.op("dve", lambda e: e.tensor_tensor(out=t1[:], in0=bk_[0][:, 0:128], in1=cosF[:, 0:128], op=ALU.mult), reads=[rb[0], r_cos], writes=[r_t1])
            return
        if "9" in b.parts:
            S.op("dve", lambda e: e.tensor_tensor(out=t1[:], in0=xb[:], in1=cosF[:, ts], op=ALU.mult), reads=[r_xb, r_cos], writes=[r_t1])
            return
        if "0" in b.parts:
            S.op("dve", lambda e: e.tensor_tensor(out=t1[:], in0=bk_[0][:, 0:128], in1=t2[:], op=ALU.mult), reads=[rb[0], r_xb], writes=[r_t1])
            return
        S.op("dve", lambda e: e.tensor_tensor(out=t1[:], in0=bk_[0][:, 0:128], in1=cosF[:, ts], op=ALU.mult), reads=[rb[0], r_cos], writes=[r_t1])
        if "6" in b.parts:
            return
        S.op("pe", lambda e: e.matmul(bk_[1][:, 0:128], lhsT=b.cmb[:, CM["perm"], :], rhs=xb[:], start=True, stop=True), reads=[r_xb, b.r_cm], writes=[rb[1]])
        if "7" in b.parts:
            return
        S.op("dve", lambda e: e.tensor_tensor(out=t2[:], in0=bk_[1][:, 0:128], in1=sinF[:, ts], op=ALU.mult), reads=[rb[1], r_sin], writes=[r_t2])
        S.op("dve", lambda e: e.tensor_tensor(out=pay[:, 0:128], in0=t1[:], in1=t2[:], op=ALU.add), reads=[r_t1, r_t2], writes=[r_pay])
        if "3" in b.parts:
            return
        for k in range(8):
            S.op("pe", (lambda k=k: lambda e: e.matmul(bk_[2][:, 0:128], lhsT=h[:, k, ts], rhs=wkv[:, k, 128:256], start=(k == 0), stop=(k == 7)))(), reads=[r_wkv, rh[k][3]], writes=[rb[2]])
        S.op("act", lambda e: e.copy(out=pay[:, 128:256], in_=bk_[2][:, 0:128]), reads=[rb[2], r_pay], writes=[r_pay])
        if "4" in b.parts:
            return
        allgather(b, sb, [(0, 256, pay[:], [r_pay])], 128, 256, F32, "att")


def attn_part(b, l, h, rh, ym, rym):
    S = b.S
    w_in = b.D["w_in"]
    bk_ = b.banks
    rb = b.rb
    with scope(b) as sb:
        cosF = sb("cosF", [128, T], F32)
        sinF = sb("sinF", [128, T], F32)
        r_cos, r_sin = S.regions(2)
        rope_tables(b, cosF, sinF, r_cos, r_sin)
        qT = sb("qT", [128, 4, T], BF16)
        rq = S.regions(4, 4)
        kT = sb("kT", [128, 128 + T], BF16)
        rk = S.regions(5)
        vt = sb("vt", [128, 17, 128], BF16)
        rv = S.regions(5)
        nm = sb("nm", [128, 3, 4, 128], BF16)
        r_nm = S.region()
        es = sb("es", [128, 4], F32)
        r_es = S.region()
        pcm_sb = sb("pcm_sb", [128, 128], F32)
        r_pcm = S.region()
        d0 = S.dsem()
        S.dma("sp", d0, lambda e: e.dma_start(out=pcm_sb[:], in_=b.D["pcm"]), writes=[r_pcm])
        for m, src in enumerate([b.cm[:, CM["ncur"], :], b.cm[:, CM["nprev"], :], pcm_sb[:]]):
            S.op("dve", (lambda m=m, src=src: lambda e: e.tensor_copy(out=nm[:, m, :, :], in_=src.unsqueeze(1).to_broadcast([128, 4, 128])))(),
                 reads=[b.r_cm, r_pcm], writes=[r_nm])
        so = LAY[("sink", l)]
        S.op("act", lambda e: e.activation(out=es[:], in_=b.pv[:, so:so + 4], func=AF.Exp), reads=[b.r_pv], writes=[r_es])
        with scope(b) as sb2:
            wq = sb2("wq", [128, 8, 512], BF16)
            wkv = sb2("wkv", [128, 8, 256], BF16)
            r_wq, r_wkv = S.regions(2)
            dq, dkv = S.dsem(), S.dsem()
            for g in range(2):
                for i in range(4):
                    S.dma("pool", dq, (lambda g=g, i=i: lambda e: e.dma_start(
                        out=wq[:, :, i * 128 + g * 64:i * 128 + g * 64 + 64], in_=wview(w_in, 0, DM, g * 256 + i * 64, 64)))(), writes=[r_wq])
            S.dma("pool", dkv, lambda e: e.dma_start(out=wkv[:], in_=wview(w_in, 0, DM, 512, 256)), writes=[r_wkv])
            xb = [sb2(f"xb{i}", [128, 512], BF16) for i in range(2)]
            t1 = [sb2(f"t1{i}", [128, 512], F32) for i in range(2)]
            t2 = [sb2(f"t2{i}", [128, 512], F32) for i in range(2)]
            r_xb, r_t1, r_t2 = S.regions(2), S.regions(2), S.regions(2)
            n = 0
            for ti in range(5):
                for tg in range(4):
                    bk = n % 4
                    i2 = n % 2
                    n += 1
                    ts = slice(tg * 512, (tg + 1) * 512)
                    for k in range(8):
                        lhs = wq[:, k, ti * 128:(ti + 1) * 128] if ti < 4 else wkv[:, k, 0:128]
                        S.op("pe", (lambda lhs=lhs, k=k, ts=ts, bk=bk: lambda e: e.matmul(bk_[bk][:], lhsT=lhs, rhs=h[:, k, ts], start=(k == 0), stop=(k == 7)))(),
                             reads=[r_wq if ti < 4 else r_wkv, rh[k][tg]], writes=[rb[bk]])
                    S.op("act", (lambda i2=i2, bk=bk: lambda e: e.copy(out=xb[i2][:], in_=bk_[bk][:]))(), reads=[rb[bk]], writes=[r_xb[i2]])
                    S.op("dve", (lambda i2=i2, bk=bk, ts=ts: lambda e: e.tensor_tensor(out=t1[i2][:], in0=bk_[bk][:], in1=cosF[:, ts], op=ALU.mult))(),
                         reads=[rb[bk], r_cos], writes=[r_t1[i2]])
                    S.op("pe", (lambda i2=i2, bk=bk: lambda e: e.matmul(bk_[4 + bk][:], lhsT=b.cmb[:, CM["perm"], :], rhs=xb[i2][:], start=True, stop=True))(),
                         reads=[r_xb[i2], b.r_cm], writes=[rb[4 + bk]])
                    S.op("dve", (lambda i2=i2, bk=bk, ts=ts: lambda e: e.tensor_tensor(out=t2[i2][:], in0=bk_[4 + bk][:], in1=sinF[:, ts], op=ALU.mult))(),
                         reads=[rb[4 + bk], r_sin], writes=[r_t2[i2]])
                    if ti < 4:
                        dst, r_dst = qT[:, ti, ts], rq[ti][tg]
                    else:
                        dst, r_dst = kT[:, 128 + tg * 512:128 + (tg + 1) * 512], rk[1 + tg]
                    S.op("pool", (lambda i2=i2, dst=dst: lambda e: e.tensor_tensor(out=dst, in0=t1[i2][:], in1=t2[i2][:], op=ALU.add))(),
                         reads=[r_t1[i2], r_t2[i2]], writes=[r_dst])
            for g4 in range(4):
                for tt4 in range(4):
                    tt = g4 * 4 + tt4
                    for k in range(8):
                        S.op("pe", (lambda g4=g4, tt4=tt4, tt=tt, k=k: lambda e: e.matmul(bk_[g4][:, tt4 * 128:(tt4 + 1) * 128], lhsT=h[:, k, tt * 128:(tt + 1) * 128],
                                                                                 rhs=wkv[:, k, 128:256], start=(k == 0), stop=(k == 7)))(),
                             reads=[r_wkv, rh[k][g4]], writes=[rb[g4]])
                S.op("act", (lambda g4=g4: lambda e: e.copy(out=vt[:, 1 + g4 * 4:5 + g4 * 4, :], in_=bk_[g4][:].rearrange("p (a c) -> p a c", c=128)))(),
                     reads=[rb[g4]], writes=[rv[1 + g4]])
        with scope(b) as sb2:
            blk, r_blk = allgather(b, sb2, [(0, 128, kT[:, T:T + 128], [rk[4]]), (128, 128, vt[:, 16, :], [rv[4]])], 128, 256, F32, "att")
            select_prev(b, kT[:, 0:128], rk[0], blk, r_blk, 128, 0, 128)
            select_prev(b, vt[:, 0, :], rv[0], blk, r_blk, 128, 128, 128)
        with scope(b) as sb2:
            Pp = [sb2(f"Pp{i}", [128, 512], BF16) for i in range(2)]
            Pc = [sb2(f"Pc{i}", [128, 512], BF16) for i in range(2)]
            r_Pp, r_Pc = S.regions(2), S.regions(2)
            dn = [sb2(f"dn{i}", [128, 512], F32) for i in range(2)]
            r_dn = S.regions(2)
            n = 0
            for qb in range(16):
                qs = slice(qb * 128, (qb + 1) * 128)
                bN, bD = 4 + (qb % 2), 6 + (qb % 2)
                for g in range(2):
                    ps = slice(64 * g, 64 * g + 64)
                    i2 = n % 2
                    bA, bB = 2 * i2, 2 * i2 + 1
                    n += 1
                    mprev = 2 if qb == 0 else 1
                    for (bS, kc0, mi, P_, rP, rkk) in ((bA, qb * 128, mprev, Pp, r_Pp, rk[(qb * 128) // 512 + (0 if qb % 4 else 0)]),
                                                      (bB, (qb + 1) * 128, 0, Pc, r_Pc, None)):
                        kcs = slice(kc0, kc0 + 128)
                        rkr = rk[0] if kc0 < 128 else rk[1 + (kc0 - 128) // 512]
                        S.op("pe", (lambda bS=bS, ps=ps, kcs=kcs, qs=qs: lambda e: e.matmul(bk_[bS][:], lhsT=kT[ps, kcs], rhs=qT[ps, :, qs], start=True, stop=False))(),
                             reads=[rkr] + [rq[i][qb // 4] for i in range(4)], writes=[rb[bS]])
                        S.op("pe", (lambda bS=bS, mi=mi: lambda e: e.matmul(bk_[bS][:], lhsT=b.cmb[:, CM["ident"], :], rhs=nm[:, mi, :, :], start=False, stop=True))(),
                             reads=[r_nm, b.r_cm], writes=[rb[bS]])
                        S.op("act", (lambda bS=bS, P_=P_, i2=i2: lambda e: e.activation(out=P_[i2][:], in_=bk_[bS][:], func=AF.Exp, scale=0.125))(),
                             reads=[rb[bS]], writes=[rP[i2]])
                    for (bO, lo) in ((bN, None), (bD, "ones")):
                        for si, (P_, rP, vtile) in enumerate(((Pp, r_Pp, qb), (Pc, r_Pc, qb + 1))):
                            lhs = vt[:, vtile, 64 * g:64 * g + 64] if lo is None else b.cmb[:, CM["ones"], 0:64]
                            rvr = rv[0] if vtile == 0 else rv[1 + (vtile - 1) // 4]
                            S.op("pe", (lambda bO=bO, ps=ps, lhs=lhs, P_=P_, i2=i2, si=si: lambda e: e.matmul(bk_[bO][ps, :], lhsT=lhs, rhs=P_[i2][:], start=(si == 0), stop=(si == 1)))(),
                                 reads=[rvr, rP[i2], b.r_cm], writes=[rb[bO]])
                j2 = qb % 2
                S.op("dve", (lambda bD=bD, j2=j2: lambda e: e.tensor_tensor(out=dn[j2][:].rearrange("p (a c) -> p a c", c=128), in0=bk_[bD][:].rearrange("p (a c) -> p a c", c=128),
                                                                        in1=es[:, 0:4].unsqueeze(2).to_broadcast([128, 4, 128]), op=ALU.add))(),
                     reads=[rb[bD], r_es], writes=[r_dn[j2]])
                S.op("dve", (lambda j2=j2: lambda e: e.reciprocal(out=dn[j2][:], in_=dn[j2][:]))(), reads=[r_dn[j2]], writes=[r_dn[j2]])
                S.op("dve", (lambda bN=bN, j2=j2, qs=qs: lambda e: e.tensor_tensor(out=ym[:, 0:4, qs], in0=bk_[bN][:].rearrange("p (a c) -> p a c", c=128),
                                                                               in1=dn[j2][:].rearrange("p (a c) -> p a c", c=128), op=ALU.mult))(),
                     reads=[rb[bN], r_dn[j2]], writes=[rym[i][qb // 4] for i in range(4)])


def persist_io(b, what, name, ap, regs, shape, dt):
    S = b.S
    if what == "dump":
        xo = b.nc.dram_tensor(f"x_p_{name}", list(shape), dt, kind="ExternalOutput").ap()
        r_o = S.region()
        d1 = S.dsem(f"po_{name}")
        S.dma("sp", d1, lambda e: e.dma_start(out=xo, in_=ap), reads=list(regs), writes=[r_o])
        b.out_regs.append(r_o)
        b.out_names.append(f"x_p_{name}")
    else:
        xi = b.nc.dram_tensor(f"l_p_{name}", list(shape), dt, kind="ExternalInput").ap()
        b.in_names.append(f"l_p_{name}")
        d1 = S.dsem()
        S.dma("sp", d1, lambda e: e.dma_start(out=ap, in_=xi), writes=list(regs))


def dbg_stop(b, tag):
    if tag not in b.parts:
        return False
    S = b.S
    b.uid += 1
    xo = b.nc.dram_tensor(f"x_dbg{b.uid}", [128, 1], F32, kind="ExternalOutput").ap()
    r_o = S.region()
    d1 = S.dsem()
    S.dma("sp", d1, lambda e: e.dma_start(out=xo, in_=b.pv[:, 0:1]), reads=[b.r_pv], writes=[r_o])
    b.out_regs.append(r_o)
    b.out_names.append(f"x_dbg{b.uid}")
    return True


def mlstm_part(b, l, hp, h, rh, ym, rym):
    S = b.S
    w_in = b.D["w_in"]
    bk_ = b.banks
    rb = b.rb
    MB = 768
    NCH = 16
    with scope(b) as sb:
        numS = sb("numS", [128, T], F32)
        denS = sb("denS", [128, T], F32)
        r_num, r_den = S.regions(NCH), S.regions(NCH)
        qseg = sb("qseg", [128, T], BF16)
        r_qseg = S.regions(NCH)
        osig = sb("osig", [128, T], BF16)
        r_osig = S.regions(4)
        Cf = sb("Cf", [128, 128], F32)
        Cb = sb("Cb", [128, 128], BF16)
        r_Cf, r_Cb = S.regions(2)
        gtot = sb("gtot", [128, 1], F32)
        r_gtot = S.region()
        if b.mode != "finish":
            with scope(b) as sb1:
                qT = sb1("mqT", [128, T], BF16)
                kT = sb1("mkT", [128, T], BF16)
                r_qT, r_kT = S.regions(4), S.regions(4)
                vaug = sb1("vtokm", [128, NCH, 2, 64], BF16)
                r_vaug = S.regions(4)
                GR = sb1("GR", [128, T], F32)
                r_GR = S.regions(4)
                r_GRall = S.region()
                gb15 = sb1("gb15", [128, 1], F32)
                r_gb = S.region()
                go = LAY[("gbias", l)]
                S.op("dve", lambda e: e.tensor_scalar(out=gb15[:], in0=b.pv[:, go:go + 1], scalar1=1.0 / 15.0, scalar2=None, op0=ALU.mult), reads=[b.r_pv], writes=[r_gb])
                with scope(b) as sb2:
                    wm = sb2("wm", [128, 8, 4, 128], BF16)
                    wgt = sb2("wgt", [128, 8, 8], BF16)
                    r_wm, r_wgt = S.regions(2)
                    dm_, dg_ = S.dsem(), S.dsem()
                    for j in range(4):
                        S.dma("pool", dm_, (lambda j=j: lambda e: e.dma_start(out=wm[:, :, j, :], in_=wview(w_in, 0, DM, MB + j * 256 + hp * 128, 128)))(), writes=[r_wm])
                    S.dma("pool", dg_, lambda e: e.dma_start(out=wgt[:], in_=wview(w_in, 0, DM, MB + 1024, 8)), writes=[r_wgt])
                    raw = [sb2(f"raw{j}", [128, 515], F32) for j in range(2)]
                    r_raw = S.regions(2)
                    ctmp = sb2("ctmp", [128, 512], F32)
                    r_ctmp = S.region()
                    tail = sb2("mtail", [128, 6], F32)
                    r_tail = S.region()
                    n = 0
                    for j in range(2):
                        for k in range(8):
                            S.op("pe", (lambda j=j, k=k: lambda e: e.matmul(bk_[j][:, 0:128], lhsT=wm[:, k, j, :], rhs=h[:, k, T - 128:T], start=(k == 0), stop=(k == 7)))(),
                                 reads=[r_wm, rh[k][3]], writes=[rb[j]])
                        S.op("act", (lambda j=j: lambda e: e.copy(out=tail[:, 3 * j:3 * j + 3], in_=bk_[j][:, 125:128]))(), reads=[rb[j]], writes=[r_tail])
                    with scope(b) as sb3:
                        blk, r_blk = allgather(b, sb3, [(0, 6, tail[:], [r_tail])], 128, 6, F32, f"mls{hp}")
                        if blk is None:
                            return
                        for j in range(2):
                            select_prev(b, raw[j][:, 0:3], r_raw[j], blk, r_blk, 128, 3 * j, 3)
                    co = LAY[("conv", l)]
                    for tg in range(4):
                        ts = slice(tg * 512, (tg + 1) * 512)
                        for j in range(2):
                            bk = n % 4
                            n += 1
                            for k in range(8):
                                S.op("pe", (lambda j=j, k=k, ts=ts, bk=bk: lambda e: e.matmul(bk_[bk][:], lhsT=wm[:, k, j, :], rhs=h[:, k, ts], start=(k == 0), stop=(k == 7)))(),
                                     reads=[r_wm, rh[k][tg]], writes=[rb[bk]])
                            S.op("act", (lambda j=j, bk=bk: lambda e: e.copy(out=raw[j][:, 3:515], in_=bk_[bk][:]))(), reads=[rb[bk], r_raw[j]], writes=[r_raw[j]])
                            tile_ = j * 2 + hp
                            wc = [b.pv[:, co + tile_ * 4 + tap:co + tile_ * 4 + tap + 1] for tap in range(4)]
                            S.op("dve", (lambda j=j, wc=wc: lambda e: e.tensor_scalar(out=ctmp[:], in0=raw[j][:, 3:515], scalar1=wc[3], scalar2=None, op0=ALU.mult))(),
                                 reads=[r_raw[j], b.r_pv], writes=[r_ctmp])
                            for tap in range(3):
                                S.op("dve", (lambda j=j, wc=wc, tap=tap: lambda e: e.scalar_tensor_tensor(out=ctmp[:], in0=raw[j][:, tap:tap + 512], scalar=wc[tap], in1=ctmp[:], op0=ALU.mult, op1=ALU.add))(),
                                     reads=[r_raw[j], r_ctmp, b.r_pv], writes=[r_ctmp])
                            S.op("dve", (lambda j=j: lambda e: e.tensor_copy(out=raw[j][:, 0:3], in_=raw[j][:, 512:515]))(), reads=[r_raw[j]], writes=[r_raw[j]])
                            if j == 0:
                                S.op("act", lambda e: e.activation(out=ctmp[:], in_=ctmp[:], func=AF.Silu), reads=[r_ctmp], writes=[r_ctmp])
                                S.op("dve", (lambda ts=ts: lambda e: e.tensor_scalar(out=qT[:, ts], in0=ctmp[:], scalar1=0.125, scalar2=None, op0=ALU.mult))(), reads=[r_ctmp], writes=[r_qT[tg]])
                            else:
                                S.op("act", (lambda ts=ts: lambda e: e.activation(out=kT[:, ts], in_=ctmp[:], func=AF.Silu))(), reads=[r_ctmp], writes=[r_kT[tg]])
                    if dbg_stop(b, "1"):
                        return
                    for tg in range(4):
                        bk = n % 8
                        n += 1
                        ts = slice(tg * 512, (tg + 1) * 512)
                        for k in range(8):
                            S.op("pe", (lambda k=k, ts=ts, bk=bk: lambda e: e.matmul(bk_[bk][:], lhsT=wm[:, k, 3, :], rhs=h[:, k, ts], start=(k == 0), stop=(k == 7)))(),
                                 reads=[r_wm, rh[k][tg]], writes=[rb[bk]])
                        S.op("act", (lambda ts=ts, bk=bk: lambda e: e.activation(out=osig[:, ts], in_=bk_[bk][:], func=AF.Sigmoid))(), reads=[rb[bk]], writes=[r_osig[tg]])
                    if dbg_stop(b, "2"):
                        return
                    for tg in range(4):
                        bk = n % 8
                        n += 1
                        ts = slice(tg * 512, (tg + 1) * 512)
                        for (p0, c0) in ((0, 0), (32, 4)):
                            for k in range(8):
                                S.op("pe", (lambda k=k, ts=ts, bk=bk, p0=p0, c0=c0: lambda e: e.matmul(bk_[bk][p0:p0 + 4, :], lhsT=wgt[:, k, c0:c0 + 4], rhs=h[:, k, ts], start=(k == 0), stop=(k == 7)))(),
                                     reads=[r_wgt, rh[k][tg]], writes=[rb[bk]])
                        for p0 in (0, 32):
                            S.op("act", (lambda ts=ts, bk=bk, p0=p0: lambda e: e.activation(out=GR[p0:p0 + 4, ts], in_=bk_[bk][p0:p0 + 4, :], func=AF.Tanh, bias=gb15[p0:p0 + 4, 0:1], scale=1.0 / 15.0))(),
                                 reads=[rb[bk], r_gb], writes=[r_GR[tg]])
                    if dbg_stop(b, "3"):
                        return
                    for g4 in range(4):
                        bk = n % 8
                        n += 1
                        for tt4 in range(4):
                            tt = g4 * 4 + tt4
                            for k in range(8):
                                S.op("pe", (lambda tt4=tt4, tt=tt, k=k, bk=bk: lambda e: e.matmul(bk_[bk][:, tt4 * 128:(tt4 + 1) * 128], lhsT=h[:, k, tt * 128:(tt + 1) * 128], rhs=wm[:, k, 2, :],
                                                                                          start=(k == 0), stop=(k == 7)))(), reads=[r_wm, rh[k][g4]], writes=[rb[bk]])
                        S.op("act", (lambda g4=g4, bk=bk: lambda e: e.copy(out=vaug[:, g4 * 4:g4 * 4 + 4, :, :], in_=bk_[bk][:].rearrange("p (a c d) -> p a c d", a=4, c=2)))(),
                             reads=[rb[bk]], writes=[r_vaug[g4]])
                    if dbg_stop(b, "4"):
                        return
                    allGR = r_GR
                    S.op("dve", lambda e: e.tensor_scalar(out=GR[0:4, :], in0=GR[0:4, :], scalar1=15.0, scalar2=None, op0=ALU.mult), reads=allGR, writes=[r_GRall])
                    S.op("act", lambda e: e.activation(out=GR[32:36, :], in_=GR[32:36, :], func=AF.Exp, scale=-15.0), reads=allGR, writes=[r_GRall])
                    S.op("act", lambda e: e.activation(out=GR[32:36, :], in_=GR[32:36, :], func=AF.Ln, bias=pvc(b, "one")[32:36, :], scale=1.0), reads=[r_GRall, b.r_pv], writes=[r_GRall])
                    S.op("dve", lambda e: e.tensor_scalar(out=GR[32:36, :], in0=GR[32:36, :], scalar1=-0.5, scalar2=None, op0=ALU.mult), reads=[r_GRall], writes=[r_GRall])
                    S.op("dve", lambda e: e.tensor_tensor_scan(out=GR[32:36, :], data0=GR[32:36, :], data1=GR[32:36, :], initial=0.0, op0=ALU.add, op1=ALU.add), reads=[r_GRall], writes=[r_GRall])
                if dbg_stop(b, "5"):
                    return
                with scope(b) as sb2:
                    S.op("pool", lambda e: e.memset(Cf[:], 0.0), writes=[r_Cf])
                    S.op("pool", lambda e: e.memset(Cb[:], 0.0), writes=[r_Cb])
                    negG = [sb2(f"negG{i}", [128, 1], F32) for i in range(2)]
                    r_negG = S.regions(2)
                    S.op("pool", lambda e: e.memset(negG[1][:], 0.0), writes=[r_negG[1]])
                    itok = [sb2(f"itok{i}", [128, 8], F32) for i in range(2)]
                    atok = [sb2(f"atok{i}", [128, 4], F32) for i in range(2)]
                    r_itok, r_atok = S.regions(2), S.regions(2)
                    eT = [sb2(f"eT{i}", [128, 2, 128], F32) for i in range(2)]
                    r_eT = S.regions(2)
                    PT = [sb2(f"PT{i}", [128, 2, 128], BF16) for i in range(2)]
                    r_PT = S.regions(2)
                    E1 = [sb2(f"E1{i}", [128, 128], F32) for i in range(2)]
                    E2 = [sb2(f"E2{i}", [128, 128], F32) for i in range(2)]
                    r_E1, r_E2 = S.regions(2), S.regions(2)
                    qh = [sb2(f"qh{i}", [128, 128], BF16) for i in range(2)]
                    r_qh = S.regions(2)
                    kh = [sb2(f"kh{i}", [128, 128], BF16) for i in range(2)]
                    r_kh = S.regions(2)
                    for j in range(NCH):
                        i2 = j % 2
                        cs = slice(j * 128, (j + 1) * 128)
                        g4 = j // 4
                        S.op("pe", (lambda cs=cs: lambda e: e.matmul(bk_[0][:, 0:4], lhsT=GR[0:4, cs], rhs=b.cm[0:4, CM["ident"], 0:4], start=True, stop=True))(), reads=[r_GRall, b.r_cm], writes=[rb[0]])
                        S.op("pe", (lambda cs=cs: lambda e: e.matmul(bk_[0][:, 4:8], lhsT=GR[32:36, cs], rhs=b.cm[32:36, CM["ident"], 32:36], start=True, stop=True))(), reads=[r_GRall, b.r_cm], writes=[rb[0]])
                        S.op("dve", (lambda i2=i2: lambda e: e.tensor_copy(out=itok[i2][:], in_=bk_[0][:, 0:8]))(), reads=[rb[0]], writes=[r_itok[i2]])
                        S.op("dve", (lambda i2=i2: lambda e: e.tensor_tensor(out=atok[i2][:], in0=itok[i2][:, 0:4], in1=itok[i2][:, 4:8], op=ALU.subtract))(), reads=[r_itok[i2]], writes=[r_atok[i2]])
                        if j == 1 and dbg_stop(b, "W"):
                            return
                        for hh in range(2):
                            hd = 2 * hp + hh
                            ps = slice(64 * hh, 64 * hh + 64)
                            S.op("pe", (lambda hh=hh, hd=hd, cs=cs: lambda e: e.matmul(bk_[1][:, hh * 128:(hh + 1) * 128], lhsT=b.cm[32:36, CM["rowsel"] + hd, :], rhs=GR[32:36, cs], start=True, stop=False))(),
                                 reads=[r_GRall, b.r_cm], writes=[rb[1]])
                            S.op("pe", (lambda hh=hh: lambda e: e.matmul(bk_[1][:, hh * 128:(hh + 1) * 128], lhsT=b.cm[:, CM["ident"], :], rhs=b.cm[:, CM["ncur"], :], start=False, stop=True))(),
                                 reads=[b.r_cm], writes=[rb[1]])
                            S.op("act", (lambda hh=hh, hd=hd, i2=i2: lambda e: e.activation(out=eT[i2][:, hh, :], in_=bk_[1][:, hh * 128:(hh + 1) * 128], func=AF.Exp, bias=atok[i2][:, hd:hd + 1], scale=1.0))(),
                                 reads=[rb[1], r_atok[i2]], writes=[r_eT[i2]])
                            S.op("pe", (lambda hh=hh, ps=ps, cs=cs: lambda e: e.matmul(bk_[2][:, hh * 128:(hh + 1) * 128], lhsT=kT[ps, cs], rhs=qT[ps, cs], start=True, stop=True))(),
                                 reads=[r_kT[g4], r_qT[g4]], writes=[rb[2]])
                            S.op("pe", (lambda hh=hh, hd=hd, ps=ps, cs=cs: lambda e: e.matmul(bk_[3][ps, 0:128], lhsT=b.cm[32:36, CM["rowsel"] + hd, 0:64], rhs=GR[32:36, cs], start=True, stop=True))(),
                                 reads=[r_GRall, b.r_cm], writes=[rb[3]])
                        if j == 1 and dbg_stop(b, "Y"):
                            return
                        S.op("dve", (lambda i2=i2: lambda e: e.tensor_tensor(out=PT[i2][:], in0=bk_[2][:, 0:256].rearrange("p (a c) -> p a c", c=128), in1=eT[i2][:], op=ALU.mult))(),
                             reads=[rb[2], r_eT[i2]], writes=[r_PT[i2]])
                        S.op("act", (lambda i2=i2: lambda e: e.activation(out=E1[i2][:], in_=bk_[3][:, 0:128], func=AF.Exp, bias=negG[1 - i2][:, 0:1], scale=1.0))(),
                             reads=[rb[3], r_negG[1 - i2]], writes=[r_E1[i2]])
                        S.op("act", (lambda i2=i2: lambda e: e.activation(out=E2[i2][:], in_=bk_[3][:, 0:128], func=AF.Exp))(), reads=[rb[3]], writes=[r_E2[i2]])
                        S.op("dve", (lambda i2=i2: lambda e: e.tensor_scalar(out=negG[i2][:], in0=bk_[3][:, 127:128], scalar1=-1.0, scalar2=None, op0=ALU.mult))(), reads=[rb[3]], writes=[r_negG[i2]])
                        if j == NCH - 1:
                            S.op("dve", lambda e: e.tensor_copy(out=gtot[:], in_=bk_[3][:, 127:128]), reads=[rb[3]], writes=[r_gtot])
                        S.op("dve", (lambda i2=i2, cs=cs: lambda e: e.tensor_tensor(out=qh[i2][:], in0=qT[:, cs], in1=E1[i2][:], op=ALU.mult))(), reads=[r_qT[g4], r_E1[i2]], writes=[r_qh[i2]])
                        S.op("pool", (lambda i2=i2, cs=cs: lambda e: e.tensor_tensor(out=qseg[:, cs], in0=qT[:, cs], in1=E2[i2][:], op=ALU.mult))(), reads=[r_qT[g4], r_E2[i2]], writes=[r_qseg[j]])
                        if j == 1 and dbg_stop(b, "Z"):
                            return
                        S.op("pe", (lambda cs=cs: lambda e: e.matmul(bk_[6][:, 0:128], lhsT=kT[:, cs], rhs=b.cmb[:, CM["ident"], :], start=True, stop=True))(), reads=[r_kT[g4], b.r_cm], writes=[rb[6]])
                        for hh in range(2):
                            S.op("dve", (lambda hh=hh, i2=i2: lambda e: e.tensor_scalar(out=kh[i2][:, hh * 64:(hh + 1) * 64], in0=bk_[6][:, hh * 64:(hh + 1) * 64],
                                                                                    scalar1=eT[i2][:, hh, 127:128], scalar2=None, op0=ALU.mult))(), reads=[rb[6], r_eT[i2]], writes=[r_kh[i2]])
                        for hh in range(2):
                            ps = slice(64 * hh, 64 * hh + 64)
                            S.op("pe", (lambda hh=hh, ps=ps, j=j, i2=i2: lambda e: e.matmul(bk_[4][ps, 0:128], lhsT=vaug[:, j, hh, :], rhs=PT[i2][:, hh, :], start=True, stop=False))(),
                                 reads=[r_vaug[g4], r_PT[i2]], writes=[rb[4]])
                            S.op("pe", (lambda hh=hh, ps=ps, i2=i2: lambda e: e.matmul(bk_[4][ps, 0:128], lhsT=Cb[ps, 0:64], rhs=qh[i2][ps, :], start=False, stop=True))(),
                                 reads=[r_Cb, r_qh[i2]], writes=[rb[4]])
                            S.op("pe", (lambda hh=hh, ps=ps, j=j, i2=i2: lambda e: e.matmul(bk_[5][ps, 0:128], lhsT=b.cmb[:, CM["ones"], 0:64], rhs=PT[i2][:, hh, :], start=True, stop=False))(),
                                 reads=[b.r_cm, r_PT[i2]], writes=[rb[5]])
                            S.op("pe", (lambda hh=hh, ps=ps, i2=i2: lambda e: e.matmul(bk_[5][ps, 0:128], lhsT=Cb[ps, 64:128], rhs=qh[i2][ps, :], start=False, stop=True))(),
                                 reads=[r_Cb, r_qh[i2]], writes=[rb[5]])
                            S.op("pe", (lambda hh=hh, ps=ps, j=j, i2=i2: lambda e: e.matmul(bk_[7][ps, 0:64], lhsT=kh[i2][:, hh * 64:(hh + 1) * 64], rhs=vaug[:, j, hh, :], start=True, stop=True))(),
                                 reads=[r_kh[i2], r_vaug[g4]], writes=[rb[7]])
                            S.op("pe", (lambda hh=hh, ps=ps, j=j, i2=i2: lambda e: e.matmul(bk_[7][ps, 64:128], lhsT=kh[i2][:, hh * 64:(hh + 1) * 64], rhs=b.cmb[:, CM["ones"], 0:64], start=True, stop=True))(),
                                 reads=[r_kh[i2], b.r_cm], writes=[rb[7]])
                        if j == 1 and dbg_stop(b, "Q"):
                            return
                        S.op("act", (lambda cs=cs: lambda e: e.copy(out=numS[:, cs], in_=bk_[4][:, 0:128]))(), reads=[rb[4]], writes=[r_num[j]])
                        S.op("act", (lambda cs=cs: lambda e: e.copy(out=denS[:, cs], in_=bk_[5][:, 0:128]))(), reads=[rb[5]], writes=[r_den[j]])
                        S.op("dve", (lambda i2=i2: lambda e: e.scalar_tensor_tensor(out=Cf[:], in0=Cf[:], scalar=E1[i2][:, 127:128], in1=bk_[7][:, 0:128], op0=ALU.mult, op1=ALU.add))(),
                             reads=[r_Cf, r_E1[i2], rb[7]], writes=[r_Cf])
                        S.op("act", lambda e: e.copy(out=Cb[:], in_=Cf[:]), reads=[r_Cf], writes=[r_Cb])
                        if j == 0 and dbg_stop(b, "6"):
                            return
                        if j == 1 and dbg_stop(b, "7"):
                            return
                        if j == NCH - 1 and dbg_stop(b, "8"):
                            return
        else:
            persist_io(b, "load", f"mnum{hp}", numS[:], r_num, [128, T], F32)
            persist_io(b, "load", f"mden{hp}", denS[:], r_den, [128, T], F32)
            persist_io(b, "load", f"mqsg{hp}", qseg[:], r_qseg, [128, T], BF16)
            persist_io(b, "load", f"mosg{hp}", osig[:], r_osig, [128, T], BF16)
        if b.mode == "local":
            persist_io(b, "dump", f"mnum{hp}", numS[:], r_num, [128, T], F32)
            persist_io(b, "dump", f"mden{hp}", denS[:], r_den, [128, T], F32)
            persist_io(b, "dump", f"mqsg{hp}", qseg[:], r_qseg, [128, T], BF16)
            persist_io(b, "dump", f"mosg{hp}", osig[:], r_osig, [128, T], BF16)
        with scope(b) as sb1:
            blk, r_blk = allgather(b, sb1, [(0, 128, Cf[:], [r_Cf]), (128, 1, gtot[:], [r_gtot])], 128, 129, F32, f"mst{hp}")
            if blk is None:
                return
            dec = sb1("dec", [128, 8], F32)
            r_dec = S.region()
            S.op("act", lambda e: e.activation(out=dec[:], in_=blk[:, :, 128], func=AF.Exp), reads=[r_blk], writes=[r_dec])
            Cs = sb1("Cs", [128, 128], F32)
            Csb = sb1("Csb", [128, 128], BF16)
            tt_ = sb1("tt_", [128, 128], F32)
            r_Cs, r_Csb, r_tt = S.regions(3)
            S.op("pool", lambda e: e.memset(Cs[:], 0.0), writes=[r_Cs])
            for jc in range(7):
                S.op("dve", (lambda jc=jc: lambda e: e.scalar_tensor_tensor(out=tt_[:], in0=Cs[:], scalar=dec[:, jc:jc + 1], in1=blk[:, jc, 0:128], op0=ALU.mult, op1=ALU.add))(),
                     reads=[r_Cs, r_dec, r_blk], writes=[r_tt])
                S.op("dve", lambda e: e.tensor_tensor(out=tt_[:], in0=tt_[:], in1=Cs[:], op=ALU.subtract), reads=[r_tt, r_Cs], writes=[r_tt])
                S.op("dve", (lambda jc=jc: lambda e: e.scalar_tensor_tensor(out=Cs[:], in0=tt_[:], scalar=b.pc[:, 8 + jc:9 + jc], in1=Cs[:], op0=ALU.mult, op1=ALU.add))(),
                     reads=[r_tt, r_Cs, b.r_pc], writes=[r_Cs])
            S.op("act", lambda e: e.copy(out=Csb[:], in_=Cs[:]), reads=[r_Cs], writes=[r_Csb])
            hT = [sb1(f"hT{i}", [128, 512], F32) for i in range(2)]
            dd = [sb1(f"dd{i}", [128, 512], F32) for i in range(2)]
            sq = [sb1(f"msq{i}", [128, 512], BF16) for i in range(2)]
            tmp = sb1("mtmp", [128, 512], F32)
            rs = [sb1(f"mrs{i}", [128, 512], F32) for i in range(2)]
            r_hT, r_dd, r_sq, r_rs = S.regions(2), S.regions(2), S.regions(2), S.regions(2)
            r_tmp = S.region()
            go2 = LAY[("mlstm_norm", l)]
            for tg in range(4):
                i2 = tg % 2
                ts = slice(tg * 512, (tg + 1) * 512)
                chs = [r for r in range(tg * 4, tg * 4 + 4)]
                bN, bD, bQ = 0 + i2, 2 + i2, 4 + i2
                for hh in range(2):
                    ps = slice(64 * hh, 64 * hh + 64)
                    S.op("pe", (lambda ps=ps, ts=ts, bN=bN: lambda e: e.matmul(bk_[bN][ps, :], lhsT=Csb[ps, 0:64], rhs=qseg[ps, ts], start=True, stop=True))(),
                         reads=[r_Csb] + [r_qseg[c] for c in chs], writes=[rb[bN]])
                    S.op("pe", (lambda ps=ps, ts=ts, bD=bD: lambda e: e.matmul(bk_[bD][ps, :], lhsT=Csb[ps, 64:128], rhs=qseg[ps, ts], start=True, stop=True))(),
                         reads=[r_Csb] + [r_qseg[c] for c in chs], writes=[rb[bD]])
                S.op("dve", (lambda i2=i2, ts=ts, bN=bN: lambda e: e.tensor_tensor(out=hT[i2][:], in0=bk_[bN][:], in1=numS[:, ts], op=ALU.add))(), reads=[rb[bN]] + [r_num[c] for c in chs], writes=[r_hT[i2]])
                S.op("dve", (lambda i2=i2, ts=ts, bD=bD: lambda e: e.tensor_tensor(out=dd[i2][:], in0=bk_[bD][:], in1=denS[:, ts], op=ALU.add))(), reads=[rb[bD]] + [r_den[c] for c in chs], writes=[r_dd[i2]])
                S.op("dve", (lambda i2=i2: lambda e: e.scalar_tensor_tensor(out=dd[i2][:], in0=dd[i2][:], scalar=-1.0, in1=dd[i2][:], op0=ALU.mult, op1=ALU.max))(), reads=[r_dd[i2]], writes=[r_dd[i2]])
                S.op("dve", (lambda i2=i2: lambda e: e.tensor_scalar(out=dd[i2][:], in0=dd[i2][:], scalar1=1.0, scalar2=None, op0=ALU.max))(), reads=[r_dd[i2]], writes=[r_dd[i2]])
                S.op("dve", (lambda i2=i2: lambda e: e.reciprocal(out=dd[i2][:], in_=dd[i2][:]))(), reads=[r_dd[i2]], writes=[r_dd[i2]])
                S.op("dve", (lambda i2=i2: lambda e: e.tensor_tensor(out=hT[i2][:], in0=hT[i2][:], in1=dd[i2][:], op=ALU.mult))(), reads=[r_hT[i2], r_dd[i2]], writes=[r_hT[i2]])
                S.op("act", (lambda i2=i2: lambda e: e.activation(out=sq[i2][:], in_=hT[i2][:], func=AF.Square))(), reads=[r_hT[i2]], writes=[r_sq[i2]])
                S.op("pe", (lambda i2=i2, bQ=bQ: lambda e: e.matmul(bk_[bQ][:], lhsT=b.cmb[:, CM["blk"], :], rhs=sq[i2][:], start=True, stop=True))(), reads=[r_sq[i2], b.r_cm], writes=[rb[bQ]])
                rstd_from_sumsq(b, rs[i2][:], r_rs[i2], tmp[:], r_tmp, bk_[bQ][:], rb[bQ], 1.0 / 64.0, "eps1")
                S.op("pool", (lambda i2=i2: lambda e: e.tensor_tensor(out=hT[i2][:], in0=hT[i2][:], in1=rs[i2][:], op=ALU.mult))(), reads=[r_hT[i2], r_rs[i2]], writes=[r_hT[i2]])
                S.op("dve", (lambda i2=i2, ts=ts: lambda e: e.scalar_tensor_tensor(out=ym[:, 4 + hp, ts], in0=hT[i2][:], scalar=b.pv[:, go2 + hp:go2 + hp + 1], in1=osig[:, ts], op0=ALU.mult, op1=ALU.mult))(),
                     reads=[r_hT[i2], r_osig[tg], b.r_pv], writes=[rym[4 + hp][tg]])


def rwkv_part(b, l, hp, h, rh, ym, rym):
    S = b.S
    w_in = b.D["w_in"]
    bk_ = b.banks
    rb = b.rb
    RB = 1800
    EM = float(np.exp(-0.5))
    cmf, cmb = b.cm, b.cmb

    def col(key, i=0):
        o = LAY[(key, l)] + i
        return b.pv[:, o:o + 1]

    last_row = {}

    def mm(out, lhsT, rhs, reads, writes, start=True, stop=True):
        tag = (lhsT.base_partition(), lhsT.partition_size())
        extra = []
        for w in writes:
            prev = last_row.get(id(w))
            if prev is not None and prev[0] != tag:
                extra.append(prev[1])
        tok = S.op("pe", lambda e: e.matmul(out, lhsT=lhsT, rhs=rhs, start=start, stop=stop), reads=reads, writes=writes, pe_wait=extra)
        for w in writes:
            last_row[id(w)] = (tag, tok)

    with scope(b) as sb:
        Yloc = sb("Yloc", [128, T], BF16)
        M2p = sb("M2p", [128, T], BF16)
        gT = sb("gT", [128, T], BF16)
        bv = sb("bv", [128, T], BF16)
        r_Yloc, r_M2p, r_gT, r_bv = S.regions(4), S.regions(4), S.regions(4), S.regions(4)
        Sf = sb("Sf", [128, 128], F32)
        Sb_ = sb("Sb", [128, 128], BF16)
        r_Sf, r_Sb = S.regions(2)
        if b.mode != "finish":
            with scope(b) as sb1:
                wr = sb1("wr", [128, 8, 5, 128], BF16)
                r_wr = S.region()
                dwr = S.dsem()
                for X, c0 in enumerate([RB + hp * 128, RB + 256 + hp * 128, RB + 512 + hp * 128, RB + 768, RB + 896]):
                    S.dma("pool", dwr, (lambda X=X, c0=c0: lambda e: e.dma_start(out=wr[:, :, X, :], in_=wview(w_in, 0, DM, c0, 128)))(), writes=[r_wr])
                wup = sb1("wup", [128, 128], BF16)
                aup = sb1("aup", [128, 128], BF16)
                gup = sb1("gup", [128, 128], BF16)
                r_lr = S.region()
                dlr = S.dsem()
                S.dma("pool", dlr, lambda e: e.dma_start(out=wup[0:64, :], in_=b.D["rwkv_w_up"][:, hp * 128:(hp + 1) * 128]), writes=[r_lr])
                S.dma("pool", dlr, lambda e: e.dma_start(out=aup[64:128, :], in_=b.D["rwkv_a_up"][:, hp * 128:(hp + 1) * 128]), writes=[r_lr])
                S.dma("pool", dlr, lambda e: e.dma_start(out=gup[:], in_=b.D["rwkv_g_up"][:, hp * 128:(hp + 1) * 128]), writes=[r_lr])
                omk = sb1("omk", [128, 1], F32)
                r_omk = S.region()
                S.op("dve", lambda e: e.tensor_scalar(out=omk[:], in0=col("rwkv_k_a", hp), scalar1=-1.0, scalar2=1.0, op0=ALU.mult, op1=ALU.add), reads=[b.r_pv], writes=[r_omk])
                carry = sb1("carry", [128, 5], F32)
                r_carry = S.regions(5)
                tail = sb1("tail", [128, 5], F32)
                r_tail = S.region()
                for X in range(5):
                    for k in range(8):
                        mm(bk_[X % 2][:, 0:128], wr[:, k, X, :], h[:, k, T - 128:T], [r_wr, rh[k][3]], [rb[X % 2]], start=(k == 0), stop=(k == 7))
                    S.op("act", (lambda X=X: lambda e: e.copy(out=tail[:, X:X + 1], in_=bk_[X % 2][:, 127:128]))(), reads=[rb[X % 2]], writes=[r_tail])
                with scope(b) as sb2:
                    blk, r_blk = allgather(b, sb2, [(0, 5, tail[:], [r_tail])], 128, 5, F32, f"rsh{hp}")
                    if blk is None:
                        return
                    r_call = S.region()
                    select_prev(b, carry[:], r_call, blk, r_blk, 128, 0, 5)
                for X in range(5):
                    r_carry[X] = r_call
                r_carry = [S.region() for _ in range(5)]
                for X in range(5):
                    r_carry[X].w = r_call.w
                S.op("pool", lambda e: e.memset(Sf[:], 0.0), writes=[r_Sf])
                S.op("dve", lambda e: e.tensor_copy(out=Sf[:, 64:128], in_=cmf[:, CM["id2"], 0:64]), reads=[b.r_cm, r_Sf], writes=[r_Sf])
                S.op("act", lambda e: e.copy(out=Sb_[:], in_=Sf[:]), reads=[r_Sf], writes=[r_Sb])
                raw = sb1("rraw", [128, 513], F32)
                r_raw = S.region()
                us = [sb1(f"us{i}", [128, 512], F32) for i in range(3)]
                r_us = S.regions(3)
                R = [sb1(f"R{i}", [128, 512], F32) for i in range(7)]
                rR = S.regions(7)
                twx = sb1("twx", [128, 512], BF16)
                sgb = sb1("sgb", [128, 512], BF16)
                sqk = sgb
                rkr = sgb
                r_twx, r_sgb = S.regions(2)
                r_sqk = r_sgb
                r_rkr = r_sgb
                base8 = sb1("base8", [128, 8], F32)
                cumC8 = sb1("cumC8", [128, 8], F32)
                ecum8 = sb1("ecum8", [128, 8], F32)
                r_base8, r_cumC8, r_ecum8 = S.regions(3)
                fm = [sb1(f"fm{i}", [128, 512], BF16) for i in range(6)]
                r_fm = S.regions(6)
                tok = [sb1(f"tok{i}", [128, 4, 128], BF16) for i in range(4)]
                r_tok = S.regions(4, 4)
                vb = sb1("vb", [128, 512], BF16)
                r_vb = S.region()
                Am = sb1("Am", [128, 4, 2, 64], BF16)
                AmT = sb1("AmT", [128, 2, 64], BF16)
                r_Am, r_AmT = S.regions(2)
                Pb = [sb1(f"Pb{i}", [128, 2, 2, 64], BF16) for i in range(2)]
                r_Pb = S.regions(2)
                Zf = sb1("Zf", [128, 2, 128], F32)
                Zb = sb1("Zb", [128, 2, 128], BF16)
                r_Zf, r_Zb = S.regions(2)
                M2b = sb1("M2b", [128, 128], BF16)
                M3Tb = sb1("M3Tb", [128, 2, 64], BF16)
                Laug = sb1("Laug", [128, 2, 128], F32)
                r_M2b, r_M3Tb, r_Laug = S.regions(3)
                S.op("pool", lambda e: e.memset(Laug[:], 0.0), writes=[r_Laug])
                S.op("pool", lambda e: e.memset(base8[:], 0.0), writes=[r_base8])
                MU = LAY[("rwkv_mu", l)]
                mucol = [MU + hp, MU + 2 + hp, MU + 4 + hp, MU + 6, MU + 7]
                for tg in range(4):
                    ts = slice(tg * 512, (tg + 1) * 512)
                    if tg == 0 and dbg_stop(b, "1"):
                        return
                    for X in range(5):
                        bkx = X % 2
                        for k in range(8):
                            mm(bk_[bkx][:], wr[:, k, X, :], h[:, k, ts], [r_wr, rh[k][tg]], [rb[bkx]], start=(k == 0), stop=(k == 7))
                        S.op("act", (lambda bkx=bkx: lambda e: e.copy(out=raw[:, 1:513], in_=bk_[bkx][:]))(), reads=[rb[bkx]], writes=[r_raw])
                        S.op("dve", (lambda X=X: lambda e: e.tensor_copy(out=raw[:, 0:1], in_=carry[:, X:X + 1]))(), reads=[r_carry[X], r_raw], writes=[r_raw])
                        S.op("dve", (lambda X=X: lambda e: e.tensor_copy(out=carry[:, X:X + 1], in_=raw[:, 512:513]))(), reads=[r_raw], writes=[r_carry[X]])
                        S.op("dve", lambda e: e.tensor_tensor(out=R[0][:], in0=raw[:, 0:512], in1=raw[:, 1:513], op=ALU.subtract), reads=[r_raw], writes=[rR[0]])
                        dstu = us[X] if X < 3 else R[1]
                        r_dstu = r_us[X] if X < 3 else rR[1]
                        S.op("dve", (lambda X=X, dstu=dstu: lambda e: e.scalar_tensor_tensor(out=dstu[:], in0=R[0][:], scalar=b.pv[:, mucol[X]:mucol[X] + 1], in1=raw[:, 1:513], op0=ALU.mult, op1=ALU.add))(),
                             reads=[rR[0], r_raw, b.r_pv], writes=[r_dstu])
                        if X == 3:
                            S.op("act", lambda e: e.activation(out=twx[0:64, :], in_=R[1][0:64, :], func=AF.Tanh), reads=[rR[1]], writes=[r_twx])
                            S.op("act", lambda e: e.copy(out=twx[64:128, :], in_=R[1][64:128, :]), reads=[rR[1]], writes=[r_twx])
                        if X == 4:
                            S.op("act", lambda e: e.activation(out=sgb[:], in_=R[1][:], func=AF.Sigmoid), reads=[rR[1]], writes=[r_sgb])
                    if tg == 0 and dbg_stop(b, "2"):
                        return
                    mm(bk_[2][:], wup[0:64, :], twx[0:64, :], [r_lr, r_twx], [rb[2]])
                    S.op("act", lambda e: e.activation(out=R[0][:], in_=bk_[2][:], func=AF.Sigmoid, bias=col("rwkv_w0", hp), scale=1.0), reads=[rb[2], b.r_pv], writes=[rR[0]])
                    S.op("dve", lambda e: e.tensor_scalar(out=R[0][:], in0=R[0][:], scalar1=-EM, scalar2=None, op0=ALU.mult), reads=[rR[0]], writes=[rR[0]])
                    mm(bk_[3][:], aup[64:128, :], twx[64:128, :], [r_lr, r_twx], [rb[3]])
                    S.op("act", lambda e: e.activation(out=R[1][:], in_=bk_[3][:], func=AF.Sigmoid, bias=col("rwkv_a0", hp), scale=1.0), reads=[rb[3], b.r_pv], writes=[rR[1]])
                    mm(bk_[4][:], gup[:], sgb[:], [r_lr, r_sgb], [rb[4]])
                    S.op("act", (lambda ts=ts: lambda e: e.copy(out=gT[:, ts], in_=bk_[4][:]))(), reads=[rb[4]], writes=[r_gT[tg]])
                    S.op("dve", lambda e: e.tensor_scalar(out=R[2][:], in0=us[1][:], scalar1=col("rwkv_k_k", hp), scalar2=None, op0=ALU.mult), reads=[r_us[1], b.r_pv], writes=[rR[2]])
                    S.op("act", lambda e: e.activation(out=sqk[:], in_=R[2][:], func=AF.Square), reads=[rR[2]], writes=[r_sqk])
                    mm(bk_[5][:], cmb[:, CM["blk"], :], sqk[:], [b.r_cm, r_sqk], [rb[5]])
                    S.op("act", lambda e: e.activation(out=R[3][:], in_=bk_[5][:], func=AF.Sqrt), reads=[rb[5]], writes=[rR[3]])
                    S.op("dve", lambda e: e.tensor_scalar(out=R[3][:], in0=R[3][:], scalar1=1e-12, scalar2=None, op0=ALU.max), reads=[rR[3]], writes=[rR[3]])
                    S.op("dve", lambda e: e.reciprocal(out=R[3][:], in_=R[3][:]), reads=[rR[3]], writes=[rR[3]])
                    S.op("dve", lambda e: e.tensor_tensor(out=R[2][:], in0=R[2][:], in1=R[3][:], op=ALU.mult), reads=[rR[2], rR[3]], writes=[rR[2]])
                    S.op("dve", lambda e: e.tensor_scalar(out=R[3][:], in0=R[1][:], scalar1=col("rwkv_k_a", hp), scalar2=omk[:, 0:1], op0=ALU.mult, op1=ALU.add), reads=[rR[1], r_omk, b.r_pv, rR[3]], writes=[rR[3]])
                    S.op("dve", lambda e: e.tensor_tensor(out=R[3][:], in0=R[3][:], in1=us[1][:], op=ALU.mult), reads=[rR[3], r_us[1]], writes=[rR[3]])
                    S.op("dve", lambda e: e.scalar_tensor_tensor(out=rkr[:], in0=us[0][:], scalar=col("rwkv_r_k", hp), in1=R[3][:], op0=ALU.mult, op1=ALU.mult), reads=[r_us[0], rR[3], b.r_pv], writes=[r_rkr])
                    mm(bk_[6][:], cmb[:, CM["blk"], :], rkr[:], [b.r_cm, r_rkr], [rb[6]])
                    S.op("dve", (lambda ts=ts: lambda e: e.tensor_tensor(out=bv[:, ts], in0=bk_[6][:], in1=us[2][:], op=ALU.mult))(), reads=[rb[6], r_us[2]], writes=[r_bv[tg]])
                    S.op("act", lambda e: e.copy(out=vb[:], in_=us[2][:]), reads=[r_us[2]], writes=[r_vb])
                    S.op("dve", lambda e: e.tensor_scalar(out=R[5][:], in0=R[0][:], scalar1=0.5, scalar2=None, op0=ALU.mult), reads=[rR[0]], writes=[rR[5]])
                    S.op("dve", lambda e: e.tensor_tensor_scan(out=R[4][:], data0=R[5][:], data1=R[5][:], initial=0.0, op0=ALU.add, op1=ALU.add), reads=[rR[5]], writes=[rR[4]])
                    S.op("dve", lambda e: e.tensor_copy(out=base8[:, 1:8], in_=R[4][:, 63:511:64]), reads=[rR[4], r_base8], writes=[r_base8])
                    S.op("dve", lambda e: e.tensor_tensor(out=R[4][:].rearrange("p (a c) -> p a c", c=64), in0=R[4][:].rearrange("p (a c) -> p a c", c=64),
                                                          in1=base8[:, 0:8].unsqueeze(2).to_broadcast([128, 8, 64]), op=ALU.subtract), reads=[rR[4], r_base8], writes=[rR[4]])
                    S.op("dve", lambda e: e.tensor_copy(out=cumC8[:], in_=R[4][:, 63:512:64]), reads=[rR[4]], writes=[r_cumC8])
                    S.op("act", lambda e: e.activation(out=R[5][:], in_=R[4][:], func=AF.Exp), reads=[rR[4]], writes=[rR[5]])
                    S.op("dve", lambda e: e.tensor_copy(out=ecum8[:], in_=R[5][:, 63:512:64]), reads=[rR[5]], writes=[r_ecum8])
                    S.op("dve", lambda e: e.tensor_tensor(out=fm[0][:], in0=us[0][:], in1=R[5][:], op=ALU.mult), reads=[r_us[0], rR[5]], writes=[r_fm[0]])
                    S.op("dve", lambda e: e.tensor_tensor(out=R[6][:], in0=R[4][:], in1=R[0][:], op=ALU.subtract), reads=[rR[4], rR[0]], writes=[rR[6]])
                    S.op("act", lambda e: e.activation(out=R[6][:], in_=R[6][:], func=AF.Exp), reads=[rR[6]], writes=[rR[6]])
                    S.op("dve", lambda e: e.scalar_tensor_tensor(out=fm[1][:], in0=R[2][:], scalar=-1.0, in1=R[6][:], op0=ALU.mult, op1=ALU.mult), reads=[rR[2], rR[6]], writes=[r_fm[1]])
                    S.op("dve", lambda e: e.tensor_tensor(out=R[1][:], in0=R[1][:], in1=R[2][:], op=ALU.mult), reads=[rR[1], rR[2]], writes=[rR[1]])
                    S.op("act", lambda e: e.activation(out=R[5][:], in_=R[4][:], func=AF.Exp, scale=-1.0), reads=[rR[4], r_ecum8, r_fm[0]], writes=[rR[5]])
                    S.op("dve", lambda e: e.tensor_tensor(out=fm[2][:], in0=R[1][:], in1=R[5][:], op=ALU.mult), reads=[rR[1], rR[5]], writes=[r_fm[2]])
                    S.op("dve", lambda e: e.tensor_tensor(out=fm[3][:], in0=R[3][:], in1=R[5][:], op=ALU.mult), reads=[rR[3], rR[5]], writes=[r_fm[3]])
                    S.op("dve", lambda e: e.tensor_tensor(out=R[6][:].rearrange("p (a c) -> p a c", c=64), in0=cumC8[:, 0:8].unsqueeze(2).to_broadcast([128, 8, 64]),
                                                          in1=R[4][:].rearrange("p (a c) -> p a c", c=64), op=ALU.subtract), reads=[rR[4], r_cumC8, rR[6], r_fm[1]], writes=[rR[6]])
                    S.op("act", lambda e: e.activation(out=R[6][:], in_=R[6][:], func=AF.Exp), reads=[rR[6]], writes=[rR[6]])
                    S.op("dve", lambda e: e.tensor_tensor(out=fm[4][:], in0=R[1][:], in1=R[6][:], op=ALU.mult), reads=[rR[1], rR[6]], writes=[r_fm[4]])
                    S.op("dve", lambda e: e.tensor_tensor(out=fm[5][:], in0=R[3][:], in1=R[6][:], op=ALU.mult), reads=[rR[3], rR[6]], writes=[r_fm[5]])
                    if tg == 0 and dbg_stop(b, "3"):
                        return
                    for qi, src, r_src in ((0, fm[1], r_fm[1]), (1, fm[4], r_fm[4]), (2, fm[5], r_fm[5]), (3, vb, r_vb)):
                        for tl in range(4):
                            S.op("pe", (lambda src=src, tl=tl, qi=qi: lambda e: e.matmul(bk_[7][:, tl * 128:(tl + 1) * 128], lhsT=src[:, tl * 128:(tl + 1) * 128], rhs=cmb[:, CM["ident"], :], start=True, stop=True))(),
                                 reads=[r_src, b.r_cm], writes=[rb[7]])
                        S.op("act" if qi % 2 == 0 else "dve", (lambda qi=qi: (lambda e: e.copy(out=tok[qi][:], in_=bk_[7][:].rearrange("p (a c) -> p a c", c=128))) if qi % 2 == 0 else
                                                               (lambda e: e.tensor_copy(out=tok[qi][:], in_=bk_[7][:].rearrange("p (a c) -> p a c", c=128))))(),
                             reads=[rb[7]], writes=r_tok[qi])
                    if tg == 0 and dbg_stop(b, "4"):
                        return
                    for tl in range(4):
                        gl = tg * 4 + tl
                        rt_, at_, bt_, kt_ = fm[0], fm[1], fm[2], fm[3]
                        for c in range(2):
                            pcs = slice(64 * c, 64 * c + 64)
                            tks = slice(tl * 128 + c * 64, tl * 128 + c * 64 + 64)
                            for hh in range(2):
                                ps = slice(64 * hh, 64 * hh + 64)
                                for wi, (lh, rh_, rl, rr_) in enumerate(((bt_, at_, r_fm[2], r_fm[1]), (kt_, at_, r_fm[3], r_fm[1]), (bt_, rt_, r_fm[2], r_fm[0]), (kt_, rt_, r_fm[3], r_fm[0]))):
                                    o0 = (wi * 2 + hh) * 64
                                    mm(bk_[0][pcs, o0:o0 + 64], lh[ps, tks], rh_[ps, tks], [rl, rr_], [rb[0]])
                                mm(bk_[1][pcs, hh * 64:(hh + 1) * 64], at_[ps, tks], bt_[ps, tks], [r_fm[1], r_fm[2]], [rb[1]])
                        S.op("dve", lambda e: e.tensor_tensor(out=Am[:].rearrange("p a b c -> p (a b c)"), in0=bk_[0][:], in1=cmb[:, CM["rm"]:CM["rm"] + 4, :].rearrange("p a c -> p (a c)"), op=ALU.mult),
                             reads=[rb[0], b.r_cm], writes=[r_Am])
                        S.op("dve", lambda e: e.tensor_tensor(out=AmT[:].rearrange("p b c -> p (b c)"), in0=bk_[1][:, 0:128], in1=cmf[:, CM["rmT"], :], op=ALU.mult), reads=[rb[1], b.r_cm], writes=[r_AmT])
                        for c in range(2):
                            pcs = slice(64 * c, 64 * c + 64)
                            for hh in range(2):
                                mm(bk_[4][pcs, hh * 64:(hh + 1) * 64], Am[pcs, 1, hh, :], tok[3][pcs, tl, hh * 64:(hh + 1) * 64], [r_Am, r_tok[3][tl]], [rb[4]])
                        S.op("dve", lambda e: e.tensor_copy(out=Zf[:, :, 0:64], in_=bk_[4][:, 0:128].rearrange("p (b c) -> p b c", c=64)), reads=[rb[4], r_Zf], writes=[r_Zf])
                        S.op("pool", (lambda tl=tl: lambda e: e.tensor_copy(out=Zf[:, :, 64:128], in_=tok[0][:, tl, :].rearrange("p (b c) -> p b c", c=64)))(), reads=[r_tok[0][tl], r_Zf], writes=[r_Zf])
                        S.op("act", lambda e: e.copy(out=Zb[:], in_=Zf[:]), reads=[r_Zf], writes=[r_Zb])
                        if gl == 0 and dbg_stop(b, "5"):
                            return
                        for lev in range(6):
                            if lev == 0:
                                Pl = lambda pcs, hh: Am[pcs, 0, hh, :]
                                PlT = lambda pcs, hh: AmT[pcs, hh, :]
                                rP = [r_Am, r_AmT]
                            else:
                                pbuf = Pb[lev % 2]
                                Pl = (lambda pbuf: lambda pcs, hh: pbuf[pcs, 0, hh, :])(pbuf)
                                PlT = (lambda pbuf: lambda pcs, hh: pbuf[pcs, 1, hh, :])(pbuf)
                                rP = [r_Pb[lev % 2]]
                            half = (lev % 2) * 256
                            for c in range(2):
                                pcs = slice(64 * c, 64 * c + 64)
                                for hh in range(2):
                                    mm(bk_[3][pcs, half + hh * 128:half + (hh + 1) * 128], Pl(pcs, hh), Zb[pcs, hh, :], rP + [r_Zb], [rb[3]])
                            if lev < 5:
                                for c in range(2):
                                    pcs = slice(64 * c, 64 * c + 64)
                                    for hh in range(2):
                                        mm(bk_[2][pcs, half + hh * 64:half + (hh + 1) * 64], PlT(pcs, hh), Pl(pcs, hh), rP, [rb[2]])
                                        mm(bk_[2][pcs, half + 128 + hh * 64:half + 128 + (hh + 1) * 64], Pl(pcs, hh), PlT(pcs, hh), rP, [rb[2]])
                                nb = Pb[(lev + 1) % 2]
                                S.op("act", (lambda nb=nb, half=half: lambda e: e.copy(out=nb[:].rearrange("p a b c -> p (a b c)"), in_=bk_[2][:, half:half + 256]))(), reads=[rb[2]], writes=[r_Pb[(lev + 1) % 2]])
                            S.op("dve", (lambda half=half: lambda e: e.tensor_tensor(out=Zf[:].rearrange("p b c -> p (b c)"), in0=bk_[3][:, half:half + 256], in1=Zf[:].rearrange("p b c -> p (b c)"), op=ALU.add))(),
                                 reads=[rb[3], r_Zf], writes=[r_Zf])
                            S.op("act", lambda e: e.copy(out=Zb[:], in_=Zf[:]), reads=[r_Zf], writes=[r_Zb])
                        if gl == 0 and dbg_stop(b, "6"):
                            return
                        for c in range(2):
                            pcs = slice(64 * c, 64 * c + 64)
                            for hh in range(2):
                                ps = slice(64 * hh, 64 * hh + 64)
                                mm(bk_[4][ps, 256 + c * 64:256 + (c + 1) * 64], Zb[pcs, hh, 64:128], Am[pcs, 2, hh, :], [r_Zb, r_Am], [rb[4]])
                                mm(bk_[5][ps, c * 64:(c + 1) * 64], Zb[pcs, hh, 64:128], tok[1][pcs, tl, hh * 64:(hh + 1) * 64], [r_Zb, r_tok[1][tl]], [rb[5]])
                        for c in range(2):
                            pcs = slice(64 * c, 64 * c + 64)
                            for hh in range(2):
                                ps = slice(64 * hh, 64 * hh + 64)
                                mm(bk_[5][ps, 128 + c * 64:128 + (c + 1) * 64], tok[2][pcs, tl, hh * 64:(hh + 1) * 64], tok[3][pcs, tl, hh * 64:(hh + 1) * 64], [r_tok[2][tl], r_tok[3][tl]], [rb[5]], start=True, stop=False)
                                mm(bk_[5][ps, 128 + c * 64:128 + (c + 1) * 64], tok[1][pcs, tl, hh * 64:(hh + 1) * 64], Zb[pcs, hh, 0:64], [r_tok[1][tl], r_Zb], [rb[5]], start=False, stop=True)
                        S.op("dve", (lambda tl=tl: lambda e: e.tensor_tensor(out=M2b[:], in0=bk_[4][:, 256:384], in1=fm[0][:, tl * 128:(tl + 1) * 128], op=ALU.add))(), reads=[rb[4], r_fm[0]], writes=[r_M2b])
                        for c in range(2):
                            ch = tl * 2 + c
                            S.op("dve", (lambda c=c, ch=ch: lambda e: e.scalar_tensor_tensor(out=M3Tb[:, c, :], in0=cmf[:, CM["id2"], 0:64], scalar=ecum8[:, ch:ch + 1], in1=bk_[5][:, c * 64:(c + 1) * 64], op0=ALU.mult, op1=ALU.add))(),
                                 reads=[rb[5], r_ecum8, b.r_cm], writes=[r_M3Tb])
                        S.op("act", lambda e: e.copy(out=Laug[:, :, 0:64], in_=bk_[5][:, 128:256].rearrange("p (a c) -> p a c", c=64)), reads=[rb[5], r_Laug], writes=[r_Laug])
                        for c in range(2):
                            pcs = slice(64 * c, 64 * c + 64)
                            for hh in range(2):
                                ps = slice(64 * hh, 64 * hh + 64)
                                oc = slice(c * 64, (c + 1) * 64)
                                mm(bk_[6][ps, oc], Zb[pcs, hh, 0:64], Am[pcs, 2, hh, :], [r_Zb, r_Am], [rb[6]], start=True, stop=False)
                                mm(bk_[6][ps, oc], tok[3][pcs, tl, hh * 64:(hh + 1) * 64], Am[pcs, 3, hh, :], [r_tok[3][tl], r_Am], [rb[6]], start=False, stop=False)
                                mm(bk_[6][ps, oc], Sb_[ps, 0:64], M2b[ps, oc], [r_Sb, r_M2b], [rb[6]], start=False, stop=True)
                                mm(bk_[6][ps, 128 + c * 64:128 + (c + 1) * 64], Sb_[ps, 64:128], M2b[ps, oc], [r_Sb, r_M2b], [rb[6]])
                                mm(bk_[7][ps, 0:128], M3Tb[ps, c, :], Sb_[ps, :], [r_M3Tb, r_Sb], [rb[7]])
                            S.op("dve", (lambda c=c: lambda e: e.tensor_tensor(out=Sf[:], in0=bk_[7][:, 0:128], in1=Laug[:, c, :], op=ALU.add))(), reads=[rb[7], r_Laug, r_Sf], writes=[r_Sf])
                            S.op("act", lambda e: e.copy(out=Sb_[:], in_=Sf[:]), reads=[r_Sf], writes=[r_Sb])
                        if gl == 0 and dbg_stop(b, "7"):
                            return
                        tsl = slice(gl * 128, (gl + 1) * 128)
                        S.op("act", (lambda tsl=tsl: lambda e: e.copy(out=Yloc[:, tsl], in_=bk_[6][:, 0:128]))(), reads=[rb[6]], writes=[r_Yloc[tg]])
                        S.op("dve", (lambda tsl=tsl: lambda e: e.tensor_copy(out=M2p[:, tsl], in_=bk_[6][:, 128:256]))(), reads=[rb[6]], writes=[r_M2p[tg]])
        else:
            persist_io(b, "load", f"ryl{hp}", Yloc[:], r_Yloc, [128, T], BF16)
            persist_io(b, "load", f"rm2{hp}", M2p[:], r_M2p, [128, T], BF16)
            persist_io(b, "load", f"rgt{hp}", gT[:], r_gT, [128, T], BF16)
            persist_io(b, "load", f"rbv{hp}", bv[:], r_bv, [128, T], BF16)
        if b.mode == "local":
            persist_io(b, "dump", f"ryl{hp}", Yloc[:], r_Yloc, [128, T], BF16)
            persist_io(b, "dump", f"rm2{hp}", M2p[:], r_M2p, [128, T], BF16)
            persist_io(b, "dump", f"rgt{hp}", gT[:], r_gT, [128, T], BF16)
            persist_io(b, "dump", f"rbv{hp}", bv[:], r_bv, [128, T], BF16)
        with scope(b) as sb1:
            blk, r_blk = allgather(b, sb1, [(0, 128, Sf[:], [r_Sf])], 128, 128, F32, f"rst{hp}")
            if blk is None:
                return
            blkb = sb1("blkb", [128, 8, 64], BF16)
            r_blkb = S.region()
            S.op("dve", lambda e: e.tensor_copy(out=blkb[:], in_=blk[:, :, 64:128]), reads=[r_blk], writes=[r_blkb])
            MTb = sb1("MTb", [128, 7, 64], BF16)
            r_MTb = S.region()
            for jc in range(7):
                for hh in range(2):
                    ps = slice(64 * hh, 64 * hh + 64)
                    S.op("pe", (lambda jc=jc, ps=ps: lambda e: e.matmul(bk_[0][ps, jc * 64:(jc + 1) * 64], lhsT=blkb[ps, jc, :], rhs=cmb[ps, CM["ident"], ps], start=True, stop=True))(),
                         reads=[r_blkb, b.r_cm], writes=[rb[0]])
            S.op("act", lambda e: e.copy(out=MTb[:].rearrange("p a c -> p (a c)"), in_=bk_[0][:, 0:448]), reads=[rb[0]], writes=[r_MTb])
            Ss = sb1("Ss", [128, 64], F32)
            Ssb = sb1("Ssb", [128, 64], BF16)
            tt_ = sb1("rtt", [128, 64], F32)
            r_Ss, r_Ssb, r_tt = S.regions(3)
            S.op("pool", lambda e: e.memset(Ss[:], 0.0), writes=[r_Ss])
            S.op("pool", lambda e: e.memset(Ssb[:], 0.0), writes=[r_Ssb])
            for jc in range(7):
                for hh in range(2):
                    ps = slice(64 * hh, 64 * hh + 64)
                    mm(bk_[1][ps, 0:64], MTb[ps, jc, :], Ssb[ps, :], [r_MTb, r_Ssb], [rb[1]])
                S.op("dve", (lambda jc=jc: lambda e: e.tensor_tensor(out=tt_[:], in0=bk_[1][:, 0:64], in1=blk[:, jc, 0:64], op=ALU.add))(), reads=[rb[1], r_blk], writes=[r_tt])
                S.op("dve", lambda e: e.tensor_tensor(out=tt_[:], in0=tt_[:], in1=Ss[:], op=ALU.subtract), reads=[r_tt, r_Ss], writes=[r_tt])
                S.op("dve", (lambda jc=jc: lambda e: e.scalar_tensor_tensor(out=Ss[:], in0=tt_[:], scalar=b.pc[:, 8 + jc:9 + jc], in1=Ss[:], op0=ALU.mult, op1=ALU.add))(), reads=[r_tt, r_Ss, b.r_pc], writes=[r_Ss])
                S.op("act", lambda e: e.copy(out=Ssb[:], in_=Ss[:]), reads=[r_Ss], writes=[r_Ssb])
            Y = [sb1(f"Yf{i}", [128, 512], F32) for i in range(2)]
            sq = [sb1(f"rsq{i}", [128, 512], BF16) for i in range(2)]
            rs = [sb1(f"rrs{i}", [128, 512], F32) for i in range(2)]
            tmp = sb1("rtmp", [128, 512], F32)
            r_Y, r_sq, r_rs = S.regions(2), S.regions(2), S.regions(2)
            r_tmp = S.region()
            for tg in range(4):
                i2 = tg % 2
                ts = slice(tg * 512, (tg + 1) * 512)
                bC, bM, bQ = 2 + i2, 4 + i2, 6 + i2
                for hh in range(2):
                    ps = slice(64 * hh, 64 * hh + 64)
                    mm(bk_[bC][ps, :], Ssb[ps, :], M2p[ps, ts], [r_Ssb, r_M2p[tg]], [rb[bC]])
                S.op("dve", (lambda i2=i2, ts=ts, bC=bC: lambda e: e.tensor_tensor(out=Y[i2][:], in0=bk_[bC][:], in1=Yloc[:, ts], op=ALU.add))(), reads=[rb[bC], r_Yloc[tg]], writes=[r_Y[i2]])
                mm(bk_[bM][:], cmf[:, CM["blk"], :], Y[i2][:], [b.r_cm, r_Y[i2]], [rb[bM]])
                S.op("dve", (lambda i2=i2, bM=bM: lambda e: e.scalar_tensor_tensor(out=Y[i2][:], in0=bk_[bM][:], scalar=-1.0 / 64.0, in1=Y[i2][:], op0=ALU.mult, op1=ALU.add))(), reads=[rb[bM], r_Y[i2]], writes=[r_Y[i2]])
                S.op("act", (lambda i2=i2: lambda e: e.activation(out=sq[i2][:], in_=Y[i2][:], func=AF.Square))(), reads=[r_Y[i2]], writes=[r_sq[i2]])
                mm(bk_[bQ][:], cmb[:, CM["blk"], :], sq[i2][:], [b.r_cm, r_sq[i2]], [rb[bQ]])
                rstd_from_sumsq(b, rs[i2][:], r_rs[i2], tmp[:], r_tmp, bk_[bQ][:], rb[bQ], 1.0 / 64.0, "gneps")
                S.op("pool", (lambda i2=i2: lambda e: e.tensor_tensor(out=Y[i2][:], in0=Y[i2][:], in1=rs[i2][:], op=ALU.mult))(), reads=[r_Y[i2], r_rs[i2]], writes=[r_Y[i2]])
                S.op("dve", (lambda i2=i2: lambda e: e.tensor_scalar(out=Y[i2][:], in0=Y[i2][:], scalar1=col("rwkv_ln_w", hp), scalar2=col("rwkv_ln_b", hp), op0=ALU.mult, op1=ALU.add))(), reads=[r_Y[i2], b.r_pv], writes=[r_Y[i2]])
                S.op("dve", (lambda i2=i2, ts=ts: lambda e: e.tensor_tensor(out=Y[i2][:], in0=Y[i2][:], in1=bv[:, ts], op=ALU.add))(), reads=[r_Y[i2], r_bv[tg]], writes=[r_Y[i2]])
                S.op("dve", (lambda i2=i2, ts=ts: lambda e: e.tensor_tensor(out=ym[:, 6 + hp, ts], in0=Y[i2][:], in1=gT[:, ts], op=ALU.mult))(), reads=[r_Y[i2], r_gT[tg]], writes=[rym[6 + hp][tg]])


def mix_layer(b, l, debug=False, parts="amr"):
    S = b.S
    bk_ = b.banks
    rb = b.rb
    w_out = b.D.get("w_out")
    with scope(b) as sb:
        ym = sb("ym", [128, 8, T], BF16)
        rym = S.regions(8, 4)
        with scope(b) as sb1:
            h = sb1("mh", [128, 8, T], BF16)
            rh = S.regions(8, 4)
            with scope(b) as sb2:
                prenorm(b, sb2, 0, T, "ln_mix_pre", l, h, rh)
            if "a" in parts:
                attn_part(b, l, h, rh, ym, rym)
            if "m" in parts:
                for hp in range(2):
                    mlstm_part(b, l, hp, h, rh, ym, rym)
            if "r" in parts:
                for hp in range(2):
                    rwkv_part(b, l, hp, h, rh, ym, rym)
        if debug:
            for c in range(8):
                for tg in range(4):
                    S.op("dve", (lambda c=c, tg=tg: lambda e: e.tensor_copy(out=b.xT[:, c, tg * 512:(tg + 1) * 512], in_=ym[:, c, tg * 512:(tg + 1) * 512]))(),
                         reads=[rym[c][tg], b.rx[c][tg]], writes=[b.rx[c][tg]])
            return
        for half in range(2):
            t0 = half * 1024
            with scope(b) as sb1:
                y = sb1("my", [128, 8, 1024], F32)
                ry = S.regions(8, 2)
                with scope(b) as sb2:
                    wo = [sb2(f"mwo{i}", [128, 8, 256], BF16) for i in range(2)]
                    rwo = S.regions(2)
                    dwo = [S.dsem() for _ in range(2)]
                    for jg in range(4):
                        s = jg % 2
                        cs_ = slice(jg * 256, (jg + 1) * 256)
                        for g in range(2):
                            S.dma("pool", dwo[s], (lambda s=s, g=g, cs_=cs_: lambda e: e.dma_start(out=wo[s][64 * g:64 * g + 64, 0:4, :], in_=w_out[g * 256:(g + 1) * 256, cs_].rearrange("(i d) n -> d i n", d=64)))(), writes=[rwo[s]])
                        S.dma("pool", dwo[s], (lambda s=s, cs_=cs_: lambda e: e.dma_start(out=wo[s][:, 4:8, :], in_=w_out[512:1024, cs_].rearrange("(c p) n -> p c n", p=128)))(), writes=[rwo[s]])
                        for ji in range(2):
                            j = jg * 2 + ji
                            par = j % 2
                            for k in range(8):
                                for tg in range(2):
                                    bk = 4 * par + tg
                                    g4 = half * 2 + tg
                                    S.op("pe", (lambda s=s, k=k, ji=ji, tg=tg, bk=bk: lambda e: e.matmul(bk_[bk][:], lhsT=wo[s][:, k, ji * 128:(ji + 1) * 128], rhs=ym[:, k, t0 + tg * 512:t0 + (tg + 1) * 512], start=(k == 0), stop=(k == 7)))(),
                                         reads=[rwo[s], rym[k][g4]], writes=[rb[bk]])
                            for tg in range(2):
                                bk = 4 * par + tg
                                if tg == 0:
                                    S.op("act", (lambda j=j, tg=tg, bk=bk: lambda e: e.copy(out=y[:, j, tg * 512:(tg + 1) * 512], in_=bk_[bk][:]))(), reads=[rb[bk]], writes=[ry[j][tg]])
                                else:
                                    S.op("dve", (lambda j=j, tg=tg, bk=bk: lambda e: e.tensor_copy(out=y[:, j, tg * 512:(tg + 1) * 512], in_=bk_[bk][:]))(), reads=[rb[bk]], writes=[ry[j][tg]])
                with scope(b) as sb2:
                    postnorm_residual(b, sb2, t0, 1024, "ln_mix_post", l, y, ry, 1.0)


_PROGS = {}


def get_prog(kind, parts="amr"):
    key = (kind, parts)
    if key not in _PROGS:
        _PROGS[key] = build_program(kind, parts)
    return _PROGS[key]


def run_prog(kind, L, inp, xT_list, gathered, parts="amr"):
    nc, b = get_prog(kind, parts)
    pv = pack_params(inp, L)
    cmat = const_mats()
    pos = np.asarray(inp["positions"], np.int32)
    nprev = np.ascontiguousarray(cmat.reshape(128, -1, 128)[:, CM["nprev"], :])
    maps = []
    for c in range(NCORES):
        sl = slice(c * T, (c + 1) * T)
        pcore = np.zeros((128, 32), np.float32)
        if c > 0:
            pcore[:, c - 1] = 1.0
        pcore[:, 8:8 + c] = 1.0
        m = {"xT": np.ascontiguousarray(xT_list[c], np.float32), "pos": np.ascontiguousarray(pos[:, sl]), "pvec": pv, "cmat": cmat,
             "pcore": pcore, "pcm": (np.full((128, 128), NEG, np.float32) if c == 0 else nprev)}
        full = {}
        for name in b.in_names:
            if name in m:
                full[name] = m[name]
            elif name == "pT":
                full[name] = np.ascontiguousarray(np.asarray(inp["p"], np.float32)[L, 0, sl].T)
            elif name.startswith("g_"):
                full[name] = gathered[name[2:]]
            elif name.startswith("l_p_"):
                full[name] = gathered["p_" + name[4:]][c]
            else:
                full[name] = np.ascontiguousarray(np.asarray(inp[name][L], np.float32))
        maps.append(full)
    res = run_bass_kernel_spmd(nc, maps, core_ids=list(range(NCORES)))
    return res.results


def gather_payloads(results, keys):
    return {k: np.ascontiguousarray(np.concatenate([np.asarray(r["x_" + k]) for r in results], axis=0)) for k in keys}


def kernel(**inputs):
    x = np.asarray(inputs["x"], np.float32)[0]
    xT = [np.ascontiguousarray(x[c * T:(c + 1) * T].T) for c in range(NCORES)]
    for L in range(DEPTH):
        rf = run_prog("F", L, inputs, xT, {})
        xT = [np.asarray(r["outT"]) for r in rf]
        ra = run_prog("A0", L, inputs, xT, {})
        g = gather_payloads(ra, HALO_KEYS)
        rb_ = run_prog("B", L, inputs, xT, g)
        g.update(gather_payloads(rb_, STATE_KEYS))
        for k_ in rb_[0]:
            if k_.startswith("x_p_"):
                g[k_[2:]] = [np.asarray(r[k_]) for r in rb_]
        rc = run_prog("C2", L, inputs, xT, g)
        xT = [np.asarray(r["outT"]) for r in rc]
        rg = run_prog("G", L, inputs, xT, {})
        xT = [np.asarray(r["outT"]) for r in rg]
    out = np.concatenate([t.T for t in xT], axis=0)
    return out[None].astype(np.float32)
```

```python
import contextlib
import numpy as np
import concourse.bass as bass
import concourse.mybir as mybir
from concourse.bass_utils import run_bass_kernel_spmd

F32 = mybir.dt.float32
BF16 = mybir.dt.bfloat16
I32 = mybir.dt.int32
AF = mybir.ActivationFunctionType
ALU = mybir.AluOpType
AX = mybir.AxisListType

NCORES = 8
T = 2048
DM = 1024
DFF = 2816
DEPTH = 2


class Reg:
    __slots__ = ("name", "w", "r", "excl")

    def __init__(self, name):
        self.name = name
        self.w = None
        self.r = {}
        self.excl = False


class DSem:
    def __init__(self, key, h):
        self.key = key
        self.h = h
        self.count = 0


class Sched:
    CE = ("pe", "act", "dve", "pool")
    ENG = ("pe", "act", "dve", "pool", "sp")

    def __init__(self, nc, es):
        self.nc = nc
        self.es = es
        self.cnt = {e: 0 for e in self.CE}
        self.sem = {e: es.enter_context(nc.semaphore(f"c_{e}")) for e in self.CE}
        self.seen = {e: {} for e in self.ENG}
        self.nds = 0
        self.nreg = 0
        self.dsems = []
        self.engs = {"pe": nc.tensor, "act": nc.scalar, "dve": nc.vector, "pool": nc.gpsimd, "sp": nc.sync}
        self.ninst = 0
        self.free_dsems = []
        self.scope_stack = []

    def region(self, name=None):
        self.nreg += 1
        return Reg(name or f"r{self.nreg}")

    def regions(self, *shape):
        if len(shape) == 1:
            return [self.region() for _ in range(shape[0])]
        return [self.regions(*shape[1:]) for _ in range(shape[0])]

    def dsem(self, name=None):
        if name is None and self.free_dsems:
            d = self.free_dsems.pop()
        else:
            self.nds += 1
            nm = name or f"d{self.nds}"
            d = DSem("dma_" + nm, self.es.enter_context(self.nc.semaphore("ds_" + nm)))
            self.dsems.append(d)
        if name is None and self.scope_stack:
            self.scope_stack[-1].append(d)
        return d

    def _deps(self, eng, reads, writes):
        waits = {}

        def add(key, sem, val, raw):
            if key == eng and not (raw and eng != "pe"):
                return
            if self.seen[eng].get(key, 0) >= val:
                return
            if key not in waits or waits[key][1] < val:
                waits[key] = (sem, val)

        for r in reads:
            if r.w is not None:
                add(*r.w, True)
        for w in writes:
            if w.w is not None:
                add(*w.w, True)
            for key, (sem, val) in w.r.items():
                add(key, sem, val, False)
        for key, (sem, val) in waits.items():
            self.seen[eng][key] = val
        return list(waits.values())

    def _mark(self, tok, reads, writes):
        key, sem, val = tok
        for r in reads:
            if key not in r.r or r.r[key][1] < val:
                r.r[key] = (sem, val)
        for w in writes:
            w.w = tok
            w.r = {}

    def _emit(self, name, waits, fn, inc):
        e = self.engs[name]
        for sem, val in waits:
            e.wait_ge(sem, val)
            self.ninst += 1
        if fn is not None:
            ins = fn(e)
            ins.then_inc(inc[0], inc[1])
            self.ninst += 1

    def op(self, eng, fn, reads=(), writes=(), pe_wait=()):
        ex = [r for r in reads if r.excl]
        if ex:
            reads = [r for r in reads if not r.excl]
            writes = list(writes) + ex
        waits = self._deps(eng, reads, writes)
        for (key, sem, val) in pe_wait:
            if self.seen[eng].get(key, 0) < val:
                waits.append((sem, val))
                self.seen[eng][key] = val
        self.cnt[eng] += 1
        tok = (eng, self.sem[eng], self.cnt[eng])
        self._emit(eng, waits, fn, (self.sem[eng], 1))
        self._mark(tok, reads, writes)
        return tok

    def dma(self, queue, dsem, fn, reads=(), writes=(), inc=16):
        waits = self._deps(queue, reads, writes)
        dsem.count += inc
        tok = (dsem.key, dsem.h, dsem.count)
        self._emit(queue, waits, fn, (dsem.h, inc))
        self._mark(tok, reads, writes)
        return tok

    def wait_all(self, eng, regs):
        waits = self._deps(eng, regs, ())
        self._emit(eng, waits, None, None)

    def barrier(self):
        for eng in self.ENG:
            waits = []
            for o in self.CE:
                if o != eng and self.cnt[o] > self.seen[eng].get(o, 0):
                    waits.append((self.sem[o], self.cnt[o]))
                    self.seen[eng][o] = self.cnt[o]
            for d in self.dsems:
                if d.count > self.seen[eng].get(d.key, 0):
                    waits.append((d.h, d.count))
                    self.seen[eng][d.key] = d.count
            self._emit(eng, waits, None, None)


_VEC8 = ["ln_ffn1_pre", "ln_ffn1_post", "ln_mix_pre", "ln_mix_post", "ln_ffn2_pre", "ln_ffn2_post",
         "ln_ple_pre", "ln_ple_post", "rwkv_mu"]
_VEC2 = ["mlstm_norm", "rwkv_w0", "rwkv_a0", "rwkv_k_k", "rwkv_k_a", "rwkv_r_k", "rwkv_ln_w", "rwkv_ln_b"]


def param_layout():
    lay = {}
    off = 0
    for l in range(1):
        for n in _VEC8:
            lay[(n, l)] = off
            off += 8
        for n in _VEC2:
            lay[(n, l)] = off
            off += 2
        lay[("conv", l)] = off
        off += 16
        lay[("sink", l)] = off
        off += 4
        lay[("gbias", l)] = off
        off += 1
    for n in ["eps1", "eps4", "gneps", "invfreq", "one", "zero"]:
        lay[n] = off
        off += 1
    lay["_n"] = off
    return lay


LAY = param_layout()


def pack_params(inp, L):
    lay = LAY
    pv = np.zeros((128, lay["_n"]), np.float32)
    l = 0
    for n in _VEC8:
        pv[:, lay[(n, l)]:lay[(n, l)] + 8] = np.asarray(inp[n][L], np.float32).reshape(8, 128).T
    for n in _VEC2:
        pv[:, lay[(n, l)]:lay[(n, l)] + 2] = np.asarray(inp[n][L], np.float32).reshape(2, 128).T
    conv = np.asarray(inp["mlstm_conv"][L], np.float32)
    for tile in range(4):
        for tap in range(4):
            pv[:, lay[("conv", l)] + tile * 4 + tap] = conv[tap, tile * 128:(tile + 1) * 128]
    sk = np.asarray(inp["attn_sinks"][L], np.float32)
    for i in range(4):
        pv[0:64, lay[("sink", l)] + i] = sk[i]
        pv[64:128, lay[("sink", l)] + i] = sk[4 + i]
    pv[0:4, lay[("gbias", l)]] = np.asarray(inp["mlstm_i_bias"][L], np.float32)
    pv[32:36, lay[("gbias", l)]] = np.asarray(inp["mlstm_f_bias"][L], np.float32)
    pv[:, lay["eps1"]] = 1e-6
    pv[:, lay["eps4"]] = 4e-6
    pv[:, lay["gneps"]] = 64e-5
    inv = (500000.0 ** (-np.arange(0, 16, 2, dtype=np.float32) / 16.0)).astype(np.float32)
    for p_ in range(128):
        d = p_ % 64
        pv[p_, lay["invfreq"]] = inv[d % 8] if d < 16 else 0.0
    pv[:, lay["one"]] = 1.0
    return pv


CM = {"ident": 0, "blk": 1, "rowsel": 2, "rmT": 6, "ncur": 7, "nprev": 8, "id2": 9, "ones": 10, "perm": 11, "rm": 12, "_n": 16}
NF32 = 10
NEG = -30000.0


def const_mats():
    cm = np.zeros((128, CM["_n"], 128), np.float32)
    cm[:, CM["ident"], :] = np.eye(128)
    cm[:, CM["ones"], :] = 1.0
    blk = np.zeros((128, 128))
    blk[:64, :64] = 1
    blk[64:, 64:] = 1
    cm[:, CM["blk"], :] = blk
    P = np.zeros((128, 128))
    for hb in (0, 64):
        for i in range(8):
            P[hb + i + 8, hb + i] = -1.0
            P[hb + i, hb + i + 8] = 1.0
    cm[:, CM["perm"], :] = P
    for h in range(4):
        for k in range(128):
            if k % 32 == h:
                cm[k, CM["rowsel"] + h, :] = 1.0
    s_ = (np.arange(128) % 64)[:, None]
    t_ = np.arange(64)[None, :]
    strict = (s_ < t_).astype(np.float32)
    incl = (s_ <= t_).astype(np.float32)
    rm = np.stack([strict, strict, strict, strict, incl, incl, incl, incl], axis=1)
    cm[:, CM["rm"]:CM["rm"] + 4, :] = rm.reshape(128, 4, 128)
    lower = (s_ > t_).astype(np.float32)
    cm[:, CM["rmT"], :] = np.concatenate([lower, lower], axis=1)
    ss = np.arange(128)[:, None]
    tt = np.arange(128)[None, :]
    cm[:, CM["ncur"], :] = np.where(ss > tt, NEG, 0.0)
    cm[:, CM["nprev"], :] = np.where(ss <= tt, NEG, 0.0)
    for p_ in range(128):
        cm[p_, CM["id2"], p_ % 64] = 1.0
    return cm.reshape(128, -1)


W_SHAPES = {
    "w_ffn1_in": [DM, 2 * DFF], "w_ffn1_out": [DFF, DM], "w_in": [DM, 2824],
    "rwkv_w_up": [64, 256], "rwkv_a_up": [64, 256], "rwkv_g_up": [128, 256],
    "w_out": [DM, DM], "w_ffn2_in": [DM, 2 * DFF], "w_ffn2_out": [DFF, DM],
    "w_ple_gate": [DM, DM], "w_ple_proj": [256, DM],
}
HALO_KEYS = ["att", "mls0", "mls1", "rsh0", "rsh1"]
STATE_KEYS = ["mst0", "mst1", "rst0", "rst1"]
PROG_W = {
    "A": ["w_ffn1_in", "w_ffn1_out", "w_in", "rwkv_w_up", "rwkv_a_up", "rwkv_g_up"],
    "A0": ["w_in", "rwkv_w_up", "rwkv_a_up", "rwkv_g_up"],
    "B": ["w_in", "rwkv_w_up", "rwkv_a_up", "rwkv_g_up"],
    "C": ["w_in", "rwkv_w_up", "rwkv_a_up", "rwkv_g_up", "w_out", "w_ffn2_in", "w_ffn2_out", "w_ple_gate", "w_ple_proj"],
    "Cdbg": ["w_in", "rwkv_w_up", "rwkv_a_up", "rwkv_g_up"],
    "R": ["w_ffn1_in", "w_ffn1_out", "w_ple_gate", "w_ple_proj"],
    "F": ["w_ffn1_in", "w_ffn1_out"],
    "C2": ["w_in", "rwkv_w_up", "rwkv_a_up", "rwkv_g_up", "w_out"],
    "G": ["w_ffn2_in", "w_ffn2_out", "w_ple_gate", "w_ple_proj"],
}


class B:
    pass


def build_program(kind, parts="amr"):
    nc = bass.Bass("TRN2", target_bir_lowering=False)
    b = B()
    b.nc = nc
    b.uid = 0
    b.kind = kind
    b.parts = parts
    b.mode = {'B': 'local', 'C2': 'finish', 'C': 'finish'}.get(kind, 'full')
    D = {}
    b.in_names, b.out_names, b.out_regs = [], [], []

    def din(name, shape, dt=F32):
        D[name] = nc.dram_tensor(name, list(shape), dt, kind="ExternalInput").ap()
        b.in_names.append(name)

    din("xT", [DM, T])
    if kind in ("C", "R", "G"):
        din("pT", [256, T])
    din("pos", [1, T], I32)
    din("pvec", [128, LAY["_n"]])
    din("cmat", [128, CM["_n"] * 128])
    din("pcore", [128, 32])
    din("pcm", [128, 128])
    for k in PROG_W[kind]:
        din(k, W_SHAPES[k])
    b.D = D
    if kind in ("A", "A0"):
        b.emit, b.consume = set(HALO_KEYS), set()
    elif kind == "B":
        b.emit, b.consume = set(STATE_KEYS), set(HALO_KEYS)
    else:
        b.emit, b.consume = set(), set(HALO_KEYS + STATE_KEYS)
    want_x_out = kind in ("A", "A0", "C", "Cdbg", "R", "F", "C2", "G")
    if want_x_out:
        outT = nc.dram_tensor("outT", [DM, T], F32, kind="ExternalOutput").ap()
        b.out_names.append("outT")

    with contextlib.ExitStack() as es:
        S = Sched(nc, es)
        b.S = S

        def sb(name, shape, dt):
            return es.enter_context(nc.sbuf_tensor(name, list(shape), dt))

        xT = sb("xT_sb", [128, 8, T], F32)
        rx = S.regions(8, 4)
        pv = sb("pv", [128, LAY["_n"]], F32)
        r_pv = S.region()
        cm = sb("cm", [128, NF32, 128], F32)
        cmb = sb("cmb", [128, CM["_n"], 128], BF16)
        r_cm = S.region()
        pc = sb("pc", [128, 32], F32)
        r_pc = S.region()
        banks = [es.enter_context(nc.psum_tensor(f"bank{i}", [128, 512], F32)) for i in range(8)]
        rb = S.regions(8)
        for r_ in rb:
            r_.excl = True
        b.xT, b.rx, b.pv, b.r_pv, b.cm, b.cmb, b.r_cm, b.pc, b.r_pc, b.banks, b.rb = xT, rx, pv, r_pv, cm, cmb, r_cm, pc, r_pc, banks, rb

        d_c = S.dsem("consts")
        S.dma("sp", d_c, lambda e: e.dma_start(out=pv[:], in_=D["pvec"]), writes=[r_pv])
        S.dma("sp", d_c, lambda e: e.dma_start(out=cm[:], in_=D["cmat"].rearrange("p (a b) -> p a b", b=128)[:, 0:NF32, :]), writes=[r_cm])
        d_cb = S.dsem("constsb")
        S.dma("pool", d_cb, lambda e: e.dma_start(out=cmb[:], in_=D["cmat"].rearrange("p (a b) -> p a b", b=128)), writes=[S.region()])
        S.dma("sp", d_c, lambda e: e.dma_start(out=pc[:], in_=D["pcore"]), writes=[r_pc])
        xv = D["xT"].rearrange("(c p) t -> p c t", p=128)
        for tg in range(4):
            d_x = S.dsem(f"x{tg}")
            S.dma("sp", d_x, (lambda tg: lambda e: e.dma_start(out=xT[:, :, tg * 512:(tg + 1) * 512], in_=xv[:, :, tg * 512:(tg + 1) * 512]))(tg),
                  writes=[rx[c][tg] for c in range(8)])
        S.barrier()

        l = 0
        if kind in ("A", "R", "F"):
            for half in range(2):
                ffn_half(b, l, 1, half * 1024)
        if kind == "R":
            for half in range(2):
                ple_half(b, l, half * 1024)
        if kind in ("A", "A0", "B"):
            with scope(b) as sb1:
                h = sb1("mh", [128, 8, T], BF16)
                rh = S.regions(8, 4)
                with scope(b) as sb2:
                    prenorm(b, sb2, 0, T, "ln_mix_pre", l, h, rh)
                if kind != "B" and "a" in parts:
                    attn_tail(b, l, h, rh)
                if "m" in parts:
                    for hp in range(2):
                        mlstm_part(b, l, hp, h, rh, None, None)
                if "r" in parts:
                    for hp in range(2):
                        rwkv_part(b, l, hp, h, rh, None, None)
        if kind in ("C", "Cdbg", "C2"):
            mix_layer(b, l, debug=(kind == "Cdbg"), parts=parts)
        if kind in ("C", "G"):
            for half in range(2):
                ffn_half(b, l, 2, half * 1024)
            for half in range(2):
                ple_half(b, l, half * 1024)
        S.barrier()

        if want_x_out:
            ov = outT.rearrange("(c p) t -> p c t", p=128)
            r_out = S.region()
            d_o = S.dsem("out")
            for tg in range(4):
                S.dma("sp", d_o, (lambda tg: lambda e: e.dma_start(out=ov[:, :, tg * 512:(tg + 1) * 512], in_=xT[:, :, tg * 512:(tg + 1) * 512]))(tg),
                      reads=[rx[c][tg] for c in range(8)], writes=[r_out])
            b.out_regs.append(r_out)
        S.wait_all("sp", b.out_regs)
    b.ninst = S.ninst
    return nc, b


def pvc(b, key, l=None, i=0):
    off = LAY[(key, l)] if l is not None else LAY[key]
    return b.pv[:, off + i:off + i + 1]


@contextlib.contextmanager
def scope(b):
    with contextlib.ExitStack() as es:
        def sb(name, shape, dt):
            b.uid += 1
            return es.enter_context(b.nc.sbuf_tensor(f"{name}_{b.uid}", list(shape), dt))
        b.S.scope_stack.append([])
        yield sb
        b.S.barrier()
        b.S.free_dsems.extend(b.S.scope_stack.pop())


def rstd_from_sumsq(b, sb_rs, r_rs, tmp, r_tmp, bank, r_bank, scale, eps_key, n=512):
    S = b.S
    S.op("act", lambda e: e.activation(out=tmp, in_=bank, func=AF.Sqrt, bias=pvc(b, eps_key), scale=scale),
         reads=[r_bank, b.r_pv], writes=[r_tmp])
    S.op("dve", lambda e: e.reciprocal(out=sb_rs, in_=tmp), reads=[r_tmp], writes=[r_rs])


def prenorm(b, sb, t0, nt, gkey, l, h, rh):
    S = b.S
    ng = nt // 512
    sq = [sb(f"pn_sq{i}", [128, 8, 512], BF16) for i in range(2)]
    r_sq = S.regions(2)
    tmp = sb("pn_tmp", [128, 512], F32)
    r_tmp = S.region()
    rs = [sb(f"pn_rs{i}", [128, 512], F32) for i in range(2)]
    r_rs = S.regions(2)
    for tg in range(ng):
        g4 = (t0 // 512) + tg
        sl = slice(t0 + tg * 512, t0 + (tg + 1) * 512)
        i = tg % 2
        S.op("act", (lambda i, sl: lambda e: e.activation(out=sq[i][:], in_=b.xT[:, :, sl], func=AF.Square))(i, sl),
             reads=[b.rx[c][g4] for c in range(8)], writes=[r_sq[i]])
        bk = 7 - i
        for c in range(8):
            S.op("pe", (lambda i, c, bk: lambda e: e.matmul(b.banks[bk][:], lhsT=b.cmb[:, CM["ones"], :], rhs=sq[i][:, c, :], start=(c == 0), stop=(c == 7)))(i, c, bk),
                 reads=[r_sq[i], b.r_cm], writes=[b.rb[bk]])
        rstd_from_sumsq(b, rs[i][:], r_rs[i], tmp[:], r_tmp, b.banks[bk][:], b.rb[bk], 1.0 / DM, "eps1")
        for c in range(8):
            S.op("dve", (lambda i, c, sl, tg: lambda e: e.scalar_tensor_tensor(
                out=h[:, c, tg * 512:(tg + 1) * 512], in0=b.xT[:, c, sl], scalar=pvc(b, gkey, l, c), in1=rs[i][:],
                op0=ALU.mult, op1=ALU.mult))(i, c, sl, tg),
                reads=[b.rx[c][g4], r_rs[i], b.r_pv], writes=[rh[c][tg]])


def postnorm_residual(b, sb, t0, nt, gkey, l, y, ry, factor):
    S = b.S
    ng = nt // 512
    sq = [sb(f"po_sq{i}", [128, 8, 512], BF16) for i in range(2)]
    r_sq = S.regions(2)
    tmp = sb("po_tmp", [128, 512], F32)
    r_tmp = S.region()
    rs = [sb(f"po_rs{i}", [128, 512], F32) for i in range(2)]
    r_rs = S.regions(2)
    t2 = [sb(f"po_t2{i}", [128, 512], F32) for i in range(2)]
    r_t2 = S.regions(2)
    scale = 1.0 / (DM * factor * factor)
    eps_key = "eps1" if factor == 1.0 else "eps4"
    n2 = 0
    for tg in range(ng):
        g4 = (t0 // 512) + tg
        sl = slice(t0 + tg * 512, t0 + (tg + 1) * 512)
        ysl = slice(tg * 512, (tg + 1) * 512)
        i = tg % 2
        S.op("act", (lambda i, ysl: lambda e: e.activation(out=sq[i][:], in_=y[:, :, ysl], func=AF.Square))(i, ysl),
             reads=[ry[c][tg] for c in range(8)], writes=[r_sq[i]])
        bk = 7 - i
        for c in range(8):
            S.op("pe", (lambda i, c, bk: lambda e: e.matmul(b.banks[bk][:], lhsT=b.cmb[:, CM["ones"], :], rhs=sq[i][:, c, :], start=(c == 0), stop=(c == 7)))(i, c, bk),
                 reads=[r_sq[i], b.r_cm], writes=[b.rb[bk]])
        rstd_from_sumsq(b, rs[i][:], r_rs[i], tmp[:], r_tmp, b.banks[bk][:], b.rb[bk], scale, eps_key)
        for c in range(8):
            j = n2 % 2
            n2 += 1
            S.op("pool", (lambda i, c, ysl, j: lambda e: e.tensor_tensor(out=t2[j][:], in0=y[:, c, ysl], in1=rs[i][:], op=ALU.mult))(i, c, ysl, j),
                 reads=[ry[c][tg], r_rs[i]], writes=[r_t2[j]])
            S.op("dve", (lambda c, sl, j: lambda e: e.scalar_tensor_tensor(
                out=b.xT[:, c, sl], in0=t2[j][:], scalar=pvc(b, gkey, l, c), in1=b.xT[:, c, sl], op0=ALU.mult, op1=ALU.add))(c, sl, j),
                reads=[r_t2[j], b.rx[c][g4], b.r_pv], writes=[b.rx[c][g4]])


def wview(ap2d, r0, nr, c0, ncol):
    return ap2d[r0:r0 + nr, c0:c0 + ncol].rearrange("(c p) n -> p c n", p=128)


def ffn_half(b, l, which, t0):
    S = b.S
    NT, NG = 1024, 2
    w_in = b.D[f"w_ffn{which}_in"]
    w_out = b.D[f"w_ffn{which}_out"]
    with scope(b) as sb:
        h = sb("h", [128, 8, NT], BF16)
        rh = S.regions(8, NG)
        act = sb("act", [128, 22, NT], BF16)
        ract = S.regions(22, NG)
        with scope(b) as sb2:
            prenorm(b, sb2, t0, NT, f"ln_ffn{which}_pre", l, h, rh)
        with scope(b) as sb2:
            wg = [sb2(f"wg{i}", [128, 8, 256], BF16) for i in range(2)]
            wu = [sb2(f"wu{i}", [128, 8, 256], BF16) for i in range(2)]
            rwg, rwu = S.regions(2), S.regions(2)
            dwg, dwu = [S.dsem() for _ in range(2)], [S.dsem() for _ in range(2)]
            sg = [sb2(f"sg{i}", [128, NT], BF16) for i in range(2)]
            rsg = S.regions(2, NG)
            for mg in range(11):
                s = mg % 2
                S.dma("pool", dwg[s], (lambda s, mg: lambda e: e.dma_start(out=wg[s][:], in_=wview(w_in, 0, DM, mg * 256, 256)))(s, mg), writes=[rwg[s]])
                S.dma("pool", dwu[s], (lambda s, mg: lambda e: e.dma_start(out=wu[s][:], in_=wview(w_in, 0, DM, DFF + mg * 256, 256)))(s, mg), writes=[rwu[s]])
                for mi in range(2):
                    m = mg * 2 + mi
                    par = m % 2
                    for (wt, rw, boff) in ((wg, rwg, 0), (wu, rwu, 2)):
                        for k in range(8):
                            for tg in range(NG):
                                bk = 4 * par + boff + tg
                                S.op("pe", (lambda wt, s, k, mi, tg, bk: lambda e: e.matmul(
                                    b.banks[bk][:], lhsT=wt[s][:, k, mi * 128:(mi + 1) * 128], rhs=h[:, k, tg * 512:(tg + 1) * 512],
                                    start=(k == 0), stop=(k == 7)))(wt, s, k, mi, tg, bk),
                                    reads=[rw[s], rh[k][tg]], writes=[b.rb[bk]])
                    for tg in range(NG):
                        bg, bu = 4 * par + tg, 4 * par + 2 + tg
                        S.op("act", (lambda par, tg, bg: lambda e: e.activation(out=sg[par][:, tg * 512:(tg + 1) * 512], in_=b.banks[bg][:], func=AF.Silu))(par, tg, bg),
                             reads=[b.rb[bg]], writes=[rsg[par][tg]])
                        S.op("dve", (lambda par, tg, bu, m: lambda e: e.tensor_tensor(
                            out=act[:, m, tg * 512:(tg + 1) * 512], in0=b.banks[bu][:], in1=sg[par][:, tg * 512:(tg + 1) * 512], op=ALU.mult))(par, tg, bu, m),
                            reads=[b.rb[bu], rsg[par][tg]], writes=[ract[m][tg]])
        with scope(b) as sb2:
            y = sb2("y", [128, 8, NT], F32)
            ry = S.regions(8, NG)
            with scope(b) as sb3:
                wo = [sb3(f"wo{i}", [128, 11, 512], BF16) for i in range(2)]
                rwo = S.regions(2)
                dwo = [S.dsem() for _ in range(2)]
                for jg in range(2):
                    for a in range(2):
                        S.dma("pool", dwo[a], (lambda a, jg: lambda e: e.dma_start(out=wo[a][:], in_=wview(w_out, a * 1408, 1408, jg * 512, 512)))(a, jg), writes=[rwo[a]])
                    for k in range(22):
                        a, kk = k // 11, k % 11
                        for j in range(4):
                            for tg in range(NG):
                                bk = j * 2 + tg
                                S.op("pe", (lambda a, kk, j, tg, bk, k: lambda e: e.matmul(
                                    b.banks[bk][:], lhsT=wo[a][:, kk, j * 128:(j + 1) * 128], rhs=act[:, k, tg * 512:(tg + 1) * 512],
                                    start=(k == 0), stop=(k == 21)))(a, kk, j, tg, bk, k),
                                    reads=[rwo[a], ract[k][tg]], writes=[b.rb[bk]])
                    for j in range(4):
                        for tg in range(NG):
                            bk = j * 2 + tg
                            c = jg * 4 + j
                            if (j + tg) % 2 == 0:
                                S.op("act", (lambda c, tg, bk: lambda e: e.copy(out=y[:, c, tg * 512:(tg + 1) * 512], in_=b.banks[bk][:]))(c, tg, bk),
                                     reads=[b.rb[bk]], writes=[ry[c][tg]])
                            else:
                                S.op("dve", (lambda c, tg, bk: lambda e: e.tensor_copy(out=y[:, c, tg * 512:(tg + 1) * 512], in_=b.banks[bk][:]))(c, tg, bk),
                                     reads=[b.rb[bk]], writes=[ry[c][tg]])
            with scope(b) as sb3:
                postnorm_residual(b, sb3, t0, NT, f"ln_ffn{which}_post", l, y, ry, 0.5)


def ple_half(b, l, t0):
    S = b.S
    NT, NG = 1024, 2
    wgd = b.D["w_ple_gate"]
    wpd = b.D["w_ple_proj"]
    with scope(b) as sb:
        h = sb("h", [128, 8, NT], BF16)
        rh = S.regions(8, NG)
        y = sb("y", [128, 8, NT], F32)
        ry = S.regions(8, NG)
        with scope(b) as sb2:
            prenorm(b, sb2, t0, NT, "ln_ple_pre", l, h, rh)
        with scope(b) as sb2:
            pt = sb2("pt", [128, 2, NT], BF16)
            r_pt = S.region()
            wp = sb2("wp", [128, 2, DM], BF16)
            r_wp = S.region()
            d1, d2 = S.dsem(), S.dsem()
            S.dma("pool", d1, lambda e: e.dma_start(out=pt[:], in_=b.D["pT"][:, t0:t0 + NT].rearrange("(c p) t -> p c t", p=128)), writes=[r_pt])
            S.dma("pool", d2, lambda e: e.dma_start(out=wp[:], in_=wview(wpd, 0, 256, 0, DM)), writes=[r_wp])
            wg = [sb2(f"wg{i}", [128, 8, 256], BF16) for i in range(2)]
            rwg = S.regions(2)
            dwg = [S.dsem() for _ in range(2)]
            sg = [sb2(f"sg{i}", [128, NT], F32) for i in range(2)]
            rsg = S.regions(2, NG)
            for jg in range(4):
                s = jg % 2
                S.dma("pool", dwg[s], (lambda s, jg: lambda e: e.dma_start(out=wg[s][:], in_=wview(wgd, 0, DM, jg * 256, 256)))(s, jg), writes=[rwg[s]])
                for ji in range(2):
                    j = jg * 2 + ji
                    par = j % 2
                    for k in range(8):
                        for tg in range(NG):
                            bk = 4 * par + tg
                            S.op("pe", (lambda s, k, ji, tg, bk: lambda e: e.matmul(
                                b.banks[bk][:], lhsT=wg[s][:, k, ji * 128:(ji + 1) * 128], rhs=h[:, k, tg * 512:(tg + 1) * 512],
                                start=(k == 0), stop=(k == 7)))(s, k, ji, tg, bk),
                                reads=[rwg[s], rh[k][tg]], writes=[b.rb[bk]])
                    for k in range(2):
                        for tg in range(NG):
                            bk = 4 * par + 2 + tg
                            S.op("pe", (lambda k, j, tg, bk: lambda e: e.matmul(
                                b.banks[bk][:], lhsT=wp[:, k, j * 128:(j + 1) * 128], rhs=pt[:, k, tg * 512:(tg + 1) * 512],
                                start=(k == 0), stop=(k == 1)))(k, j, tg, bk),
                                reads=[r_wp, r_pt], writes=[b.rb[bk]])
                    for tg in range(NG):
                        bg, bu = 4 * par + tg, 4 * par + 2 + tg
                        S.op("act", (lambda par, tg, bg: lambda e: e.activation(out=sg[par][:, tg * 512:(tg + 1) * 512], in_=b.banks[bg][:], func=AF.Sigmoid))(par, tg, bg),
                             reads=[b.rb[bg]], writes=[rsg[par][tg]])
                        S.op("dve", (lambda par, tg, bu, j: lambda e: e.tensor_tensor(
                            out=y[:, j, tg * 512:(tg + 1) * 512], in0=b.banks[bu][:], in1=sg[par][:, tg * 512:(tg + 1) * 512], op=ALU.mult))(par, tg, bu, j),
                            reads=[b.rb[bu], rsg[par][tg]], writes=[ry[j][tg]])
        with scope(b) as sb2:
            postnorm_residual(b, sb2, t0, NT, "ln_ple_post", l, y, ry, 1.0)


def allgather(b, sb, srcs, nrow, ncol, dt, name):
    S, nc = b.S, b.nc
    if name in b.emit:
        xo = nc.dram_tensor(f"x_{name}", [nrow, ncol], dt, kind="ExternalOutput").ap()
        r_o = S.region()
        d1 = S.dsem(f"xo_{name}")
        for (c0, ncs, ap, rr) in srcs:
            S.dma("sp", d1, (lambda c0=c0, ncs=ncs, ap=ap: lambda e: e.dma_start(out=xo[:, c0:c0 + ncs], in_=ap, allow_slow_non_contiguous=True))(), reads=rr, writes=[r_o])
        b.out_regs.append(r_o)
        b.out_names.append(f"x_{name}")
        return None, None
    assert name in b.consume, name
    gi = nc.dram_tensor(f"g_{name}", [NCORES * nrow, ncol], dt, kind="ExternalInput").ap()
    b.in_names.append(f"g_{name}")
    r_blk = S.region()
    d3 = S.dsem()
    blk = sb(f"agblk_{name}", [nrow, NCORES, ncol], dt)
    S.dma("sp", d3, lambda e: e.dma_start(out=blk[:], in_=gi.rearrange("(j p) n -> p j n", p=nrow)), writes=[r_blk])
    return blk, r_blk


def select_prev(b, dst, r_dst, blk, r_blk, nrow, c0, ncs):
    S = b.S
    S.op("dve", lambda e: e.tensor_scalar(out=dst, in0=blk[:, 0, c0:c0 + ncs], scalar1=b.pc[0:nrow, 0:1], scalar2=None, op0=ALU.mult),
         reads=[r_blk, b.r_pc], writes=[r_dst])
    for j in range(1, 7):
        S.op("dve", (lambda j=j: lambda e: e.scalar_tensor_tensor(out=dst, in0=blk[:, j, c0:c0 + ncs], scalar=b.pc[0:nrow, j:j + 1], in1=dst,
                                                                 op0=ALU.mult, op1=ALU.add))(), reads=[r_blk, b.r_pc, r_dst], writes=[r_dst])


def rope_tables(b, cosF, sinF, r_cos, r_sin):
    S = b.S
    PI = float(np.pi)
    C1 = 6.28125
    C2 = float(2 * np.pi - 6.28125)
    with scope(b) as sb:
        posi = sb("posi", [128, T], I32)
        ang = sb("ang", [128, T], F32)
        kf = sb("kf", [128, T], F32)
        ki = sb("ki", [128, T], I32)
        rr = sb("rr", [128, T], F32)
        r_posi, r_ang, r_kf, r_ki, r_rr = S.regions(5)
        d = S.dsem()
        S.dma("sp", d, lambda e: e.dma_start(out=posi[:], in_=b.D["pos"].partition_broadcast(128)), writes=[r_posi])
        S.op("dve", lambda e: e.tensor_copy(out=ang[:], in_=posi[:]), reads=[r_posi], writes=[r_ang])
        S.op("dve", lambda e: e.tensor_scalar(out=ang[:], in0=ang[:], scalar1=pvc(b, "invfreq"), scalar2=None, op0=ALU.mult), reads=[r_ang, b.r_pv], writes=[r_ang])
        for (dst, r_dst, shift) in ((sinF, r_sin, 0.0), (cosF, r_cos, PI / 2)):
            S.op("dve", (lambda shift=shift: lambda e: e.tensor_scalar(out=kf[:], in0=ang[:], scalar1=1.0 / (2 * PI), scalar2=0.5 + shift / (2 * PI), op0=ALU.mult, op1=ALU.add))(),
                 reads=[r_ang], writes=[r_kf])
            S.op("dve", lambda e: e.tensor_copy(out=ki[:], in_=kf[:]), reads=[r_kf], writes=[r_ki])
            S.op("dve", lambda e: e.tensor_copy(out=kf[:], in_=ki[:]), reads=[r_ki], writes=[r_kf])
            S.op("dve", lambda e: e.scalar_tensor_tensor(out=rr[:], in0=kf[:], scalar=-C1, in1=ang[:], op0=ALU.mult, op1=ALU.add), reads=[r_kf, r_ang], writes=[r_rr])
            S.op("dve", lambda e: e.scalar_tensor_tensor(out=rr[:], in0=kf[:], scalar=-C2, in1=rr[:], op0=ALU.mult, op1=ALU.add), reads=[r_kf, r_rr], writes=[r_rr])
            if shift:
                S.op("dve", (lambda shift=shift: lambda e: e.tensor_scalar(out=rr[:], in0=rr[:], scalar1=shift, scalar2=None, op0=ALU.add))(), reads=[r_rr], writes=[r_rr])
            S.op("dve", lambda e: e.tensor_scalar(out=kf[:], in0=rr[:], scalar1=-PI, scalar2=2 * PI, op0=ALU.is_lt, op1=ALU.mult), reads=[r_rr], writes=[r_kf])
            S.op("dve", lambda e: e.tensor_tensor(out=rr[:], in0=rr[:], in1=kf[:], op=ALU.add), reads=[r_rr, r_kf], writes=[r_rr])
            S.op("dve", lambda e: e.tensor_scalar(out=kf[:], in0=rr[:], scalar1=PI, scalar2=-2 * PI, op0=ALU.is_gt, op1=ALU.mult), reads=[r_rr], writes=[r_kf])
            S.op("dve", lambda e: e.tensor_tensor(out=rr[:], in0=rr[:], in1=kf[:], op=ALU.add), reads=[r_rr, r_kf], writes=[r_rr])
            S.op("dve", lambda e: e.tensor_scalar(out=rr[:], in0=rr[:], scalar1=-PI, scalar2=PI, op0=ALU.max, op1=ALU.min), reads=[r_rr], writes=[r_rr])
            S.op("act", (lambda dst=dst: lambda e: e.activation(out=dst[:], in_=rr[:], func=AF.Sin))(), reads=[r_rr], writes=[r_dst])


def attn_tail(b, l, h, rh):
    S = b.S
    w_in = b.D["w_in"]
    bk_ = b.banks
    rb = b.rb
    with scope(b) as sb:
        cosF = sb("cosF", [128, T], F32)
        sinF = sb("sinF", [128, T], F32)
        r_cos, r_sin = S.regions(2)
        rope_tables(b, cosF, sinF, r_cos, r_sin)
        if "1" in b.parts:
            return
        wkv = sb("wkv", [128, 8, 256], BF16)
        r_wkv = S.region()
        dkv = S.dsem()
        S.dma("pool", dkv, lambda e: e.dma_start(out=wkv[:], in_=wview(w_in, 0, DM, 512, 256)), writes=[r_wkv])
        xb = sb("xb", [128, 128], BF16)
        t1 = sb("t1", [128, 128], F32)
        t2 = sb("t2", [128, 128], F32)
        pay = sb("pay", [128, 256], F32)
        r_xb, r_t1, r_t2, r_pay = S.regions(4)
        ts = slice(T - 128, T)
        for k in range(8):
            S.op("pe", (lambda k=k: lambda e: e.matmul(bk_[0][:, 0:128], lhsT=wkv[:, k, 0:128], rhs=h[:, k, ts], start=(k == 0), stop=(k == 7)))(), reads=[r_wkv, rh[k][3]], writes=[rb[0]])
        if "2" in b.parts:
            return
        S.op("act", lambda e: e.copy(out=xb[:], in_=bk_[0][:, 0:128]), reads=[rb[0]], writes=[r_xb])
        if "5" in b.parts:
            return
        if "8" in b.parts:
            S.op("dve", lambda e: e.tensor_tensor(out=t1[:], in0=bk_[0][:, 0:128], in1=cosF[:, 0:128], op=ALU.mult), reads=[rb[0], r_cos], writes=[r_t1])
            return
        if "9" in b.parts:
            S.op("dve", lambda e: e.tensor_tensor(out=t1[:], in0=xb[:], in1=cosF[:, ts], op=ALU.mult), reads=[r_xb, r_cos], writes=[r_t1])
            return
        if "0" in b.parts:
            S.op("dve", lambda e: e.tensor_tensor(out=t1[:], in0=bk_[0][:, 0:128], in1=t2[:], op=ALU.mult), reads=[rb[0], r_xb], writes=[r_t1])
            return
        S.op("dve", lambda e: e.tensor_tensor(out=t1[:], in0=bk_[0][:, 0:128], in1=cosF[:, ts], op=ALU.mult), reads=[rb[0], r_cos], writes=[r_t1])
        if "6" in b.parts:
            return
        S.op("pe", lambda e: e.matmul(bk_[1][:, 0:128], lhsT=b.cmb[:, CM["perm"], :], rhs=xb[:], start=True, stop=True), reads=[r_xb, b.r_cm], writes=[rb[1]])
        if "7" in b.parts:
            return
        S.op("dve", lambda e: e.tensor_tensor(out=t2[:], in0=bk_[1][:, 0:128], in1=sinF[:, ts], op=ALU.mult), reads=[rb[1], r_sin], writes=[r_t2])
        S.op("dve", lambda e: e.tensor_tensor(out=pay[:, 0:128], in0=t1[:], in1=t2[:], op=ALU.add), reads=[r_t1, r_t2], writes=[r_pay])
        if "3" in b.parts:
            return
        for k in range(8):
            S.op("pe", (lambda k=k: lambda e: e.matmul(bk_[2][:, 0:128], lhsT=h[:, k, ts], rhs=wkv[:, k, 128:256], start=(k == 0), stop=(k == 7)))(), reads=[r_wkv, rh[k][3]], writes=[rb[2]])
        S.op("act", lambda e: e.copy(out=pay[:, 128:256], in_=bk_[2][:, 0:128]), reads=[rb[2], r_pay], writes=[r_pay])
        if "4" in b.parts:
            return
        allgather(b, sb, [(0, 256, pay[:], [r_pay])], 128, 256, F32, "att")


def attn_part(b, l, h, rh, ym, rym):
    S = b.S
    w_in = b.D["w_in"]
    bk_ = b.banks
    rb = b.rb
    with scope(b) as sb:
        cosF = sb("cosF", [128, T], F32)
        sinF = sb("sinF", [128, T], F32)
        r_cos, r_sin = S.regions(2)
        rope_tables(b, cosF, sinF, r_cos, r_sin)
        qT = sb("qT", [128, 4, T], BF16)
        rq = S.regions(4, 4)
        kT = sb("kT", [128, 128 + T], BF16)
        rk = S.regions(5)
        vt = sb("vt", [128, 17, 128], BF16)
        rv = S.regions(5)
        nm = sb("nm", [128, 3, 4, 128], BF16)
        r_nm = S.region()
        es = sb("es", [128, 4], F32)
        r_es = S.region()
        pcm_sb = sb("pcm_sb", [128, 128], F32)
        r_pcm = S.region()
        d0 = S.dsem()
        S.dma("sp", d0, lambda e: e.dma_start(out=pcm_sb[:], in_=b.D["pcm"]), writes=[r_pcm])
        for m, src in enumerate([b.cm[:, CM["ncur"], :], b.cm[:, CM["nprev"], :], pcm_sb[:]]):
            S.op("dve", (lambda m=m, src=src: lambda e: e.tensor_copy(out=nm[:, m, :, :], in_=src.unsqueeze(1).to_broadcast([128, 4, 128])))(),
                 reads=[b.r_cm, r_pcm], writes=[r_nm])
        so = LAY[("sink", l)]
        S.op("act", lambda e: e.activation(out=es[:], in_=b.pv[:, so:so + 4], func=AF.Exp), reads=[b.r_pv], writes=[r_es])
        with scope(b) as sb2:
            wq = sb2("wq", [128, 8, 512], BF16)
            wkv = sb2("wkv", [128, 8, 256], BF16)
            r_wq, r_wkv = S.regions(2)
            dq, dkv = S.dsem(), S.dsem()
            for g in range(2):
                for i in range(4):
                    S.dma("pool", dq, (lambda g=g, i=i: lambda e: e.dma_start(
                        out=wq[:, :, i * 128 + g * 64:i * 128 + g * 64 + 64], in_=wview(w_in, 0, DM, g * 256 + i * 64, 64)))(), writes=[r_wq])
            S.dma("pool", dkv, lambda e: e.dma_start(out=wkv[:], in_=wview(w_in, 0, DM, 512, 256)), writes=[r_wkv])
            xb = [sb2(f"xb{i}", [128, 512], BF16) for i in range(2)]
            t1 = [sb2(f"t1{i}", [128, 512], F32) for i in range(2)]
            t2 = [sb2(f"t2{i}", [128, 512], F32) for i in range(2)]
            r_xb, r_t1, r_t2 = S.regions(2), S.regions(2), S.regions(2)
            n = 0
            for ti in range(5):
                for tg in range(4):
                    bk = n % 4
                    i2 = n % 2
                    n += 1
                    ts = slice(tg * 512, (tg + 1) * 512)
                    for k in range(8):
                        lhs = wq[:, k, ti * 128:(ti + 1) * 128] if ti < 4 else wkv[:, k, 0:128]
                        S.op("pe", (lambda lhs=lhs, k=k, ts=ts, bk=bk: lambda e: e.matmul(bk_[bk][:], lhsT=lhs, rhs=h[:, k, ts], start=(k == 0), stop=(k == 7)))(),
                             reads=[r_wq if ti < 4 else r_wkv, rh[k][tg]], writes=[rb[bk]])
                    S.op("act", (lambda i2=i2, bk=bk: lambda e: e.copy(out=xb[i2][:], in_=bk_[bk][:]))(), reads=[rb[bk]], writes=[r_xb[i2]])
                    S.op("dve", (lambda i2=i2, bk=bk, ts=ts: lambda e: e.tensor_tensor(out=t1[i2][:], in0=bk_[bk][:], in1=cosF[:, ts], op=ALU.mult))(),
                         reads=[rb[bk], r_cos], writes=[r_t1[i2]])
                    S.op("pe", (lambda i2=i2, bk=bk: lambda e: e.matmul(bk_[4 + bk][:], lhsT=b.cmb[:, CM["perm"], :], rhs=xb[i2][:], start=True, stop=True))(),
                         reads=[r_xb[i2], b.r_cm], writes=[rb[4 + bk]])
                    S.op("dve", (lambda i2=i2, bk=bk, ts=ts: lambda e: e.tensor_tensor(out=t2[i2][:], in0=bk_[4 + bk][:], in1=sinF[:, ts], op=ALU.mult))(),
                         reads=[rb[4 + bk], r_sin], writes=[r_t2[i2]])
                    if ti < 4:
                        dst, r_dst = qT[:, ti, ts], rq[ti][tg]
                    else:
                        dst, r_dst = kT[:, 128 + tg * 512:128 + (tg + 1) * 512], rk[1 + tg]
                    S.op("pool", (lambda i2=i2, dst=dst: lambda e: e.tensor_tensor(out=dst, in0=t1[i2][:], in1=t2[i2][:], op=ALU.add))(),
                         reads=[r_t1[i2], r_t2[i2]], writes=[r_dst])
            for g4 in range(4):
                for tt4 in range(4):
                    tt = g4 * 4 + tt4
                    for k in range(8):
                        S.op("pe", (lambda g4=g4, tt4=tt4, tt=tt, k=k: lambda e: e.matmul(bk_[g4][:, tt4 * 128:(tt4 + 1) * 128], lhsT=h[:, k, tt * 128:(tt + 1) * 128],
                                                                                 rhs=wkv[:, k, 128:256], start=(k == 0), stop=(k == 7)))(),
                             reads=[r_wkv, rh[k][g4]], writes=[rb[g4]])
                S.op("act", (lambda g4=g4: lambda e: e.copy(out=vt[:, 1 + g4 * 4:5 + g4 * 4, :], in_=bk_[g4][:].rearrange("p (a c) -> p a c", c=128)))(),
                     reads=[rb[g4]], writes=[rv[1 + g4]])
        with scope(b) as sb2:
            blk, r_blk = allgather(b, sb2, [(0, 128, kT[:, T:T + 128], [rk[4]]), (128, 128, vt[:, 16, :], [rv[4]])], 128, 256, F32, "att")
            select_prev(b, kT[:, 0:128], rk[0], blk, r_blk, 128, 0, 128)
            select_prev(b, vt[:, 0, :], rv[0], blk, r_blk, 128, 128, 128)
        with scope(b) as sb2:
            Pp = [sb2(f"Pp{i}", [128, 512], BF16) for i in range(2)]
            Pc = [sb2(f"Pc{i}", [128, 512], BF16) for i in range(2)]
            r_Pp, r_Pc = S.regions(2), S.regions(2)
            dn = [sb2(f"dn{i}", [128, 512], F32) for i in range(2)]
            r_dn = S.regions(2)
            n = 0
            for qb in range(16):
                qs = slice(qb * 128, (qb + 1) * 128)
                bN, bD = 4 + (qb % 2), 6 + (qb % 2)
                for g in range(2):
                    ps = slice(64 * g, 64 * g + 64)
                    i2 = n % 2
                    bA, bB = 2 * i2, 2 * i2 + 1
                    n += 1
                    mprev = 2 if qb == 0 else 1
                    for (bS, kc0, mi, P_, rP, rkk) in ((bA, qb * 128, mprev, Pp, r_Pp, rk[(qb * 128) // 512 + (0 if qb % 4 else 0)]),
                                                      (bB, (qb + 1) * 128, 0, Pc, r_Pc, None)):
                        kcs = slice(kc0, kc0 + 128)
                        rkr = rk[0] if kc0 < 128 else rk[1 + (kc0 - 128) // 512]
                        S.op("pe", (lambda bS=bS, ps=ps, kcs=kcs, qs=qs: lambda e: e.matmul(bk_[bS][:], lhsT=kT[ps, kcs], rhs=qT[ps, :, qs], start=True, stop=False))(),
                             reads=[rkr] + [rq[i][qb // 4] for i in range(4)], writes=[rb[bS]])
                        S.op("pe", (lambda bS=bS, mi=mi: lambda e: e.matmul(bk_[bS][:], lhsT=b.cmb[:, CM["ident"], :], rhs=nm[:, mi, :, :], start=False, stop=True))(),
                             reads=[r_nm, b.r_cm], writes=[rb[bS]])
                        S.op("act", (lambda bS=bS, P_=P_, i2=i2: lambda e: e.activation(out=P_[i2][:], in_=bk_[bS][:], func=AF.Exp, scale=0.125))(),
                             reads=[rb[bS]], writes=[rP[i2]])
                    for (bO, lo) in ((bN, None), (bD, "ones")):
                        for si, (P_, rP, vtile) in enumerate(((Pp, r_Pp, qb), (Pc, r_Pc, qb + 1))):
                            lhs = vt[:, vtile, 64 * g:64 * g + 64] if lo is None else b.cmb[:, CM["ones"], 0:64]
                            rvr = rv[0] if vtile == 0 else rv[1 + (vtile - 1) // 4]
                            S.op("pe", (lambda bO=bO, ps=ps, lhs=lhs, P_=P_, i2=i2, si=si: lambda e: e.matmul(bk_[bO][ps, :], lhsT=lhs, rhs=P_[i2][:], start=(si == 0), stop=(si == 1)))(),
                                 reads=[rvr, rP[i2], b.r_cm], writes=[rb[bO]])
                j2 = qb % 2
                S.op("dve", (lambda bD=bD, j2=j2: lambda e: e.tensor_tensor(out=dn[j2][:].rearrange("p (a c) -> p a c", c=128), in0=bk_[bD][:].rearrange("p (a c) -> p a c", c=128),
                                                                        in1=es[:, 0:4].unsqueeze(2).to_broadcast([128, 4, 128]), op=ALU.add))(),
                     reads=[rb[bD], r_es], writes=[r_dn[j2]])
                S.op("dve", (lambda j2=j2: lambda e: e.reciprocal(out=dn[j2][:], in_=dn[j2][:]))(), reads=[r_dn[j2]], writes=[r_dn[j2]])
                S.op("dve", (lambda bN=bN, j2=j2, qs=qs: lambda e: e.tensor_tensor(out=ym[:, 0:4, qs], in0=bk_[bN][:].rearrange("p (a c) -> p a c", c=128),
                                                                               in1=dn[j2][:].rearrange("p (a c) -> p a c", c=128), op=ALU.mult))(),
                     reads=[rb[bN], r_dn[j2]], writes=[rym[i][qb // 4] for i in range(4)])


def persist_io(b, what, name, ap, regs, shape, dt):
    S = b.S
    if what == "dump":
        xo = b.nc.dram_tensor(f"x_p_{name}", list(shape), dt, kind="ExternalOutput").ap()
        r_o = S.region()
        d1 = S.dsem(f"po_{name}")
        S.dma("sp", d1, lambda e: e.dma_start(out=xo, in_=ap), reads=list(regs), writes=[r_o])
        b.out_regs.append(r_o)
        b.out_names.append(f"x_p_{name}")
    else:
        xi = b.nc.dram_tensor(f"l_p_{name}", list(shape), dt, kind="ExternalInput").ap()
        b.in_names.append(f"l_p_{name}")
        d1 = S.dsem()
        S.dma("sp", d1, lambda e: e.dma_start(out=ap, in_=xi), writes=list(regs))


def dbg_stop(b, tag):
    if tag not in b.parts:
        return False
    S = b.S
    b.uid += 1
    xo = b.nc.dram_tensor(f"x_dbg{b.uid}", [128, 1], F32, kind="ExternalOutput").ap()
    r_o = S.region()
    d1 = S.dsem()
    S.dma("sp", d1, lambda e: e.dma_start(out=xo, in_=b.pv[:, 0:1]), reads=[b.r_pv], writes=[r_o])
    b.out_regs.append(r_o)
    b.out_names.append(f"x_dbg{b.uid}")
    return True


def mlstm_part(b, l, hp, h, rh, ym, rym):
    S = b.S
    w_in = b.D["w_in"]
    bk_ = b.banks
    rb = b.rb
    MB = 768
    NCH = 16
    with scope(b) as sb:
        numS = sb("numS", [128, T], F32)
        denS = sb("denS", [128, T], F32)
        r_num, r_den = S.regions(NCH), S.regions(NCH)
        qseg = sb("qseg", [128, T], BF16)
        r_qseg = S.regions(NCH)
        osig = sb("osig", [128, T], BF16)
        r_osig = S.regions(4)
        Cf = sb("Cf", [128, 128], F32)
        Cb = sb("Cb", [128, 128], BF16)
        r_Cf, r_Cb = S.regions(2)
        gtot = sb("gtot", [128, 1], F32)
        r_gtot = S.region()
        if b.mode != "finish":
            with scope(b) as sb1:
                qT = sb1("mqT", [128, T], BF16)
                kT = sb1("mkT", [128, T], BF16)
                r_qT, r_kT = S.regions(4), S.regions(4)
                vaug = sb1("vtokm", [128, NCH, 2, 64], BF16)
                r_vaug = S.regions(4)
                GR = sb1("GR", [128, T], F32)
                r_GR = S.regions(4)
                r_GRall = S.region()
                gb15 = sb1("gb15", [128, 1], F32)
                r_gb = S.region()
                go = LAY[("gbias", l)]
                S.op("dve", lambda e: e.tensor_scalar(out=gb15[:], in0=b.pv[:, go:go + 1], scalar1=1.0 / 15.0, scalar2=None, op0=ALU.mult), reads=[b.r_pv], writes=[r_gb])
                with scope(b) as sb2:
                    wm = sb2("wm", [128, 8, 4, 128], BF16)
                    wgt = sb2("wgt", [128, 8, 8], BF16)
                    r_wm, r_wgt = S.regions(2)
                    dm_, dg_ = S.dsem(), S.dsem()
                    for j in range(4):
                        S.dma("pool", dm_, (lambda j=j: lambda e: e.dma_start(out=wm[:, :, j, :], in_=wview(w_in, 0, DM, MB + j * 256 + hp * 128, 128)))(), writes=[r_wm])
                    S.dma("pool", dg_, lambda e: e.dma_start(out=wgt[:], in_=wview(w_in, 0, DM, MB + 1024, 8)), writes=[r_wgt])
                    raw = [sb2(f"raw{j}", [128, 515], F32) for j in range(2)]
                    r_raw = S.regions(2)
                    ctmp = sb2("ctmp", [128, 512], F32)
                    r_ctmp = S.region()
                    tail = sb2("mtail", [128, 6], F32)
                    r_tail = S.region()
                    n = 0
                    for j in range(2):
                        for k in range(8):
                            S.op("pe", (lambda j=j, k=k: lambda e: e.matmul(bk_[j][:, 0:128], lhsT=wm[:, k, j, :], rhs=h[:, k, T - 128:T], start=(k == 0), stop=(k == 7)))(),
                                 reads=[r_wm, rh[k][3]], writes=[rb[j]])
                        S.op("act", (lambda j=j: lambda e: e.copy(out=tail[:, 3 * j:3 * j + 3], in_=bk_[j][:, 125:128]))(), reads=[rb[j]], writes=[r_tail])
                    with scope(b) as sb3:
                        blk, r_blk = allgather(b, sb3, [(0, 6, tail[:], [r_tail])], 128, 6, F32, f"mls{hp}")
                        if blk is None:
                            return
                        for j in range(2):
                            select_prev(b, raw[j][:, 0:3], r_raw[j], blk, r_blk, 128, 3 * j, 3)
                    co = LAY[("conv", l)]
                    for tg in range(4):
                        ts = slice(tg * 512, (tg + 1) * 512)
                        for j in range(2):
                            bk = n % 4
                            n += 1
                            for k in range(8):
                                S.op("pe", (lambda j=j, k=k, ts=ts, bk=bk: lambda e: e.matmul(bk_[bk][:], lhsT=wm[:, k, j, :], rhs=h[:, k, ts], start=(k == 0), stop=(k == 7)))(),
                                     reads=[r_wm, rh[k][tg]], writes=[rb[bk]])
                            S.op("act", (lambda j=j, bk=bk: lambda e: e.copy(out=raw[j][:, 3:515], in_=bk_[bk][:]))(), reads=[rb[bk], r_raw[j]], writes=[r_raw[j]])
                            tile_ = j * 2 + hp
                            wc = [b.pv[:, co + tile_ * 4 + tap:co + tile_ * 4 + tap + 1] for tap in range(4)]
                            S.op("dve", (lambda j=j, wc=wc: lambda e: e.tensor_scalar(out=ctmp[:], in0=raw[j][:, 3:515], scalar1=wc[3], scalar2=None, op0=ALU.mult))(),
                                 reads=[r_raw[j], b.r_pv], writes=[r_ctmp])
                            for tap in range(3):
                                S.op("dve", (lambda j=j, wc=wc, tap=tap: lambda e: e.scalar_tensor_tensor(out=ctmp[:], in0=raw[j][:, tap:tap + 512], scalar=wc[tap], in1=ctmp[:], op0=ALU.mult, op1=ALU.add))(),
                                     reads=[r_raw[j], r_ctmp, b.r_pv], writes=[r_ctmp])
                            S.op("dve", (lambda j=j: lambda e: e.tensor_copy(out=raw[j][:, 0:3], in_=raw[j][:, 512:515]))(), reads=[r_raw[j]], writes=[r_raw[j]])
                            if j == 0:
                                S.op("act", lambda e: e.activation(out=ctmp[:], in_=ctmp[:], func=AF.Silu), reads=[r_ctmp], writes=[r_ctmp])
                                S.op("dve", (lambda ts=ts: lambda e: e.tensor_scalar(out=qT[:, ts], in0=ctmp[:], scalar1=0.125, scalar2=None, op0=ALU.mult))(), reads=[r_ctmp], writes=[r_qT[tg]])
                            else:
                                S.op("act", (lambda ts=ts: lambda e: e.activation(out=kT[:, ts], in_=ctmp[:], func=AF.Silu))(), reads=[r_ctmp], writes=[r_kT[tg]])
                    if dbg_stop(b, "1"):
                        return
                    for tg in range(4):
                        bk = n % 8
                        n += 1
                        ts = slice(tg * 512, (tg + 1) * 512)
                        for k in range(8):
                            S.op("pe", (lambda k=k, ts=ts, bk=bk: lambda e: e.matmul(bk_[bk][:], lhsT=wm[:, k, 3, :], rhs=h[:, k, ts], start=(k == 0), stop=(k == 7)))(),
                                 reads=[r_wm, rh[k][tg]], writes=[rb[bk]])
                        S.op("act", (lambda ts=ts, bk=bk: lambda e: e.activation(out=osig[:, ts], in_=bk_[bk][:], func=AF.Sigmoid))(), reads=[rb[bk]], writes=[r_osig[tg]])
                    if dbg_stop(b, "2"):
                        return
                    for tg in range(4):
                        bk = n % 8
                        n += 1
                        ts = slice(tg * 512, (tg + 1) * 512)
                        for (p0, c0) in ((0, 0), (32, 4)):
                            for k in range(8):
                                S.op("pe", (lambda k=k, ts=ts, bk=bk, p0=p0, c0=c0: lambda e: e.matmul(bk_[bk][p0:p0 + 4, :], lhsT=wgt[:, k, c0:c0 + 4], rhs=h[:, k, ts], start=(k == 0), stop=(k == 7)))(),
                                     reads=[r_wgt, rh[k][tg]], writes=[rb[bk]])
                        for p0 in (0, 32):
                            S.op("act", (lambda ts=ts, bk=bk, p0=p0: lambda e: e.activation(out=GR[p0:p0 + 4, ts], in_=bk_[bk][p0:p0 + 4, :], func=AF.Tanh, bias=gb15[p0:p0 + 4, 0:1], scale=1.0 / 15.0))(),
                                 reads=[rb[bk], r_gb], writes=[r_GR[tg]])
                    if dbg_stop(b, "3"):
                        return
                    for g4 in range(4):
                        bk = n % 8
                        n += 1
                        for tt4 in range(4):
                            tt = g4 * 4 + tt4
                            for k in range(8):
                                S.op("pe", (lambda tt4=tt4, tt=tt, k=k, bk=bk: lambda e: e.matmul(bk_[bk][:, tt4 * 128:(tt4 + 1) * 128], lhsT=h[:, k, tt * 128:(tt + 1) * 128], rhs=wm[:, k, 2, :],
                                                                                          start=(k == 0), stop=(k == 7)))(), reads=[r_wm, rh[k][g4]], writes=[rb[bk]])
                        S.op("act", (lambda g4=g4, bk=bk: lambda e: e.copy(out=vaug[:, g4 * 4:g4 * 4 + 4, :, :], in_=bk_[bk][:].rearrange("p (a c d) -> p a c d", a=4, c=2)))(),
                             reads=[rb[bk]], writes=[r_vaug[g4]])
                    if dbg_stop(b, "4"):
                        return
                    allGR = r_GR
                    S.op("dve", lambda e: e.tensor_scalar(out=GR[0:4, :], in0=GR[0:4, :], scalar1=15.0, scalar2=None, op0=ALU.mult), reads=allGR, writes=[r_GRall])
                    S.op("act", lambda e: e.activation(out=GR[32:36, :], in_=GR[32:36, :], func=AF.Exp, scale=-15.0), reads=allGR, writes=[r_GRall])
                    S.op("act", lambda e: e.activation(out=GR[32:36, :], in_=GR[32:36, :], func=AF.Ln, bias=pvc(b, "one")[32:36, :], scale=1.0), reads=[r_GRall, b.r_pv], writes=[r_GRall])
                    S.op("dve", lambda e: e.tensor_scalar(out=GR[32:36, :], in0=GR[32:36, :], scalar1=-0.5, scalar2=None, op0=ALU.mult), reads=[r_GRall], writes=[r_GRall])
                    S.op("dve", lambda e: e.tensor_tensor_scan(out=GR[32:36, :], data0=GR[32:36, :], data1=GR[32:36, :], initial=0.0, op0=ALU.add, op1=ALU.add), reads=[r_GRall], writes=[r_GRall])
                if dbg_stop(b, "5"):
                    return
                with scope(b) as sb2:
                    S.op("pool", lambda e: e.memset(Cf[:], 0.0), writes=[r_Cf])
                    S.op("pool", lambda e: e.memset(Cb[:], 0.0), writes=[r_Cb])
                    negG = [sb2(f"negG{i}", [128, 1], F32) for i in range(2)]
                    r_negG = S.regions(2)
                    S.op("pool", lambda e: e.memset(negG[1][:], 0.0), writes=[r_negG[1]])
                    itok = [sb2(f"itok{i}", [128, 8], F32) for i in range(2)]
                    atok = [sb2(f"atok{i}", [128, 4], F32) for i in range(2)]
                    r_itok, r_atok = S.regions(2), S.regions(2)
                    eT = [sb2(f"eT{i}", [128, 2, 128], F32) for i in range(2)]
                    r_eT = S.regions(2)
                    PT = [sb2(f"PT{i}", [128, 2, 128], BF16) for i in range(2)]
                    r_PT = S.regions(2)
                    E1 = [sb2(f"E1{i}", [128, 128], F32) for i in range(2)]
                    E2 = [sb2(f"E2{i}", [128, 128], F32) for i in range(2)]
                    r_E1, r_E2 = S.regions(2), S.regions(2)
                    qh = [sb2(f"qh{i}", [128, 128], BF16) for i in range(2)]
                    r_qh = S.regions(2)
                    kh = [sb2(f"kh{i}", [128, 128], BF16) for i in range(2)]
                    r_kh = S.regions(2)
                    for j in range(NCH):
                        i2 = j % 2
                        cs = slice(j * 128, (j + 1) * 128)
                        g4 = j // 4
                        S.op("pe", (lambda cs=cs: lambda e: e.matmul(bk_[0][:, 0:4], lhsT=GR[0:4, cs], rhs=b.cm[0:4, CM["ident"], 0:4], start=True, stop=True))(), reads=[r_GRall, b.r_cm], writes=[rb[0]])
                        S.op("pe", (lambda cs=cs: lambda e: e.matmul(bk_[0][:, 4:8], lhsT=GR[32:36, cs], rhs=b.cm[32:36, CM["ident"], 32:36], start=True, stop=True))(), reads=[r_GRall, b.r_cm], writes=[rb[0]])
                        S.op("dve", (lambda i2=i2: lambda e: e.tensor_copy(out=itok[i2][:], in_=bk_[0][:, 0:8]))(), reads=[rb[0]], writes=[r_itok[i2]])
                        S.op("dve", (lambda i2=i2: lambda e: e.tensor_tensor(out=atok[i2][:], in0=itok[i2][:, 0:4], in1=itok[i2][:, 4:8], op=ALU.subtract))(), reads=[r_itok[i2]], writes=[r_atok[i2]])
                        if j == 1 and dbg_stop(b, "W"):
                            return
                        for hh in range(2):
                            hd = 2 * hp + hh
                            ps = slice(64 * hh, 64 * hh + 64)
                            S.op("pe", (lambda hh=hh, hd=hd, cs=cs: lambda e: e.matmul(bk_[1][:, hh * 128:(hh + 1) * 128], lhsT=b.cm[32:36, CM["rowsel"] + hd, :], rhs=GR[32:36, cs], start=True, stop=False))(),
                                 reads=[r_GRall, b.r_cm], writes=[rb[1]])
                            S.op("pe", (lambda hh=hh: lambda e: e.matmul(bk_[1][:, hh * 128:(hh + 1) * 128], lhsT=b.cm[:, CM["ident"], :], rhs=b.cm[:, CM["ncur"], :], start=False, stop=True))(),
                                 reads=[b.r_cm], writes=[rb[1]])
                            S.op("act", (lambda hh=hh, hd=hd, i2=i2: lambda e: e.activation(out=eT[i2][:, hh, :], in_=bk_[1][:, hh * 128:(hh + 1) * 128], func=AF.Exp, bias=atok[i2][:, hd:hd + 1], scale=1.0))(),
                                 reads=[rb[1], r_atok[i2]], writes=[r_eT[i2]])
                            S.op("pe", (lambda hh=hh, ps=ps, cs=cs: lambda e: e.matmul(bk_[2][:, hh * 128:(hh + 1) * 128], lhsT=kT[ps, cs], rhs=qT[ps, cs], start=True, stop=True))(),
                                 reads=[r_kT[g4], r_qT[g4]], writes=[rb[2]])
                            S.op("pe", (lambda hh=hh, hd=hd, ps=ps, cs=cs: lambda e: e.matmul(bk_[3][ps, 0:128], lhsT=b.cm[32:36, CM["rowsel"] + hd, 0:64], rhs=GR[32:36, cs], start=True, stop=True))(),
                                 reads=[r_GRall, b.r_cm], writes=[rb[3]])
                        if j == 1 and dbg_stop(b, "Y"):
                            return
                        S.op("dve", (lambda i2=i2: lambda e: e.tensor_tensor(out=PT[i2][:], in0=bk_[2][:, 0:256].rearrange("p (a c) -> p a c", c=128), in1=eT[i2][:], op=ALU.mult))(),
                             reads=[rb[2], r_eT[i2]], writes=[r_PT[i2]])
                        S.op("act", (lambda i2=i2: lambda e: e.activation(out=E1[i2][:], in_=bk_[3][:, 0:128], func=AF.Exp, bias=negG[1 - i2][:, 0:1], scale=1.0))(),
                             reads=[rb[3], r_negG[1 - i2]], writes=[r_E1[i2]])
                        S.op("act", (lambda i2=i2: lambda e: e.activation(out=E2[i2][:], in_=bk_[3][:, 0:128], func=AF.Exp))(), reads=[rb[3]], writes=[r_E2[i2]])
                        S.op("dve", (lambda i2=i2: lambda e: e.tensor_scalar(out=negG[i2][:], in0=bk_[3][:, 127:128], scalar1=-1.0, scalar2=None, op0=ALU.mult))(), reads=[rb[3]], writes=[r_negG[i2]])
                        if j == NCH - 1:
                            S.op("dve", lambda e: e.tensor_copy(out=gtot[:], in_=bk_[3][:, 127:128]), reads=[rb[3]], writes=[r_gtot])
                        S.op("dve", (lambda i2=i2, cs=cs: lambda e: e.tensor_tensor(out=qh[i2][:], in0=qT[:, cs], in1=E1[i2][:], op=ALU.mult))(), reads=[r_qT[g4], r_E1[i2]], writes=[r_qh[i2]])
                        S.op("pool", (lambda i2=i2, cs=cs: lambda e: e.tensor_tensor(out=qseg[:, cs], in0=qT[:, cs], in1=E2[i2][:], op=ALU.mult))(), reads=[r_qT[g4], r_E2[i2]], writes=[r_qseg[j]])
                        if j == 1 and dbg_stop(b, "Z"):
                            return
                        S.op("pe", (lambda cs=cs: lambda e: e.matmul(bk_[6][:, 0:128], lhsT=kT[:, cs], rhs=b.cmb[:, CM["ident"], :], start=True, stop=True))(), reads=[r_kT[g4], b.r_cm], writes=[rb[6]])
                        for hh in range(2):
                            S.op("dve", (lambda hh=hh, i2=i2: lambda e: e.tensor_scalar(out=kh[i2][:, hh * 64:(hh + 1) * 64], in0=bk_[6][:, hh * 64:(hh + 1) * 64],
                                                                                    scalar1=eT[i2][:, hh, 127:128], scalar2=None, op0=ALU.mult))(), reads=[rb[6], r_eT[i2]], writes=[r_kh[i2]])
                        for hh in range(2):
                            ps = slice(64 * hh, 64 * hh + 64)
                            S.op("pe", (lambda hh=hh, ps=ps, j=j, i2=i2: lambda e: e.matmul(bk_[4][ps, 0:128], lhsT=vaug[:, j, hh, :], rhs=PT[i2][:, hh, :], start=True, stop=False))(),
                                 reads=[r_vaug[g4], r_PT[i2]], writes=[rb[4]])
                            S.op("pe", (lambda hh=hh, ps=ps, i2=i2: lambda e: e.matmul(bk_[4][ps, 0:128], lhsT=Cb[ps, 0:64], rhs=qh[i2][ps, :], start=False, stop=True))(),
                                 reads=[r_Cb, r_qh[i2]], writes=[rb[4]])
                            S.op("pe", (lambda hh=hh, ps=ps, j=j, i2=i2: lambda e: e.matmul(bk_[5][ps, 0:128], lhsT=b.cmb[:, CM["ones"], 0:64], rhs=PT[i2][:, hh, :], start=True, stop=False))(),
                                 reads=[b.r_cm, r_PT[i2]], writes=[rb[5]])
                            S.op("pe", (lambda hh=hh, ps=ps, i2=i2: lambda e: e.matmul(bk_[5][ps, 0:128], lhsT=Cb[ps, 64:128], rhs=qh[i2][ps, :], start=False, stop=True))(),
                                 reads=[r_Cb, r_qh[i2]], writes=[rb[5]])
                            S.op("pe", (lambda hh=hh, ps=ps, j=j, i2=i2: lambda e: e.matmul(bk_[7][ps, 0:64], lhsT=kh[i2][:, hh * 64:(hh + 1) * 64], rhs=vaug[:, j, hh, :], start=True, stop=True))(),
                                 reads=[r_kh[i2], r_vaug[g4]], writes=[rb[7]])
                            S.op("pe", (lambda hh=hh, ps=ps, j=j, i2=i2: lambda e: e.matmul(bk_[7][ps, 64:128], lhsT=kh[i2][:, hh * 64:(hh + 1) * 64], rhs=b.cmb[:, CM["ones"], 0:64], start=True, stop=True))(),
                                 reads=[r_kh[i2], b.r_cm], writes=[rb[7]])
                        if j == 1 and dbg_stop(b, "Q"):
                            return
                        S.op("act", (lambda cs=cs: lambda e: e.copy(out=numS[:, cs], in_=bk_[4][:, 0:128]))(), reads=[rb[4]], writes=[r_num[j]])
                        S.op("act", (lambda cs=cs: lambda e: e.copy(out=denS[:, cs], in_=bk_[5][:, 0:128]))(), reads=[rb[5]], writes=[r_den[j]])
                        S.op("dve", (lambda i2=i2: lambda e: e.scalar_tensor_tensor(out=Cf[:], in0=Cf[:], scalar=E1[i2][:, 127:128], in1=bk_[7][:, 0:128], op0=ALU.mult, op1=ALU.add))(),
                             reads=[r_Cf, r_E1[i2], rb[7]], writes=[r_Cf])
                        S.op("act", lambda e: e.copy(out=Cb[:], in_=Cf[:]), reads=[r_Cf], writes=[r_Cb])
                        if j == 0 and dbg_stop(b, "6"):
                            return
                        if j == 1 and dbg_stop(b, "7"):
                            return
                        if j == NCH - 1 and dbg_stop(b, "8"):
                            return
        else:
            persist_io(b, "load", f"mnum{hp}", numS[:], r_num, [128, T], F32)
            persist_io(b, "load", f"mden{hp}", denS[:], r_den, [128, T], F32)
            persist_io(b, "load", f"mqsg{hp}", qseg[:], r_qseg, [128, T], BF16)
            persist_io(b, "load", f"mosg{hp}", osig[:], r_osig, [128, T], BF16)
        if b.mode == "local":
            persist_io(b, "dump", f"mnum{hp}", numS[:], r_num, [128, T], F32)
            persist_io(b, "dump", f"mden{hp}", denS[:], r_den, [128, T], F32)
            persist_io(b, "dump", f"mqsg{hp}", qseg[:], r_qseg, [128, T], BF16)
            persist_io(b, "dump", f"mosg{hp}", osig[:], r_osig, [128, T], BF16)
        with scope(b) as sb1:
            blk, r_blk = allgather(b, sb1, [(0, 128, Cf[:], [r_Cf]), (128, 1, gtot[:], [r_gtot])], 128, 129, F32, f"mst{hp}")
            if blk is None:
                return
            dec = sb1("dec", [128, 8], F32)
            r_dec = S.region()
            S.op("act", lambda e: e.activation(out=dec[:], in_=blk[:, :, 128], func=AF.Exp), reads=[r_blk], writes=[r_dec])
            Cs = sb1("Cs", [128, 128], F32)
            Csb = sb1("Csb", [128, 128], BF16)
            tt_ = sb1("tt_", [128, 128], F32)
            r_Cs, r_Csb, r_tt = S.regions(3)
            S.op("pool", lambda e: e.memset(Cs[:], 0.0), writes=[r_Cs])
            for jc in range(7):
                S.op("dve", (lambda jc=jc: lambda e: e.scalar_tensor_tensor(out=tt_[:], in0=Cs[:], scalar=dec[:, jc:jc + 1], in1=blk[:, jc, 0:128], op0=ALU.mult, op1=ALU.add))(),
                     reads=[r_Cs, r_dec, r_blk], writes=[r_tt])
                S.op("dve", lambda e: e.tensor_tensor(out=tt_[:], in0=tt_[:], in1=Cs[:], op=ALU.subtract), reads=[r_tt, r_Cs], writes=[r_tt])
                S.op("dve", (lambda jc=jc: lambda e: e.scalar_tensor_tensor(out=Cs[:], in0=tt_[:], scalar=b.pc[:, 8 + jc:9 + jc], in1=Cs[:], op0=ALU.mult, op1=ALU.add))(),
                     reads=[r_tt, r_Cs, b.r_pc], writes=[r_Cs])
            S.op("act", lambda e: e.copy(out=Csb[:], in_=Cs[:]), reads=[r_Cs], writes=[r_Csb])
            hT = [sb1(f"hT{i}", [128, 512], F32) for i in range(2)]
            dd = [sb1(f"dd{i}", [128, 512], F32) for i in range(2)]
            sq = [sb1(f"msq{i}", [128, 512], BF16) for i in range(2)]
            tmp = sb1("mtmp", [128, 512], F32)
            rs = [sb1(f"mrs{i}", [128, 512], F32) for i in range(2)]
            r_hT, r_dd, r_sq, r_rs = S.regions(2), S.regions(2), S.regions(2), S.regions(2)
            r_tmp = S.region()
            go2 = LAY[("mlstm_norm", l)]
            for tg in range(4):
                i2 = tg % 2
                ts = slice(tg * 512, (tg + 1) * 512)
                chs = [r for r in range(tg * 4, tg * 4 + 4)]
                bN, bD, bQ = 0 + i2, 2 + i2, 4 + i2
                for hh in range(2):
                    ps = slice(64 * hh, 64 * hh + 64)
                    S.op("pe", (lambda ps=ps, ts=ts, bN=bN: lambda e: e.matmul(bk_[bN][ps, :], lhsT=Csb[ps, 0:64], rhs=qseg[ps, ts], start=True, stop=True))(),
                         reads=[r_Csb] + [r_qseg[c] for c in chs], writes=[rb[bN]])
                    S.op("pe", (lambda ps=ps, ts=ts, bD=bD: lambda e: e.matmul(bk_[bD][ps, :], lhsT=Csb[ps, 64:128], rhs=qseg[ps, ts], start=True, stop=True))(),
                         reads=[r_Csb] + [r_qseg[c] for c in chs], writes=[rb[bD]])
                S.op("dve", (lambda i2=i2, ts=ts, bN=bN: lambda e: e.tensor_tensor(out=hT[i2][:], in0=bk_[bN][:], in1=numS[:, ts], op=ALU.add))(), reads=[rb[bN]] + [r_num[c] for c in chs], writes=[r_hT[i2]])
                S.op("dve", (lambda i2=i2, ts=ts, bD=bD: lambda e: e.tensor_tensor(out=dd[i2][:], in0=bk_[bD][:], in1=denS[:, ts], op=ALU.add))(), reads=[rb[bD]] + [r_den[c] for c in chs], writes=[r_dd[i2]])
                S.op("dve", (lambda i2=i2: lambda e: e.scalar_tensor_tensor(out=dd[i2][:], in0=dd[i2][:], scalar=-1.0, in1=dd[i2][:], op0=ALU.mult, op1=ALU.max))(), reads=[r_dd[i2]], writes=[r_dd[i2]])
                S.op("dve", (lambda i2=i2: lambda e: e.tensor_scalar(out=dd[i2][:], in0=dd[i2][:], scalar1=1.0, scalar2=None, op0=ALU.max))(), reads=[r_dd[i2]], writes=[r_dd[i2]])
                S.op("dve", (lambda i2=i2: lambda e: e.reciprocal(out=dd[i2][:], in_=dd[i2][:]))(), reads=[r_dd[i2]], writes=[r_dd[i2]])
                S.op("dve", (lambda i2=i2: lambda e: e.tensor_tensor(out=hT[i2][:], in0=hT[i2][:], in1=dd[i2][:], op=ALU.mult))(), reads=[r_hT[i2], r_dd[i2]], writes=[r_hT[i2]])
                S.op("act", (lambda i2=i2: lambda e: e.activation(out=sq[i2][:], in_=hT[i2][:], func=AF.Square))(), reads=[r_hT[i2]], writes=[r_sq[i2]])
                S.op("pe", (lambda i2=i2, bQ=bQ: lambda e: e.matmul(bk_[bQ][:], lhsT=b.cmb[:, CM["blk"], :], rhs=sq[i2][:], start=True, stop=True))(), reads=[r_sq[i2], b.r_cm], writes=[rb[bQ]])
                rstd_from_sumsq(b, rs[i2][:], r_rs[i2], tmp[:], r_tmp, bk_[bQ][:], rb[bQ], 1.0 / 64.0, "eps1")
                S.op("pool", (lambda i2=i2: lambda e: e.tensor_tensor(out=hT[i2][:], in0=hT[i2][:], in1=rs[i2][:], op=ALU.mult))(), reads=[r_hT[i2], r_rs[i2]], writes=[r_hT[i2]])
                S.op("dve", (lambda i2=i2, ts=ts: lambda e: e.scalar_tensor_tensor(out=ym[:, 4 + hp, ts], in0=hT[i2][:], scalar=b.pv[:, go2 + hp:go2 + hp + 1], in1=osig[:, ts], op0=ALU.mult, op1=ALU.mult))(),
                     reads=[r_hT[i2], r_osig[tg], b.r_pv], writes=[rym[4 + hp][tg]])


def rwkv_part(b, l, hp, h, rh, ym, rym):
    S = b.S
    w_in = b.D["w_in"]
    bk_ = b.banks
    rb = b.rb
    RB = 1800
    EM = float(np.exp(-0.5))
    cmf, cmb = b.cm, b.cmb

    def col(key, i=0):
        o = LAY[(key, l)] + i
        return b.pv[:, o:o + 1]

    last_row = {}

    def mm(out, lhsT, rhs, reads, writes, start=True, stop=True):
        tag = (lhsT.base_partition(), lhsT.partition_size())
        extra = []
        for w in writes:
            prev = last_row.get(id(w))
            if prev is not None and prev[0] != tag:
                extra.append(prev[1])
        tok = S.op("pe", lambda e: e.matmul(out, lhsT=lhsT, rhs=rhs, start=start, stop=stop), reads=reads, writes=writes, pe_wait=extra)
        for w in writes:
            last_row[id(w)] = (tag, tok)

    with scope(b) as sb:
        Yloc = sb("Yloc", [128, T], BF16)
        M2p = sb("M2p", [128, T], BF16)
        gT = sb("gT", [128, T], BF16)
        bv = sb("bv", [128, T], BF16)
        r_Yloc, r_M2p, r_gT, r_bv = S.regions(4), S.regions(4), S.regions(4), S.regions(4)
        Sf = sb("Sf", [128, 128], F32)
        Sb_ = sb("Sb", [128, 128], BF16)
        r_Sf, r_Sb = S.regions(2)
        if b.mode != "finish":
            with scope(b) as sb1:
                wr = sb1("wr", [128, 8, 5, 128], BF16)
                r_wr = S.region()
                dwr = S.dsem()
                for X, c0 in enumerate([RB + hp * 128, RB + 256 + hp * 128, RB + 512 + hp * 128, RB + 768, RB + 896]):
                    S.dma("pool", dwr, (lambda X=X, c0=c0: lambda e: e.dma_start(out=wr[:, :, X, :], in_=wview(w_in, 0, DM, c0, 128)))(), writes=[r_wr])
                wup = sb1("wup", [128, 128], BF16)
                aup = sb1("aup", [128, 128], BF16)
                gup = sb1("gup", [128, 128], BF16)
                r_lr = S.region()
                dlr = S.dsem()
                S.dma("pool", dlr, lambda e: e.dma_start(out=wup[0:64, :], in_=b.D["rwkv_w_up"][:, hp * 128:(hp + 1) * 128]), writes=[r_lr])
                S.dma("pool", dlr, lambda e: e.dma_start(out=aup[64:128, :], in_=b.D["rwkv_a_up"][:, hp * 128:(hp + 1) * 128]), writes=[r_lr])
                S.dma("pool", dlr, lambda e: e.dma_start(out=gup[:], in_=b.D["rwkv_g_up"][:, hp * 128:(hp + 1) * 128]), writes=[r_lr])
                omk = sb1("omk", [128, 1], F32)
                r_omk = S.region()
                S.op("dve", lambda e: e.tensor_scalar(out=omk[:], in0=col("rwkv_k_a", hp), scalar1=-1.0, scalar2=1.0, op0=ALU.mult, op1=ALU.add), reads=[b.r_pv], writes=[r_omk])
                carry = sb1("carry", [128, 5], F32)
                r_carry = S.regions(5)
                tail = sb1("tail", [128, 5], F32)
                r_tail = S.region()
                for X in range(5):
                    for k in range(8):
                        mm(bk_[X % 2][:, 0:128], wr[:, k, X, :], h[:, k, T - 128:T], [r_wr, rh[k][3]], [rb[X % 2]], start=(k == 0), stop=(k == 7))
                    S.op("act", (lambda X=X: lambda e: e.copy(out=tail[:, X:X + 1], in_=bk_[X % 2][:, 127:128]))(), reads=[rb[X % 2]], writes=[r_tail])
                with scope(b) as sb2:
                    blk, r_blk = allgather(b, sb2, [(0, 5, tail[:], [r_tail])], 128, 5, F32, f"rsh{hp}")
                    if blk is None:
                        return
                    r_call = S.region()
                    select_prev(b, carry[:], r_call, blk, r_blk, 128, 0, 5)
                for X in range(5):
                    r_carry[X] = r_call
                r_carry = [S.region() for _ in range(5)]
                for X in range(5):
                    r_carry[X].w = r_call.w
                S.op("pool", lambda e: e.memset(Sf[:], 0.0), writes=[r_Sf])
                S.op("dve", lambda e: e.tensor_copy(out=Sf[:, 64:128], in_=cmf[:, CM["id2"], 0:64]), reads=[b.r_cm, r_Sf], writes=[r_Sf])
                S.op("act", lambda e: e.copy(out=Sb_[:], in_=Sf[:]), reads=[r_Sf], writes=[r_Sb])
                raw = sb1("rraw", [128, 513], F32)
                r_raw = S.region()
                us = [sb1(f"us{i}", [128, 512], F32) for i in range(3)]
                r_us = S.regions(3)
                R = [sb1(f"R{i}", [128, 512], F32) for i in range(7)]
                rR = S.regions(7)
                twx = sb1("twx", [128, 512], BF16)
                sgb = sb1("sgb", [128, 512], BF16)
                sqk = sgb
                rkr = sgb
                r_twx, r_sgb = S.regions(2)
                r_sqk = r_sgb
                r_rkr = r_sgb
                base8 = sb1("base8", [128, 8], F32)
                cumC8 = sb1("cumC8", [128, 8], F32)
                ecum8 = sb1("ecum8", [128, 8], F32)
                r_base8, r_cumC8, r_ecum8 = S.regions(3)
                fm = [sb1(f"fm{i}", [128, 512], BF16) for i in range(6)]
                r_fm = S.regions(6)
                tok = [sb1(f"tok{i}", [128, 4, 128], BF16) for i in range(4)]
                r_tok = S.regions(4, 4)
                vb = sb1("vb", [128, 512], BF16)
                r_vb = S.region()
                Am = sb1("Am", [128, 4, 2, 64], BF16)
                AmT = sb1("AmT", [128, 2, 64], BF16)
                r_Am, r_AmT = S.regions(2)
                Pb = [sb1(f"Pb{i}", [128, 2, 2, 64], BF16) for i in range(2)]
                r_Pb = S.regions(2)
                Zf = sb1("Zf", [128, 2, 128], F32)
                Zb = sb1("Zb", [128, 2, 128], BF16)
                r_Zf, r_Zb = S.regions(2)
                M2b = sb1("M2b", [128, 128], BF16)
                M3Tb = sb1("M3Tb", [128, 2, 64], BF16)
                Laug = sb1("Laug", [128, 2, 128], F32)
                r_M2b, r_M3Tb, r_Laug = S.regions(3)
                S.op("pool", lambda e: e.memset(Laug[:], 0.0), writes=[r_Laug])
                S.op("pool", lambda e: e.memset(base8[:], 0.0), writes=[r_base8])
                MU = LAY[("rwkv_mu", l)]
                mucol = [MU + hp, MU + 2 + hp, MU + 4 + hp, MU + 6, MU + 7]
                for tg in range(4):
                    ts = slice(tg * 512, (tg + 1) * 512)
                    if tg == 0 and dbg_stop(b, "1"):
                        return
                    for X in range(5):
                        bkx = X % 2
                        for k in range(8):
                            mm(bk_[bkx][:], wr[:, k, X, :], h[:, k, ts], [r_wr, rh[k][tg]], [rb[bkx]], start=(k == 0), stop=(k == 7))
                        S.op("act", (lambda bkx=bkx: lambda e: e.copy(out=raw[:, 1:513], in_=bk_[bkx][:]))(), reads=[rb[bkx]], writes=[r_raw])
                        S.op("dve", (lambda X=X: lambda e: e.tensor_copy(out=raw[:, 0:1], in_=carry[:, X:X + 1]))(), reads=[r_carry[X], r_raw], writes=[r_raw])
                        S.op("dve", (lambda X=X: lambda e: e.tensor_copy(out=carry[:, X:X + 1], in_=raw[:, 512:513]))(), reads=[r_raw], writes=[r_carry[X]])
                        S.op("dve", lambda e: e.tensor_tensor(out=R[0][:], in0=raw[:, 0:512], in1=raw[:, 1:513], op=ALU.subtract), reads=[r_raw], writes=[rR[0]])
                        dstu = us[X] if X < 3 else R[1]
                        r_dstu = r_us[X] if X < 3 else rR[1]
                        S.op("dve", (lambda X=X, dstu=dstu: lambda e: e.scalar_tensor_tensor(out=dstu[:], in0=R[0][:], scalar=b.pv[:, mucol[X]:mucol[X] + 1], in1=raw[:, 1:513], op0=ALU.mult, op1=ALU.add))(),
                             reads=[rR[0], r_raw, b.r_pv], writes=[r_dstu])
                        if X == 3:
                            S.op("act", lambda e: e.activation(out=twx[0:64, :], in_=R[1][0:64, :], func=AF.Tanh), reads=[rR[1]], writes=[r_twx])
                            S.op("act", lambda e: e.copy(out=twx[64:128, :], in_=R[1][64:128, :]), reads=[rR[1]], writes=[r_twx])
                        if X == 4:
                            S.op("act", lambda e: e.activation(out=sgb[:], in_=R[1][:], func=AF.Sigmoid), reads=[rR[1]], writes=[r_sgb])
                    if tg == 0 and dbg_stop(b, "2"):
                        return
                    mm(bk_[2][:], wup[0:64, :], twx[0:64, :], [r_lr, r_twx], [rb[2]])
                    S.op("act", lambda e: e.activation(out=R[0][:], in_=bk_[2][:], func=AF.Sigmoid, bias=col("rwkv_w0", hp), scale=1.0), reads=[rb[2], b.r_pv], writes=[rR[0]])
                    S.op("dve", lambda e: e.tensor_scalar(out=R[0][:], in0=R[0][:], scalar1=-EM, scalar2=None, op0=ALU.mult), reads=[rR[0]], writes=[rR[0]])
                    mm(bk_[3][:], aup[64:128, :], twx[64:128, :], [r_lr, r_twx], [rb[3]])
                    S.op("act", lambda e: e.activation(out=R[1][:], in_=bk_[3][:], func=AF.Sigmoid, bias=col("rwkv_a0", hp), scale=1.0), reads=[rb[3], b.r_pv], writes=[rR[1]])
                    mm(bk_[4][:], gup[:], sgb[:], [r_lr, r_sgb], [rb[4]])
                    S.op("act", (lambda ts=ts: lambda e: e.copy(out=gT[:, ts], in_=bk_[4][:]))(), reads=[rb[4]], writes=[r_gT[tg]])
                    S.op("dve", lambda e: e.tensor_scalar(out=R[2][:], in0=us[1][:], scalar1=col("rwkv_k_k", hp), scalar2=None, op0=ALU.mult), reads=[r_us[1], b.r_pv], writes=[rR[2]])
                    S.op("act", lambda e: e.activation(out=sqk[:], in_=R[2][:], func=AF.Square), reads=[rR[2]], writes=[r_sqk])
                    mm(bk_[5][:], cmb[:, CM["blk"], :], sqk[:], [b.r_cm, r_sqk], [rb[5]])
                    S.op("act", lambda e: e.activation(out=R[3][:], in_=bk_[5][:], func=AF.Sqrt), reads=[rb[5]], writes=[rR[3]])
                    S.op("dve", lambda e: e.tensor_scalar(out=R[3][:], in0=R[3][:], scalar1=1e-12, scalar2=None, op0=ALU.max), reads=[rR[3]], writes=[rR[3]])
                    S.op("dve", lambda e: e.reciprocal(out=R[3][:], in_=R[3][:]), reads=[rR[3]], writes=[rR[3]])
                    S.op("dve", lambda e: e.tensor_tensor(out=R[2][:], in0=R[2][:], in1=R[3][:], op=ALU.mult), reads=[rR[2], rR[3]], writes=[rR[2]])
                    S.op("dve", lambda e: e.tensor_scalar(out=R[3][:], in0=R[1][:], scalar1=col("rwkv_k_a", hp), scalar2=omk[:, 0:1], op0=ALU.mult, op1=ALU.add), reads=[rR[1], r_omk, b.r_pv, rR[3]], writes=[rR[3]])
                    S.op("dve", lambda e: e.tensor_tensor(out=R[3][:], in0=R[3][:], in1=us[1][:], op=ALU.mult), reads=[rR[3], r_us[1]], writes=[rR[3]])
                    S.op("dve", lambda e: e.scalar_tensor_tensor(out=rkr[:], in0=us[0][:], scalar=col("rwkv_r_k", hp), in1=R[3][:], op0=ALU.mult, op1=ALU.mult), reads=[r_us[0], rR[3], b.r_pv], writes=[r_rkr])
                    mm(bk_[6][:], cmb[:, CM["blk"], :], rkr[:], [b.r_cm, r_rkr], [rb[6]])
                    S.op("dve", (lambda ts=ts: lambda e: e.tensor_tensor(out=bv[:, ts], in0=bk_[6][:], in1=us[2][:], op=ALU.mult))(), reads=[rb[6], r_us[2]], writes=[r_bv[tg]])
                    S.op("act", lambda e: e.copy(out=vb[:], in_=us[2][:]), reads=[r_us[2]], writes=[r_vb])
                    S.op("dve", lambda e: e.tensor_scalar(out=R[5][:], in0=R[0][:], scalar1=0.5, scalar2=None, op0=ALU.mult), reads=[rR[0]], writes=[rR[5]])
                    S.op("dve", lambda e: e.tensor_tensor_scan(out=R[4][:], data0=R[5][:], data1=R[5][:], initial=0.0, op0=ALU.add, op1=ALU.add), reads=[rR[5]], writes=[rR[4]])
                    S.op("dve", lambda e: e.tensor_copy(out=base8[:, 1:8], in_=R[4][:, 63:511:64]), reads=[rR[4], r_base8], writes=[r_base8])
                    S.op("dve", lambda e: e.tensor_tensor(out=R[4][:].rearrange("p (a c) -> p a c", c=64), in0=R[4][:].rearrange("p (a c) -> p a c", c=64),
                                                          in1=base8[:, 0:8].unsqueeze(2).to_broadcast([128, 8, 64]), op=ALU.subtract), reads=[rR[4], r_base8], writes=[rR[4]])
                    S.op("dve", lambda e: e.tensor_copy(out=cumC8[:], in_=R[4][:, 63:512:64]), reads=[rR[4]], writes=[r_cumC8])
                    S.op("act", lambda e: e.activation(out=R[5][:], in_=R[4][:], func=AF.Exp), reads=[rR[4]], writes=[rR[5]])
                    S.op("dve", lambda e: e.tensor_copy(out=ecum8[:], in_=R[5][:, 63:512:64]), reads=[rR[5]], writes=[r_ecum8])
                    S.op("dve", lambda e: e.tensor_tensor(out=fm[0][:], in0=us[0][:], in1=R[5][:], op=ALU.mult), reads=[r_us[0], rR[5]], writes=[r_fm[0]])
                    S.op("dve", lambda e: e.tensor_tensor(out=R[6][:], in0=R[4][:], in1=R[0][:], op=ALU.subtract), reads=[rR[4], rR[0]], writes=[rR[6]])
                    S.op("act", lambda e: e.activation(out=R[6][:], in_=R[6][:], func=AF.Exp), reads=[rR[6]], writes=[rR[6]])
                    S.op("dve", lambda e: e.scalar_tensor_tensor(out=fm[1][:], in0=R[2][:], scalar=-1.0, in1=R[6][:], op0=ALU.mult, op1=ALU.mult), reads=[rR[2], rR[6]], writes=[r_fm[1]])
                    S.op("dve", lambda e: e.tensor_tensor(out=R[1][:], in0=R[1][:], in1=R[2][:], op=ALU.mult), reads=[rR[1], rR[2]], writes=[rR[1]])
                    S.op("act", lambda e: e.activation(out=R[5][:], in_=R[4][:], func=AF.Exp, scale=-1.0), reads=[rR[4], r_ecum8, r_fm[0]], writes=[rR[5]])
                    S.op("dve", lambda e: e.tensor_tensor(out=fm[2][:], in0=R[1][:], in1=R[5][:], op=ALU.mult), reads=[rR[1], rR[5]], writes=[r_fm[2]])
                    S.op("dve", lambda e: e.tensor_tensor(out=fm[3][:], in0=R[3][:], in1=R[5][:], op=ALU.mult), reads=[rR[3], rR[5]], writes=[r_fm[3]])
                    S.op("dve", lambda e: e.tensor_tensor(out=R[6][:].rearrange("p (a c) -> p a c", c=64), in0=cumC8[:, 0:8].unsqueeze(2).to_broadcast([128, 8, 64]),
                                                          in1=R[4][:].rearrange("p (a c) -> p a c", c=64), op=ALU.subtract), reads=[rR[4], r_cumC8, rR[6], r_fm[1]], writes=[rR[6]])
                    S.op("act", lambda e: e.activation(out=R[6][:], in_=R[6][:], func=AF.Exp), reads=[rR[6]], writes=[rR[6]])
                    S.op("dve", lambda e: e.tensor_tensor(out=fm[4][:], in0=R[1][:], in1=R[6][:], op=ALU.mult), reads=[rR[1], rR[6]], writes=[r_fm[4]])
                    S.op("dve", lambda e: e.tensor_tensor(out=fm[5][:], in0=R[3][:], in1=R[6][:], op=ALU.mult), reads=[rR[3], rR[6]], writes=[r_fm[5]])
                    if tg == 0 and dbg_stop(b, "3"):
                        return
                    for qi, src, r_src in ((0, fm[1], r_fm[1]), (1, fm[4], r_fm[4]), (2, fm[5], r_fm[5]), (3, vb, r_vb)):
                        for tl in range(4):
                            S.op("pe", (lambda src=src, tl=tl, qi=qi: lambda e: e.matmul(bk_[7][:, tl * 128:(tl + 1) * 128], lhsT=src[:, tl * 128:(tl + 1) * 128], rhs=cmb[:, CM["ident"], :], start=True, stop=True))(),
                                 reads=[r_src, b.r_cm], writes=[rb[7]])
                        S.op("act" if qi % 2 == 0 else "dve", (lambda qi=qi: (lambda e: e.copy(out=tok[qi][:], in_=bk_[7][:].rearrange("p (a c) -> p a c", c=128))) if qi % 2 == 0 else
                                                               (lambda e: e.tensor_copy(out=tok[qi][:], in_=bk_[7][:].rearrange("p (a c) -> p a c", c=128))))(),
                             reads=[rb[7]], writes=r_tok[qi])
                    if tg == 0 and dbg_stop(b, "4"):
                        return
                    for tl in range(4):
                        gl = tg * 4 + tl
                        rt_, at_, bt_, kt_ = fm[0], fm[1], fm[2], fm[3]
                        for c in range(2):
                            pcs = slice(64 * c, 64 * c + 64)
                            tks = slice(tl * 128 + c * 64, tl * 128 + c * 64 + 64)
                            for hh in range(2):
                                ps = slice(64 * hh, 64 * hh + 64)
                                for wi, (lh, rh_, rl, rr_) in enumerate(((bt_, at_, r_fm[2], r_fm[1]), (kt_, at_, r_fm[3], r_fm[1]), (bt_, rt_, r_fm[2], r_fm[0]), (kt_, rt_, r_fm[3], r_fm[0]))):
                                    o0 = (wi * 2 + hh) * 64
                                    mm(bk_[0][pcs, o0:o0 + 64], lh[ps, tks], rh_[ps, tks], [rl, rr_], [rb[0]])
                                mm(bk_[1][pcs, hh * 64:(hh + 1) * 64], at_[ps, tks], bt_[ps, tks], [r_fm[1], r_fm[2]], [rb[1]])
                        S.op("dve", lambda e: e.tensor_tensor(out=Am[:].rearrange("p a b c -> p (a b c)"), in0=bk_[0][:], in1=cmb[:, CM["rm"]:CM["rm"] + 4, :].rearrange("p a c -> p (a c)"), op=ALU.mult),
                             reads=[rb[0], b.r_cm], writes=[r_Am])
                        S.op("dve", lambda e: e.tensor_tensor(out=AmT[:].rearrange("p b c -> p (b c)"), in0=bk_[1][:, 0:128], in1=cmf[:, CM["rmT"], :], op=ALU.mult), reads=[rb[1], b.r_cm], writes=[r_AmT])
                        for c in range(2):
                            pcs = slice(64 * c, 64 * c + 64)
                            for hh in range(2):
                                mm(bk_[4][pcs, hh * 64:(hh + 1) * 64], Am[pcs, 1, hh, :], tok[3][pcs, tl, hh * 64:(hh + 1) * 64], [r_Am, r_tok[3][tl]], [rb[4]])
                        S.op("dve", lambda e: e.tensor_copy(out=Zf[:, :, 0:64], in_=bk_[4][:, 0:128].rearrange("p (b c) -> p b c", c=64)), reads=[rb[4], r_Zf], writes=[r_Zf])
                        S.op("pool", (lambda tl=tl: lambda e: e.tensor_copy(out=Zf[:, :, 64:128], in_=tok[0][:, tl, :].rearrange("p (b c) -> p b c", c=64)))(), reads=[r_tok[0][tl], r_Zf], writes=[r_Zf])
                        S.op("act", lambda e: e.copy(out=Zb[:], in_=Zf[:]), reads=[r_Zf], writes=[r_Zb])
                        if gl == 0 and dbg_stop(b, "5"):
                            return
                        for lev in range(6):
                            if lev == 0:
                                Pl = lambda pcs, hh: Am[pcs, 0, hh, :]
                                PlT = lambda pcs, hh: AmT[pcs, hh, :]
                                rP = [r_Am, r_AmT]
                            else:
                                pbuf = Pb[lev % 2]
                                Pl = (lambda pbuf: lambda pcs, hh: pbuf[pcs, 0, hh, :])(pbuf)
                                PlT = (lambda pbuf: lambda pcs, hh: pbuf[pcs, 1, hh, :])(pbuf)
                                rP = [r_Pb[lev % 2]]
                            half = (lev % 2) * 256
                            for c in range(2):
                                pcs = slice(64 * c, 64 * c + 64)
                                for hh in range(2):
                                    mm(bk_[3][pcs, half + hh * 128:half + (hh + 1) * 128], Pl(pcs, hh), Zb[pcs, hh, :], rP + [r_Zb], [rb[3]])
                            if lev < 5:
                                for c in range(2):
                                    pcs = slice(64 * c, 64 * c + 64)
                                    for hh in range(2):
                                        mm(bk_[2][pcs, half + hh * 64:half + (hh + 1) * 64], PlT(pcs, hh), Pl(pcs, hh), rP, [rb[2]])
                                        mm(bk_[2][pcs, half + 128 + hh * 64:half + 128 + (hh + 1) * 64], Pl(pcs, hh), PlT(pcs, hh), rP, [rb[2]])
                                nb = Pb[(lev + 1) % 2]
                                S.op("act", (lambda nb=nb, half=half: lambda e: e.copy(out=nb[:].rearrange("p a b c -> p (a b c)"), in_=bk_[2][:, half:half + 256]))(), reads=[rb[2]], writes=[r_Pb[(lev + 1) % 2]])
                            S.op("dve", (lambda half=half: lambda e: e.tensor_tensor(out=Zf[:].rearrange("p b c -> p (b c)"), in0=bk_[3][:, half:half + 256], in1=Zf[:].rearrange("p b c -> p (b c)"), op=ALU.add))(),
                                 reads=[rb[3], r_Zf], writes=[r_Zf])
                            S.op("act", lambda e: e.copy(out=Zb[:], in_=Zf[:]), reads=[r_Zf], writes=[r_Zb])
                        if gl == 0 and dbg_stop(b, "6"):
                            return
                        for c in range(2):
                            pcs = slice(64 * c, 64 * c + 64)
                            for hh in range(2):
                                ps = slice(64 * hh, 64 * hh + 64)
                                mm(bk_[4][ps, 256 + c * 64:256 + (c + 1) * 64], Zb[pcs, hh, 64:128], Am[pcs, 2, hh, :], [r_Zb, r_Am], [rb[4]])
                                mm(bk_[5][ps, c * 64:(c + 1) * 64], Zb[pcs, hh, 64:128], tok[1][pcs, tl, hh * 64:(hh + 1) * 64], [r_Zb, r_tok[1][tl]], [rb[5]])
                        for c in range(2):
                            pcs = slice(64 * c, 64 * c + 64)
                            for hh in range(2):
                                ps = slice(64 * hh, 64 * hh + 64)
                                mm(bk_[5][ps, 128 + c * 64:128 + (c + 1) * 64], tok[2][pcs, tl, hh * 64:(hh + 1) * 64], tok[3][pcs, tl, hh * 64:(hh + 1) * 64], [r_tok[2][tl], r_tok[3][tl]], [rb[5]], start=True, stop=False)
                                mm(bk_[5][ps, 128 + c * 64:128 + (c + 1) * 64], tok[1][pcs, tl, hh * 64:(hh + 1) * 64], Zb[pcs, hh, 0:64], [r_tok[1][tl], r_Zb], [rb[5]], start=False, stop=True)
                        S.op("dve", (lambda tl=tl: lambda e: e.tensor_tensor(out=M2b[:], in0=bk_[4][:, 256:384], in1=fm[0][:, tl * 128:(tl + 1) * 128], op=ALU.add))(), reads=[rb[4], r_fm[0]], writes=[r_M2b])
                        for c in range(2):
                            ch = tl * 2 + c
                            S.op("dve", (lambda c=c, ch=ch: lambda e: e.scalar_tensor_tensor(out=M3Tb[:, c, :], in0=cmf[:, CM["id2"], 0:64], scalar=ecum8[:, ch:ch + 1], in1=bk_[5][:, c * 64:(c + 1) * 64], op0=ALU.mult, op1=ALU.add))(),
                                 reads=[rb[5], r_ecum8, b.r_cm], writes=[r_M3Tb])
                        S.op("act", lambda e: e.copy(out=Laug[:, :, 0:64], in_=bk_[5][:, 128:256].rearrange("p (a c) -> p a c", c=64)), reads=[rb[5], r_Laug], writes=[r_Laug])
                        for c in range(2):
                            pcs = slice(64 * c, 64 * c + 64)
                            for hh in range(2):
                                ps = slice(64 * hh, 64 * hh + 64)
                                oc = slice(c * 64, (c + 1) * 64)
                                mm(bk_[6][ps, oc], Zb[pcs, hh, 0:64], Am[pcs, 2, hh, :], [r_Zb, r_Am], [rb[6]], start=True, stop=False)
                                mm(bk_[6][ps, oc], tok[3][pcs, tl, hh * 64:(hh + 1) * 64], Am[pcs, 3, hh, :], [r_tok[3][tl], r_Am], [rb[6]], start=False, stop=False)
                                mm(bk_[6][ps, oc], Sb_[ps, 0:64], M2b[ps, oc], [r_Sb, r_M2b], [rb[6]], start=False, stop=True)
                                mm(bk_[6][ps, 128 + c * 64:128 + (c + 1) * 64], Sb_[ps, 64:128], M2b[ps, oc], [r_Sb, r_M2b], [rb[6]])
                                mm(bk_[7][ps, 0:128], M3Tb[ps, c, :], Sb_[ps, :], [r_M3Tb, r_Sb], [rb[7]])
                            S.op("dve", (lambda c=c: lambda e: e.tensor_tensor(out=Sf[:], in0=bk_[7][:, 0:128], in1=Laug[:, c, :], op=ALU.add))(), reads=[rb[7], r_Laug, r_Sf], writes=[r_Sf])
                            S.op("act", lambda e: e.copy(out=Sb_[:], in_=Sf[:]), reads=[r_Sf], writes=[r_Sb])
                        if gl == 0 and dbg_stop(b, "7"):
                            return
                        tsl = slice(gl * 128, (gl + 1) * 128)
                        S.op("act", (lambda tsl=tsl: lambda e: e.copy(out=Yloc[:, tsl], in_=bk_[6][:, 0:128]))(), reads=[rb[6]], writes=[r_Yloc[tg]])
                        S.op("dve", (lambda tsl=tsl: lambda e: e.tensor_copy(out=M2p[:, tsl], in_=bk_[6][:, 128:256]))(), reads=[rb[6]], writes=[r_M2p[tg]])
        else:
            persist_io(b, "load", f"ryl{hp}", Yloc[:], r_Yloc, [128, T], BF16)
            persist_io(b, "load", f"rm2{hp}", M2p[:], r_M2p, [128, T], BF16)
            persist_io(b, "load", f"rgt{hp}", gT[:], r_gT, [128, T], BF16)
            persist_io(b, "load", f"rbv{hp}", bv[:], r_bv, [128, T], BF16)
        if b.mode == "local":
            persist_io(b, "dump", f"ryl{hp}", Yloc[:], r_Yloc, [128, T], BF16)
            persist_io(b, "dump", f"rm2{hp}", M2p[:], r_M2p, [128, T], BF16)
            persist_io(b, "dump", f"rgt{hp}", gT[:], r_gT, [128, T], BF16)
            persist_io(b, "dump", f"rbv{hp}", bv[:], r_bv, [128, T], BF16)
        with scope(b) as sb1:
            blk, r_blk = allgather(b, sb1, [(0, 128, Sf[:], [r_Sf])], 128, 128, F32, f"rst{hp}")
            if blk is None:
                return
            blkb = sb1("blkb", [128, 8, 64], BF16)
            r_blkb = S.region()
            S.op("dve", lambda e: e.tensor_copy(out=blkb[:], in_=blk[:, :, 64:128]), reads=[r_blk], writes=[r_blkb])
            MTb = sb1("MTb", [128, 7, 64], BF16)
            r_MTb = S.region()
            for jc in range(7):
                for hh in range(2):
                    ps = slice(64 * hh, 64 * hh + 64)
                    S.op("pe", (lambda jc=jc, ps=ps: lambda e: e.matmul(bk_[0][ps, jc * 64:(jc + 1) * 64], lhsT=blkb[ps, jc, :], rhs=cmb[ps, CM["ident"], ps], start=True, stop=True))(),
                         reads=[r_blkb, b.r_cm], writes=[rb[0]])
            S.op("act", lambda e: e.copy(out=MTb[:].rearrange("p a c -> p (a c)"), in_=bk_[0][:, 0:448]), reads=[rb[0]], writes=[r_MTb])
            Ss = sb1("Ss", [128, 64], F32)
            Ssb = sb1("Ssb", [128, 64], BF16)
            tt_ = sb1("rtt", [128, 64], F32)
            r_Ss, r_Ssb, r_tt = S.regions(3)
            S.op("pool", lambda e: e.memset(Ss[:], 0.0), writes=[r_Ss])
            S.op("pool", lambda e: e.memset(Ssb[:], 0.0), writes=[r_Ssb])
            for jc in range(7):
                for hh in range(2):
                    ps = slice(64 * hh, 64 * hh + 64)
                    mm(bk_[1][ps, 0:64], MTb[ps, jc, :], Ssb[ps, :], [r_MTb, r_Ssb], [rb[1]])
                S.op("dve", (lambda jc=jc: lambda e: e.tensor_tensor(out=tt_[:], in0=bk_[1][:, 0:64], in1=blk[:, jc, 0:64], op=ALU.add))(), reads=[rb[1], r_blk], writes=[r_tt])
                S.op("dve", lambda e: e.tensor_tensor(out=tt_[:], in0=tt_[:], in1=Ss[:], op=ALU.subtract), reads=[r_tt, r_Ss], writes=[r_tt])
                S.op("dve", (lambda jc=jc: lambda e: e.scalar_tensor_tensor(out=Ss[:], in0=tt_[:], scalar=b.pc[:, 8 + jc:9 + jc], in1=Ss[:], op0=ALU.mult, op1=ALU.add))(), reads=[r_tt, r_Ss, b.r_pc], writes=[r_Ss])
                S.op("act", lambda e: e.copy(out=Ssb[:], in_=Ss[:]), reads=[r_Ss], writes=[r_Ssb])
            Y = [sb1(f"Yf{i}", [128, 512], F32) for i in range(2)]
            sq = [sb1(f"rsq{i}", [128, 512], BF16) for i in range(2)]
            rs = [sb1(f"rrs{i}", [128, 512], F32) for i in range(2)]
            tmp = sb1("rtmp", [128, 512], F32)
            r_Y, r_sq, r_rs = S.regions(2), S.regions(2), S.regions(2)
            r_tmp = S.region()
            for tg in range(4):
                i2 = tg % 2
                ts = slice(tg * 512, (tg + 1) * 512)
                bC, bM, bQ = 2 + i2, 4 + i2, 6 + i2
                for hh in range(2):
                    ps = slice(64 * hh, 64 * hh + 64)
                    mm(bk_[bC][ps, :], Ssb[ps, :], M2p[ps, ts], [r_Ssb, r_M2p[tg]], [rb[bC]])
                S.op("dve", (lambda i2=i2, ts=ts, bC=bC: lambda e: e.tensor_tensor(out=Y[i2][:], in0=bk_[bC][:], in1=Yloc[:, ts], op=ALU.add))(), reads=[rb[bC], r_Yloc[tg]], writes=[r_Y[i2]])
                mm(bk_[bM][:], cmf[:, CM["blk"], :], Y[i2][:], [b.r_cm, r_Y[i2]], [rb[bM]])
                S.op("dve", (lambda i2=i2, bM=bM: lambda e: e.scalar_tensor_tensor(out=Y[i2][:], in0=bk_[bM][:], scalar=-1.0 / 64.0, in1=Y[i2][:], op0=ALU.mult, op1=ALU.add))(), reads=[rb[bM], r_Y[i2]], writes=[r_Y[i2]])
                S.op("act", (lambda i2=i2: lambda e: e.activation(out=sq[i2][:], in_=Y[i2][:], func=AF.Square))(), reads=[r_Y[i2]], writes=[r_sq[i2]])
                mm(bk_[bQ][:], cmb[:, CM["blk"], :], sq[i2][:], [b.r_cm, r_sq[i2]], [rb[bQ]])
                rstd_from_sumsq(b, rs[i2][:], r_rs[i2], tmp[:], r_tmp, bk_[bQ][:], rb[bQ], 1.0 / 64.0, "gneps")
                S.op("pool", (lambda i2=i2: lambda e: e.tensor_tensor(out=Y[i2][:], in0=Y[i2][:], in1=rs[i2][:], op=ALU.mult))(), reads=[r_Y[i2], r_rs[i2]], writes=[r_Y[i2]])
                S.op("dve", (lambda i2=i2: lambda e: e.tensor_scalar(out=Y[i2][:], in0=Y[i2][:], scalar1=col("rwkv_ln_w", hp), scalar2=col("rwkv_ln_b", hp), op0=ALU.mult, op1=ALU.add))(), reads=[r_Y[i2], b.r_pv], writes=[r_Y[i2]])
                S.op("dve", (lambda i2=i2, ts=ts: lambda e: e.tensor_tensor(out=Y[i2][:], in0=Y[i2][:], in1=bv[:, ts], op=ALU.add))(), reads=[r_Y[i2], r_bv[tg]], writes=[r_Y[i2]])
                S.op("dve", (lambda i2=i2, ts=ts: lambda e: e.tensor_tensor(out=ym[:, 6 + hp, ts], in0=Y[i2][:], in1=gT[:, ts], op=ALU.mult))(), reads=[r_Y[i2], r_gT[tg]], writes=[rym[6 + hp][tg]])


def mix_layer(b, l, debug=False, parts="amr"):
    S = b.S
    bk_ = b.banks
    rb = b.rb
    w_out = b.D.get("w_out")
    with scope(b) as sb:
        ym = sb("ym", [128, 8, T], BF16)
        rym = S.regions(8, 4)
        with scope(b) as sb1:
            h = sb1("mh", [128, 8, T], BF16)
            rh = S.regions(8, 4)
            with scope(b) as sb2:
                prenorm(b, sb2, 0, T, "ln_mix_pre", l, h, rh)
            if "a" in parts:
                attn_part(b, l, h, rh, ym, rym)
            if "m" in parts:
                for hp in range(2):
                    mlstm_part(b, l, hp, h, rh, ym, rym)
            if "r" in parts:
                for hp in range(2):
                    rwkv_part(b, l, hp, h, rh, ym, rym)
        if debug:
            for c in range(8):
                for tg in range(4):
                    S.op("dve", (lambda c=c, tg=tg: lambda e: e.tensor_copy(out=b.xT[:, c, tg * 512:(tg + 1) * 512], in_=ym[:, c, tg * 512:(tg + 1) * 512]))(),
                         reads=[rym[c][tg], b.rx[c][tg]], writes=[b.rx[c][tg]])
            return
        for half in range(2):
            t0 = half * 1024
            with scope(b) as sb1:
                y = sb1("my", [128, 8, 1024], F32)
                ry = S.regions(8, 2)
                with scope(b) as sb2:
                    wo = [sb2(f"mwo{i}", [128, 8, 256], BF16) for i in range(2)]
                    rwo = S.regions(2)
                    dwo = [S.dsem() for _ in range(2)]
                    for jg in range(4):
                        s = jg % 2
                        cs_ = slice(jg * 256, (jg + 1) * 256)
                        for g in range(2):
                            S.dma("pool", dwo[s], (lambda s=s, g=g, cs_=cs_: lambda e: e.dma_start(out=wo[s][64 * g:64 * g + 64, 0:4, :], in_=w_out[g * 256:(g + 1) * 256, cs_].rearrange("(i d) n -> d i n", d=64)))(), writes=[rwo[s]])
                        S.dma("pool", dwo[s], (lambda s=s, cs_=cs_: lambda e: e.dma_start(out=wo[s][:, 4:8, :], in_=w_out[512:1024, cs_].rearrange("(c p) n -> p c n", p=128)))(), writes=[rwo[s]])
                        for ji in range(2):
                            j = jg * 2 + ji
                            par = j % 2
                            for k in range(8):
                                for tg in range(2):
                                    bk = 4 * par + tg
                                    g4 = half * 2 + tg
                                    S.op("pe", (lambda s=s, k=k, ji=ji, tg=tg, bk=bk: lambda e: e.matmul(bk_[bk][:], lhsT=wo[s][:, k, ji * 128:(ji + 1) * 128], rhs=ym[:, k, t0 + tg * 512:t0 + (tg + 1) * 512], start=(k == 0), stop=(k == 7)))(),
                                         reads=[rwo[s], rym[k][g4]], writes=[rb[bk]])
                            for tg in range(2):
                                bk = 4 * par + tg
                                if tg == 0:
                                    S.op("act", (lambda j=j, tg=tg, bk=bk: lambda e: e.copy(out=y[:, j, tg * 512:(tg + 1) * 512], in_=bk_[bk][:]))(), reads=[rb[bk]], writes=[ry[j][tg]])
                                else:
                                    S.op("dve", (lambda j=j, tg=tg, bk=bk: lambda e: e.tensor_copy(out=y[:, j, tg * 512:(tg + 1) * 512], in_=bk_[bk][:]))(), reads=[rb[bk]], writes=[ry[j][tg]])
                with scope(b) as sb2:
                    postnorm_residual(b, sb2, t0, 1024, "ln_mix_post", l, y, ry, 1.0)


_PROGS = {}


def get_prog(kind, parts="amr"):
    key = (kind, parts)
    if key not in _PROGS:
        _PROGS[key] = build_program(kind, parts)
    return _PROGS[key]


def run_prog(kind, L, inp, xT_list, gathered, parts="amr"):
    nc, b = get_prog(kind, parts)
    pv = pack_params(inp, L)
    cmat = const_mats()
    pos = np.asarray(inp["positions"], np.int32)
    nprev = np.ascontiguousarray(cmat.reshape(128, -1, 128)[:, CM["nprev"], :])
    maps = []
    for c in range(NCORES):
        sl = slice(c * T, (c + 1) * T)
        pcore = np.zeros((128, 32), np.float32)
        if c > 0:
            pcore[:, c - 1] = 1.0
        pcore[:, 8:8 + c] = 1.0
        m = {"xT": np.ascontiguousarray(xT_list[c], np.float32), "pos": np.ascontiguousarray(pos[:, sl]), "pvec": pv, "cmat": cmat,
             "pcore": pcore, "pcm": (np.full((128, 128), NEG, np.float32) if c == 0 else nprev)}
        full = {}
        for name in b.in_names:
            if name in m:
                full[name] = m[name]
            elif name == "pT":
                full[name] = np.ascontiguousarray(np.asarray(inp["p"], np.float32)[L, 0, sl].T)
            elif name.startswith("g_"):
                full[name] = gathered[name[2:]]
            elif name.startswith("l_p_"):
                full[name] = gathered["p_" + name[4:]][c]
            else:
                full[name] = np.ascontiguousarray(np.asarray(inp[name][L], np.float32))
        maps.append(full)
    res = run_bass_kernel_spmd(nc, maps, core_ids=list(range(NCORES)))
    return res.results


def gather_payloads(results, keys):
    return {k: np.ascontiguousarray(np.concatenate([np.asarray(r["x_" + k]) for r in results], axis=0)) for k in keys}


def kernel(**inputs):
    x = np.asarray(inputs["x"], np.float32)[0]
    xT = [np.ascontiguousarray(x[c * T:(c + 1) * T].T) for c in range(NCORES)]
    for L in range(DEPTH):
        rf = run_prog("F", L, inputs, xT, {})
        xT = [np.asarray(r["outT"]) for r in rf]
        ra = run_prog("A0", L, inputs, xT, {})
        g = gather_payloads(ra, HALO_KEYS)
        rb_ = run_prog("B", L, inputs, xT, g)
        g.update(gather_payloads(rb_, STATE_KEYS))
        for k_ in rb_[0]:
            if k_.startswith("x_p_"):
                g[k_[2:]] = [np.asarray(r[k_]) for r in rb_]
        rc = run_prog("C2", L, inputs, xT, g)
        xT = [np.asarray(r["outT"]) for r in rc]
        rg = run_prog("G", L, inputs, xT, {})
        xT = [np.asarray(r["outT"]) for r in rg]
    out = np.concatenate([t.T for t in xT], axis=0)
    return out[None].astype(np.float32)
```

```python
import contextlib
import numpy as np
import concourse.bass as bass
import concourse.mybir as mybir
from concourse.bass_utils import run_bass_kernel_spmd

F32 = mybir.dt.float32
BF16 = mybir.dt.bfloat16
I32 = mybir.dt.int32
AF = mybir.ActivationFunctionType
ALU = mybir.AluOpType
AX = mybir.AxisListType

NCORES = 8
T = 2048
DM = 1024
DFF = 2816
DEPTH = 2


class Reg:
    __slots__ = ("name", "w", "r", "excl")

    def __init__(self, name):
        self.name = name
        self.w = None
        self.r = {}
        self.excl = False


class DSem:
    def __init__(self, key, h):
        self.key = key
        self.h = h
        self.count = 0


class Sched:
    CE = ("pe", "act", "dve", "pool")
    ENG = ("pe", "act", "dve", "pool", "sp")

    def __init__(self, nc, es):
        self.nc = nc
        self.es = es
        self.cnt = {e: 0 for e in self.CE}
        self.sem = {e: es.enter_context(nc.semaphore(f"c_{e}")) for e in self.CE}
        self.seen = {e: {} for e in self.ENG}
        self.nds = 0
        self.nreg = 0
        self.dsems = []
        self.engs = {"pe": nc.tensor, "act": nc.scalar, "dve": nc.vector, "pool": nc.gpsimd, "sp": nc.sync}
        self.ninst = 0
        self.free_dsems = []
        self.scope_stack = []

    def region(self, name=None):
        self.nreg += 1
        return Reg(name or f"r{self.nreg}")

    def regions(self, *shape):
        if len(shape) == 1:
            return [self.region() for _ in range(shape[0])]
        return [self.regions(*shape[1:]) for _ in range(shape[0])]

    def dsem(self, name=None):
        if name is None and self.free_dsems:
            d = self.free_dsems.pop()
        else:
            self.nds += 1
            nm = name or f"d{self.nds}"
            d = DSem("dma_" + nm, self.es.enter_context(self.nc.semaphore("ds_" + nm)))
            self.dsems.append(d)
        if name is None and self.scope_stack:
            self.scope_stack[-1].append(d)
        return d

    def _deps(self, eng, reads, writes):
        waits = {}

        def add(key, sem, val, raw):
            if key == eng and not (raw and eng != "pe"):
                return
            if self.seen[eng].get(key, 0) >= val:
                return
            if key not in waits or waits[key][1] < val:
                waits[key] = (sem, val)

        for r in reads:
            if r.w is not None:
                add(*r.w, True)
        for w in writes:
            if w.w is not None:
                add(*w.w, True)
            for key, (sem, val) in w.r.items():
                add(key, sem, val, False)
        for key, (sem, val) in waits.items():
            self.seen[eng][key] = val
        return list(waits.values())

    def _mark(self, tok, reads, writes):
        key, sem, val = tok
        for r in reads:
            if key not in r.r or r.r[key][1] < val:
                r.r[key] = (sem, val)
        for w in writes:
            w.w = tok
            w.r = {}

    def _emit(self, name, waits, fn, inc):
        e = self.engs[name]
        for sem, val in waits:
            e.wait_ge(sem, val)
            self.ninst += 1
        if fn is not None:
            ins = fn(e)
            ins.then_inc(inc[0], inc[1])
            self.ninst += 1

    def op(self, eng, fn, reads=(), writes=(), pe_wait=()):
        ex = [r for r in reads if r.excl]
        if ex:
            reads = [r for r in reads if not r.excl]
            writes = list(writes) + ex
        waits = self._deps(eng, reads, writes)
        for (key, sem, val) in pe_wait:
            if self.seen[eng].get(key, 0) < val:
                waits.append((sem, val))
                self.seen[eng][key] = val
        self.cnt[eng] += 1
        tok = (eng, self.sem[eng], self.cnt[eng])
        self._emit(eng, waits, fn, (self.sem[eng], 1))
        self._mark(tok, reads, writes)
        return tok

    def dma(self, queue, dsem, fn, reads=(), writes=(), inc=16):
        waits = self._deps(queue, reads, writes)
        dsem.count += inc
        tok = (dsem.key, dsem.h, dsem.count)
        self._emit(queue, waits, fn, (dsem.h, inc))
        self._mark(tok, reads, writes)
        return tok

    def wait_all(self, eng, regs):
        waits = self._deps(eng, regs, ())
        self._emit(eng, waits, None, None)

    def barrier(self):
        for eng in self.ENG:
            waits = []
            for o in self.CE:
                if o != eng and self.cnt[o] > self.seen[eng].get(o, 0):
                    waits.append((self.sem[o], self.cnt[o]))
                    self.seen[eng][o] = self.cnt[o]
            for d in self.dsems:
                if d.count > self.seen[eng].get(d.key, 0):
                    waits.append((d.h, d.count))
                    self.seen[eng][d.key] = d.count
            self._emit(eng, waits, None, None)


_VEC8 = ["ln_ffn1_pre", "ln_ffn1_post", "ln_mix_pre", "ln_mix_post", "ln_ffn2_pre", "ln_ffn2_post",
         "ln_ple_pre", "ln_ple_post", "rwkv_mu"]
_VEC2 = ["mlstm_norm", "rwkv_w0", "rwkv_a0", "rwkv_k_k", "rwkv_k_a", "rwkv_r_k", "rwkv_ln_w", "rwkv_ln_b"]


def param_layout():
    lay = {}
    off = 0
    for l in range(1):
        for n in _VEC8:
            lay[(n, l)] = off
            off += 8
        for n in _VEC2:
            lay[(n, l)] = off
            off += 2
        lay[("conv", l)] = off
        off += 16
        lay[("sink", l)] = off
        off += 4
        lay[("gbias", l)] = off
        off += 1
    for n in ["eps1", "eps4", "gneps", "invfreq", "one", "zero"]:
        lay[n] = off
        off += 1
    lay["_n"] = off
    return lay


LAY = param_layout()


def pack_params(inp, L):
    lay = LAY
    pv = np.zeros((128, lay["_n"]), np.float32)
    l = 0
    for n in _VEC8:
        pv[:, lay[(n, l)]:lay[(n, l)] + 8] = np.asarray(inp[n][L], np.float32).reshape(8, 128).T
    for n in _VEC2:
        pv[:, lay[(n, l)]:lay[(n, l)] + 2] = np.asarray(inp[n][L], np.float32).reshape(2, 128).T
    conv = np.asarray(inp["mlstm_conv"][L], np.float32)
    for tile in range(4):
        for tap in range(4):
            pv[:, lay[("conv", l)] + tile * 4 + tap] = conv[tap, tile * 128:(tile + 1) * 128]
    sk = np.asarray(inp["attn_sinks"][L], np.float32)
    for i in range(4):
        pv[0:64, lay[("sink", l)] + i] = sk[i]
        pv[64:128, lay[("sink", l)] + i] = sk[4 + i]
    pv[0:4, lay[("gbias", l)]] = np.asarray(inp["mlstm_i_bias"][L], np.float32)
    pv[32:36, lay[("gbias", l)]] = np.asarray(inp["mlstm_f_bias"][L], np.float32)
    pv[:, lay["eps1"]] = 1e-6
    pv[:, lay["eps4"]] = 4e-6
    pv[:, lay["gneps"]] = 64e-5
    inv = (500000.0 ** (-np.arange(0, 16, 2, dtype=np.float32) / 16.0)).astype(np.float32)
    for p_ in range(128):
        d = p_ % 64
        pv[p_, lay["invfreq"]] = inv[d % 8] if d < 16 else 0.0
    pv[:, lay["one"]] = 1.0
    return pv


CM = {"ident": 0, "blk": 1, "rowsel": 2, "rmT": 6, "ncur": 7, "nprev": 8, "id2": 9, "ones": 10, "perm": 11, "rm": 12, "_n": 16}
NF32 = 10
NEG = -30000.0


def const_mats():
    cm = np.zeros((128, CM["_n"], 128), np.float32)
    cm[:, CM["ident"], :] = np.eye(128)
    cm[:, CM["ones"], :] = 1.0
    blk = np.zeros((128, 128))
    blk[:64, :64] = 1
    blk[64:, 64:] = 1
    cm[:, CM["blk"], :] = blk
    P = np.zeros((128, 128))
    for hb in (0, 64):
        for i in range(8):
            P[hb + i + 8, hb + i] = -1.0
            P[hb + i, hb + i + 8] = 1.0
    cm[:, CM["perm"], :] = P
    for h in range(4):
        for k in range(128):
            if k % 32 == h:
                cm[k, CM["rowsel"] + h, :] = 1.0
    s_ = (np.arange(128) % 64)[:, None]
    t_ = np.arange(64)[None, :]
    strict = (s_ < t_).astype(np.float32)
    incl = (s_ <= t_).astype(np.float32)
    rm = np.stack([strict, strict, strict, strict, incl, incl, incl, incl], axis=1)
    cm[:, CM["rm"]:CM["rm"] + 4, :] = rm.reshape(128, 4, 128)
    lower = (s_ > t_).astype(np.float32)
    cm[:, CM["rmT"], :] = np.concatenate([lower, lower], axis=1)
    ss = np.arange(128)[:, None]
    tt = np.arange(128)[None, :]
    cm[:, CM["ncur"], :] = np.where(ss > tt, NEG, 0.0)
    cm[:, CM["nprev"], :] = np.where(ss <= tt, NEG, 0.0)
    for p_ in range(128):
        cm[p_, CM["id2"], p_ % 64] = 1.0
    return cm.reshape(128, -1)


W_SHAPES = {
    "w_ffn1_in": [DM, 2 * DFF], "w_ffn1_out": [DFF, DM], "w_in": [DM, 2824],
    "rwkv_w_up": [64, 256], "rwkv_a_up": [64, 256], "rwkv_g_up": [128, 256],
    "w_out": [DM, DM], "w_ffn2_in": [DM, 2 * DFF], "w_ffn2_out": [DFF, DM],
    "w_ple_gate": [DM, DM], "w_ple_proj": [256, DM],
}
HALO_KEYS = ["att", "mls0", "mls1", "rsh0", "rsh1"]
STATE_KEYS = ["mst0", "mst1", "rst0", "rst1"]
PROG_W = {
    "A": ["w_ffn1_in", "w_ffn1_out", "w_in", "rwkv_w_up", "rwkv_a_up", "rwkv_g_up"],
    "A0": ["w_in", "rwkv_w_up", "rwkv_a_up", "rwkv_g_up"],
    "B": ["w_in", "rwkv_w_up", "rwkv_a_up", "rwkv_g_up"],
    "C": ["w_in", "rwkv_w_up", "rwkv_a_up", "rwkv_g_up", "w_out", "w_ffn2_in", "w_ffn2_out", "w_ple_gate", "w_ple_proj"],
    "Cdbg": ["w_in", "rwkv_w_up", "rwkv_a_up", "rwkv_g_up"],
    "R": ["w_ffn1_in", "w_ffn1_out", "w_ple_gate", "w_ple_proj"],
    "F": ["w_ffn1_in", "w_ffn1_out"],
    "C2": ["w_in", "rwkv_w_up", "rwkv_a_up", "rwkv_g_up", "w_out"],
    "G": ["w_ffn2_in", "w_ffn2_out", "w_ple_gate", "w_ple_proj"],
}


class B:
    pass


def build_program(kind, parts="amr"):
    nc = bass.Bass("TRN2", target_bir_lowering=False)
    b = B()
    b.nc = nc
    b.uid = 0
    b.kind = kind
    b.parts = parts
    b.mode = {'B': 'local', 'C2': 'finish', 'C': 'finish'}.get(kind, 'full')
    D = {}
    b.in_names, b.out_names, b.out_regs = [], [], []

    def din(name, shape, dt=F32):
        D[name] = nc.dram_tensor(name, list(shape), dt, kind="ExternalInput").ap()
        b.in_names.append(name)

    din("xT", [DM, T])
    if kind in ("C", "R", "G"):
        din("pT", [256, T])
    din("pos", [1, T], I32)
    din("pvec", [128, LAY["_n"]])
    din("cmat", [128, CM["_n"] * 128])
    din("pcore", [128, 32])
    din("pcm", [128, 128])
    for k in PROG_W[kind]:
        din(k, W_SHAPES[k])
    b.D = D
    if kind in ("A", "A0"):
        b.emit, b.consume = set(HALO_KEYS), set()
    elif kind == "B":
        b.emit, b.consume = set(STATE_KEYS), set(HALO_KEYS)
    else:
        b.emit, b.consume = set(), set(HALO_KEYS + STATE_KEYS)
    want_x_out = kind in ("A", "A0", "C", "Cdbg", "R", "F", "C2", "G")
    if want_x_out:
        outT = nc.dram_tensor("outT", [DM, T], F32, kind="ExternalOutput").ap()
        b.out_names.append("outT")

    with contextlib.ExitStack() as es:
        S = Sched(nc, es)
        b.S = S

        def sb(name, shape, dt):
            return es.enter_context(nc.sbuf_tensor(name, list(shape), dt))

        xT = sb("xT_sb", [128, 8, T], F32)
        rx = S.regions(8, 4)
        pv = sb("pv", [128, LAY["_n"]], F32)
        r_pv = S.region()
        cm = sb("cm", [128, NF32, 128], F32)
        cmb = sb("cmb", [128, CM["_n"], 128], BF16)
        r_cm = S.region()
        pc = sb("pc", [128, 32], F32)
        r_pc = S.region()
        banks = [es.enter_context(nc.psum_tensor(f"bank{i}", [128, 512], F32)) for i in range(8)]
        rb = S.regions(8)
        for r_ in rb:
            r_.excl = True
        b.xT, b.rx, b.pv, b.r_pv, b.cm, b.cmb, b.r_cm, b.pc, b.r_pc, b.banks, b.rb = xT, rx, pv, r_pv, cm, cmb, r_cm, pc, r_pc, banks, rb

        d_c = S.dsem("consts")
        S.dma("sp", d_c, lambda e: e.dma_start(out=pv[:], in_=D["pvec"]), writes=[r_pv])
        S.dma("sp", d_c, lambda e: e.dma_start(out=cm[:], in_=D["cmat"].rearrange("p (a b) -> p a b", b=128)[:, 0:NF32, :]), writes=[r_cm])
        d_cb = S.dsem("constsb")
        S.dma("pool", d_cb, lambda e: e.dma_start(out=cmb[:], in_=D["cmat"].rearrange("p (a b) -> p a b", b=128)), writes=[S.region()])
        S.dma("sp", d_c, lambda e: e.dma_start(out=pc[:], in_=D["pcore"]), writes=[r_pc])
        xv = D["xT"].rearrange("(c p) t -> p c t", p=128)
        for tg in range(4):
            d_x = S.dsem(f"x{tg}")
            S.dma("sp", d_x, (lambda tg: lambda e: e.dma_start(out=xT[:, :, tg * 512:(tg + 1) * 512], in_=xv[:, :, tg * 512:(tg + 1) * 512]))(tg),
                  writes=[rx[c][tg] for c in range(8)])
        S.barrier()

        l = 0
        if kind in ("A", "R", "F"):
            for half in range(2):
                ffn_half(b, l, 1, half * 1024)
        if kind == "R":
            for half in range(2):
                ple_half(b, l, half * 1024)
        if kind in ("A", "A0", "B"):
            with scope(b) as sb1:
                h = sb1("mh", [128, 8, T], BF16)
                rh = S.regions(8, 4)
                with scope(b) as sb2:
                    prenorm(b, sb2, 0, T, "ln_mix_pre", l, h, rh)
                if kind != "B" and "a" in parts:
                    attn_tail(b, l, h, rh)
                if "m" in parts:
                    for hp in range(2):
                        mlstm_part(b, l, hp, h, rh, None, None)
                if "r" in parts:
                    for hp in range(2):
                        rwkv_part(b, l, hp, h, rh, None, None)
        if kind in ("C", "Cdbg", "C2"):
            mix_layer(b, l, debug=(kind == "Cdbg"), parts=parts)
        if kind in ("C", "G"):
            for half in range(2):
                ffn_half(b, l, 2, half * 1024)
            for half in range(2):
                ple_half(b, l, half * 1024)
        S.barrier()

        if want_x_out:
            ov = outT.rearrange("(c p) t -> p c t", p=128)
            r_out = S.region()
            d_o = S.dsem("out")
            for tg in range(4):
                S.dma("sp", d_o, (lambda tg: lambda e: e.dma_start(out=ov[:, :, tg * 512:(tg + 1) * 512], in_=xT[:, :, tg * 512:(tg + 1) * 512]))(tg),
                      reads=[rx[c][tg] for c in range(8)], writes=[r_out])
            b.out_regs.append(r_out)
        S.wait_all("sp", b.out_regs)
    b.ninst = S.ninst
    return nc, b


def pvc(b, key, l=None, i=0):
    off = LAY[(key, l)] if l is not None else LAY[key]
    return b.pv[:, off + i:off + i + 1]


@contextlib.contextmanager
def scope(b):
    with contextlib.ExitStack() as es:
        def sb(name, shape, dt):
            b.uid += 1
            return es.enter_context(b.nc.sbuf_tensor(f"{name}_{b.uid}", list(shape), dt))
        b.S.scope_stack.append([])
        yield sb
        b.S.barrier()
        b.S.free_dsems.extend(b.S.scope_stack.pop())


def rstd_from_sumsq(b, sb_rs, r_rs, tmp, r_tmp, bank, r_bank, scale, eps_key, n=512):
    S = b.S
    S.op("act", lambda e: e.activation(out=tmp, in_=bank, func=AF.Sqrt, bias=pvc(b, eps_key), scale=scale),
         reads=[r_bank, b.r_pv], writes=[r_tmp])
    S.op("dve", lambda e: e.reciprocal(out=sb_rs, in_=tmp), reads=[r_tmp], writes=[r_rs])


def prenorm(b, sb, t0, nt, gkey, l, h, rh):
    S = b.S
    ng = nt // 512
    sq = [sb(f"pn_sq{i}", [128, 8, 512], BF16) for i in range(2)]
    r_sq = S.regions(2)
    tmp = sb("pn_tmp", [128, 512], F32)
    r_tmp = S.region()
    rs = [sb(f"pn_rs{i}", [128, 512], F32) for i in range(2)]
    r_rs = S.regions(2)
    for tg in range(ng):
        g4 = (t0 // 512) + tg
        sl = slice(t0 + tg * 512, t0 + (tg + 1) * 512)
        i = tg % 2
        S.op("act", (lambda i, sl: lambda e: e.activation(out=sq[i][:], in_=b.xT[:, :, sl], func=AF.Square))(i, sl),
             reads=[b.rx[c][g4] for c in range(8)], writes=[r_sq[i]])
        bk = 7 - i
        for c in range(8):
            S.op("pe", (lambda i, c, bk: lambda e: e.matmul(b.banks[bk][:], lhsT=b.cmb[:, CM["ones"], :], rhs=sq[i][:, c, :], start=(c == 0), stop=(c == 7)))(i, c, bk),
                 reads=[r_sq[i], b.r_cm], writes=[b.rb[bk]])
        rstd_from_sumsq(b, rs[i][:], r_rs[i], tmp[:], r_tmp, b.banks[bk][:], b.rb[bk], 1.0 / DM, "eps1")
        for c in range(8):
            S.op("dve", (lambda i, c, sl, tg: lambda e: e.scalar_tensor_tensor(
                out=h[:, c, tg * 512:(tg + 1) * 512], in0=b.xT[:, c, sl], scalar=pvc(b, gkey, l, c), in1=rs[i][:],
                op0=ALU.mult, op1=ALU.mult))(i, c, sl, tg),
                reads=[b.rx[c][g4], r_rs[i], b.r_pv], writes=[rh[c][tg]])


def postnorm_residual(b, sb, t0, nt, gkey, l, y, ry, factor):
    S = b.S
    ng = nt // 512
    sq = [sb(f"po_sq{i}", [128, 8, 512], BF16) for i in range(2)]
    r_sq = S.regions(2)
    tmp = sb("po_tmp", [128, 512], F32)
    r_tmp = S.region()
    rs = [sb(f"po_rs{i}", [128, 512], F32) for i in range(2)]
    r_rs = S.regions(2)
    t2 = [sb(f"po_t2{i}", [128, 512], F32) for i in range(2)]
    r_t2 = S.regions(2)
    scale = 1.0 / (DM * factor * factor)
    eps_key = "eps1" if factor == 1.0 else "eps4"
    n2 = 0
    for tg in range(ng):
        g4 = (t0 // 512) + tg
        sl = slice(t0 + tg * 512, t0 + (tg + 1) * 512)
        ysl = slice(tg * 512, (tg + 1) * 512)
        i = tg % 2
        S.op("act", (lambda i, ysl: lambda e: e.activation(out=sq[i][:], in_=y[:, :, ysl], func=AF.Square))(i, ysl),
             reads=[ry[c][tg] for c in range(8)], writes=[r_sq[i]])
        bk = 7 - i
        for c in range(8):
            S.op("pe", (lambda i, c, bk: lambda e: e.matmul(b.banks[bk][:], lhsT=b.cmb[:, CM["ones"], :], rhs=sq[i][:, c, :], start=(c == 0), stop=(c == 7)))(i, c, bk),
                 reads=[r_sq[i], b.r_cm], writes=[b.rb[bk]])
        rstd_from_sumsq(b, rs[i][:], r_rs[i], tmp[:], r_tmp, b.banks[bk][:], b.rb[bk], scale, eps_key)
        for c in range(8):
            j = n2 % 2
            n2 += 1
            S.op("pool", (lambda i, c, ysl, j: lambda e: e.tensor_tensor(out=t2[j][:], in0=y[:, c, ysl], in1=rs[i][:], op=ALU.mult))(i, c, ysl, j),
                 reads=[ry[c][tg], r_rs[i]], writes=[r_t2[j]])
            S.op("dve", (lambda c, sl, j: lambda e: e.scalar_tensor_tensor(
                out=b.xT[:, c, sl], in0=t2[j][:], scalar=pvc(b, gkey, l, c), in1=b.xT[:, c, sl], op0=ALU.mult, op1=ALU.add))(c, sl, j),
                reads=[r_t2[j], b.rx[c][g4], b.r_pv], writes=[b.rx[c][g4]])


def wview(ap2d, r0, nr, c0, ncol):
    return ap2d[r0:r0 + nr, c0:c0 + ncol].rearrange("(c p) n -> p c n", p=128)


def ffn_half(b, l, which, t0):
    S = b.S
    NT, NG = 1024, 2
    w_in = b.D[f"w_ffn{which}_in"]
    w_out = b.D[f"w_ffn{which}_out"]
    with scope(b) as sb:
        h = sb("h", [128, 8, NT], BF16)
        rh = S.regions(8, NG)
        act = sb("act", [128, 22, NT], BF16)
        ract = S.regions(22, NG)
        with scope(b) as sb2:
            prenorm(b, sb2, t0, NT, f"ln_ffn{which}_pre", l, h, rh)
        with scope(b) as sb2:
            wg = [sb2(f"wg{i}", [128, 8, 256], BF16) for i in range(2)]
            wu = [sb2(f"wu{i}", [128, 8, 256], BF16) for i in range(2)]
            rwg, rwu = S.regions(2), S.regions(2)
            dwg, dwu = [S.dsem() for _ in range(2)], [S.dsem() for _ in range(2)]
            sg = [sb2(f"sg{i}", [128, NT], BF16) for i in range(2)]
            rsg = S.regions(2, NG)
            for mg in range(11):
                s = mg % 2
                S.dma("pool", dwg[s], (lambda s, mg: lambda e: e.dma_start(out=wg[s][:], in_=wview(w_in, 0, DM, mg * 256, 256)))(s, mg), writes=[rwg[s]])
                S.dma("pool", dwu[s], (lambda s, mg: lambda e: e.dma_start(out=wu[s][:], in_=wview(w_in, 0, DM, DFF + mg * 256, 256)))(s, mg), writes=[rwu[s]])
                for mi in range(2):
                    m = mg * 2 + mi
                    par = m % 2
                    for (wt, rw, boff) in ((wg, rwg, 0), (wu, rwu, 2)):
                        for k in range(8):
                            for tg in range(NG):
                                bk = 4 * par + boff + tg
                                S.op("pe", (lambda wt, s, k, mi, tg, bk: lambda e: e.matmul(
                                    b.banks[bk][:], lhsT=wt[s][:, k, mi * 128:(mi + 1) * 128], rhs=h[:, k, tg * 512:(tg + 1) * 512],
                                    start=(k == 0), stop=(k == 7)))(wt, s, k, mi, tg, bk),
                                    reads=[rw[s], rh[k][tg]], writes=[b.rb[bk]])
                    for tg in range(NG):
                        bg, bu = 4 * par + tg, 4 * par + 2 + tg
                        S.op("act", (lambda par, tg, bg: lambda e: e.activation(out=sg[par][:, tg * 512:(tg + 1) * 512], in_=b.banks[bg][:], func=AF.Silu))(par, tg, bg),
                             reads=[b.rb[bg]], writes=[rsg[par][tg]])
                        S.op("dve", (lambda par, tg, bu, m: lambda e: e.tensor_tensor(
                            out=act[:, m, tg * 512:(tg + 1) * 512], in0=b.banks[bu][:], in1=sg[par][:, tg * 512:(tg + 1) * 512], op=ALU.mult))(par, tg, bu, m),
                            reads=[b.rb[bu], rsg[par][tg]], writes=[ract[m][tg]])
        with scope(b) as sb2:
            y = sb2("y", [128, 8, NT], F32)
            ry = S.regions(8, NG)
            with scope(b) as sb3:
                wo = [sb3(f"wo{i}", [128, 11, 512], BF16) for i in range(2)]
                rwo = S.regions(2)
                dwo = [S.dsem() for _ in range(2)]
                for jg in range(2):
                    for a in range(2):
                        S.dma("pool", dwo[a], (lambda a, jg: lambda e: e.dma_start(out=wo[a][:], in_=wview(w_out, a * 1408, 1408, jg * 512, 512)))(a, jg), writes=[rwo[a]])
                    for k in range(22):
                        a, kk = k // 11, k % 11
                        for j in range(4):
                            for tg in range(NG):
                                bk = j * 2 + tg
                                S.op("pe", (lambda a, kk, j, tg, bk, k: lambda e: e.matmul(
                                    b.banks[bk][:], lhsT=wo[a][:, kk, j * 128:(j + 1) * 128], rhs=act[:, k, tg * 512:(tg + 1) * 512],
                                    start=(k == 0), stop=(k == 21)))(a, kk, j, tg, bk, k),
                                    reads=[rwo[a], ract[k][tg]], writes=[b.rb[bk]])
                    for j in range(4):
                        for tg in range(NG):
                            bk = j * 2 + tg
                            c = jg * 4 + j
                            if (j + tg) % 2 == 0:
                                S.op("act", (lambda c, tg, bk: lambda e: e.copy(out=y[:, c, tg * 512:(tg + 1) * 512], in_=b.banks[bk][:]))(c, tg, bk),
                                     reads=[b.rb[bk]], writes=[ry[c][tg]])
                            else:
                                S.op("dve", (lambda c, tg, bk: lambda e: e.tensor_copy(out=y[:, c, tg * 512:(tg + 1) * 512], in_=b.banks[bk][:]))(c, tg, bk),
                                     reads=[b.rb[bk]], writes=[ry[c][tg]])
            with scope(b) as sb3:
                postnorm_residual(b, sb3, t0, NT, f"ln_ffn{which}_post", l, y, ry, 0.5)


def ple_half(b, l, t0):
    S = b.S
    NT, NG = 1024, 2
    wgd = b.D["w_ple_gate"]
    wpd = b.D["w_ple_proj"]
    with scope(b) as sb:
        h = sb("h", [128, 8, NT], BF16)
        rh = S.regions(8, NG)
        y = sb("y", [128, 8, NT], F32)
        ry = S.regions(8, NG)
        with scope(b) as sb2:
            prenorm(b, sb2, t0, NT, "ln_ple_pre", l, h, rh)
        with scope(b) as sb2:
            pt = sb2("pt", [128, 2, NT], BF16)
            r_pt = S.region()
            wp = sb2("wp", [128, 2, DM], BF16)
            r_wp = S.region()
            d1, d2 = S.dsem(), S.dsem()
            S.dma("pool", d1, lambda e: e.dma_start(out=pt[:], in_=b.D["pT"][:, t0:t0 + NT].rearrange("(c p) t -> p c t", p=128)), writes=[r_pt])
            S.dma("pool", d2, lambda e: e.dma_start(out=wp[:], in_=wview(wpd, 0, 256, 0, DM)), writes=[r_wp])
            wg = [sb2(f"wg{i}", [128, 8, 256], BF16) for i in range(2)]
            rwg = S.regions(2)
            dwg = [S.dsem() for _ in range(2)]
            sg = [sb2(f"sg{i}", [128, NT], F32) for i in range(2)]
            rsg = S.regions(2, NG)
            for jg in range(4):
                s = jg % 2
                S.dma("pool", dwg[s], (lambda s, jg: lambda e: e.dma_start(out=wg[s][:], in_=wview(wgd, 0, DM, jg * 256, 256)))(s, jg), writes=[rwg[s]])
                for ji in range(2):
                    j = jg * 2 + ji
                    par = j % 2
                    for k in range(8):
                        for tg in range(NG):
                            bk = 4 * par + tg
                            S.op("pe", (lambda s, k, ji, tg, bk: lambda e: e.matmul(
                                b.banks[bk][:], lhsT=wg[s][:, k, ji * 128:(ji + 1) * 128], rhs=h[:, k, tg * 512:(tg + 1) * 512],
                                start=(k == 0), stop=(k == 7)))(s, k, ji, tg, bk),
                                reads=[rwg[s], rh[k][tg]], writes=[b.rb[bk]])
                    for k in range(2):
                        for tg in range(NG):
                            bk = 4 * par + 2 + tg
                            S.op("pe", (lambda k, j, tg, bk: lambda e: e.matmul(
                                b.banks[bk][:], lhsT=wp[:, k, j * 128:(j + 1) * 128], rhs=pt[:, k, tg * 512:(tg + 1) * 512],
                                start=(k == 0), stop=(k == 1)))(k, j, tg, bk),
                                reads=[r_wp, r_pt], writes=[b.rb[bk]])
                    for tg in range(NG):
                        bg, bu = 4 * par + tg, 4 * par + 2 + tg
                        S.op("act", (lambda par, tg, bg: lambda e: e.activation(out=sg[par][:, tg * 512:(tg + 1) * 512], in_=b.banks[bg][:], func=AF.Sigmoid))(par, tg, bg),
                             reads=[b.rb[bg]], writes=[rsg[par][tg]])
                        S.op("dve", (lambda par, tg, bu, j: lambda e: e.tensor_tensor(
                            out=y[:, j, tg * 512:(tg + 1) * 512], in0=b.banks[bu][:], in1=sg[par][:, tg * 512:(tg + 1) * 512], op=ALU.mult))(par, tg, bu, j),
                            reads=[b.rb[bu], rsg[par][tg]], writes=[ry[j][tg]])
        with scope(b) as sb2:
            postnorm_residual(b, sb2, t0, NT, "ln_ple_post", l, y, ry, 1.0)


def allgather(b, sb, srcs, nrow, ncol, dt, name):
    S, nc = b.S, b.nc
    if name in b.emit:
        xo = nc.dram_tensor(f"x_{name}", [nrow, ncol], dt, kind="ExternalOutput").ap()
        r_o = S.region()
        d1 = S.dsem(f"xo_{name}")
        for (c0, ncs, ap, rr) in srcs:
            S.dma("sp", d1, (lambda c0=c0, ncs=ncs, ap=ap: lambda e: e.dma_start(out=xo[:, c0:c0 + ncs], in_=ap, allow_slow_non_contiguous=True))(), reads=rr, writes=[r_o])
        b.out_regs.append(r_o)
        b.out_names.append(f"x_{name}")
        return None, None
    assert name in b.consume, name
    gi = nc.dram_tensor(f"g_{name}", [NCORES * nrow, ncol], dt, kind="ExternalInput").ap()
    b.in_names.append(f"g_{name}")
    r_blk = S.region()
    d3 = S.dsem()
    blk = sb(f"agblk_{name}", [nrow, NCORES, ncol], dt)
    S.dma("sp", d3, lambda e: e.dma_start(out=blk[:], in_=gi.rearrange("(j p) n -> p j n", p=nrow)), writes=[r_blk])
    return blk, r_blk


def select_prev(b, dst, r_dst, blk, r_blk, nrow, c0, ncs):
    S = b.S
    S.op("dve", lambda e: e.tensor_scalar(out=dst, in0=blk[:, 0, c0:c0 + ncs], scalar1=b.pc[0:nrow, 0:1], scalar2=None, op0=ALU.mult),
         reads=[r_blk, b.r_pc], writes=[r_dst])
    for j in range(1, 7):
        S.op("dve", (lambda j=j: lambda e: e.scalar_tensor_tensor(out=dst, in0=blk[:, j, c0:c0 + ncs], scalar=b.pc[0:nrow, j:j + 1], in1=dst,
                                                                 op0=ALU.mult, op1=ALU.add))(), reads=[r_blk, b.r_pc, r_dst], writes=[r_dst])


def rope_tables(b, cosF, sinF, r_cos, r_sin, t0=0, n=T):
    S = b.S
    PI = float(np.pi)
    C1 = 6.28125
    C2 = float(2 * np.pi - 6.28125)
    with scope(b) as sb:
        posi = sb("posi", [128, n], I32)
        ang = sb("ang", [128, n], F32)
        kf = sb("kf", [128, n], F32)
        ki = sb("ki", [128, n], I32)
        rr = sb("rr", [128, n], F32)
        r_posi, r_ang, r_kf, r_ki, r_rr = S.regions(5)
        d = S.dsem()
        S.dma("sp", d, lambda e: e.dma_start(out=posi[:], in_=b.D["pos"][:, t0:t0 + n].partition_broadcast(128)), writes=[r_posi])
        S.op("dve", lambda e: e.tensor_copy(out=ang[:], in_=posi[:]), reads=[r_posi], writes=[r_ang])
        S.op("dve", lambda e: e.tensor_scalar(out=ang[:], in0=ang[:], scalar1=pvc(b, "invfreq"), scalar2=None, op0=ALU.mult), reads=[r_ang, b.r_pv], writes=[r_ang])
        for (dst, r_dst, shift) in ((sinF, r_sin, 0.0), (cosF, r_cos, PI / 2)):
            S.op("dve", (lambda shift=shift: lambda e: e.tensor_scalar(out=kf[:], in0=ang[:], scalar1=1.0 / (2 * PI), scalar2=0.5 + shift / (2 * PI), op0=ALU.mult, op1=ALU.add))(),
                 reads=[r_ang], writes=[r_kf])
            S.op("dve", lambda e: e.tensor_copy(out=ki[:], in_=kf[:]), reads=[r_kf], writes=[r_ki])
            S.op("dve", lambda e: e.tensor_copy(out=kf[:], in_=ki[:]), reads=[r_ki], writes=[r_kf])
            S.op("dve", lambda e: e.scalar_tensor_tensor(out=rr[:], in0=kf[:], scalar=-C1, in1=ang[:], op0=ALU.mult, op1=ALU.add), reads=[r_kf, r_ang], writes=[r_rr])
            S.op("dve", lambda e: e.scalar_tensor_tensor(out=rr[:], in0=kf[:], scalar=-C2, in1=rr[:], op0=ALU.mult, op1=ALU.add), reads=[r_kf, r_rr], writes=[r_rr])
            if shift:
                S.op("dve", (lambda shift=shift: lambda e: e.tensor_scalar(out=rr[:], in0=rr[:], scalar1=shift, scalar2=None, op0=ALU.add))(), reads=[r_rr], writes=[r_rr])
            S.op("dve", lambda e: e.tensor_scalar(out=kf[:], in0=rr[:], scalar1=-PI, scalar2=2 * PI, op0=ALU.is_lt, op1=ALU.mult), reads=[r_rr], writes=[r_kf])
            S.op("dve", lambda e: e.tensor_tensor(out=rr[:], in0=rr[:], in1=kf[:], op=ALU.add), reads=[r_rr, r_kf], writes=[r_rr])
            S.op("dve", lambda e: e.tensor_scalar(out=kf[:], in0=rr[:], scalar1=PI, scalar2=-2 * PI, op0=ALU.is_gt, op1=ALU.mult), reads=[r_rr], writes=[r_kf])
            S.op("dve", lambda e: e.tensor_tensor(out=rr[:], in0=rr[:], in1=kf[:], op=ALU.add), reads=[r_rr, r_kf], writes=[r_rr])
            S.op("dve", lambda e: e.tensor_scalar(out=rr[:], in0=rr[:], scalar1=-PI, scalar2=PI, op0=ALU.max, op1=ALU.min), reads=[r_rr], writes=[r_rr])
            S.op("act", (lambda dst=dst: lambda e: e.activation(out=dst[:], in_=rr[:], func=AF.Sin))(), reads=[r_rr], writes=[r_dst])


def attn_tail(b, l, h, rh):
    S = b.S
    w_in = b.D["w_in"]
    bk_ = b.banks
    rb = b.rb
    with scope(b) as sb:
        cosF = sb("cosF", [128, 128], F32)
        sinF = sb("sinF", [128, 128], F32)
        r_cos, r_sin = S.regions(2)
        rope_tables(b, cosF, sinF, r_cos, r_sin, t0=T - 128, n=128)
        if "1" in b.parts:
            return
        wkv = sb("wkv", [128, 8, 256], BF16)
        r_wkv = S.region()
        dkv = S.dsem()
        S.dma("pool", dkv, lambda e: e.dma_start(out=wkv[:], in_=wview(w_in, 0, DM, 512, 256)), writes=[r_wkv])
        xb = sb("xb", [128, 128], BF16)
        t1 = sb("t1", [128, 128], F32)
        t2 = sb("t2", [128, 128], F32)
        pay = sb("pay", [128, 256], F32)
        r_xb, r_t1, r_t2, r_pay = S.regions(4)
        ts = slice(T - 128, T)
        for k in range(8):
            S.op("pe", (lambda k=k: lambda e: e.matmul(bk_[0][:, 0:128], lhsT=wkv[:, k, 0:128], rhs=h[:, k, ts], start=(k == 0), stop=(k == 7)))(), reads=[r_wkv, rh[k][3]], writes=[rb[0]])
        if "2" in b.parts:
            return
        S.op("act", lambda e: e.copy(out=xb[:], in_=bk_[0][:, 0:128]), reads=[rb[0]], writes=[r_xb])
        if "5" in b.parts:
            return
        if "8" in b.parts:
            S.op("dve", lambda e: e.tensor_tensor(out=t1[:], in0=bk_[0][:, 0:128], in1=cosF[:, 0:128], op=ALU.mult), reads=[rb[0], r_cos], writes=[r_t1])
            return
        if "9" in b.parts:
            S.op("dve", lambda e: e.tensor_tensor(out=t1[:], in0=xb[:], in1=cosF[:, 0:128], op=ALU.mult), reads=[r_xb, r_cos], writes=[r_t1])
            return
        if "0" in b.parts:
            S.op("dve", lambda e: e.tensor_tensor(out=t1[:], in0=bk_[0][:, 0:128], in1=t2[:], op=ALU.mult), reads=[rb[0], r_xb], writes=[r_t1])
            return
        S.op("dve", lambda e: e.tensor_tensor(out=t1[:], in0=bk_[0][:, 0:128], in1=cosF[:, 0:128], op=ALU.mult), reads=[rb[0], r_cos], writes=[r_t1])
        if "6" in b.parts:
            return
        S.op("pe", lambda e: e.matmul(bk_[1][:, 0:128], lhsT=b.cmb[:, CM["perm"], :], rhs=xb[:], start=True, stop=True), reads=[r_xb, b.r_cm], writes=[rb[1]])
        if "7" in b.parts:
            return
        S.op("dve", lambda e: e.tensor_tensor(out=t2[:], in0=bk_[1][:, 0:128], in1=sinF[:, 0:128], op=ALU.mult), reads=[rb[1], r_sin], writes=[r_t2])
        S.op("dve", lambda e: e.tensor_tensor(out=pay[:, 0:128], in0=t1[:], in1=t2[:], op=ALU.add), reads=[r_t1, r_t2], writes=[r_pay])
        if "3" in b.parts:
            return
        for k in range(8):
            S.op("pe", (lambda k=k: lambda e: e.matmul(bk_[2][:, 0:128], lhsT=h[:, k, ts], rhs=wkv[:, k, 128:256], start=(k == 0), stop=(k == 7)))(), reads=[r_wkv, rh[k][3]], writes=[rb[2]])
        S.op("act", lambda e: e.copy(out=pay[:, 128:256], in_=bk_[2][:, 0:128]), reads=[rb[2], r_pay], writes=[r_pay])
        if "4" in b.parts:
            return
        allgather(b, sb, [(0, 256, pay[:], [r_pay])], 128, 256, F32, "att")


def attn_part(b, l, h, rh, ym, rym):
    S = b.S
    w_in = b.D["w_in"]
    bk_ = b.banks
    rb = b.rb
    with scope(b) as sb:
        cosF = sb("cosF", [128, T], F32)
        sinF = sb("sinF", [128, T], F32)
        r_cos, r_sin = S.regions(2)
        rope_tables(b, cosF, sinF, r_cos, r_sin)
        qT = sb("qT", [128, 4, T], BF16)
        rq = S.regions(4, 4)
        kT = sb("kT", [128, 128 + T], BF16)
        rk = S.regions(5)
        vt = sb("vt", [128, 17, 128], BF16)
        rv = S.regions(5)
        nm = sb("nm", [128, 3, 4, 128], BF16)
        r_nm = S.region()
        es = sb("es", [128, 4], F32)
        r_es = S.region()
        pcm_sb = sb("pcm_sb", [128, 128], F32)
        r_pcm = S.region()
        d0 = S.dsem()
        S.dma("sp", d0, lambda e: e.dma_start(out=pcm_sb[:], in_=b.D["pcm"]), writes=[r_pcm])
        for m, src in enumerate([b.cm[:, CM["ncur"], :], b.cm[:, CM["nprev"], :], pcm_sb[:]]):
            S.op("dve", (lambda m=m, src=src: lambda e: e.tensor_copy(out=nm[:, m, :, :], in_=src.unsqueeze(1).to_broadcast([128, 4, 128])))(),
                 reads=[b.r_cm, r_pcm], writes=[r_nm])
        so = LAY[("sink", l)]
        S.op("act", lambda e: e.activation(out=es[:], in_=b.pv[:, so:so + 4], func=AF.Exp), reads=[b.r_pv], writes=[r_es])
        with scope(b) as sb2:
            wq = sb2("wq", [128, 8, 512], BF16)
            wkv = sb2("wkv", [128, 8, 256], BF16)
            r_wq, r_wkv = S.regions(2)
            dq, dkv = S.dsem(), S.dsem()
            for g in range(2):
                for i in range(4):
                    S.dma("pool", dq, (lambda g=g, i=i: lambda e: e.dma_start(
                        out=wq[:, :, i * 128 + g * 64:i * 128 + g * 64 + 64], in_=wview(w_in, 0, DM, g * 256 + i * 64, 64)))(), writes=[r_wq])
            S.dma("pool", dkv, lambda e: e.dma_start(out=wkv[:], in_=wview(w_in, 0, DM, 512, 256)), writes=[r_wkv])
            xb = [sb2(f"xb{i}", [128, 512], BF16) for i in range(2)]
            t1 = [sb2(f"t1{i}", [128, 512], F32) for i in range(2)]
            t2 = [sb2(f"t2{i}", [128, 512], F32) for i in range(2)]
            r_xb, r_t1, r_t2 = S.regions(2), S.regions(2), S.regions(2)
            n = 0
            for ti in range(5):
                for tg in range(4):
                    bk = n % 4
                    i2 = n % 2
                    n += 1
                    ts = slice(tg * 512, (tg + 1) * 512)
                    for k in range(8):
                        lhs = wq[:, k, ti * 128:(ti + 1) * 128] if ti < 4 else wkv[:, k, 0:128]
                        S.op("pe", (lambda lhs=lhs, k=k, ts=ts, bk=bk: lambda e: e.matmul(bk_[bk][:], lhsT=lhs, rhs=h[:, k, ts], start=(k == 0), stop=(k == 7)))(),
                             reads=[r_wq if ti < 4 else r_wkv, rh[k][tg]], writes=[rb[bk]])
                    S.op("act", (lambda i2=i2, bk=bk: lambda e: e.copy(out=xb[i2][:], in_=bk_[bk][:]))(), reads=[rb[bk]], writes=[r_xb[i2]])
                    S.op("dve", (lambda i2=i2, bk=bk, ts=ts: lambda e: e.tensor_tensor(out=t1[i2][:], in0=bk_[bk][:], in1=cosF[:, ts], op=ALU.mult))(),
                         reads=[rb[bk], r_cos], writes=[r_t1[i2]])
                    S.op("pe", (lambda i2=i2, bk=bk: lambda e: e.matmul(bk_[4 + bk][:], lhsT=b.cmb[:, CM["perm"], :], rhs=xb[i2][:], start=True, stop=True))(),
                         reads=[r_xb[i2], b.r_cm], writes=[rb[4 + bk]])
                    S.op("dve", (lambda i2=i2, bk=bk, ts=ts: lambda e: e.tensor_tensor(out=t2[i2][:], in0=bk_[4 + bk][:], in1=sinF[:, ts], op=ALU.mult))(),
                         reads=[rb[4 + bk], r_sin], writes=[r_t2[i2]])
                    if ti < 4:
                        dst, r_dst = qT[:, ti, ts], rq[ti][tg]
                    else:
                        dst, r_dst = kT[:, 128 + tg * 512:128 + (tg + 1) * 512], rk[1 + tg]
                    S.op("pool", (lambda i2=i2, dst=dst: lambda e: e.tensor_tensor(out=dst, in0=t1[i2][:], in1=t2[i2][:], op=ALU.add))(),
                         reads=[r_t1[i2], r_t2[i2]], writes=[r_dst])
            for g4 in range(4):
                for tt4 in range(4):
                    tt = g4 * 4 + tt4
                    for k in range(8):
                        S.op("pe", (lambda g4=g4, tt4=tt4, tt=tt, k=k: lambda e: e.matmul(bk_[g4][:, tt4 * 128:(tt4 + 1) * 128], lhsT=h[:, k, tt * 128:(tt + 1) * 128],
                                                                                 rhs=wkv[:, k, 128:256], start=(k == 0), stop=(k == 7)))(),
                             reads=[r_wkv, rh[k][g4]], writes=[rb[g4]])
                S.op("act", (lambda g4=g4: lambda e: e.copy(out=vt[:, 1 + g4 * 4:5 + g4 * 4, :], in_=bk_[g4][:].rearrange("p (a c) -> p a c", c=128)))(),
                     reads=[rb[g4]], writes=[rv[1 + g4]])
        with scope(b) as sb2:
            blk, r_blk = allgather(b, sb2, [(0, 128, kT[:, T:T + 128], [rk[4]]), (128, 128, vt[:, 16, :], [rv[4]])], 128, 256, F32, "att")
            select_prev(b, kT[:, 0:128], rk[0], blk, r_blk, 128, 0, 128)
            select_prev(b, vt[:, 0, :], rv[0], blk, r_blk, 128, 128, 128)
        with scope(b) as sb2:
            Pp = [sb2(f"Pp{i}", [128, 512], BF16) for i in range(2)]
            Pc = [sb2(f"Pc{i}", [128, 512], BF16) for i in range(2)]
            r_Pp, r_Pc = S.regions(2), S.regions(2)
            dn = [sb2(f"dn{i}", [128, 512], F32) for i in range(2)]
            r_dn = S.regions(2)
            n = 0
            for qb in range(16):
                qs = slice(qb * 128, (qb + 1) * 128)
                bN, bD = 4 + (qb % 2), 6 + (qb % 2)
                for g in range(2):
                    ps = slice(64 * g, 64 * g + 64)
                    i2 = n % 2
                    bA, bB = 2 * i2, 2 * i2 + 1
                    n += 1
                    mprev = 2 if qb == 0 else 1
                    for (bS, kc0, mi, P_, rP, rkk) in ((bA, qb * 128, mprev, Pp, r_Pp, rk[(qb * 128) // 512 + (0 if qb % 4 else 0)]),
                                                      (bB, (qb + 1) * 128, 0, Pc, r_Pc, None)):
                        kcs = slice(kc0, kc0 + 128)
                        rkr = rk[0] if kc0 < 128 else rk[1 + (kc0 - 128) // 512]
                        S.op("pe", (lambda bS=bS, ps=ps, kcs=kcs, qs=qs: lambda e: e.matmul(bk_[bS][:], lhsT=kT[ps, kcs], rhs=qT[ps, :, qs], start=True, stop=False))(),
                             reads=[rkr] + [rq[i][qb // 4] for i in range(4)], writes=[rb[bS]])
                        S.op("pe", (lambda bS=bS, mi=mi: lambda e: e.matmul(bk_[bS][:], lhsT=b.cmb[:, CM["ident"], :], rhs=nm[:, mi, :, :], start=False, stop=True))(),
                             reads=[r_nm, b.r_cm], writes=[rb[bS]])
                        S.op("act", (lambda bS=bS, P_=P_, i2=i2: lambda e: e.activation(out=P_[i2][:], in_=bk_[bS][:], func=AF.Exp, scale=0.125))(),
                             reads=[rb[bS]], writes=[rP[i2]])
                    for (bO, lo) in ((bN, None), (bD, "ones")):
                        for si, (P_, rP, vtile) in enumerate(((Pp, r_Pp, qb), (Pc, r_Pc, qb + 1))):
                            lhs = vt[:, vtile, 64 * g:64 * g + 64] if lo is None else b.cmb[:, CM["ones"], 0:64]
                            rvr = rv[0] if vtile == 0 else rv[1 + (vtile - 1) // 4]
                            S.op("pe", (lambda bO=bO, ps=ps, lhs=lhs, P_=P_, i2=i2, si=si: lambda e: e.matmul(bk_[bO][ps, :], lhsT=lhs, rhs=P_[i2][:], start=(si == 0), stop=(si == 1)))(),
                                 reads=[rvr, rP[i2], b.r_cm], writes=[rb[bO]])
                j2 = qb % 2
                S.op("dve", (lambda bD=bD, j2=j2: lambda e: e.tensor_tensor(out=dn[j2][:].rearrange("p (a c) -> p a c", c=128), in0=bk_[bD][:].rearrange("p (a c) -> p a c", c=128),
                                                                        in1=es[:, 0:4].unsqueeze(2).to_broadcast([128, 4, 128]), op=ALU.add))(),
                     reads=[rb[bD], r_es], writes=[r_dn[j2]])
                S.op("dve", (lambda j2=j2: lambda e: e.reciprocal(out=dn[j2][:], in_=dn[j2][:]))(), reads=[r_dn[j2]], writes=[r_dn[j2]])
                S.op("dve", (lambda bN=bN, j2=j2, qs=qs: lambda e: e.tensor_tensor(out=ym[:, 0:4, qs], in0=bk_[bN][:].rearrange("p (a c) -> p a c", c=128),
                                                                               in1=dn[j2][:].rearrange("p (a c) -> p a c", c=128), op=ALU.mult))(),
                     reads=[rb[bN], r_dn[j2]], writes=[rym[i][qb // 4] for i in range(4)])


def persist_io(b, what, name, ap, regs, shape, dt):
    S = b.S
    if what == "dump":
        xo = b.nc.dram_tensor(f"x_p_{name}", list(shape), dt, kind="ExternalOutput").ap()
        r_o = S.region()
        d1 = S.dsem(f"po_{name}")
        S.dma("sp", d1, lambda e: e.dma_start(out=xo, in_=ap), reads=list(regs), writes=[r_o])
        b.out_regs.append(r_o)
        b.out_names.append(f"x_p_{name}")
    else:
        xi = b.nc.dram_tensor(f"l_p_{name}", list(shape), dt, kind="ExternalInput").ap()
        b.in_names.append(f"l_p_{name}")
        d1 = S.dsem()
        S.dma("sp", d1, lambda e: e.dma_start(out=ap, in_=xi), writes=list(regs))


def dbg_stop(b, tag):
    if tag not in b.parts:
        return False
    S = b.S
    b.uid += 1
    xo = b.nc.dram_tensor(f"x_dbg{b.uid}", [128, 1], F32, kind="ExternalOutput").ap()
    r_o = S.region()
    d1 = S.dsem()
    S.dma("sp", d1, lambda e: e.dma_start(out=xo, in_=b.pv[:, 0:1]), reads=[b.r_pv], writes=[r_o])
    b.out_regs.append(r_o)
    b.out_names.append(f"x_dbg{b.uid}")
    return True


def mlstm_part(b, l, hp, h, rh, ym, rym):
    S = b.S
    w_in = b.D["w_in"]
    bk_ = b.banks
    rb = b.rb
    MB = 768
    NCH = 16
    with scope(b) as sb:
        numS = sb("numS", [128, T], F32)
        denS = sb("denS", [128, T], F32)
        r_num, r_den = S.regions(NCH), S.regions(NCH)
        qseg = sb("qseg", [128, T], BF16)
        r_qseg = S.regions(NCH)
        osig = sb("osig", [128, T], BF16)
        r_osig = S.regions(4)
        Cf = sb("Cf", [128, 128], F32)
        Cb = sb("Cb", [128, 128], BF16)
        r_Cf, r_Cb = S.regions(2)
        gtot = sb("gtot", [128, 1], F32)
        r_gtot = S.region()
        if b.mode != "finish":
            with scope(b) as sb1:
                qT = sb1("mqT", [128, T], BF16)
                kT = sb1("mkT", [128, T], BF16)
                r_qT, r_kT = S.regions(4), S.regions(4)
                vaug = sb1("vtokm", [128, NCH, 2, 64], BF16)
                r_vaug = S.regions(4)
                GR = sb1("GR", [128, T], F32)
                r_GR = S.regions(4)
                r_GRall = S.region()
                gb15 = sb1("gb15", [128, 1], F32)
                r_gb = S.region()
                go = LAY[("gbias", l)]
                S.op("dve", lambda e: e.tensor_scalar(out=gb15[:], in0=b.pv[:, go:go + 1], scalar1=1.0 / 15.0, scalar2=None, op0=ALU.mult), reads=[b.r_pv], writes=[r_gb])
                with scope(b) as sb2:
                    wm = sb2("wm", [128, 8, 4, 128], BF16)
                    wgt = sb2("wgt", [128, 8, 8], BF16)
                    r_wm, r_wgt = S.regions(2)
                    dm_, dg_ = S.dsem(), S.dsem()
                    for j in range(4):
                        S.dma("pool", dm_, (lambda j=j: lambda e: e.dma_start(out=wm[:, :, j, :], in_=wview(w_in, 0, DM, MB + j * 256 + hp * 128, 128)))(), writes=[r_wm])
                    S.dma("pool", dg_, lambda e: e.dma_start(out=wgt[:], in_=wview(w_in, 0, DM, MB + 1024, 8)), writes=[r_wgt])
                    raw = [sb2(f"raw{j}", [128, 515], F32) for j in range(2)]
                    r_raw = S.regions(2)
                    ctmp = sb2("ctmp", [128, 512], F32)
                    r_ctmp = S.region()
                    tail = sb2("mtail", [128, 6], F32)
                    r_tail = S.region()
                    n = 0
                    for j in range(2):
                        for k in range(8):
                            S.op("pe", (lambda j=j, k=k: lambda e: e.matmul(bk_[j][:, 0:128], lhsT=wm[:, k, j, :], rhs=h[:, k, T - 128:T], start=(k == 0), stop=(k == 7)))(),
                                 reads=[r_wm, rh[k][3]], writes=[rb[j]])
                        S.op("act", (lambda j=j: lambda e: e.copy(out=tail[:, 3 * j:3 * j + 3], in_=bk_[j][:, 125:128]))(), reads=[rb[j]], writes=[r_tail])
                    with scope(b) as sb3:
                        blk, r_blk = allgather(b, sb3, [(0, 6, tail[:], [r_tail])], 128, 6, F32, f"mls{hp}")
                        if blk is None:
                            return
                        for j in range(2):
                            select_prev(b, raw[j][:, 0:3], r_raw[j], blk, r_blk, 128, 3 * j, 3)
                    co = LAY[("conv", l)]
                    for tg in range(4):
                        ts = slice(tg * 512, (tg + 1) * 512)
                        for j in range(2):
                            bk = n % 4
                            n += 1
                            for k in range(8):
                                S.op("pe", (lambda j=j, k=k, ts=ts, bk=bk: lambda e: e.matmul(bk_[bk][:], lhsT=wm[:, k, j, :], rhs=h[:, k, ts], start=(k == 0), stop=(k == 7)))(),
                                     reads=[r_wm, rh[k][tg]], writes=[rb[bk]])
                            S.op("act", (lambda j=j, bk=bk: lambda e: e.copy(out=raw[j][:, 3:515], in_=bk_[bk][:]))(), reads=[rb[bk], r_raw[j]], writes=[r_raw[j]])
                            tile_ = j * 2 + hp
                            wc = [b.pv[:, co + tile_ * 4 + tap:co + tile_ * 4 + tap + 1] for tap in range(4)]
                            S.op("dve", (lambda j=j, wc=wc: lambda e: e.tensor_scalar(out=ctmp[:], in0=raw[j][:, 3:515], scalar1=wc[3], scalar2=None, op0=ALU.mult))(),
                                 reads=[r_raw[j], b.r_pv], writes=[r_ctmp])
                            for tap in range(3):
                                S.op("dve", (lambda j=j, wc=wc, tap=tap: lambda e: e.scalar_tensor_tensor(out=ctmp[:], in0=raw[j][:, tap:tap + 512], scalar=wc[tap], in1=ctmp[:], op0=ALU.mult, op1=ALU.add))(),
                                     reads=[r_raw[j], r_ctmp, b.r_pv], writes=[r_ctmp])
                            S.op("dve", (lambda j=j: lambda e: e.tensor_copy(out=raw[j][:, 0:3], in_=raw[j][:, 512:515]))(), reads=[r_raw[j]], writes=[r_raw[j]])
                            if j == 0:
                                S.op("act", lambda e: e.activation(out=ctmp[:], in_=ctmp[:], func=AF.Silu), reads=[r_ctmp], writes=[r_ctmp])
                                S.op("dve", (lambda ts=ts: lambda e: e.tensor_scalar(out=qT[:, ts], in0=ctmp[:], scalar1=0.125, scalar2=None, op0=ALU.mult))(), reads=[r_ctmp], writes=[r_qT[tg]])
                            else:
                                S.op("act", (lambda ts=ts: lambda e: e.activation(out=kT[:, ts], in_=ctmp[:], func=AF.Silu))(), reads=[r_ctmp], writes=[r_kT[tg]])
                    if dbg_stop(b, "1"):
                        return
                    for tg in range(4):
                        bk = n % 8
                        n += 1
                        ts = slice(tg * 512, (tg + 1) * 512)
                        for k in range(8):
                            S.op("pe", (lambda k=k, ts=ts, bk=bk: lambda e: e.matmul(bk_[bk][:], lhsT=wm[:, k, 3, :], rhs=h[:, k, ts], start=(k == 0), stop=(k == 7)))(),
                                 reads=[r_wm, rh[k][tg]], writes=[rb[bk]])
                        S.op("act", (lambda ts=ts, bk=bk: lambda e: e.activation(out=osig[:, ts], in_=bk_[bk][:], func=AF.Sigmoid))(), reads=[rb[bk]], writes=[r_osig[tg]])
                    if dbg_stop(b, "2"):
                        return
                    for tg in range(4):
                        bk = n % 8
                        n += 1
                        ts = slice(tg * 512, (tg + 1) * 512)
                        for (p0, c0) in ((0, 0), (32, 4)):
                            for k in range(8):
                                S.op("pe", (lambda k=k, ts=ts, bk=bk, p0=p0, c0=c0: lambda e: e.matmul(bk_[bk][p0:p0 + 4, :], lhsT=wgt[:, k, c0:c0 + 4], rhs=h[:, k, ts], start=(k == 0), stop=(k == 7)))(),
                                     reads=[r_wgt, rh[k][tg]], writes=[rb[bk]])
                        for p0 in (0, 32):
                            S.op("act", (lambda ts=ts, bk=bk, p0=p0: lambda e: e.activation(out=GR[p0:p0 + 4, ts], in_=bk_[bk][p0:p0 + 4, :], func=AF.Tanh, bias=gb15[p0:p0 + 4, 0:1], scale=1.0 / 15.0))(),
                                 reads=[rb[bk], r_gb], writes=[r_GR[tg]])
                    if dbg_stop(b, "3"):
                        return
                    for g4 in range(4):
                        bk = n % 8
                        n += 1
                        for tt4 in range(4):
                            tt = g4 * 4 + tt4
                            for k in range(8):
                                S.op("pe", (lambda tt4=tt4, tt=tt, k=k, bk=bk: lambda e: e.matmul(bk_[bk][:, tt4 * 128:(tt4 + 1) * 128], lhsT=h[:, k, tt * 128:(tt + 1) * 128], rhs=wm[:, k, 2, :],
                                                                                          start=(k == 0), stop=(k == 7)))(), reads=[r_wm, rh[k][g4]], writes=[rb[bk]])
                        S.op("act", (lambda g4=g4, bk=bk: lambda e: e.copy(out=vaug[:, g4 * 4:g4 * 4 + 4, :, :], in_=bk_[bk][:].rearrange("p (a c d) -> p a c d", a=4, c=2)))(),
                             reads=[rb[bk]], writes=[r_vaug[g4]])
                    if dbg_stop(b, "4"):
                        return
                    allGR = r_GR
                    S.op("dve", lambda e: e.tensor_scalar(out=GR[0:4, :], in0=GR[0:4, :], scalar1=15.0, scalar2=None, op0=ALU.mult), reads=allGR, writes=[r_GRall])
                    S.op("act", lambda e: e.activation(out=GR[32:36, :], in_=GR[32:36, :], func=AF.Exp, scale=-15.0), reads=allGR, writes=[r_GRall])
                    S.op("act", lambda e: e.activation(out=GR[32:36, :], in_=GR[32:36, :], func=AF.Ln, bias=pvc(b, "one")[32:36, :], scale=1.0), reads=[r_GRall, b.r_pv], writes=[r_GRall])
                    S.op("dve", lambda e: e.tensor_scalar(out=GR[32:36, :], in0=GR[32:36, :], scalar1=-0.5, scalar2=None, op0=ALU.mult), reads=[r_GRall], writes=[r_GRall])
                    S.op("dve", lambda e: e.tensor_tensor_scan(out=GR[32:36, :], data0=GR[32:36, :], data1=GR[32:36, :], initial=0.0, op0=ALU.add, op1=ALU.add), reads=[r_GRall], writes=[r_GRall])
                if dbg_stop(b, "5"):
                    return
                with scope(b) as sb2:
                    S.op("pool", lambda e: e.memset(Cf[:], 0.0), writes=[r_Cf])
                    S.op("pool", lambda e: e.memset(Cb[:], 0.0), writes=[r_Cb])
                    negG = [sb2(f"negG{i}", [128, 1], F32) for i in range(2)]
                    r_negG = S.regions(2)
                    S.op("pool", lambda e: e.memset(negG[1][:], 0.0), writes=[r_negG[1]])
                    itok = [sb2(f"itok{i}", [128, 8], F32) for i in range(2)]
                    atok = [sb2(f"atok{i}", [128, 4], F32) for i in range(2)]
                    r_itok, r_atok = S.regions(2), S.regions(2)
                    eT = [sb2(f"eT{i}", [128, 2, 128], F32) for i in range(2)]
                    r_eT = S.regions(2)
                    PT = [sb2(f"PT{i}", [128, 2, 128], BF16) for i in range(2)]
                    r_PT = S.regions(2)
                    E1 = [sb2(f"E1{i}", [128, 128], F32) for i in range(2)]
                    E2 = [sb2(f"E2{i}", [128, 128], F32) for i in range(2)]
                    r_E1, r_E2 = S.regions(2), S.regions(2)
                    qh = [sb2(f"qh{i}", [128, 128], BF16) for i in range(2)]
                    r_qh = S.regions(2)
                    kh = [sb2(f"kh{i}", [128, 128], BF16) for i in range(2)]
                    r_kh = S.regions(2)
                    for j in range(NCH):
                        i2 = j % 2
                        cs = slice(j * 128, (j + 1) * 128)
                        g4 = j // 4
                        S.op("pe", (lambda cs=cs: lambda e: e.matmul(bk_[0][:, 0:4], lhsT=GR[0:4, cs], rhs=b.cm[0:4, CM["ident"], 0:4], start=True, stop=True))(), reads=[r_GRall, b.r_cm], writes=[rb[0]])
                        S.op("pe", (lambda cs=cs: lambda e: e.matmul(bk_[0][:, 4:8], lhsT=GR[32:36, cs], rhs=b.cm[32:36, CM["ident"], 32:36], start=True, stop=True))(), reads=[r_GRall, b.r_cm], writes=[rb[0]])
                        S.op("dve", (lambda i2=i2: lambda e: e.tensor_copy(out=itok[i2][:], in_=bk_[0][:, 0:8]))(), reads=[rb[0]], writes=[r_itok[i2]])
                        S.op("dve", (lambda i2=i2: lambda e: e.tensor_tensor(out=atok[i2][:], in0=itok[i2][:, 0:4], in1=itok[i2][:, 4:8], op=ALU.subtract))(), reads=[r_itok[i2]], writes=[r_atok[i2]])
                        if j == 1 and dbg_stop(b, "W"):
                            return
                        for hh in range(2):
                            hd = 2 * hp + hh
                            ps = slice(64 * hh, 64 * hh + 64)
                            S.op("pe", (lambda hh=hh, hd=hd, cs=cs: lambda e: e.matmul(bk_[1][:, hh * 128:(hh + 1) * 128], lhsT=b.cm[32:36, CM["rowsel"] + hd, :], rhs=GR[32:36, cs], start=True, stop=False))(),
                                 reads=[r_GRall, b.r_cm], writes=[rb[1]])
                            S.op("pe", (lambda hh=hh: lambda e: e.matmul(bk_[1][:, hh * 128:(hh + 1) * 128], lhsT=b.cm[:, CM["ident"], :], rhs=b.cm[:, CM["ncur"], :], start=False, stop=True))(),
                                 reads=[b.r_cm], writes=[rb[1]])
                            S.op("act", (lambda hh=hh, hd=hd, i2=i2: lambda e: e.activation(out=eT[i2][:, hh, :], in_=bk_[1][:, hh * 128:(hh + 1) * 128], func=AF.Exp, bias=atok[i2][:, hd:hd + 1], scale=1.0))(),
                                 reads=[rb[1], r_atok[i2]], writes=[r_eT[i2]])
                            S.op("pe", (lambda hh=hh, ps=ps, cs=cs: lambda e: e.matmul(bk_[2][:, hh * 128:(hh + 1) * 128], lhsT=kT[ps, cs], rhs=qT[ps, cs], start=True, stop=True))(),
                                 reads=[r_kT[g4], r_qT[g4]], writes=[rb[2]])
                            S.op("pe", (lambda hh=hh, hd=hd, ps=ps, cs=cs: lambda e: e.matmul(bk_[3][ps, 0:128], lhsT=b.cm[32:36, CM["rowsel"] + hd, 0:64], rhs=GR[32:36, cs], start=True, stop=True))(),
                                 reads=[r_GRall, b.r_cm], writes=[rb[3]])
                        if j == 1 and dbg_stop(b, "Y"):
                            return
                        S.op("dve", (lambda i2=i2: lambda e: e.tensor_tensor(out=PT[i2][:], in0=bk_[2][:, 0:256].rearrange("p (a c) -> p a c", c=128), in1=eT[i2][:], op=ALU.mult))(),
                             reads=[rb[2], r_eT[i2]], writes=[r_PT[i2]])
                        S.op("act", (lambda i2=i2: lambda e: e.activation(out=E1[i2][:], in_=bk_[3][:, 0:128], func=AF.Exp, bias=negG[1 - i2][:, 0:1], scale=1.0))(),
                             reads=[rb[3], r_negG[1 - i2]], writes=[r_E1[i2]])
                        S.op("act", (lambda i2=i2: lambda e: e.activation(out=E2[i2][:], in_=bk_[3][:, 0:128], func=AF.Exp))(), reads=[rb[3]], writes=[r_E2[i2]])
                        S.op("dve", (lambda i2=i2: lambda e: e.tensor_scalar(out=negG[i2][:], in0=bk_[3][:, 127:128], scalar1=-1.0, scalar2=None, op0=ALU.mult))(), reads=[rb[3]], writes=[r_negG[i2]])
                        if j == NCH - 1:
                            S.op("dve", lambda e: e.tensor_copy(out=gtot[:], in_=bk_[3][:, 127:128]), reads=[rb[3]], writes=[r_gtot])
                        S.op("dve", (lambda i2=i2, cs=cs: lambda e: e.tensor_tensor(out=qh[i2][:], in0=qT[:, cs], in1=E1[i2][:], op=ALU.mult))(), reads=[r_qT[g4], r_E1[i2]], writes=[r_qh[i2]])
                        S.op("pool", (lambda i2=i2, cs=cs: lambda e: e.tensor_tensor(out=qseg[:, cs], in0=qT[:, cs], in1=E2[i2][:], op=ALU.mult))(), reads=[r_qT[g4], r_E2[i2]], writes=[r_qseg[j]])
                        if j == 1 and dbg_stop(b, "Z"):
                            return
                        S.op("pe", (lambda cs=cs: lambda e: e.matmul(bk_[6][:, 0:128], lhsT=kT[:, cs], rhs=b.cmb[:, CM["ident"], :], start=True, stop=True))(), reads=[r_kT[g4], b.r_cm], writes=[rb[6]])
                        for hh in range(2):
                            S.op("dve", (lambda hh=hh, i2=i2: lambda e: e.tensor_scalar(out=kh[i2][:, hh * 64:(hh + 1) * 64], in0=bk_[6][:, hh * 64:(hh + 1) * 64],
                                                                                    scalar1=eT[i2][:, hh, 127:128], scalar2=None, op0=ALU.mult))(), reads=[rb[6], r_eT[i2]], writes=[r_kh[i2]])
                        for hh in range(2):
                            ps = slice(64 * hh, 64 * hh + 64)
                            S.op("pe", (lambda hh=hh, ps=ps, j=j, i2=i2: lambda e: e.matmul(bk_[4][ps, 0:128], lhsT=vaug[:, j, hh, :], rhs=PT[i2][:, hh, :], start=True, stop=False))(),
                                 reads=[r_vaug[g4], r_PT[i2]], writes=[rb[4]])
                            S.op("pe", (lambda hh=hh, ps=ps, i2=i2: lambda e: e.matmul(bk_[4][ps, 0:128], lhsT=Cb[ps, 0:64], rhs=qh[i2][ps, :], start=False, stop=True))(),
                                 reads=[r_Cb, r_qh[i2]], writes=[rb[4]])
                            S.op("pe", (lambda hh=hh, ps=ps, j=j, i2=i2: lambda e: e.matmul(bk_[5][ps, 0:128], lhsT=b.cmb[:, CM["ones"], 0:64], rhs=PT[i2][:, hh, :], start=True, stop=False))(),
                                 reads=[b.r_cm, r_PT[i2]], writes=[rb[5]])
                            S.op("pe", (lambda hh=hh, ps=ps, i2=i2: lambda e: e.matmul(bk_[5][ps, 0:128], lhsT=Cb[ps, 64:128], rhs=qh[i2][ps, :], start=False, stop=True))(),
                                 reads=[r_Cb, r_qh[i2]], writes=[rb[5]])
                            S.op("pe", (lambda hh=hh, ps=ps, j=j, i2=i2: lambda e: e.matmul(bk_[7][ps, 0:64], lhsT=kh[i2][:, hh * 64:(hh + 1) * 64], rhs=vaug[:, j, hh, :], start=True, stop=True))(),
                                 reads=[r_kh[i2], r_vaug[g4]], writes=[rb[7]])
                            S.op("pe", (lambda hh=hh, ps=ps, j=j, i2=i2: lambda e: e.matmul(bk_[7][ps, 64:128], lhsT=kh[i2][:, hh * 64:(hh + 1) * 64], rhs=b.cmb[:, CM["ones"], 0:64], start=True, stop=True))(),
                                 reads=[r_kh[i2], b.r_cm], writes=[rb[7]])
                        if j == 1 and dbg_stop(b, "Q"):
                            return
                        S.op("act", (lambda cs=cs: lambda e: e.copy(out=numS[:, cs], in_=bk_[4][:, 0:128]))(), reads=[rb[4]], writes=[r_num[j]])
                        S.op("act", (lambda cs=cs: lambda e: e.copy(out=denS[:, cs], in_=bk_[5][:, 0:128]))(), reads=[rb[5]], writes=[r_den[j]])
                        S.op("dve", (lambda i2=i2: lambda e: e.scalar_tensor_tensor(out=Cf[:], in0=Cf[:], scalar=E1[i2][:, 127:128], in1=bk_[7][:, 0:128], op0=ALU.mult, op1=ALU.add))(),
                             reads=[r_Cf, r_E1[i2], rb[7]], writes=[r_Cf])
                        S.op("act", lambda e: e.copy(out=Cb[:], in_=Cf[:]), reads=[r_Cf], writes=[r_Cb])
                        if j == 0 and dbg_stop(b, "6"):
                            return
                        if j == 1 and dbg_stop(b, "7"):
                            return
                        if j == NCH - 1 and dbg_stop(b, "8"):
                            return
        else:
            persist_io(b, "load", f"mnum{hp}", numS[:], r_num, [128, T], F32)
            persist_io(b, "load", f"mden{hp}", denS[:], r_den, [128, T], F32)
            persist_io(b, "load", f"mqsg{hp}", qseg[:], r_qseg, [128, T], BF16)
            persist_io(b, "load", f"mosg{hp}", osig[:], r_osig, [128, T], BF16)
        if b.mode == "local":
            persist_io(b, "dump", f"mnum{hp}", numS[:], r_num, [128, T], F32)
            persist_io(b, "dump", f"mden{hp}", denS[:], r_den, [128, T], F32)
            persist_io(b, "dump", f"mqsg{hp}", qseg[:], r_qseg, [128, T], BF16)
            persist_io(b, "dump", f"mosg{hp}", osig[:], r_osig, [128, T], BF16)
        with scope(b) as sb1:
            blk, r_blk = allgather(b, sb1, [(0, 128, Cf[:], [r_Cf]), (128, 1, gtot[:], [r_gtot])], 128, 129, F32, f"mst{hp}")
            if blk is None:
                return
            dec = sb1("dec", [128, 8], F32)
            r_dec = S.region()
            S.op("act", lambda e: e.activation(out=dec[:], in_=blk[:, :, 128], func=AF.Exp), reads=[r_blk], writes=[r_dec])
            Cs = sb1("Cs", [128, 128], F32)
            Csb = sb1("Csb", [128, 128], BF16)
            tt_ = sb1("tt_", [128, 128], F32)
            r_Cs, r_Csb, r_tt = S.regions(3)
            S.op("pool", lambda e: e.memset(Cs[:], 0.0), writes=[r_Cs])
            for jc in range(7):
                S.op("dve", (lambda jc=jc: lambda e: e.scalar_tensor_tensor(out=tt_[:], in0=Cs[:], scalar=dec[:, jc:jc + 1], in1=blk[:, jc, 0:128], op0=ALU.mult, op1=ALU.add))(),
                     reads=[r_Cs, r_dec, r_blk], writes=[r_tt])
                S.op("dve", lambda e: e.tensor_tensor(out=tt_[:], in0=tt_[:], in1=Cs[:], op=ALU.subtract), reads=[r_tt, r_Cs], writes=[r_tt])
                S.op("dve", (lambda jc=jc: lambda e: e.scalar_tensor_tensor(out=Cs[:], in0=tt_[:], scalar=b.pc[:, 8 + jc:9 + jc], in1=Cs[:], op0=ALU.mult, op1=ALU.add))(),
                     reads=[r_tt, r_Cs, b.r_pc], writes=[r_Cs])
            S.op("act", lambda e: e.copy(out=Csb[:], in_=Cs[:]), reads=[r_Cs], writes=[r_Csb])
            hT = [sb1(f"hT{i}", [128, 512], F32) for i in range(2)]
            dd = [sb1(f"dd{i}", [128, 512], F32) for i in range(2)]
            sq = [sb1(f"msq{i}", [128, 512], BF16) for i in range(2)]
            tmp = sb1("mtmp", [128, 512], F32)
            rs = [sb1(f"mrs{i}", [128, 512], F32) for i in range(2)]
            r_hT, r_dd, r_sq, r_rs = S.regions(2), S.regions(2), S.regions(2), S.regions(2)
            r_tmp = S.region()
            go2 = LAY[("mlstm_norm", l)]
            for tg in range(4):
                i2 = tg % 2
                ts = slice(tg * 512, (tg + 1) * 512)
                chs = [r for r in range(tg * 4, tg * 4 + 4)]
                bN, bD, bQ = 0 + i2, 2 + i2, 4 + i2
                for hh in range(2):
                    ps = slice(64 * hh, 64 * hh + 64)
                    S.op("pe", (lambda ps=ps, ts=ts, bN=bN: lambda e: e.matmul(bk_[bN][ps, :], lhsT=Csb[ps, 0:64], rhs=qseg[ps, ts], start=True, stop=True))(),
                         reads=[r_Csb] + [r_qseg[c] for c in chs], writes=[rb[bN]])
                    S.op("pe", (lambda ps=ps, ts=ts, bD=bD: lambda e: e.matmul(bk_[bD][ps, :], lhsT=Csb[ps, 64:128], rhs=qseg[ps, ts], start=True, stop=True))(),
                         reads=[r_Csb] + [r_qseg[c] for c in chs], writes=[rb[bD]])
                S.op("dve", (lambda i2=i2, ts=ts, bN=bN: lambda e: e.tensor_tensor(out=hT[i2][:], in0=bk_[bN][:], in1=numS[:, ts], op=ALU.add))(), reads=[rb[bN]] + [r_num[c] for c in chs], writes=[r_hT[i2]])
                S.op("dve", (lambda i2=i2, ts=ts, bD=bD: lambda e: e.tensor_tensor(out=dd[i2][:], in0=bk_[bD][:], in1=denS[:, ts], op=ALU.add))(), reads=[rb[bD]] + [r_den[c] for c in chs], writes=[r_dd[i2]])
                S.op("dve", (lambda i2=i2: lambda e: e.scalar_tensor_tensor(out=dd[i2][:], in0=dd[i2][:], scalar=-1.0, in1=dd[i2][:], op0=ALU.mult, op1=ALU.max))(), reads=[r_dd[i2]], writes=[r_dd[i2]])
                S.op("dve", (lambda i2=i2: lambda e: e.tensor_scalar(out=dd[i2][:], in0=dd[i2][:], scalar1=1.0, scalar2=None, op0=ALU.max))(), reads=[r_dd[i2]], writes=[r_dd[i2]])
                S.op("dve", (lambda i2=i2: lambda e: e.reciprocal(out=dd[i2][:], in_=dd[i2][:]))(), reads=[r_dd[i2]], writes=[r_dd[i2]])
                S.op("dve", (lambda i2=i2: lambda e: e.tensor_tensor(out=hT[i2][:], in0=hT[i2][:], in1=dd[i2][:], op=ALU.mult))(), reads=[r_hT[i2], r_dd[i2]], writes=[r_hT[i2]])
                S.op("act", (lambda i2=i2: lambda e: e.activation(out=sq[i2][:], in_=hT[i2][:], func=AF.Square))(), reads=[r_hT[i2]], writes=[r_sq[i2]])
                S.op("pe", (lambda i2=i2, bQ=bQ: lambda e: e.matmul(bk_[bQ][:], lhsT=b.cmb[:, CM["blk"], :], rhs=sq[i2][:], start=True, stop=True))(), reads=[r_sq[i2], b.r_cm], writes=[rb[bQ]])
                rstd_from_sumsq(b, rs[i2][:], r_rs[i2], tmp[:], r_tmp, bk_[bQ][:], rb[bQ], 1.0 / 64.0, "eps1")
                S.op("pool", (lambda i2=i2: lambda e: e.tensor_tensor(out=hT[i2][:], in0=hT[i2][:], in1=rs[i2][:], op=ALU.mult))(), reads=[r_hT[i2], r_rs[i2]], writes=[r_hT[i2]])
                S.op("dve", (lambda i2=i2, ts=ts: lambda e: e.scalar_tensor_tensor(out=ym[:, 4 + hp, ts], in0=hT[i2][:], scalar=b.pv[:, go2 + hp:go2 + hp + 1], in1=osig[:, ts], op0=ALU.mult, op1=ALU.mult))(),
                     reads=[r_hT[i2], r_osig[tg], b.r_pv], writes=[rym[4 + hp][tg]])


def rwkv_part(b, l, hp, h, rh, ym, rym):
    S = b.S
    w_in = b.D["w_in"]
    bk_ = b.banks
    rb = b.rb
    RB = 1800
    EM = float(np.exp(-0.5))
    cmf, cmb = b.cm, b.cmb

    def col(key, i=0):
        o = LAY[(key, l)] + i
        return b.pv[:, o:o + 1]

    last_row = {}

    def mm(out, lhsT, rhs, reads, writes, start=True, stop=True):
        tag = (lhsT.base_partition(), lhsT.partition_size())
        extra = []
        for w in writes:
            prev = last_row.get(id(w))
            if prev is not None and prev[0] != tag:
                extra.append(prev[1])
        tok = S.op("pe", lambda e: e.matmul(out, lhsT=lhsT, rhs=rhs, start=start, stop=stop), reads=reads, writes=writes, pe_wait=extra)
        for w in writes:
            last_row[id(w)] = (tag, tok)

    with scope(b) as sb:
        Yloc = sb("Yloc", [128, T], BF16)
        M2p = sb("M2p", [128, T], BF16)
        gT = sb("gT", [128, T], BF16)
        bv = sb("bv", [128, T], BF16)
        r_Yloc, r_M2p, r_gT, r_bv = S.regions(4), S.regions(4), S.regions(4), S.regions(4)
        Sf = sb("Sf", [128, 128], F32)
        Sb_ = sb("Sb", [128, 128], BF16)
        r_Sf, r_Sb = S.regions(2)
        if b.mode != "finish":
            with scope(b) as sb1:
                wr = sb1("wr", [128, 8, 5, 128], BF16)
                r_wr = S.region()
                dwr = S.dsem()
                for X, c0 in enumerate([RB + hp * 128, RB + 256 + hp * 128, RB + 512 + hp * 128, RB + 768, RB + 896]):
                    S.dma("pool", dwr, (lambda X=X, c0=c0: lambda e: e.dma_start(out=wr[:, :, X, :], in_=wview(w_in, 0, DM, c0, 128)))(), writes=[r_wr])
                wup = sb1("wup", [128, 128], BF16)
                aup = sb1("aup", [128, 128], BF16)
                gup = sb1("gup", [128, 128], BF16)
                r_lr = S.region()
                dlr = S.dsem()
                S.dma("pool", dlr, lambda e: e.dma_start(out=wup[0:64, :], in_=b.D["rwkv_w_up"][:, hp * 128:(hp + 1) * 128]), writes=[r_lr])
                S.dma("pool", dlr, lambda e: e.dma_start(out=aup[64:128, :], in_=b.D["rwkv_a_up"][:, hp * 128:(hp + 1) * 128]), writes=[r_lr])
                S.dma("pool", dlr, lambda e: e.dma_start(out=gup[:], in_=b.D["rwkv_g_up"][:, hp * 128:(hp + 1) * 128]), writes=[r_lr])
                omk = sb1("omk", [128, 1], F32)
                r_omk = S.region()
                S.op("dve", lambda e: e.tensor_scalar(out=omk[:], in0=col("rwkv_k_a", hp), scalar1=-1.0, scalar2=1.0, op0=ALU.mult, op1=ALU.add), reads=[b.r_pv], writes=[r_omk])
                carry = sb1("carry", [128, 5], F32)
                r_carry = S.regions(5)
                tail = sb1("tail", [128, 5], F32)
                r_tail = S.region()
                for X in range(5):
                    for k in range(8):
                        mm(bk_[X % 2][:, 0:128], wr[:, k, X, :], h[:, k, T - 128:T], [r_wr, rh[k][3]], [rb[X % 2]], start=(k == 0), stop=(k == 7))
                    S.op("act", (lambda X=X: lambda e: e.copy(out=tail[:, X:X + 1], in_=bk_[X % 2][:, 127:128]))(), reads=[rb[X % 2]], writes=[r_tail])
                with scope(b) as sb2:
                    blk, r_blk = allgather(b, sb2, [(0, 5, tail[:], [r_tail])], 128, 5, F32, f"rsh{hp}")
                    if blk is None:
                        return
                    r_call = S.region()
                    select_prev(b, carry[:], r_call, blk, r_blk, 128, 0, 5)
                for X in range(5):
                    r_carry[X] = r_call
                r_carry = [S.region() for _ in range(5)]
                for X in range(5):
                    r_carry[X].w = r_call.w
                S.op("pool", lambda e: e.memset(Sf[:], 0.0), writes=[r_Sf])
                S.op("dve", lambda e: e.tensor_copy(out=Sf[:, 64:128], in_=cmf[:, CM["id2"], 0:64]), reads=[b.r_cm, r_Sf], writes=[r_Sf])
                S.op("act", lambda e: e.copy(out=Sb_[:], in_=Sf[:]), reads=[r_Sf], writes=[r_Sb])
                raw = sb1("rraw", [128, 513], F32)
                r_raw = S.region()
                us = [sb1(f"us{i}", [128, 512], F32) for i in range(3)]
                r_us = S.regions(3)
                R = [sb1(f"R{i}", [128, 512], F32) for i in range(7)]
                rR = S.regions(7)
                twx = sb1("twx", [128, 512], BF16)
                sgb = sb1("sgb", [128, 512], BF16)
                sqk = sgb
                rkr = sgb
                r_twx, r_sgb = S.regions(2)
                r_sqk = r_sgb
                r_rkr = r_sgb
                base8 = sb1("base8", [128, 8], F32)
                cumC8 = sb1("cumC8", [128, 8], F32)
                ecum8 = sb1("ecum8", [128, 8], F32)
                r_base8, r_cumC8, r_ecum8 = S.regions(3)
                fm = [sb1(f"fm{i}", [128, 512], BF16) for i in range(6)]
                r_fm = S.regions(6)
                tok = [sb1(f"tok{i}", [128, 4, 128], BF16) for i in range(4)]
                r_tok = S.regions(4, 4)
                vb = sb1("vb", [128, 512], BF16)
                r_vb = S.region()
                Am = sb1("Am", [128, 4, 2, 64], BF16)
                AmT = sb1("AmT", [128, 2, 64], BF16)
                r_Am, r_AmT = S.regions(2)
                Pb = [sb1(f"Pb{i}", [128, 2, 2, 64], BF16) for i in range(2)]
                r_Pb = S.regions(2)
                Zf = sb1("Zf", [128, 2, 128], F32)
                Zb = sb1("Zb", [128, 2, 128], BF16)
                r_Zf, r_Zb = S.regions(2)
                M2b = sb1("M2b", [128, 128], BF16)
                M3Tb = sb1("M3Tb", [128, 2, 64], BF16)
                Laug = sb1("Laug", [128, 2, 128], F32)
                r_M2b, r_M3Tb, r_Laug = S.regions(3)
                S.op("pool", lambda e: e.memset(Laug[:], 0.0), writes=[r_Laug])
                S.op("pool", lambda e: e.memset(base8[:], 0.0), writes=[r_base8])
                MU = LAY[("rwkv_mu", l)]
                mucol = [MU + hp, MU + 2 + hp, MU + 4 + hp, MU + 6, MU + 7]
                for tg in range(4):
                    ts = slice(tg * 512, (tg + 1) * 512)
                    if tg == 0 and dbg_stop(b, "1"):
                        return
                    for X in range(5):
                        bkx = X % 2
                        for k in range(8):
                            mm(bk_[bkx][:], wr[:, k, X, :], h[:, k, ts], [r_wr, rh[k][tg]], [rb[bkx]], start=(k == 0), stop=(k == 7))
                        S.op("act", (lambda bkx=bkx: lambda e: e.copy(out=raw[:, 1:513], in_=bk_[bkx][:]))(), reads=[rb[bkx]], writes=[r_raw])
                        S.op("dve", (lambda X=X: lambda e: e.tensor_copy(out=raw[:, 0:1], in_=carry[:, X:X + 1]))(), reads=[r_carry[X], r_raw], writes=[r_raw])
                        S.op("dve", (lambda X=X: lambda e: e.tensor_copy(out=carry[:, X:X + 1], in_=raw[:, 512:513]))(), reads=[r_raw], writes=[r_carry[X]])
                        S.op("dve", lambda e: e.tensor_tensor(out=R[0][:], in0=raw[:, 0:512], in1=raw[:, 1:513], op=ALU.subtract), reads=[r_raw], writes=[rR[0]])
                        dstu = us[X] if X < 3 else R[1]
                        r_dstu = r_us[X] if X < 3 else rR[1]
                        S.op("dve", (lambda X=X, dstu=dstu: lambda e: e.scalar_tensor_tensor(out=dstu[:], in0=R[0][:], scalar=b.pv[:, mucol[X]:mucol[X] + 1], in1=raw[:, 1:513], op0=ALU.mult, op1=ALU.add))(),
                             reads=[rR[0], r_raw, b.r_pv], writes=[r_dstu])
                        if X == 3:
                            S.op("act", lambda e: e.activation(out=twx[0:64, :], in_=R[1][0:64, :], func=AF.Tanh), reads=[rR[1]], writes=[r_twx])
                            S.op("act", lambda e: e.copy(out=twx[64:128, :], in_=R[1][64:128, :]), reads=[rR[1]], writes=[r_twx])
                        if X == 4:
                            S.op("act", lambda e: e.activation(out=sgb[:], in_=R[1][:], func=AF.Sigmoid), reads=[rR[1]], writes=[r_sgb])
                    if tg == 0 and dbg_stop(b, "2"):
                        return
                    mm(bk_[2][:], wup[0:64, :], twx[0:64, :], [r_lr, r_twx], [rb[2]])
                    S.op("act", lambda e: e.activation(out=R[0][:], in_=bk_[2][:], func=AF.Sigmoid, bias=col("rwkv_w0", hp), scale=1.0), reads=[rb[2], b.r_pv], writes=[rR[0]])
                    S.op("dve", lambda e: e.tensor_scalar(out=R[0][:], in0=R[0][:], scalar1=-EM, scalar2=None, op0=ALU.mult), reads=[rR[0]], writes=[rR[0]])
                    mm(bk_[3][:], aup[64:128, :], twx[64:128, :], [r_lr, r_twx], [rb[3]])
                    S.op("act", lambda e: e.activation(out=R[1][:], in_=bk_[3][:], func=AF.Sigmoid, bias=col("rwkv_a0", hp), scale=1.0), reads=[rb[3], b.r_pv], writes=[rR[1]])
                    mm(bk_[4][:], gup[:], sgb[:], [r_lr, r_sgb], [rb[4]])
                    S.op("act", (lambda ts=ts: lambda e: e.copy(out=gT[:, ts], in_=bk_[4][:]))(), reads=[rb[4]], writes=[r_gT[tg]])
                    S.op("dve", lambda e: e.tensor_scalar(out=R[2][:], in0=us[1][:], scalar1=col("rwkv_k_k", hp), scalar2=None, op0=ALU.mult), reads=[r_us[1], b.r_pv], writes=[rR[2]])
                    S.op("act", lambda e: e.activation(out=sqk[:], in_=R[2][:], func=AF.Square), reads=[rR[2]], writes=[r_sqk])
                    mm(bk_[5][:], cmb[:, CM["blk"], :], sqk[:], [b.r_cm, r_sqk], [rb[5]])
                    S.op("act", lambda e: e.activation(out=R[3][:], in_=bk_[5][:], func=AF.Sqrt), reads=[rb[5]], writes=[rR[3]])
                    S.op("dve", lambda e: e.tensor_scalar(out=R[3][:], in0=R[3][:], scalar1=1e-12, scalar2=None, op0=ALU.max), reads=[rR[3]], writes=[rR[3]])
                    S.op("dve", lambda e: e.reciprocal(out=R[3][:], in_=R[3][:]), reads=[rR[3]], writes=[rR[3]])
                    S.op("dve", lambda e: e.tensor_tensor(out=R[2][:], in0=R[2][:], in1=R[3][:], op=ALU.mult), reads=[rR[2], rR[3]], writes=[rR[2]])
                    S.op("dve", lambda e: e.tensor_scalar(out=R[3][:], in0=R[1][:], scalar1=col("rwkv_k_a", hp), scalar2=omk[:, 0:1], op0=ALU.mult, op1=ALU.add), reads=[rR[1], r_omk, b.r_pv, rR[3]], writes=[rR[3]])
                    S.op("dve", lambda e: e.tensor_tensor(out=R[3][:], in0=R[3][:], in1=us[1][:], op=ALU.mult), reads=[rR[3], r_us[1]], writes=[rR[3]])
                    S.op("dve", lambda e: e.scalar_tensor_tensor(out=rkr[:], in0=us[0][:], scalar=col("rwkv_r_k", hp), in1=R[3][:], op0=ALU.mult, op1=ALU.mult), reads=[r_us[0], rR[3], b.r_pv], writes=[r_rkr])
                    mm(bk_[6][:], cmb[:, CM["blk"], :], rkr[:], [b.r_cm, r_rkr], [rb[6]])
                    S.op("dve", (lambda ts=ts: lambda e: e.tensor_tensor(out=bv[:, ts], in0=bk_[6][:], in1=us[2][:], op=ALU.mult))(), reads=[rb[6], r_us[2]], writes=[r_bv[tg]])
                    S.op("act", lambda e: e.copy(out=vb[:], in_=us[2][:]), reads=[r_us[2]], writes=[r_vb])
                    S.op("dve", lambda e: e.tensor_scalar(out=R[5][:], in0=R[0][:], scalar1=0.5, scalar2=None, op0=ALU.mult), reads=[rR[0]], writes=[rR[5]])
                    S.op("dve", lambda e: e.tensor_tensor_scan(out=R[4][:], data0=R[5][:], data1=R[5][:], initial=0.0, op0=ALU.add, op1=ALU.add), reads=[rR[5]], writes=[rR[4]])
                    S.op("dve", lambda e: e.tensor_copy(out=base8[:, 1:8], in_=R[4][:, 63:511:64]), reads=[rR[4], r_base8], writes=[r_base8])
                    S.op("dve", lambda e: e.tensor_tensor(out=R[4][:].rearrange("p (a c) -> p a c", c=64), in0=R[4][:].rearrange("p (a c) -> p a c", c=64),
                                                          in1=base8[:, 0:8].unsqueeze(2).to_broadcast([128, 8, 64]), op=ALU.subtract), reads=[rR[4], r_base8], writes=[rR[4]])
                    S.op("dve", lambda e: e.tensor_copy(out=cumC8[:], in_=R[4][:, 63:512:64]), reads=[rR[4]], writes=[r_cumC8])
                    S.op("act", lambda e: e.activation(out=R[5][:], in_=R[4][:], func=AF.Exp), reads=[rR[4]], writes=[rR[5]])
                    S.op("dve", lambda e: e.tensor_copy(out=ecum8[:], in_=R[5][:, 63:512:64]), reads=[rR[5]], writes=[r_ecum8])
                    S.op("dve", lambda e: e.tensor_tensor(out=fm[0][:], in0=us[0][:], in1=R[5][:], op=ALU.mult), reads=[r_us[0], rR[5]], writes=[r_fm[0]])
                    S.op("dve", lambda e: e.tensor_tensor(out=R[6][:], in0=R[4][:], in1=R[0][:], op=ALU.subtract), reads=[rR[4], rR[0]], writes=[rR[6]])
                    S.op("act", lambda e: e.activation(out=R[6][:], in_=R[6][:], func=AF.Exp), reads=[rR[6]], writes=[rR[6]])
                    S.op("dve", lambda e: e.scalar_tensor_tensor(out=fm[1][:], in0=R[2][:], scalar=-1.0, in1=R[6][:], op0=ALU.mult, op1=ALU.mult), reads=[rR[2], rR[6]], writes=[r_fm[1]])
                    S.op("dve", lambda e: e.tensor_tensor(out=R[1][:], in0=R[1][:], in1=R[2][:], op=ALU.mult), reads=[rR[1], rR[2]], writes=[rR[1]])
                    S.op("act", lambda e: e.activation(out=R[5][:], in_=R[4][:], func=AF.Exp, scale=-1.0), reads=[rR[4], r_ecum8, r_fm[0]], writes=[rR[5]])
                    S.op("dve", lambda e: e.tensor_tensor(out=fm[2][:], in0=R[1][:], in1=R[5][:], op=ALU.mult), reads=[rR[1], rR[5]], writes=[r_fm[2]])
                    S.op("dve", lambda e: e.tensor_tensor(out=fm[3][:], in0=R[3][:], in1=R[5][:], op=ALU.mult), reads=[rR[3], rR[5]], writes=[r_fm[3]])
                    S.op("dve", lambda e: e.tensor_tensor(out=R[6][:].rearrange("p (a c) -> p a c", c=64), in0=cumC8[:, 0:8].unsqueeze(2).to_broadcast([128, 8, 64]),
                                                          in1=R[4][:].rearrange("p (a c) -> p a c", c=64), op=ALU.subtract), reads=[rR[4], r_cumC8, rR[6], r_fm[1]], writes=[rR[6]])
                    S.op("act", lambda e: e.activation(out=R[6][:], in_=R[6][:], func=AF.Exp), reads=[rR[6]], writes=[rR[6]])
                    S.op("dve", lambda e: e.tensor_tensor(out=fm[4][:], in0=R[1][:], in1=R[6][:], op=ALU.mult), reads=[rR[1], rR[6]], writes=[r_fm[4]])
                    S.op("dve", lambda e: e.tensor_tensor(out=fm[5][:], in0=R[3][:], in1=R[6][:], op=ALU.mult), reads=[rR[3], rR[6]], writes=[r_fm[5]])
                    if tg == 0 and dbg_stop(b, "3"):
                        return
                    for qi, src, r_src in ((0, fm[1], r_fm[1]), (1, fm[4], r_fm[4]), (2, fm[5], r_fm[5]), (3, vb, r_vb)):
                        for tl in range(4):
                            S.op("pe", (lambda src=src, tl=tl, qi=qi: lambda e: e.matmul(bk_[7][:, tl * 128:(tl + 1) * 128], lhsT=src[:, tl * 128:(tl + 1) * 128], rhs=cmb[:, CM["ident"], :], start=True, stop=True))(),
                                 reads=[r_src, b.r_cm], writes=[rb[7]])
                        S.op("act" if qi % 2 == 0 else "dve", (lambda qi=qi: (lambda e: e.copy(out=tok[qi][:], in_=bk_[7][:].rearrange("p (a c) -> p a c", c=128))) if qi % 2 == 0 else
                                                               (lambda e: e.tensor_copy(out=tok[qi][:], in_=bk_[7][:].rearrange("p (a c) -> p a c", c=128))))(),
                             reads=[rb[7]], writes=r_tok[qi])
                    if tg == 0 and dbg_stop(b, "4"):
                        return
                    for tl in range(4):
                        gl = tg * 4 + tl
                        rt_, at_, bt_, kt_ = fm[0], fm[1], fm[2], fm[3]
                        for c in range(2):
                            pcs = slice(64 * c, 64 * c + 64)
                            tks = slice(tl * 128 + c * 64, tl * 128 + c * 64 + 64)
                            for hh in range(2):
                                ps = slice(64 * hh, 64 * hh + 64)
                                for wi, (lh, rh_, rl, rr_) in enumerate(((bt_, at_, r_fm[2], r_fm[1]), (kt_, at_, r_fm[3], r_fm[1]), (bt_, rt_, r_fm[2], r_fm[0]), (kt_, rt_, r_fm[3], r_fm[0]))):
                                    o0 = (wi * 2 + hh) * 64
                                    mm(bk_[0][pcs, o0:o0 + 64], lh[ps, tks], rh_[ps, tks], [rl, rr_], [rb[0]])
                                mm(bk_[1][pcs, hh * 64:(hh + 1) * 64], at_[ps, tks], bt_[ps, tks], [r_fm[1], r_fm[2]], [rb[1]])
                        S.op("dve", lambda e: e.tensor_tensor(out=Am[:].rearrange("p a b c -> p (a b c)"), in0=bk_[0][:], in1=cmb[:, CM["rm"]:CM["rm"] + 4, :].rearrange("p a c -> p (a c)"), op=ALU.mult),
                             reads=[rb[0], b.r_cm], writes=[r_Am])
                        S.op("dve", lambda e: e.tensor_tensor(out=AmT[:].rearrange("p b c -> p (b c)"), in0=bk_[1][:, 0:128], in1=cmf[:, CM["rmT"], :], op=ALU.mult), reads=[rb[1], b.r_cm], writes=[r_AmT])
                        for c in range(2):
                            pcs = slice(64 * c, 64 * c + 64)
                            for hh in range(2):
                                mm(bk_[4][pcs, hh * 64:(hh + 1) * 64], Am[pcs, 1, hh, :], tok[3][pcs, tl, hh * 64:(hh + 1) * 64], [r_Am, r_tok[3][tl]], [rb[4]])
                        S.op("dve", lambda e: e.tensor_copy(out=Zf[:, :, 0:64], in_=bk_[4][:, 0:128].rearrange("p (b c) -> p b c", c=64)), reads=[rb[4], r_Zf], writes=[r_Zf])
                        S.op("pool", (lambda tl=tl: lambda e: e.tensor_copy(out=Zf[:, :, 64:128], in_=tok[0][:, tl, :].rearrange("p (b c) -> p b c", c=64)))(), reads=[r_tok[0][tl], r_Zf], writes=[r_Zf])
                        S.op("act", lambda e: e.copy(out=Zb[:], in_=Zf[:]), reads=[r_Zf], writes=[r_Zb])
                        if gl == 0 and dbg_stop(b, "5"):
                            return
                        for lev in range(6):
                            if lev == 0:
                                Pl = lambda pcs, hh: Am[pcs, 0, hh, :]
                                PlT = lambda pcs, hh: AmT[pcs, hh, :]
                                rP = [r_Am, r_AmT]
                            else:
                                pbuf = Pb[lev % 2]
                                Pl = (lambda pbuf: lambda pcs, hh: pbuf[pcs, 0, hh, :])(pbuf)
                                PlT = (lambda pbuf: lambda pcs, hh: pbuf[pcs, 1, hh, :])(pbuf)
                                rP = [r_Pb[lev % 2]]
                            half = (lev % 2) * 256
                            for c in range(2):
                                pcs = slice(64 * c, 64 * c + 64)
                                for hh in range(2):
                                    mm(bk_[3][pcs, half + hh * 128:half + (hh + 1) * 128], Pl(pcs, hh), Zb[pcs, hh, :], rP + [r_Zb], [rb[3]])
                            if lev < 5:
                                for c in range(2):
                                    pcs = slice(64 * c, 64 * c + 64)
                                    for hh in range(2):
                                        mm(bk_[2][pcs, half + hh * 64:half + (hh + 1) * 64], PlT(pcs, hh), Pl(pcs, hh), rP, [rb[2]])
                                        mm(bk_[2][pcs, half + 128 + hh * 64:half + 128 + (hh + 1) * 64], Pl(pcs, hh), PlT(pcs, hh), rP, [rb[2]])
                                nb = Pb[(lev + 1) % 2]
                                S.op("act", (lambda nb=nb, half=half: lambda e: e.copy(out=nb[:].rearrange("p a b c -> p (a b c)"), in_=bk_[2][:, half:half + 256]))(), reads=[rb[2]], writes=[r_Pb[(lev + 1) % 2]])
                            S.op("dve", (lambda half=half: lambda e: e.tensor_tensor(out=Zf[:].rearrange("p b c -> p (b c)"), in0=bk_[3][:, half:half + 256], in1=Zf[:].rearrange("p b c -> p (b c)"), op=ALU.add))(),
                                 reads=[rb[3], r_Zf], writes=[r_Zf])
                            S.op("act", lambda e: e.copy(out=Zb[:], in_=Zf[:]), reads=[r_Zf], writes=[r_Zb])
                        if gl == 0 and dbg_stop(b, "6"):
                            return
                        for c in range(2):
                            pcs = slice(64 * c, 64 * c + 64)
                            for hh in range(2):
                                ps = slice(64 * hh, 64 * hh + 64)
                                mm(bk_[4][ps, 256 + c * 64:256 + (c + 1) * 64], Zb[pcs, hh, 64:128], Am[pcs, 2, hh, :], [r_Zb, r_Am], [rb[4]])
                                mm(bk_[5][ps, c * 64:(c + 1) * 64], Zb[pcs, hh, 64:128], tok[1][pcs, tl, hh * 64:(hh + 1) * 64], [r_Zb, r_tok[1][tl]], [rb[5]])
                        for c in range(2):
                            pcs = slice(64 * c, 64 * c + 64)
                            for hh in range(2):
                                ps = slice(64 * hh, 64 * hh + 64)
                                mm(bk_[5][ps, 128 + c * 64:128 + (c + 1) * 64], tok[2][pcs, tl, hh * 64:(hh + 1) * 64], tok[3][pcs, tl, hh * 64:(hh + 1) * 64], [r_tok[2][tl], r_tok[3][tl]], [rb[5]], start=True, stop=False)
                                mm(bk_[5][ps, 128 + c * 64:128 + (c + 1) * 64], tok[1][pcs, tl, hh * 64:(hh + 1) * 64], Zb[pcs, hh, 0:64], [r_tok[1][tl], r_Zb], [rb[5]], start=False, stop=True)
                        S.op("dve", (lambda tl=tl: lambda e: e.tensor_tensor(out=M2b[:], in0=bk_[4][:, 256:384], in1=fm[0][:, tl * 128:(tl + 1) * 128], op=ALU.add))(), reads=[rb[4], r_fm[0]], writes=[r_M2b])
                        for c in range(2):
                            ch = tl * 2 + c
                            S.op("dve", (lambda c=c, ch=ch: lambda e: e.scalar_tensor_tensor(out=M3Tb[:, c, :], in0=cmf[:, CM["id2"], 0:64], scalar=ecum8[:, ch:ch + 1], in1=bk_[5][:, c * 64:(c + 1) * 64], op0=ALU.mult, op1=ALU.add))(),
                                 reads=[rb[5], r_ecum8, b.r_cm], writes=[r_M3Tb])
                        S.op("act", lambda e: e.copy(out=Laug[:, :, 0:64], in_=bk_[5][:, 128:256].rearrange("p (a c) -> p a c", c=64)), reads=[rb[5], r_Laug], writes=[r_Laug])
                        for c in range(2):
                            pcs = slice(64 * c, 64 * c + 64)
                            for hh in range(2):
                                ps = slice(64 * hh, 64 * hh + 64)
                                oc = slice(c * 64, (c + 1) * 64)
                                mm(bk_[6][ps, oc], Zb[pcs, hh, 0:64], Am[pcs, 2, hh, :], [r_Zb, r_Am], [rb[6]], start=True, stop=False)
                                mm(bk_[6][ps, oc], tok[3][pcs, tl, hh * 64:(hh + 1) * 64], Am[pcs, 3, hh, :], [r_tok[3][tl], r_Am], [rb[6]], start=False, stop=False)
                                mm(bk_[6][ps, oc], Sb_[ps, 0:64], M2b[ps, oc], [r_Sb, r_M2b], [rb[6]], start=False, stop=True)
                                mm(bk_[6][ps, 128 + c * 64:128 + (c + 1) * 64], Sb_[ps, 64:128], M2b[ps, oc], [r_Sb, r_M2b], [rb[6]])
                                mm(bk_[7][ps, 0:128], M3Tb[ps, c, :], Sb_[ps, :], [r_M3Tb, r_Sb], [rb[7]])
                            S.op("dve", (lambda c=c: lambda e: e.tensor_tensor(out=Sf[:], in0=bk_[7][:, 0:128], in1=Laug[:, c, :], op=ALU.add))(), reads=[rb[7], r_Laug, r_Sf], writes=[r_Sf])
                            S.op("act", lambda e: e.copy(out=Sb_[:], in_=Sf[:]), reads=[r_Sf], writes=[r_Sb])
                        if gl == 0 and dbg_stop(b, "7"):
                            return
                        tsl = slice(gl * 128, (gl + 1) * 128)
                        S.op("act", (lambda tsl=tsl: lambda e: e.copy(out=Yloc[:, tsl], in_=bk_[6][:, 0:128]))(), reads=[rb[6]], writes=[r_Yloc[tg]])
                        S.op("dve", (lambda tsl=tsl: lambda e: e.tensor_copy(out=M2p[:, tsl], in_=bk_[6][:, 128:256]))(), reads=[rb[6]], writes=[r_M2p[tg]])
        else:
            persist_io(b, "load", f"ryl{hp}", Yloc[:], r_Yloc, [128, T], BF16)
            persist_io(b, "load", f"rm2{hp}", M2p[:], r_M2p, [128, T], BF16)
            persist_io(b, "load", f"rgt{hp}", gT[:], r_gT, [128, T], BF16)
            persist_io(b, "load", f"rbv{hp}", bv[:], r_bv, [128, T], BF16)
        if b.mode == "local":
            persist_io(b, "dump", f"ryl{hp}", Yloc[:], r_Yloc, [128, T], BF16)
            persist_io(b, "dump", f"rm2{hp}", M2p[:], r_M2p, [128, T], BF16)
            persist_io(b, "dump", f"rgt{hp}", gT[:], r_gT, [128, T], BF16)
            persist_io(b, "dump", f"rbv{hp}", bv[:], r_bv, [128, T], BF16)
        with scope(b) as sb1:
            blk, r_blk = allgather(b, sb1, [(0, 128, Sf[:], [r_Sf])], 128, 128, F32, f"rst{hp}")
            if blk is None:
                return
            blkb = sb1("blkb", [128, 8, 64], BF16)
            r_blkb = S.region()
            S.op("dve", lambda e: e.tensor_copy(out=blkb[:], in_=blk[:, :, 64:128]), reads=[r_blk], writes=[r_blkb])
            MTb = sb1("MTb", [128, 7, 64], BF16)
            r_MTb = S.region()
            for jc in range(7):
                for hh in range(2):
                    ps = slice(64 * hh, 64 * hh + 64)
                    S.op("pe", (lambda jc=jc, ps=ps: lambda e: e.matmul(bk_[0][ps, jc * 64:(jc + 1) * 64], lhsT=blkb[ps, jc, :], rhs=cmb[ps, CM["ident"], ps], start=True, stop=True))(),
                         reads=[r_blkb, b.r_cm], writes=[rb[0]])
            S.op("act", lambda e: e.copy(out=MTb[:].rearrange("p a c -> p (a c)"), in_=bk_[0][:, 0:448]), reads=[rb[0]], writes=[r_MTb])
            Ss = sb1("Ss", [128, 64], F32)
            Ssb = sb1("Ssb", [128, 64], BF16)
            tt_ = sb1("rtt", [128, 64], F32)
            r_Ss, r_Ssb, r_tt = S.regions(3)
            S.op("pool", lambda e: e.memset(Ss[:], 0.0), writes=[r_Ss])
            S.op("pool", lambda e: e.memset(Ssb[:], 0.0), writes=[r_Ssb])
            for jc in range(7):
                for hh in range(2):
                    ps = slice(64 * hh, 64 * hh + 64)
                    mm(bk_[1][ps, 0:64], MTb[ps, jc, :], Ssb[ps, :], [r_MTb, r_Ssb], [rb[1]])
                S.op("dve", (lambda jc=jc: lambda e: e.tensor_tensor(out=tt_[:], in0=bk_[1][:, 0:64], in1=blk[:, jc, 0:64], op=ALU.add))(), reads=[rb[1], r_blk], writes=[r_tt])
                S.op("dve", lambda e: e.tensor_tensor(out=tt_[:], in0=tt_[:], in1=Ss[:], op=ALU.subtract), reads=[r_tt, r_Ss], writes=[r_tt])
                S.op("dve", (lambda jc=jc: lambda e: e.scalar_tensor_tensor(out=Ss[:], in0=tt_[:], scalar=b.pc[:, 8 + jc:9 + jc], in1=Ss[:], op0=ALU.mult, op1=ALU.add))(), reads=[r_tt, r_Ss, b.r_pc], writes=[r_Ss])
                S.op("act", lambda e: e.copy(out=Ssb[:], in_=Ss[:]), reads=[r_Ss], writes=[r_Ssb])
            Y = [sb1(f"Yf{i}", [128, 512], F32) for i in range(2)]
            sq = [sb1(f"rsq{i}", [128, 512], BF16) for i in range(2)]
            rs = [sb1(f"rrs{i}", [128, 512], F32) for i in range(2)]
            tmp = sb1("rtmp", [128, 512], F32)
            r_Y, r_sq, r_rs = S.regions(2), S.regions(2), S.regions(2)
            r_tmp = S.region()
            for tg in range(4):
                i2 = tg % 2
                ts = slice(tg * 512, (tg + 1) * 512)
                bC, bM, bQ = 2 + i2, 4 + i2, 6 + i2
                for hh in range(2):
                    ps = slice(64 * hh, 64 * hh + 64)
                    mm(bk_[bC][ps, :], Ssb[ps, :], M2p[ps, ts], [r_Ssb, r_M2p[tg]], [rb[bC]])
                S.op("dve", (lambda i2=i2, ts=ts, bC=bC: lambda e: e.tensor_tensor(out=Y[i2][:], in0=bk_[bC][:], in1=Yloc[:, ts], op=ALU.add))(), reads=[rb[bC], r_Yloc[tg]], writes=[r_Y[i2]])
                mm(bk_[bM][:], cmf[:, CM["blk"], :], Y[i2][:], [b.r_cm, r_Y[i2]], [rb[bM]])
                S.op("dve", (lambda i2=i2, bM=bM: lambda e: e.scalar_tensor_tensor(out=Y[i2][:], in0=bk_[bM][:], scalar=-1.0 / 64.0, in1=Y[i2][:], op0=ALU.mult, op1=ALU.add))(), reads=[rb[bM], r_Y[i2]], writes=[r_Y[i2]])
                S.op("act", (lambda i2=i2: lambda e: e.activation(out=sq[i2][:], in_=Y[i2][:], func=AF.Square))(), reads=[r_Y[i2]], writes=[r_sq[i2]])
                mm(bk_[bQ][:], cmb[:, CM["blk"], :], sq[i2][:], [b.r_cm, r_sq[i2]], [rb[bQ]])
                rstd_from_sumsq(b, rs[i2][:], r_rs[i2], tmp[:], r_tmp, bk_[bQ][:], rb[bQ], 1.0 / 64.0, "gneps")
                S.op("pool", (lambda i2=i2: lambda e: e.tensor_tensor(out=Y[i2][:], in0=Y[i2][:], in1=rs[i2][:], op=ALU.mult))(), reads=[r_Y[i2], r_rs[i2]], writes=[r_Y[i2]])
                S.op("dve", (lambda i2=i2: lambda e: e.tensor_scalar(out=Y[i2][:], in0=Y[i2][:], scalar1=col("rwkv_ln_w", hp), scalar2=col("rwkv_ln_b", hp), op0=ALU.mult, op1=ALU.add))(), reads=[r_Y[i2], b.r_pv], writes=[r_Y[i2]])
                S.op("dve", (lambda i2=i2, ts=ts: lambda e: e.tensor_tensor(out=Y[i2][:], in0=Y[i2][:], in1=bv[:, ts], op=ALU.add))(), reads=[r_Y[i2], r_bv[tg]], writes=[r_Y[i2]])
                S.op("dve", (lambda i2=i2, ts=ts: lambda e: e.tensor_tensor(out=ym[:, 6 + hp, ts], in0=Y[i2][:], in1=gT[:, ts], op=ALU.mult))(), reads=[r_Y[i2], r_gT[tg]], writes=[rym[6 + hp][tg]])


def mix_layer(b, l, debug=False, parts="amr"):
    S = b.S
    bk_ = b.banks
    rb = b.rb
    w_out = b.D.get("w_out")
    with scope(b) as sb:
        ym = sb("ym", [128, 8, T], BF16)
        rym = S.regions(8, 4)
        with scope(b) as sb1:
            h = sb1("mh", [128, 8, T], BF16)
            rh = S.regions(8, 4)
            with scope(b) as sb2:
                prenorm(b, sb2, 0, T, "ln_mix_pre", l, h, rh)
            if "a" in parts:
                attn_part(b, l, h, rh, ym, rym)
            if "m" in parts:
                for hp in range(2):
                    mlstm_part(b, l, hp, h, rh, ym, rym)
            if "r" in parts:
                for hp in range(2):
                    rwkv_part(b, l, hp, h, rh, ym, rym)
        if debug:
            for c in range(8):
                for tg in range(4):
                    S.op("dve", (lambda c=c, tg=tg: lambda e: e.tensor_copy(out=b.xT[:, c, tg * 512:(tg + 1) * 512], in_=ym[:, c, tg * 512:(tg + 1) * 512]))(),
                         reads=[rym[c][tg], b.rx[c][tg]], writes=[b.rx[c][tg]])
            return
        for half in range(2):
            t0 = half * 1024
            with scope(b) as sb1:
                y = sb1("my", [128, 8, 1024], F32)
                ry = S.regions(8, 2)
                with scope(b) as sb2:
                    wo = [sb2(f"mwo{i}", [128, 8, 256], BF16) for i in range(2)]
                    rwo = S.regions(2)
                    dwo = [S.dsem() for _ in range(2)]
                    for jg in range(4):
                        s = jg % 2
                        cs_ = slice(jg * 256, (jg + 1) * 256)
                        for g in range(2):
                            S.dma("pool", dwo[s], (lambda s=s, g=g, cs_=cs_: lambda e: e.dma_start(out=wo[s][64 * g:64 * g + 64, 0:4, :], in_=w_out[g * 256:(g + 1) * 256, cs_].rearrange("(i d) n -> d i n", d=64)))(), writes=[rwo[s]])
                        S.dma("pool", dwo[s], (lambda s=s, cs_=cs_: lambda e: e.dma_start(out=wo[s][:, 4:8, :], in_=w_out[512:1024, cs_].rearrange("(c p) n -> p c n", p=128)))(), writes=[rwo[s]])
                        for ji in range(2):
                            j = jg * 2 + ji
                            par = j % 2
                            for k in range(8):
                                for tg in range(2):
                                    bk = 4 * par + tg
                                    g4 = half * 2 + tg
                                    S.op("pe", (lambda s=s, k=k, ji=ji, tg=tg, bk=bk: lambda e: e.matmul(bk_[bk][:], lhsT=wo[s][:, k, ji * 128:(ji + 1) * 128], rhs=ym[:, k, t0 + tg * 512:t0 + (tg + 1) * 512], start=(k == 0), stop=(k == 7)))(),
                                         reads=[rwo[s], rym[k][g4]], writes=[rb[bk]])
                            for tg in range(2):
                                bk = 4 * par + tg
                                if tg == 0:
                                    S.op("act", (lambda j=j, tg=tg, bk=bk: lambda e: e.copy(out=y[:, j, tg * 512:(tg + 1) * 512], in_=bk_[bk][:]))(), reads=[rb[bk]], writes=[ry[j][tg]])
                                else:
                                    S.op("dve", (lambda j=j, tg=tg, bk=bk: lambda e: e.tensor_copy(out=y[:, j, tg * 512:(tg + 1) * 512], in_=bk_[bk][:]))(), reads=[rb[bk]], writes=[ry[j][tg]])
                with scope(b) as sb2:
                    postnorm_residual(b, sb2, t0, 1024, "ln_mix_post", l, y, ry, 1.0)


_PROGS = {}


def get_prog(kind, parts="amr"):
    key = (kind, parts)
    if key not in _PROGS:
        _PROGS[key] = build_program(kind, parts)
    return _PROGS[key]


def run_prog(kind, L, inp, xT_list, gathered, parts="amr"):
    nc, b = get_prog(kind, parts)
    pv = pack_params(inp, L)
    cmat = const_mats()
    pos = np.asarray(inp["positions"], np.int32)
    nprev = np.ascontiguousarray(cmat.reshape(128, -1, 128)[:, CM["nprev"], :])
    maps = []
    for c in range(NCORES):
        sl = slice(c * T, (c + 1) * T)
        pcore = np.zeros((128, 32), np.float32)
        if c > 0:
            pcore[:, c - 1] = 1.0
        pcore[:, 8:8 + c] = 1.0
        m = {"xT": np.ascontiguousarray(xT_list[c], np.float32), "pos": np.ascontiguousarray(pos[:, sl]), "pvec": pv, "cmat": cmat,
             "pcore": pcore, "pcm": (np.full((128, 128), NEG, np.float32) if c == 0 else nprev)}
        full = {}
        for name in b.in_names:
            if name in m:
                full[name] = m[name]
            elif name == "pT":
                full[name] = np.ascontiguousarray(np.asarray(inp["p"], np.float32)[L, 0, sl].T)
            elif name.startswith("g_"):
                full[name] = gathered[name[2:]]
            elif name.startswith("l_p_"):
                full[name] = gathered["p_" + name[4:]][c]
            else:
                full[name] = np.ascontiguousarray(np.asarray(inp[name][L], np.float32))
        maps.append(full)
    res = run_bass_kernel_spmd(nc, maps, core_ids=list(range(NCORES)))
    return res.results


def gather_payloads(results, keys):
    return {k: np.ascontiguousarray(np.concatenate([np.asarray(r["x_" + k]) for r in results], axis=0)) for k in keys}


def kernel(**inputs):
    x = np.asarray(inputs["x"], np.float32)[0]
    xT = [np.ascontiguousarray(x[c * T:(c + 1) * T].T) for c in range(NCORES)]
    for L in range(DEPTH):
        rf = run_prog("F", L, inputs, xT, {})
        xT = [np.asarray(r["outT"]) for r in rf]
        ra = run_prog("A0", L, inputs, xT, {})
        g = gather_payloads(ra, HALO_KEYS)
        rb_ = run_prog("B", L, inputs, xT, g)
        g.update(gather_payloads(rb_, STATE_KEYS))
        for k_ in rb_[0]:
            if k_.startswith("x_p_"):
                g[k_[2:]] = [np.asarray(r[k_]) for r in rb_]
        rc = run_prog("C2", L, inputs, xT, g)
        xT = [np.asarray(r["outT"]) for r in rc]
        rg = run_prog("G", L, inputs, xT, {})
        xT = [np.asarray(r["outT"]) for r in rg]
    out = np.concatenate([t.T for t in xT], axis=0)
    return out[None].astype(np.float32)
```
